# Optimizing a Trainium2 kernel written in Bass

```python
import jax, jax.numpy as jnp
from jax import lax
import numpy as np

D_MODEL = 1024
BATCH = 4
SEQ = 4096
DEPTH = 2
DEC_BATCH = 32
DEC_SEQ = 8
PAST_LEN = 8192
PAGE_SIZE = 128

A_GROUPS = ((128, 1), (512, 4), (2048, 16))
N_GROUPS = 3
A_HEADS = 4
A_HD = 128
A_WIDTH = A_HEADS * A_HD
A_BLOCK = 128
B_HEADS = 4
B_DK = 128
B_DV = 128
B_CONV = 4
B_QK = B_HEADS * B_DK
B_V = B_HEADS * B_DV
B_CONV_CH = 2 * B_QK + B_V
C_HEADS = 4
C_DK = 64
C_DV = 128
C_QK = C_HEADS * C_DK
C_V = C_HEADS * C_DV
CHUNK = 64
ROPE_THETA = 10000.0
EPS = 1e-6
D_FF = -(-(8 * D_MODEL) // (3 * 256)) * 256
IN_SIZES = (3 * N_GROUPS * A_WIDTH, B_CONV_CH, B_V, B_HEADS, B_HEADS, C_QK, C_QK, C_V, C_V, 3 * D_MODEL)
N_IN = sum(IN_SIZES)
F32 = jnp.float32

kernel_name = 'hybrid_dilated_delta_retention_decoder_step'


def _rmsnorm(x, g):
    xf = x.astype(F32)
    y = xf * lax.rsqrt(jnp.mean(xf * xf, axis=-1, keepdims=True) + EPS)
    return (y * g.astype(F32)).astype(x.dtype)


def _l2norm(x):
    return x * lax.rsqrt(jnp.sum(x * x, axis=-1, keepdims=True) + EPS)


def _rope(x, pos):
    hd = x.shape[-1]
    inv = ROPE_THETA ** (-jnp.arange(0, hd, 2, dtype=F32) / hd)
    ang = pos.astype(F32)[:, None] * inv[None, :]
    shape = (ang.shape[0],) + (1,) * (x.ndim - 3) + (hd // 2,)
    cos = jnp.cos(ang).reshape(shape)
    sin = jnp.sin(ang).reshape(shape)
    xf = x.astype(F32)
    x1, x2 = xf[..., :hd // 2], xf[..., hd // 2:]
    return jnp.concatenate([x1 * cos - x2 * sin, x2 * cos + x1 * sin], axis=-1).astype(x.dtype)


def _causal_conv(x, buf, w):
    T = x.shape[1]
    xp = jnp.concatenate([buf.astype(x.dtype), x], axis=1)
    out = xp[:, 0:T] * w[0]
    for i in range(1, B_CONV):
        out = out + xp[:, i:i + T] * w[i]
    return out, xp[:, -(B_CONV - 1):]


def _to_chunks(x, C):
    B, T = x.shape[:2]
    nc = -(-T // C)
    x = jnp.pad(x, [(0, 0), (0, nc * C - T)] + [(0, 0)] * (x.ndim - 2))
    x = x.reshape((B, nc, C) + x.shape[2:])
    return x.transpose((1, 0, 3, 2) + tuple(range(4, x.ndim)))


def _from_chunks(x, T):
    nc, B, H, C, d = x.shape
    return x.transpose(1, 0, 3, 2, 4).reshape(B, nc * C, H, d)[:, :T]


def _decay_matrix(gc, incl):
    C = gc.shape[-1]
    i = jnp.arange(C)[:, None]
    j = jnp.arange(C)[None, :]
    mask = (i >= j) if incl else (i > j)
    diff = gc[..., :, None] - gc[..., None, :]
    return jnp.where(mask, jnp.exp(jnp.where(mask, diff, 0.0)), 0.0)


def _gated_delta_rule(q, k, v, beta, g, S0):
    T = q.shape[1]
    dv = v.shape[-1]
    C = min(CHUNK, T)
    qc, kc, vc = (_to_chunks(t.astype(F32), C) for t in (q, k, v))
    bc = _to_chunks(beta.astype(F32), C)
    gc = jnp.cumsum(_to_chunks(g.astype(F32), C), axis=-1)
    kb = kc * bc[..., None]
    eye = jnp.eye(C, dtype=F32)
    lower = _decay_matrix(gc, False) * jnp.einsum('nbhid,nbhjd->nbhij', kb, kc)
    rhs = jnp.concatenate([vc * bc[..., None], kb * jnp.exp(gc)[..., None]], axis=-1)
    sol = lax.linalg.triangular_solve(eye + lower, rhs, left_side=True, lower=True, unit_diagonal=True)
    u, w = sol[..., :dv], sol[..., dv:]
    attn = _decay_matrix(gc, True) * jnp.einsum('nbhid,nbhjd->nbhij', qc, kc)

    def step(S, xs):
        qi, ki, gi, ui, wi, ai = xs
        v_new = ui - jnp.einsum('bhck,bhkv->bhcv', wi, S)
        o = jnp.einsum('bhck,bhkv->bhcv', qi * jnp.exp(gi)[..., None], S) + jnp.einsum('bhij,bhjv->bhiv', ai, v_new)
        gl = gi[..., -1:]
        S = S * jnp.exp(gl)[..., None] + jnp.einsum('bhck,bhcv->bhkv', ki * jnp.exp(gl - gi)[..., None], v_new)
        return S, o

    S, o = lax.scan(step, S0.astype(F32), (qc, kc, gc, u, w, attn))
    return _from_chunks(o, T), S


def _retention(q, k, v, g, R0):
    T = q.shape[1]
    C = min(CHUNK, T)
    qc, kc, vc = (_to_chunks(t.astype(F32), C) for t in (q, k, v))
    gc = jnp.cumsum(_to_chunks(g.astype(F32), C), axis=-1)
    attn = _decay_matrix(gc, True) * jnp.einsum('nbhid,nbhjd->nbhij', qc, kc)
    intra = jnp.einsum('nbhij,nbhjv->nbhiv', attn, vc)

    def step(R, xs):
        qi, ki, vi, gi, oi = xs
        o = oi + jnp.einsum('bhck,bhkv->bhcv', qi * jnp.exp(gi)[..., None], R)
        gl = gi[..., -1:]
        R = R * jnp.exp(gl)[..., None] + jnp.einsum('bhck,bhcv->bhkv', ki * jnp.exp(gl - gi)[..., None], vi)
        return R, o

    R, o = lax.scan(step, R0.astype(F32), (qc, kc, vc, gc, intra))
    return _from_chunks(o, T), R


def _dilated_window_prompt(q, k, v, window, dil):
    B, T, H, hd = q.shape
    Ls = T // dil
    w = window // dil
    bq = min(A_BLOCK, Ls)
    nb = -(-Ls // bq)
    Lp = nb * bq

    def sub(x, front):
        x = x.reshape(B, Ls, dil, H, hd).transpose(0, 2, 1, 3, 4).astype(F32)
        return jnp.pad(x, ((0, 0), (0, 0), (front, Lp - Ls), (0, 0), (0, 0)))

    qb = sub(q, 0).reshape(B, dil, nb, bq, H, hd)
    idx = jnp.arange(nb)[:, None] * bq + jnp.arange(w + bq)[None, :]
    kb = jnp.take(sub(k, w), idx, axis=2)
    vb = jnp.take(sub(v, w), idx, axis=2)
    s = jnp.einsum('brnqhd,brnkhd->brnhqk', qb, kb) * (hd ** -0.5)
    qi = jnp.arange(bq)[:, None]
    kj = jnp.arange(w + bq)[None, :]
    dist = qi - kj + w
    kpos = jnp.arange(nb)[:, None, None] * bq + kj[None] - w
    valid = (dist >= 0) & (dist <= w) & (kpos >= 0)
    s = jnp.where(valid[:, None], s, -jnp.inf)
    m = jnp.max(s, axis=-1, keepdims=True)
    p = jnp.exp(s - m)
    den = jnp.sum(p, axis=-1)
    o = jnp.einsum('brnhqk,brnkhd->brnqhd', p, vb) / jnp.swapaxes(den, -1, -2)[..., None]
    lse = jnp.swapaxes(m[..., 0] + jnp.log(den), -1, -2)

    def unsub(x):
        x = x.reshape((B, dil, Lp) + x.shape[4:])[:, :, :Ls]
        return jnp.swapaxes(x, 1, 2).reshape((B, T) + x.shape[3:])

    return unsub(o), unsub(lse)


def _dilated_window_sample(q, k, v, kv_buf, window, dil):
    DB, S, H, hd = q.shape
    L = kv_buf.shape[1]
    kk = jnp.concatenate([kv_buf[:, :, 0].astype(k.dtype), k], axis=1).astype(F32)
    vv = jnp.concatenate([kv_buf[:, :, 1].astype(v.dtype), v], axis=1).astype(F32)
    n = window // dil + 1
    idx = L + jnp.arange(S)[:, None] - dil * jnp.arange(n)[None, :]
    valid = idx >= 0
    idx = jnp.maximum(idx, 0)
    kg = kk[:, idx]
    vg = vv[:, idx]
    s = jnp.einsum('bqhd,bqkhd->bhqk', q.astype(F32), kg) * (hd ** -0.5)
    s = jnp.where(valid, s, -jnp.inf)
    m = jnp.max(s, axis=-1, keepdims=True)
    p = jnp.exp(s - m)
    den = jnp.sum(p, axis=-1)
    o = jnp.einsum('bhqk,bqkhd->bqhd', p, vg) / jnp.swapaxes(den, 1, 2)[..., None]
    lse = jnp.swapaxes(m[..., 0] + jnp.log(den), 1, 2)
    return o, lse


def _layer(x, pos, kv_bufs, conv_buf, S0, R0, prompt, norm1_g, w_in, a_q_norm_g, a_k_norm_g, b_conv_w,
           b_a_log, b_dt_bias, b_out_norm_g, c_out_norm_g, w_out_a, w_out_b, w_out_c, w_out, norm2_g,
           w_ffn_in, w_ffn_out):
    Bsz, T, _ = x.shape
    h = _rmsnorm(x, norm1_g)
    proj = h @ w_in
    a_qkv, b_qkv, b_z, b_beta, b_a, c_q, c_k, c_v, c_z, gates = jnp.split(
        proj, np.cumsum(IN_SIZES)[:-1].tolist(), axis=-1)

    a_qkv = a_qkv.reshape(Bsz, T, 3, N_GROUPS, A_HEADS, A_HD)
    a_q = _rope(_rmsnorm(a_qkv[:, :, 0], a_q_norm_g), pos)
    a_k = _rope(_rmsnorm(a_qkv[:, :, 1], a_k_norm_g), pos)
    a_v = a_qkv[:, :, 2]
    outs, lses, new_kv = [], [], []
    for gi, (win, dil) in enumerate(A_GROUPS):
        qg, kg, vg = a_q[:, :, gi], a_k[:, :, gi], a_v[:, :, gi]
        if prompt:
            o, lse = _dilated_window_prompt(qg, kg, vg, win, dil)
            keep = min(win, T)
            new_kv.append(jnp.stack([kg[:, T - keep:], vg[:, T - keep:]], axis=2))
        else:
            o, lse = _dilated_window_sample(qg, kg, vg, kv_bufs[gi], win, dil)
            new_kv.append(jnp.stack([kg, vg], axis=2))
        outs.append(o)
        lses.append(lse)
    wts = jax.nn.softmax(jnp.stack(lses), axis=0)
    o_a = jnp.sum(wts[..., None] * jnp.stack(outs), axis=0).reshape(Bsz, T, A_WIDTH).astype(x.dtype)

    b_act, conv_new = _causal_conv(b_qkv, conv_buf, b_conv_w)
    b_act = jax.nn.silu(b_act)
    bq, bk, bv = jnp.split(b_act, [B_QK, 2 * B_QK], axis=-1)
    bq = _l2norm(bq.reshape(Bsz, T, B_HEADS, B_DK).astype(F32)) * (B_DK ** -0.5)
    bk = _l2norm(bk.reshape(Bsz, T, B_HEADS, B_DK).astype(F32))
    bv = bv.reshape(Bsz, T, B_HEADS, B_DV)
    beta = jax.nn.sigmoid(b_beta.astype(F32))
    g = -jnp.exp(b_a_log.astype(F32)) * jax.nn.softplus(b_a.astype(F32) + b_dt_bias.astype(F32))
    o_b, S_new = _gated_delta_rule(bq, bk, bv, beta, g, S0)
    o_b = _rmsnorm(o_b, b_out_norm_g) * jax.nn.silu(b_z.reshape(Bsz, T, B_HEADS, B_DV).astype(F32))
    o_b = o_b.reshape(Bsz, T, B_V).astype(x.dtype)

    cq = _rope(c_q.reshape(Bsz, T, C_HEADS, C_DK), pos)
    ck = _rope(c_k.reshape(Bsz, T, C_HEADS, C_DK), pos) * (C_DK ** -0.5)
    cv = c_v.reshape(Bsz, T, C_HEADS, C_DV)
    log_gamma = jnp.log1p(-jnp.exp2(-5.0 - jnp.arange(C_HEADS, dtype=F32)))
    o_c, R_new = _retention(cq, ck, cv, jnp.broadcast_to(log_gamma, (Bsz, T, C_HEADS)), R0)
    o_c = _rmsnorm(o_c, c_out_norm_g) * jax.nn.silu(c_z.reshape(Bsz, T, C_HEADS, C_DV).astype(F32))
    o_c = o_c.reshape(Bsz, T, C_V).astype(x.dtype)

    gts = jax.nn.sigmoid(gates.reshape(Bsz, T, 3, D_MODEL))
    merged = gts[:, :, 0] * (o_a @ w_out_a) + gts[:, :, 1] * (o_b @ w_out_b) + gts[:, :, 2] * (o_c @ w_out_c)
    x = x + (merged @ w_out).astype(x.dtype)

    ff_g, ff_u = jnp.split(_rmsnorm(x, norm2_g) @ w_ffn_in, 2, axis=-1)
    x = x + ((jax.nn.silu(ff_g) * ff_u) @ w_ffn_out).astype(x.dtype)
    return x, new_kv, conv_new, S_new, R_new


def setup_inputs(seed: int = 0) -> dict:
    key = jax.random.key(seed)
    ks = jax.random.split(key, 32)

    def nrm(k, shape, scale):
        return jax.random.normal(k, shape, F32) * scale

    lens = [min(w, PAST_LEN) for w, _ in A_GROUPS]
    dt = jnp.exp(jax.random.uniform(ks[14], (DEPTH, B_HEADS), F32, np.log(1e-3), np.log(1e-1)))
    return {
        'x_prompt': nrm(ks[0], (BATCH, SEQ, D_MODEL), 1.0),
        'x_sample': nrm(ks[1], (DEC_BATCH, DEC_SEQ, D_MODEL), 1.0),
        'cache_a_kv0': nrm(ks[2], (DEPTH, DEC_BATCH, lens[0], 2, A_HEADS, A_HD), 1.0),
        'cache_a_kv1': nrm(ks[3], (DEPTH, DEC_BATCH, lens[1], 2, A_HEADS, A_HD), 1.0),
        'cache_a_kv2': nrm(ks[4], (DEPTH, DEC_BATCH, lens[2], 2, A_HEADS, A_HD), 1.0),
        'state_b_conv': nrm(ks[5], (DEPTH, DEC_BATCH, B_CONV - 1, B_CONV_CH), 1.0),
        'state_b_S': nrm(ks[6], (DEPTH, DEC_BATCH, B_HEADS, B_DK, B_DV), 0.1),
        'state_c_R': nrm(ks[7], (DEPTH, DEC_BATCH, C_HEADS, C_DK, C_DV), 0.3),
        'norm1_g': 1.0 + nrm(ks[8], (DEPTH, D_MODEL), 0.02),
        'w_in': nrm(ks[9], (DEPTH, D_MODEL, N_IN), D_MODEL ** -0.5),
        'a_q_norm_g': 1.0 + nrm(ks[10], (DEPTH, A_HD), 0.02),
        'a_k_norm_g': 1.0 + nrm(ks[11], (DEPTH, A_HD), 0.02),
        'b_conv_w': nrm(ks[12], (DEPTH, B_CONV, B_CONV_CH), 0.5),
        'b_a_log': jnp.log(jax.random.uniform(ks[13], (DEPTH, B_HEADS), F32, 1.0, 16.0)),
        'b_dt_bias': jnp.log(jnp.expm1(dt)),
        'b_out_norm_g': 1.0 + nrm(ks[15], (DEPTH, B_DV), 0.02),
        'c_out_norm_g': 1.0 + nrm(ks[16], (DEPTH, C_DV), 0.02),
        'w_out_a': nrm(ks[17], (DEPTH, A_WIDTH, D_MODEL), A_WIDTH ** -0.5),
        'w_out_b': nrm(ks[18], (DEPTH, B_V, D_MODEL), B_V ** -0.5),
        'w_out_c': nrm(ks[19], (DEPTH, C_V, D_MODEL), C_V ** -0.5),
        'w_out': nrm(ks[20], (DEPTH, D_MODEL, D_MODEL), D_MODEL ** -0.5),
        'norm2_g': 1.0 + nrm(ks[21], (DEPTH, D_MODEL), 0.02),
        'w_ffn_in': nrm(ks[22], (DEPTH, D_MODEL, 2 * D_FF), D_MODEL ** -0.5),
        'w_ffn_out': nrm(ks[23], (DEPTH, D_FF, D_MODEL), D_FF ** -0.5),
    }


def reference(x_prompt, x_sample, cache_a_kv0, cache_a_kv1, cache_a_kv2, state_b_conv, state_b_S, state_c_R,
              norm1_g, w_in, a_q_norm_g, a_k_norm_g, b_conv_w, b_a_log, b_dt_bias, b_out_norm_g, c_out_norm_g,
              w_out_a, w_out_b, w_out_c, w_out, norm2_g, w_ffn_in, w_ffn_out):
    Bp, T = x_prompt.shape[:2]
    S = x_sample.shape[1]
    pos_p = jnp.arange(T)
    pos_s = PAST_LEN + jnp.arange(S)
    yp, ys = x_prompt, x_sample
    pk, sk = ([], [], []), ([], [], [])
    pc, pS, pR, sc, sS, sR = [], [], [], [], [], []
    for l in range(DEPTH):
        lw = (norm1_g[l], w_in[l], a_q_norm_g[l], a_k_norm_g[l], b_conv_w[l], b_a_log[l], b_dt_bias[l],
              b_out_norm_g[l], c_out_norm_g[l], w_out_a[l], w_out_b[l], w_out_c[l], w_out[l], norm2_g[l],
              w_ffn_in[l], w_ffn_out[l])
        yp, kv, cv, Sn, Rn = _layer(
            yp, pos_p, None, jnp.zeros((Bp, B_CONV - 1, B_CONV_CH), x_prompt.dtype),
            jnp.zeros((Bp, B_HEADS, B_DK, B_DV), F32), jnp.zeros((Bp, C_HEADS, C_DK, C_DV), F32), True, *lw)
        for gi in range(N_GROUPS):
            pk[gi].append(kv[gi])
        pc.append(cv)
        pS.append(Sn)
        pR.append(Rn)
        ys, kv, cv, Sn, Rn = _layer(
            ys, pos_s, (cache_a_kv0[l], cache_a_kv1[l], cache_a_kv2[l]), state_b_conv[l], state_b_S[l],
            state_c_R[l], False, *lw)
        for gi in range(N_GROUPS):
            sk[gi].append(kv[gi])
        sc.append(cv)
        sS.append(Sn)
        sR.append(Rn)
    new_a_kv0_prompt = jnp.stack(pk[0])
    new_a_kv1_prompt = jnp.stack(pk[1])
    new_a_kv2_prompt = jnp.stack(pk[2])
    new_b_conv_prompt = jnp.stack(pc)
    new_b_S_prompt = jnp.stack(pS)
    new_c_R_prompt = jnp.stack(pR)
    new_a_kv0_sample = jnp.stack(sk[0])
    new_a_kv1_sample = jnp.stack(sk[1])
    new_a_kv2_sample = jnp.stack(sk[2])
    new_b_conv_sample = jnp.stack(sc)
    new_b_S_sample = jnp.stack(sS)
    new_c_R_sample = jnp.stack(sR)
    return (yp, ys, new_a_kv0_prompt, new_a_kv1_prompt, new_a_kv2_prompt, new_b_conv_prompt, new_b_S_prompt,
            new_c_R_prompt, new_a_kv0_sample, new_a_kv1_sample, new_a_kv2_sample, new_b_conv_sample,
            new_b_S_sample, new_c_R_sample)
```

```python
import contextlib
import numpy as np
import concourse.bass as bass
import concourse.mybir as mybir
from concourse.bass_utils import run_bass_kernel_spmd

F32 = mybir.dt.float32
BF16 = mybir.dt.bfloat16
AF = mybir.ActivationFunctionType
ALU = mybir.AluOpType

PAST_LEN = 8192
EPS = 1e-6
A_OFF, BQKV, BZ, BBETA, BA, CQ, CK, CV, CZ, GOFF = 0, 4608, 6144, 6656, 6660, 6664, 6920, 7176, 7688, 8200
NEG = -30000.0
SHIFT = 12.0


class Cfg:
    def __init__(self, D=1024, DFF=2816, T=4096, NS=4, DEPTH=2, dbg=False, mixers="abc"):
        self.D, self.DFF, self.T, self.NS, self.DEPTH, self.dbg = D, DFF, T, NS, DEPTH, dbg
        self.mixers = mixers
        self.KD = D // 128
        self.KF = DFF // 128
        self.NTT = T // 128
        self.NT = self.NTT + 2
        self.NTOK = self.NT * 128
        self.NIN = GOFF + 3 * D


class Sched:
    ENGS = ("pe", "act", "dve", "pool", "sp")
    DMAQ = ("sp", "act", "pool")

    def __init__(self, nc, stack, nslots=6):
        self.nc = nc
        self.prog = {e: [] for e in self.ENGS}
        self.cnt = {e: 0 for e in self.ENGS}
        self.sem = {e: stack.enter_context(nc.semaphore("s_" + e)) for e in self.ENGS}
        self.nslots = nslots
        self.dsem = {q: [stack.enter_context(nc.semaphore("d_%s%d" % (q, i))) for i in range(nslots)] for q in self.DMAQ}
        self.dval = {q: [0] * nslots for q in self.DMAQ}
        self.dnext = {q: 0 for q in self.DMAQ}
        self.seen = {e: {} for e in self.ENGS}
        self.lastw = {}
        self.readers = {}

    def _semof(self, ev):
        if ev[0] == "E":
            return ("E", ev[1]), self.sem[ev[1]], ev[2]
        return ("D", ev[1], ev[2]), self.dsem[ev[1]][ev[2]], ev[3]

    def _emit_waits(self, e, waits):
        best = {}
        for ev in waits:
            key, sem, val = self._semof(ev)
            if self.seen[e].get(key, 0) >= val:
                continue
            if key not in best or best[key][1] < val:
                best[key] = (sem, val)
        for key, (sem, val) in best.items():
            self.seen[e][key] = val
            self.prog[e].append(("w", sem, val))

    def _deps(self, R, W):
        waits = []
        for k in R:
            w = self.lastw.get(k)
            if w is not None:
                waits.append(w)
            if k in PSUM_KEYS:
                rd = self.readers.get(k)
                if rd:
                    waits.extend(rd.values())
        for k in W:
            w = self.lastw.get(k)
            if w is not None:
                waits.append(w)
            rd = self.readers.get(k)
            if rd:
                waits.extend(rd.values())
        return waits

    def _record(self, ev, R, W):
        key = self._semof(ev)[0]
        for k in R:
            self.readers.setdefault(k, {})[key] = ev
        for k in W:
            self.lastw[k] = ev
            self.readers[k] = {}

    def op(self, e, fn, R=(), W=(), A=()):
        NOPS[0] += 1
        if NOPS[0] > LIMIT:
            return None
        self._emit_waits(e, self._deps(R, W))
        self.cnt[e] += 1
        self.prog[e].append(("i", fn, self.sem[e]))
        ev = ("E", e, self.cnt[e])
        self._record(ev, R, list(W) + list(A))
        return ev

    def dma(self, q, out, in_, R=(), W=(), **kw):
        NOPS[0] += 1
        if NOPS[0] > LIMIT:
            return None
        waits = self._deps(R, W)
        s = self.dnext[q]
        self.dnext[q] = (s + 1) % self.nslots
        if self.dval[q][s] > 0:
            waits.append(("D", q, s, self.dval[q][s]))
        self._emit_waits(q, waits)
        self.dval[q][s] += 16
        self.prog[q].append(("d", out, in_, kw, self.dsem[q][s]))
        ev = ("D", q, s, self.dval[q][s])
        self._record(ev, R, W)
        return ev

    def barrier(self):
        evs = [("E", e, self.cnt[e]) for e in self.ENGS if self.cnt[e] > 0]
        for q in self.DMAQ:
            for s in range(self.nslots):
                if self.dval[q][s] > 0:
                    evs.append(("D", q, s, self.dval[q][s]))
        for e in self.ENGS:
            self._emit_waits(e, evs)
        self.lastw.clear()
        self.readers.clear()

    def emit(self):
        nc = self.nc
        for q in self.DMAQ:
            self._emit_waits(q, [("D", q, s, self.dval[q][s]) for s in range(self.nslots) if self.dval[q][s] > 0])
        self._emit_waits("sp", [("E", e, self.cnt[e]) for e in self.ENGS if e != "sp" and self.cnt[e] > 0])

        def run(e):
            def body(eng):
                for it in self.prog[e]:
                    if it[0] == "w":
                        eng.wait_ge(it[1], it[2])
                    elif it[0] == "i":
                        it[1](eng).then_inc(it[2], 1)
                    else:
                        eng.dma_start(out=it[1], in_=it[2], **it[3]).then_inc(it[4], 16)
            return body

        with nc.Block() as block:
            block.tensor(run("pe"))
            block.scalar(run("act"))
            block.vector(run("dve"))
            block.gpsimd(run("pool"))
            block.sync(run("sp"))

    def mm(self, out, lhsT, rhs, start=True, stop=True, R=(), W=(), A=()):
        return self.op("pe", lambda e: e.matmul(out, lhsT, rhs, start=start, stop=stop), R, W, A)

    def tr(self, out, in_, ident, R=(), W=(), A=()):
        return self.op("pe", lambda e: e.transpose(out, in_, ident), R, W, A)

    def act(self, out, in_, func, bias=0.0, scale=1.0, accum_out=None, R=(), W=()):
        if accum_out is None:
            return self.op("act", lambda e: e.activation(out, in_, func, bias=bias, scale=scale), R, W)
        return self.op("act", lambda e: e.activation(out, in_, func, bias=bias, scale=scale, accum_out=accum_out), R, W)

    def tt(self, eng, out, in0, in1, op, R=(), W=()):
        return self.op(eng, lambda e: e.tensor_tensor(out, in0, in1, op), R, W)

    def ts(self, eng, out, in0, s1, s2, op0, op1=None, R=(), W=()):
        if op1 is None:
            return self.op(eng, lambda e: e.tensor_scalar(out, in0, s1, None, op0), R, W)
        return self.op(eng, lambda e: e.tensor_scalar(out, in0, s1, s2, op0, op1), R, W)

    def stt(self, eng, out, in0, scalar, in1, op0, op1, R=(), W=()):
        return self.op(eng, lambda e: e.scalar_tensor_tensor(out, in0, scalar, in1, op0, op1), R, W)

    def cp(self, eng, out, in_, R=(), W=()):
        if eng == "act":
            return self.op("act", lambda e: e.copy(out, in_), R, W)
        return self.op(eng, lambda e: e.tensor_copy(out, in_), R, W)

    def memset(self, eng, ap, val, W=()):
        return self.op(eng, lambda e: e.memset(ap, val), (), W)


class Rec:
    def __init__(self):
        self.calls = []

    def __getattr__(self, name):
        def f(*a, **k):
            self.calls.append((name, a, k))
        return f


def replay_interleaved(S, recs):
    n = max(len(r.calls) for r in recs)
    for i in range(n):
        for r in recs:
            if i < len(r.calls):
                name, a, k = r.calls[i]
                getattr(S, name)(*a, **k)


def _tile_consts(bs, nvalid, gam):
    p = np.arange(128)
    blk, loc = p // bs, p % bs
    same = blk[:, None] == blk[None, :]
    val = loc < nvalid
    c = {}
    c["tri"] = (same & (p[:, None] <= p[None, :])).astype(np.float32)
    c["blk"] = same.astype(np.float32)
    c["negS"] = np.where(same & (p[:, None] > p[None, :]), 0.0, NEG).astype(np.float32)
    c["negIT"] = np.where(same & (p[None, :] >= p[:, None]), 0.0, NEG).astype(np.float32)
    c["valid"] = val.astype(np.float32)[:, None].copy()
    dtc = np.zeros((128, 4, 128), np.float64)
    for h in range(4):
        d = (p[None, :] - p[:, None]).astype(np.float64)
        dtc[:, h, :] = np.where(same & (d >= 0) & val[:, None] & val[None, :], gam[h] ** np.maximum(d, 0), 0.0)
    c["dtc"] = dtc.reshape(128, 512).astype(np.float32)
    qd = np.zeros((128, 2, 128), np.float64)
    gcol = np.zeros((128, 2), np.float64)
    for pr in range(2):
        for half in range(2):
            h = 2 * pr + half
            qd[64 * half:64 * half + 64, pr, :] = (gam[h] ** (loc + 1.0))[None, :]
            gcol[64 * half:64 * half + 64, pr] = gam[h] ** nvalid
    c["qdec"] = qd.reshape(128, 256).astype(np.float32)
    c["gC"] = gcol.astype(np.float32)
    kd = np.zeros((128, 4, 64), np.float64)
    for h in range(4):
        kd[:, h, :] = np.where(val, gam[h] ** np.maximum(nvalid - 1.0 - loc, 0), 0.0)[:, None]
    c["kdec"] = kd.reshape(128, 256).astype(np.float32)
    return c


def make_consts(cfg):
    T, NTOK = cfg.T, cfg.NTOK
    pos = np.zeros(NTOK, np.float32)
    pos[:T] = np.arange(T)
    for s in range(4):
        pos[T + 64 * s:T + 64 * s + 64] = PAST_LEN + np.arange(64)
    C = {}
    C["ident"] = np.eye(128, dtype=np.float32)
    C["ones"] = np.ones((128, 128), np.float32)
    d = np.arange(128)
    invA = (np.float32(10000.0) ** (-np.arange(0, 128, 2, dtype=np.float32) / np.float32(128))).astype(np.float32)
    angA = (pos[None, :] * invA[d % 64][:, None]).astype(np.float32).astype(np.float64)
    C["cosA"] = np.cos(angA).astype(np.float32)
    C["sinA"] = (np.sin(angA) * np.where(d < 64, -1.0, 1.0)[:, None]).astype(np.float32)
    rotA = np.zeros((128, 128), np.float32)
    rotA[(d + 64) % 128, d] = 1.0
    C["rotA"] = rotA
    invC = (np.float32(10000.0) ** (-np.arange(0, 64, 2, dtype=np.float32) / np.float32(64))).astype(np.float32)
    dd = d % 64
    angC = (pos[None, :] * invC[dd % 32][:, None]).astype(np.float32).astype(np.float64)
    C["cosC"] = np.cos(angC).astype(np.float32)
    C["sinC"] = (np.sin(angC) * np.where(dd < 32, -1.0, 1.0)[:, None]).astype(np.float32)
    rotC = np.zeros((128, 128), np.float32)
    rotC[(d // 64) * 64 + (dd + 32) % 64, d] = 1.0
    C["rotC"] = rotC
    k = np.arange(128)
    C["mcur"] = np.tile((k[:, None] <= k[None, :]).astype(np.float32), (1, 4))
    C["mprev"] = np.tile((k[:, None] >= k[None, :]).astype(np.float32), (1, 4))
    msc = np.zeros((13, 128, 8), np.float32)
    msn = np.zeros((3, 2, 128, 8), np.float32)
    idx = 0
    for g, dil in enumerate((1, 4, 16)):
        for r in range(min(dil, 8)):
            for i in range(8):
                if i % dil == r:
                    msc[idx, :, i] = (k >= i // dil)
            idx += 1
        for s2 in range(2):
            for i in range(8):
                for j in range(8):
                    if j <= i and (i - j) % dil == 0:
                        msn[g, s2, 64 * s2 + j, i] = 1.0
    C["msc"] = np.ascontiguousarray(msc.transpose(1, 0, 2)).reshape(128, 13 * 8)
    C["msn"] = np.ascontiguousarray(msn.transpose(2, 0, 1, 3)).reshape(128, 48)
    gam = [1.0 - 2.0 ** (-5.0 - h) for h in range(4)]
    for nm, (bs, nv) in (("p", (64, 64)), ("s", (64, 8))):
        for kk, v in _tile_consts(bs, nv, gam).items():
            C[kk + "_" + nm] = v
    return C


CUT = 99
LIMIT = 10 ** 9
PADOPS = 0
PSUM_KEYS = frozenset(["P%d" % i for i in range(7)] + ["PT"])
VAR = 0
NOPS = [0]


def build(cfg):
    D, DFF, T, NS, DEPTH = cfg.D, cfg.DFF, cfg.T, cfg.NS, cfg.DEPTH
    KD, KF, NTT, NT, NTOK, NIN = cfg.KD, cfg.KF, cfg.NTT, cfg.NT, cfg.NTOK, cfg.NIN
    assert NS == 4 and T % 2048 == 0
    nc = bass.Bass("TRN2", target_bir_lowering=False)
    consts = make_consts(cfg)

    def din(name, shape, dt=F32):
        return nc.dram_tensor(name, list(shape), dt, kind="ExternalInput").ap()

    def dout(name, shape, dt=F32):
        return nc.dram_tensor(name, list(shape), dt, kind="ExternalOutput").ap()

    def dscr(name, shape, dt):
        return nc.dram_tensor(name, list(shape), dt, kind="ExternalOutput" if cfg.dbg else "Internal").ap()

    I = {}
    I["xp"] = din("xp", [T, D])
    I["xs"] = din("xs", [NS * 8, D])
    I["kv0"] = din("kv0", [DEPTH, NS, 128, 1024])
    I["kv1"] = din("kv1", [DEPTH, NS, 512, 1024])
    I["kv2"] = din("kv2", [DEPTH, NS, 2048, 1024])
    I["sconv"] = din("sconv", [DEPTH, NS * 3, 1536])
    I["sS"] = din("sS", [DEPTH, NS, 4, 128, 128])
    I["sR"] = din("sR", [DEPTH, NS, 2, 128, 128])
    I["g1B"] = din("g1B", [DEPTH, 128, D])
    I["g2B"] = din("g2B", [DEPTH, 128, D])
    I["w_in"] = din("w_in", [DEPTH, D, NIN])
    I["aqg"] = din("aqg", [DEPTH, 128, 1])
    I["akg"] = din("akg", [DEPTH, 128, 1])
    I["convw"] = din("convw", [DEPTH, 128, 12, 4])
    I["alogB"] = din("alogB", [DEPTH, 128, 4])
    I["dtbB"] = din("dtbB", [DEPTH, 128, 4])
    I["bog"] = din("bog", [DEPTH, 128, 1])
    I["cog"] = din("cog", [DEPTH, 128, 1])
    I["w_oa"] = din("w_oa", [DEPTH, 512, D])
    I["w_ob"] = din("w_ob", [DEPTH, 512, D])
    I["w_oc"] = din("w_oc", [DEPTH, 512, D])
    I["w_out"] = din("w_out", [DEPTH, D, D])
    I["w_fi"] = din("w_fi", [DEPTH, D, 2 * DFF])
    I["w_fo"] = din("w_fo", [DEPTH, DFF, D])
    for k, v in consts.items():
        I["c_" + k] = din("c_" + k, v.shape)

    O = {}
    O["yp"] = dout("yp", [T, D])
    O["ys"] = dout("ys", [NS * 8, D])
    KEEP = (128, 512, min(2048, T))
    for g in range(3):
        O["kvp%d" % g] = dout("kvp%d" % g, [DEPTH, KEEP[g], 2, 4, 128])
        O["kvs%d" % g] = dout("kvs%d" % g, [DEPTH, NS, 8, 2, 4, 128])
    O["convp"] = dout("convp", [DEPTH, 3, 1536])
    O["Sp"] = dout("Sp", [DEPTH, 4, 128, 128])
    O["Rp"] = dout("Rp", [DEPTH, 2, 128, 128])
    O["convs"] = dout("convs", [DEPTH, NS * 3, 1536])
    O["Ss"] = dout("Ss", [DEPTH, NS, 4, 128, 128])
    O["Rs"] = dout("Rs", [DEPTH, NS, 2, 128, 128])
    xres = dscr("xres", [NTOK, D], F32)
    oT_d = dscr("oT_d", [12 * 128, NTOK], BF16)
    sg_d = dscr("sg_d", [3 * D, NTOK], BF16)
    WB = {}
    for nm, shp in (("w_oa", [512, D]), ("w_ob", [512, D]), ("w_oc", [512, D]), ("w_out", [D, D]), ("w_fi", [D, 2 * DFF]), ("w_fo", [DFF, D])):
        WB[nm] = nc.dram_tensor("wbf_" + nm, [DEPTH] + shp, BF16, kind="Internal").ap()

    with contextlib.ExitStack() as st:
        S = Sched(nc, st)
        sb = lambda n, s, d=F32: st.enter_context(nc.sbuf_tensor(n, list(s), d))
        P = [st.enter_context(nc.psum_tensor("p%d" % i, [128, 512], F32)) for i in range(7)]
        PT = st.enter_context(nc.psum_tensor("pT", [128, 1024], BF16))

        ident = sb("ident", [128, 128])
        identb = sb("identb", [128, 128], BF16)
        ones = sb("ones", [128, 128])
        onesb = sb("onesb", [128, 128], BF16)
        S.dma("sp", ident[:], I["c_ident"], W=["ident"])
        S.dma("pool", identb[:], I["c_ident"], W=["identb"])
        S.dma("sp", ones[:], I["c_ones"], W=["ones"])
        S.dma("pool", onesb[:], I["c_ones"], W=["onesb"])
        for l_ in range(DEPTH):
            for nm in ("w_oa", "w_ob", "w_oc", "w_out", "w_fi", "w_fo"):
                rows = WB[nm].shape[1]
                for r0 in range(0, rows, 128):
                    S.dma("pool", WB[nm][l_, r0:r0 + 128, :], I[nm][l_, r0:r0 + 128, :], W=["wbf_%s_%d_%d" % (nm, l_, r0)])
        NBLK = (NTOK + 511) // 512
        blocks = [(b * 512, min(512, NTOK - b * 512)) for b in range(NBLK)]

        def wview(ap2d, c0, ncols):
            return ap2d.rearrange("(kc p) n -> p kc n", p=128)[:, :, c0:c0 + ncols]

        def rstd_from_ss(dst, src, scale, R, W):
            S.act(dst, src, AF.Ln, bias=EPS, scale=scale, R=R, W=W)
            S.act(dst, dst, AF.Exp, scale=-0.5, R=W, W=W)

        def norm_tile(x_t, xkey, gB, h_t, hkey, junk, ss, hTdst, tcols, hTkey):
            S.memset("dve", ss[:, 0:1], 0.0, W=["ss"])
            S.act(junk[:], x_t, AF.Square, accum_out=ss[:, 0:1], R=[xkey, "ss"], W=["junk", "ss"])
            rstd_from_ss(ss[:, 0:1], ss[:, 0:1], 1.0 / D, ["ss"], ["ss"])
            S.stt("dve", h_t, x_t, ss[:, 0:1], gB, ALU.mult, ALU.mult, R=[xkey, "ss", "gB"], W=[hkey])
            for kc in range(KD):
                S.tr(PT[:, kc * 128:(kc + 1) * 128], h_t[:, kc * 128:(kc + 1) * 128], identb[:],
                     R=[hkey, "identb"], W=["PT"] if kc == 0 else [], A=["PT"] if kc > 0 else [])
            S.cp("dve", hTdst[:, :, tcols], PT[:, 0:KD * 128].rearrange("p (k n) -> p k n", k=KD), R=["PT"], W=[hTkey])

        for l in range(DEPTH):
            w_in = I["w_in"][l]
            lay = contextlib.ExitStack()
            hT = lay.enter_context(nc.sbuf_tensor("hT%d" % l, [128, KD, NTOK], BF16))
            with contextlib.ExitStack() as ph:
                sbp = lambda n, s, d=F32: ph.enter_context(nc.sbuf_tensor("a%d_%s" % (l, n), list(s), d))
                gB = sbp("gB", [128, D])
                S.dma("sp", gB[:], I["g1B"][l], W=["gB"])
                xt = [sbp("xt%d" % i, [128, D]) for i in range(2)]
                ht = [sbp("ht%d" % i, [128, D], BF16) for i in range(2)]
                junk = sbp("junk", [128, D])
                ss = sbp("ss", [128, 1])
                for i in range(NT):
                    x_t, h_t = xt[i % 2], ht[i % 2]
                    xk, hk = "xt%d" % (i % 2), "ht%d" % (i % 2)
                    if l == 0:
                        if i < NTT:
                            S.dma("sp", x_t[:], I["xp"][i * 128:(i + 1) * 128, :], W=[xk])
                        else:
                            S.memset("dve", x_t[:], 0.0, W=[xk])
                            for s2 in range(2):
                                s = 2 * (i - NTT) + s2
                                S.dma("sp", x_t[64 * s2:64 * s2 + 8, :], I["xs"][8 * s:8 * s + 8, :], W=[xk])
                    else:
                        S.dma("sp", x_t[:], xres[i * 128:(i + 1) * 128, :], W=[xk])
                    norm_tile(x_t[:], xk, gB[:], h_t[:], hk, junk, ss, hT, slice(i * 128, (i + 1) * 128), "hT")
            S.barrier()
            with contextlib.ExitStack() as ph:
                sbp = lambda n, s, d=F32: ph.enter_context(nc.sbuf_tensor("g%d_%s" % (l, n), list(s), d))
                wg = [sbp("wg%d" % i, [128, KD, 512], BF16) for i in range(2)]
                sgo = [sbp("sgo%d" % i, [128, 4, 512], BF16) for i in range(2)]
                ncb = 3 * D // 512
                it = 0
                for cb in range(ncb):
                    w_t, wk = wg[cb % 2], "wg%d" % (cb % 2)
                    S.dma("pool", w_t[:], wview(w_in, GOFF + cb * 512, 512), W=[wk])
                    for (t0, n) in blocks:
                        so, sk = sgo[it % 2], "sgo%d" % (it % 2)
                        for j in range(4):
                            pp, pk = P[(it * 4 + j) % 4], "P%d" % ((it * 4 + j) % 4)
                            for kc in range(KD):
                                S.mm(pp[:, 0:n], w_t[:, kc, j * 128:(j + 1) * 128], hT[:, kc, t0:t0 + n],
                                     start=(kc == 0), stop=(kc == KD - 1), R=[wk, "hT"],
                                     W=[pk] if kc == 0 else [], A=[pk] if kc > 0 else [])
                            S.act(so[:, j, 0:n], pp[:, 0:n], AF.Sigmoid, R=[pk], W=[sk])
                        S.dma("sp", sg_d[cb * 512:(cb + 1) * 512, t0:t0 + n].rearrange("(j p) n -> p j n", p=128),
                              so[:, :, 0:n], R=[sk])
                        it += 1
            S.barrier()
            X = dict(nc=nc, S=S, cfg=cfg, l=l, I=I, O=O, P=P, PT=PT, hT=hT, oT_d=oT_d, ident=ident, identb=identb,
                     ones=ones, onesb=onesb, blocks=blocks, wview=wview, rstd=rstd_from_ss, w_in=w_in)
            for mi, ch in enumerate("abc"):
                if ch not in cfg.mixers:
                    zt = lay.enter_context(nc.sbuf_tensor("zt%d_%d" % (l, mi), [128, 4, NTOK], BF16))
                    S.memset("dve", zt[:], 0.0, W=["zt"])
                    S.dma("sp", oT_d[mi * 512:(mi + 1) * 512, :].rearrange("(c p) n -> p c n", p=128), zt[:], R=["zt"])
            S.barrier()
            if "c" in cfg.mixers:
                mixer_c(X)
                S.barrier()
            if "b" in cfg.mixers:
                mixer_b(X)
                S.barrier()
            if "a" in cfg.mixers:
                mixer_a(X)
                S.barrier()
            lay.close()
            with contextlib.ExitStack() as ph:
                sbp = lambda n, s, d=F32: ph.enter_context(nc.sbuf_tensor("d%d_%s" % (l, n), list(s), d))
                g2B = sbp("g2B", [128, D])
                S.dma("sp", g2B[:], I["g2B"][l], W=["gB"])
                oTb = sbp("oTb", [128, 12, 512], BF16)
                sgb = sbp("sgb", [128, 3 * KD, 512], BF16)
                mT = sbp("mT", [128, KD, 512], BF16)
                h2T = sbp("h2T", [128, KD, 512], BF16)
                aT = sbp("aT", [128, KF, 512], BF16)
                x1 = [sbp("x1_%d" % i, [128, D]) for i in range(4)]
                h2 = sbp("h2", [128, D], BF16)
                junk = sbp("junk", [128, D])
                ss = sbp("ss", [128, 1])
                tmp = [sbp("tmp%d" % i, [128, 512]) for i in range(3)]
                NW = 3
                wb = [sbp("wb%d" % i, [128, 8, 512], BF16) for i in range(NW)]
                wfo = sbp("wfo", [128, KF, 512], BF16)
                wctr = [0]

                def loadw(src2d, c0, ncols, k0, nk):
                    i = wctr[0] % NW
                    wctr[0] += 1
                    v = src2d.rearrange("(kc p) n -> p kc n", p=128)[:, k0:k0 + nk, c0:c0 + ncols]
                    S.dma("sp", wb[i][:, 0:nk, 0:ncols], v, W=["wb%d" % i])
                    return wb[i], "wb%d" % i

                pctr = [0]

                def nextp():
                    i = pctr[0] % 6
                    pctr[0] += 1
                    return P[i], "P%d" % i

                for (t0, n) in blocks:
                    ntl = n // 128
                    S.dma("sp", oTb[:, :, 0:n], oT_d[:, t0:t0 + n].rearrange("(c p) n -> p c n", p=128), W=["oTb"])
                    S.dma("sp", sgb[:, :, 0:n], sg_d[:, t0:t0 + n].rearrange("(c p) n -> p c n", p=128), W=["sgb"])
                    for cb in range(D // 512):
                        ws = []
                        for j, nm in enumerate(("w_oa", "w_ob", "w_oc")):
                            ws.append(loadw(WB[nm][l], cb * 512, 512, 0, 4))
                        for oc4 in range(4):
                            oc = cb * 4 + oc4
                            for j in range(3):
                                w_t, wk = ws[j]
                                pp, pk = nextp()
                                for kc in range(4):
                                    S.mm(pp[:, 0:n], w_t[:, kc, oc4 * 128:(oc4 + 1) * 128], oTb[:, 4 * j + kc, 0:n],
                                         start=(kc == 0), stop=(kc == 3), R=[wk, "oTb"],
                                         W=[pk] if kc == 0 else [], A=[pk] if kc > 0 else [])
                                S.tt("dve", tmp[j][:, 0:n], pp[:, 0:n], sgb[:, j * KD + oc, 0:n], ALU.mult,
                                     R=[pk, "sgb"], W=["tmp%d" % j])
                            S.tt("pool", tmp[0][:, 0:n], tmp[0][:, 0:n], tmp[1][:, 0:n], ALU.add, R=["tmp0", "tmp1"], W=["tmp0"])
                            S.tt("pool", mT[:, oc, 0:n], tmp[0][:, 0:n], tmp[2][:, 0:n], ALU.add, R=["tmp0", "tmp2"], W=["mT"])
                    for tt_ in range(ntl):
                        tok0 = t0 + tt_ * 128
                        ti = tok0 // 128
                        xk = "x1_%d" % tt_
                        x_t = x1[tt_]
                        if l == 0:
                            if ti < NTT:
                                S.dma("sp", x_t[:], I["xp"][tok0:tok0 + 128, :], W=[xk])
                            else:
                                S.memset("dve", x_t[:], 0.0, W=[xk])
                                for s2 in range(2):
                                    s = 2 * (ti - NTT) + s2
                                    S.dma("sp", x_t[64 * s2:64 * s2 + 8, :], I["xs"][8 * s:8 * s + 8, :], W=[xk])
                        else:
                            S.dma("sp", x_t[:], xres[tok0:tok0 + 128, :], W=[xk])
                    for cb in range(D // 512):
                        w_t, wk = loadw(WB["w_out"][l], cb * 512, 512, 0, KD)
                        for tt_ in range(ntl):
                            pp, pk = nextp()
                            xk = "x1_%d" % tt_
                            for kc in range(KD):
                                S.mm(pp[:], mT[:, kc, tt_ * 128:(tt_ + 1) * 128], w_t[:, kc, :], start=(kc == 0), stop=(kc == KD - 1),
                                     R=[wk, "mT"], W=[pk] if kc == 0 else [], A=[pk] if kc > 0 else [])
                            S.tt("dve", x1[tt_][:, cb * 512:(cb + 1) * 512], x1[tt_][:, cb * 512:(cb + 1) * 512], pp[:], ALU.add,
                                 R=[pk, xk], W=[xk])
                    for tt_ in range(ntl):
                        norm_tile(x1[tt_][:], "x1_%d" % tt_, g2B[:], h2[:], "h2", junk, ss, h2T,
                                  slice(tt_ * 128, (tt_ + 1) * 128), "h2T")
                    for fb in range((DFF + 511) // 512):
                        f0 = fb * 512
                        fn_ = min(512, DFF - f0)
                        wg_t, wgk = loadw(WB["w_fi"][l], f0, fn_, 0, KD)
                        wu_t, wuk = loadw(WB["w_fi"][l], DFF + f0, fn_, 0, KD)
                        for fc in range(fn_ // 128):
                            pg, pgk = nextp()
                            pu, puk = nextp()
                            for kc in range(KD):
                                S.mm(pg[:, 0:n], wg_t[:, kc, fc * 128:(fc + 1) * 128], h2T[:, kc, 0:n], start=(kc == 0), stop=(kc == KD - 1),
                                     R=[wgk, "h2T"], W=[pgk] if kc == 0 else [], A=[pgk] if kc > 0 else [])
                            for kc in range(KD):
                                S.mm(pu[:, 0:n], wu_t[:, kc, fc * 128:(fc + 1) * 128], h2T[:, kc, 0:n], start=(kc == 0), stop=(kc == KD - 1),
                                     R=[wuk, "h2T"], W=[puk] if kc == 0 else [], A=[puk] if kc > 0 else [])
                            S.act(tmp[0][:, 0:n], pg[:, 0:n], AF.Silu, R=[pgk], W=["tmp0"])
                            S.tt("dve", aT[:, f0 // 128 + fc, 0:n], tmp[0][:, 0:n], pu[:, 0:n], ALU.mult, R=["tmp0", puk], W=["aT"])
                    for cb in range(D // 512):
                        S.dma("sp", wfo[:], WB["w_fo"][l].rearrange("(kc p) n -> p kc n", p=128)[:, :, cb * 512:(cb + 1) * 512], W=["wfo"])
                        for tt_ in range(ntl):
                            pp, pk = nextp()
                            xk = "x1_%d" % tt_
                            for fc in range(KF):
                                S.mm(pp[:], aT[:, fc, tt_ * 128:(tt_ + 1) * 128], wfo[:, fc, :], start=(fc == 0), stop=(fc == KF - 1),
                                     R=["wfo", "aT"], W=[pk] if fc == 0 else [], A=[pk] if fc > 0 else [])
                            S.tt("dve", x1[tt_][:, cb * 512:(cb + 1) * 512], x1[tt_][:, cb * 512:(cb + 1) * 512], pp[:], ALU.add,
                                 R=[pk, xk], W=[xk])
                    dst = xres if l < DEPTH - 1 else None
                    for tt_ in range(ntl):
                        tok0 = t0 + tt_ * 128
                        ti = tok0 // 128
                        xk = "x1_%d" % tt_
                        if dst is not None:
                            S.dma("sp", xres[tok0:tok0 + 128, :], x1[tt_][:], R=[xk])
                        else:
                            if ti < NTT:
                                S.dma("sp", O["yp"][tok0:tok0 + 128, :], x1[tt_][:], R=[xk])
                            else:
                                for s2 in range(2):
                                    s = 2 * (ti - NTT) + s2
                                    S.dma("sp", O["ys"][8 * s:8 * s + 8, :], x1[tt_][64 * s2:64 * s2 + 8, :], R=[xk])
            S.barrier()
        for _ in range(PADOPS):
            S.memset("dve", ones[:], 1.0, W=["ones"])
        S.emit()
    return nc


def prep_core_inputs(cfg, inp, core, consts):
    D, NS, DEPTH = cfg.D, cfg.NS, cfg.DEPTH
    f = lambda a: np.ascontiguousarray(np.asarray(a, dtype=np.float32))
    nb = inp["x_prompt"].shape[0]
    ss = slice(core * NS, (core + 1) * NS)
    m = {}
    m["xp"] = f(inp["x_prompt"][core % nb])
    m["xs"] = f(inp["x_sample"][ss]).reshape(NS * 8, D)
    m["kv0"] = f(inp["cache_a_kv0"][:, ss]).reshape(DEPTH, NS, -1, 1024)
    m["kv1"] = f(inp["cache_a_kv1"][:, ss]).reshape(DEPTH, NS, -1, 1024)
    m["kv2"] = f(inp["cache_a_kv2"][:, ss]).reshape(DEPTH, NS, -1, 1024)
    m["sconv"] = f(inp["state_b_conv"][:, ss]).reshape(DEPTH, NS * 3, 1536)
    m["sS"] = f(inp["state_b_S"][:, ss])
    m["sR"] = f(inp["state_c_R"][:, ss]).reshape(DEPTH, NS, 2, 128, 128)
    m["g1B"] = f(np.broadcast_to(np.asarray(inp["norm1_g"])[:, None, :], (DEPTH, 128, D)))
    m["g2B"] = f(np.broadcast_to(np.asarray(inp["norm2_g"])[:, None, :], (DEPTH, 128, D)))
    m["w_in"] = f(inp["w_in"])
    m["aqg"] = f(inp["a_q_norm_g"]).reshape(DEPTH, 128, 1)
    m["akg"] = f(inp["a_k_norm_g"]).reshape(DEPTH, 128, 1)
    m["convw"] = f(np.asarray(inp["b_conv_w"]).reshape(DEPTH, 4, 12, 128).transpose(0, 3, 2, 1))
    m["alogB"] = f(np.broadcast_to(np.asarray(inp["b_a_log"])[:, None, :], (DEPTH, 128, 4)))
    m["dtbB"] = f(np.broadcast_to(np.asarray(inp["b_dt_bias"])[:, None, :], (DEPTH, 128, 4)))
    m["bog"] = f(inp["b_out_norm_g"]).reshape(DEPTH, 128, 1)
    m["cog"] = f(inp["c_out_norm_g"]).reshape(DEPTH, 128, 1)
    m["w_oa"] = f(inp["w_out_a"])
    m["w_ob"] = f(inp["w_out_b"])
    m["w_oc"] = f(inp["w_out_c"])
    m["w_out"] = f(inp["w_out"])
    m["w_fi"] = f(inp["w_ffn_in"])
    m["w_fo"] = f(inp["w_ffn_out"])
    for k, v in consts.items():
        m["c_" + k] = v
    return m


def assemble(cfg, res, nb, ncores):
    DEPTH, NS = cfg.DEPTH, cfg.NS
    r = res
    st = lambda k, cores: np.stack([np.asarray(r[c][k]) for c in cores])
    pc = list(range(nb))
    ac = list(range(ncores))
    yp = st("yp", pc)
    ys = np.concatenate([np.asarray(r[c]["ys"]).reshape(NS, 8, cfg.D) for c in ac], axis=0)
    outs = [yp, ys]
    for g in range(3):
        outs.append(st("kvp%d" % g, pc).transpose(1, 0, 2, 3, 4, 5))
    outs.append(st("convp", pc).transpose(1, 0, 2, 3))
    outs.append(st("Sp", pc).transpose(1, 0, 2, 3, 4))
    outs.append(st("Rp", pc).transpose(1, 0, 2, 3, 4).reshape(DEPTH, nb, 4, 64, 128))
    for g in range(3):
        outs.append(np.concatenate([np.asarray(r[c]["kvs%d" % g]) for c in ac], axis=1))
    outs.append(np.concatenate([np.asarray(r[c]["convs"]).reshape(DEPTH, NS, 3, 1536) for c in ac], axis=1))
    outs.append(np.concatenate([np.asarray(r[c]["Ss"]) for c in ac], axis=1))
    outs.append(np.concatenate([np.asarray(r[c]["Rs"]).reshape(DEPTH, NS, 4, 64, 128) for c in ac], axis=1))
    return tuple(np.ascontiguousarray(o, dtype=np.float32) for o in outs)


def kernel(**inputs):
    cfg = Cfg()
    ncores = 8
    consts = make_consts(cfg)
    nc = build(cfg)
    in_maps = [prep_core_inputs(cfg, inputs, c, consts) for c in range(ncores)]
    res = run_bass_kernel_spmd(nc, in_maps, core_ids=list(range(ncores)))
    return assemble(cfg, res.results, inputs["x_prompt"].shape[0], ncores)


def _gated_norm_epilogue(X, sbp_bufs, P_o, pok, zcol0, gcol, gkey, tok0, out_stage, okey, stage_cols, Pz, pzk, Pss, pssk, wz, wzk):
    S, hT, KD, onesb = X["S"], X["hT"], X["cfg"].KD, X["onesb"]
    sq, rs, on, sz = sbp_bufs
    poks = list(pok) if isinstance(pok, (list, tuple)) else [pok]
    S.act(sq[:], P_o[:], AF.Square, R=poks, W=["e_sq"])
    S.mm(Pss[:], onesb[:], sq[:], R=["onesb", "e_sq"], W=[pssk])
    X["rstd"](rs[:], Pss[:], 1.0 / 128, [pssk], ["e_rs"])
    S.stt("dve", on[:], P_o[:], gcol, rs[:], ALU.mult, ALU.mult, R=poks + [gkey, "e_rs"], W=["e_on"])
    for h in range(4):
        for kc in range(KD):
            S.mm(Pz[:, h * 128:(h + 1) * 128], wz[:, kc, zcol0 + h * 128:zcol0 + (h + 1) * 128], hT[:, kc, tok0:tok0 + 128],
                 start=(kc == 0), stop=(kc == KD - 1), R=[wzk, "hT"],
                 W=[pzk] if (h == 0 and kc == 0) else [], A=[pzk] if not (h == 0 and kc == 0) else [])
    S.act(sz[:], Pz[:], AF.Silu, R=[pzk], W=["e_sz"])
    S.tt("dve", out_stage[:, :, stage_cols], on[:].rearrange("p (h n) -> p h n", h=4), sz[:].rearrange("p (h n) -> p h n", h=4),
         ALU.mult, R=["e_on", "e_sz"], W=[okey])


def mixer_c(X):
    nc, S, cfg, l, I, O, P, PT, hT = X["nc"], X["S"], X["cfg"], X["l"], X["I"], X["O"], X["P"], X["PT"], X["hT"]
    KD, NTT, NS = cfg.KD, cfg.NTT, cfg.NS
    w_in, oT_d, identb, wview = X["w_in"], X["oT_d"], X["identb"], X["wview"]
    with contextlib.ExitStack() as ph:
        sbp = lambda n, s, d=F32: ph.enter_context(nc.sbuf_tensor("c%d_%s" % (l, n), list(s), d))
        rotC = sbp("rotC", [128, 128], BF16)
        S.dma("pool", rotC[:], I["c_rotC"], W=["rotC"])
        tab = {}
        for ty in "ps":
            for nm, w in (("dtc", 512), ("qdec", 256), ("gC", 2), ("kdec", 256)):
                tab[nm + ty] = sbp(nm + ty, [128, w])
                S.dma("sp", tab[nm + ty][:], I["c_%s_%s" % (nm, ty)], W=["tab"])
        cog = sbp("cog", [128, 1])
        S.dma("sp", cog[:], I["cog"][l], W=["cog"])
        wq = sbp("wq", [128, KD, 256], BF16)
        wk = sbp("wk", [128, KD, 256], BF16)
        wv = sbp("wv", [128, KD, 512], BF16)
        wz = sbp("wz", [128, KD, 512], BF16)
        for t_, c0, w_ in ((wq, CQ, 256), (wk, CK, 256), (wv, CV, 512), (wz, CZ, 512)):
            S.dma("pool", t_[:], wview(w_in, c0, w_), W=["wc"])
        ct = sbp("ct", [128, 512])
        sn = sbp("sn", [128, 512])
        qT = sbp("qT", [128, 2, 2, 512], BF16)
        S.memset("pool", qT[:], 0.0, W=["qT"])
        qdT = sbp("qdT", [128, 2, 512], BF16)
        kT = sbp("kT", [128, 2, 512], BF16)
        qb = sbp("qb", [128, 512], BF16)
        t1 = sbp("t1", [128, 512])
        t2 = sbp("t2", [128, 512])
        v_t = sbp("v_t", [128, 512], BF16)
        kd_t = sbp("kd_t", [128, 256], BF16)
        attn = sbp("attn", [128, 512], BF16)
        R32 = sbp("R32", [128, 2, 128])
        Rb = [sbp("Rb%d" % i, [128, 2, 128], BF16) for i in range(2)]
        ebuf = (sbp("e_sq", [128, 512], BF16), sbp("e_rs", [128, 512]), sbp("e_on", [128, 512]), sbp("e_sz", [128, 512]))
        ocs = sbp("ocs", [128, 4, 512], BF16)
        S.memset("dve", R32[:], 0.0, W=["R32"])
        rbi = 0
        for (t0, n) in X["blocks"]:
            ntl = n // 128
            sample = (t0 // 128 >= NTT)
            ty = "s" if sample else "p"
            if CUT <= 0:
                continue
            S.dma("sp", ct[:, 0:n], I["c_cosC"][:, t0:t0 + n], W=["ct"])
            S.dma("sp", sn[:, 0:n], I["c_sinC"][:, t0:t0 + n], W=["sn"])
            for which, w_t, dst in (("q", wq, qT), ("k", wk, kT)):
                sc = 1.0 if which == "q" else 0.125
                for c in range(2):
                    for kc in range(KD):
                        S.mm(P[0][:, 0:n], w_t[:, kc, c * 128:(c + 1) * 128], hT[:, kc, t0:t0 + n], start=(kc == 0), stop=(kc == KD - 1),
                             R=["wc", "hT"], W=["P0"] if kc == 0 else [], A=["P0"] if kc > 0 else [])
                    S.cp("act", qb[:, 0:n], P[0][:, 0:n], R=["P0"], W=["qb"])
                    S.mm(P[1][:, 0:n], rotC[:], qb[:, 0:n], R=["rotC", "qb"], W=["P1"])
                    if VAR == 1:
                        S.cp("dve", t1[:, 0:n], ct[:, 0:n], R=["ct"], W=["t1"])
                    elif VAR == 2:
                        S.cp("dve", t1[:, 0:n], P[0][:, 0:n], R=["P0"], W=["t1"])
                    elif VAR == 3:
                        S.cp("dve", t1[:, 0:n], X["ones"][:, 0:1].broadcast_to([128, n]) if False else t2[:, 0:n], R=[], W=["t1"])
                    else:
                        S.tt("dve", t1[:, 0:n], P[0][:, 0:n], ct[:, 0:n], ALU.mult, R=["P0", "ct"], W=["t1"])
                    S.tt("dve", t2[:, 0:n], P[1][:, 0:n], sn[:, 0:n], ALU.mult, R=["P1", "sn"], W=["t2"])
                    S.tt("pool", t1[:, 0:n], t1[:, 0:n], t2[:, 0:n], ALU.add, R=["t1", "t2"], W=["t1"])
                    if which == "k":
                        S.act(dst[:, c, 0:n], t1[:, 0:n], AF.Copy, scale=sc, R=["t1"], W=[which + "T"])
                    else:
                        for half in range(2):
                            hr = slice(64 * half, 64 * half + 64)
                            S.act(dst[hr, c, half, 0:n], t1[hr, 0:n], AF.Copy, R=["t1"], W=["qT"])
                    if which == "q":
                        for tt_ in range(ntl):
                            S.tt("pool", qdT[:, c, tt_ * 128:(tt_ + 1) * 128], t1[:, tt_ * 128:(tt_ + 1) * 128],
                                 tab["qdec" + ty][:, c * 128:(c + 1) * 128], ALU.mult, R=["t1", "tab"], W=["qdT"])
            if CUT <= 1:
                continue
            for tt_ in range(ntl):
                tok0 = t0 + tt_ * 128
                tc = slice(tt_ * 128, (tt_ + 1) * 128)
                for kc in range(KD):
                    S.mm(P[2][:], hT[:, kc, tok0:tok0 + 128], wv[:, kc, :], start=(kc == 0), stop=(kc == KD - 1),
                         R=["wc", "hT"], W=["P2"] if kc == 0 else [], A=["P2"] if kc > 0 else [])
                S.cp("act", v_t[:], P[2][:], R=["P2"], W=["v_t"])
                for c in range(2):
                    S.tr(PT[:, c * 128:(c + 1) * 128], kT[:, c, tc], identb[:], R=["kT", "identb"],
                         W=["PT"] if c == 0 else [], A=["PT"] if c > 0 else [])
                S.tt("dve", kd_t[:], PT[:, 0:256], tab["kdec" + ty][:], ALU.mult, R=["PT", "tab"], W=["kd_t"])
                if CUT <= 2:
                    continue
                for h in range(4):
                    c, rows = h // 2, slice(64 * (h % 2), 64 * (h % 2) + 64)
                    S.mm(P[3][:, h * 128:(h + 1) * 128], kT[:, c, tc], qT[:, c, h % 2, tc], R=["kT", "qT"],
                         W=["P3"] if h == 0 else [], A=["P3"] if h > 0 else [])
                S.tt("dve", attn[:], P[3][:], tab["dtc" + ty][:], ALU.mult, R=["P3", "tab"], W=["attn"])
                if CUT <= 3:
                    continue
                bs = 64
                rbs = []
                for b in range(2):
                    br = slice(b * bs, (b + 1) * bs)
                    sq_ = 2 * (tok0 // 128 - NTT) + b
                    if sample:
                        S.dma("sp", R32[:], I["sR"][l, sq_].rearrange("pr p d -> p pr d"), W=["R32"])
                    rb, rbk = Rb[b], "Rb%d" % b
                    rbs.append((rb, rbk))
                    S.cp("act", rb[:], R32[:], R=["R32"], W=[rbk])
                    for pr in range(2):
                        S.mm(P[4][:, pr * 256:(pr + 1) * 256], kd_t[br, pr * 128:(pr + 1) * 128], v_t[br, pr * 256:(pr + 1) * 256],
                             R=["kd_t", "v_t"], W=["P4"] if pr == 0 else [], A=["P4"] if pr > 0 else [])
                    for pr in range(2):
                        for half in range(2):
                            rows = slice(64 * half, 64 * half + 64)
                            S.stt("dve", R32[rows, pr, :], R32[rows, pr, :], tab["gC" + ty][rows, pr:pr + 1],
                                  P[4][rows, pr * 256 + half * 128:pr * 256 + half * 128 + 128], ALU.mult, ALU.add,
                                  R=["R32", "P4", "tab"], W=["R32"])
                    if sample:
                        S.dma("sp", O["Rs"][l, sq_].rearrange("pr p d -> p pr d"), R32[:], R=["R32"])
                    elif tok0 + 128 == cfg.T and b == 1:
                        S.dma("sp", O["Rp"][l].rearrange("pr p d -> p pr d"), R32[:], R=["R32"])
                if CUT <= 4:
                    continue
                for h in range(4):
                    c, rows = h // 2, slice(64 * (h % 2), 64 * (h % 2) + 64)
                    S.mm(P[5][:, h * 128:(h + 1) * 128], v_t[:, h * 128:(h + 1) * 128], attn[:, h * 128:(h + 1) * 128],
                         start=True, stop=False, R=["v_t", "attn"], W=["P5"] if h == 0 else [], A=["P5"] if h > 0 else [])
                    for b in range(2):
                        rb, rbk = rbs[b]
                        S.mm(P[5][:, h * 128 + b * bs:h * 128 + (b + 1) * bs], rb[rows, c, :], qdT[rows, c, tt_ * 128 + b * bs:tt_ * 128 + (b + 1) * bs],
                             start=False, stop=(b == 1), R=[rbk, "qdT"], A=["P5"])
                if CUT <= 5:
                    continue
                _gated_norm_epilogue(X, ebuf, P[5], "P5", 0, cog[:, 0:1], "cog", tok0, ocs, "ocs", tc, P[2], "P2", P[6], "P6", wz, "wc")
            if CUT <= 6:
                continue
            S.dma("sp", oT_d[8 * 128:12 * 128, t0:t0 + n].rearrange("(c p) n -> p c n", p=128), ocs[:, :, 0:n], R=["ocs"])


def mixer_b(X):
    nc, S, cfg, l, I, O, P, PT, hT = X["nc"], X["S"], X["cfg"], X["l"], X["I"], X["O"], X["P"], X["PT"], X["hT"]
    KD, NTT, NS, T = cfg.KD, cfg.NTT, cfg.NS, cfg.T
    w_in, oT_d, ident, ones, onesb, wview = X["w_in"], X["oT_d"], X["ident"], X["ones"], X["onesb"], X["wview"]
    with contextlib.ExitStack() as ph:
        sbp = lambda n, s, d=F32: ph.enter_context(nc.sbuf_tensor("b%d_%s" % (l, n), list(s), d))
        tab = {}
        for ty in "ps":
            for nm, w in (("tri", 128), ("blk", 128), ("negS", 128), ("negIT", 128), ("valid", 1)):
                tab[nm + ty] = sbp(nm + ty, [128, w])
                S.dma("sp", tab[nm + ty][:], I["c_%s_%s" % (nm, ty)], W=["tab"])
        bog = sbp("bog", [128, 1]); S.dma("sp", bog[:], I["bog"][l], W=["bog"])
        cw = sbp("cw", [128, 12, 4]); S.dma("sp", cw[:], I["convw"][l], W=["cw"])
        alog = sbp("alog", [128, 4]); S.dma("sp", alog[:], I["alogB"][l], W=["alog"])
        dtb = sbp("dtb", [128, 4]); S.dma("sp", dtb[:], I["dtbB"][l], W=["dtb"])
        S.act(alog[:], alog[:], AF.Exp, R=["alog"], W=["alog"])
        wqh = [sbp("wqh%d" % i, [128, KD, 3, 128], BF16) for i in range(2)]
        wz = sbp("wz", [128, KD, 512], BF16)
        wbg = sbp("wbg", [128, KD, 8], BF16)
        S.dma("pool", wz[:], wview(w_in, BZ, 512), W=["wb"])
        S.dma("pool", wbg[:], wview(w_in, BBETA, 8), W=["wb"])
        pre = sbp("pre", [128, 3 + 512])
        carry = sbp("carry", [128, 12, 3])
        S.memset("pool", carry[:], 0.0, W=["carry"])
        pres = sbp("pres", [128, 4, 67])
        cst = sbp("cst", [12, 1536]); S.dma("sp", cst[:], I["sconv"][l], W=["cst"])
        cso = sbp("cso", [128, 12, 12])
        csin = sbp("csin", [128, 12, 12])
        cpo = sbp("cpo", [128, 12, 3])
        cout = sbp("cout", [12, 1536])
        acc = sbp("acc", [128, 512])
        yv = sbp("yv", [128, 512])
        sqb = sbp("sqb", [128, 512], BF16)
        rn = sbp("rn", [128, 512])
        fT = [{r: sbp("%sT%d" % (r, h), [128, 512]) for r in "qkv"} for h in range(4)]
        bgt = sbp("bgt", [128, 4, 8])
        nbe = sbp("nbe", [128, 4, 4])
        BN = ("sA", "sB", "sC", "egc", "Nm", "NTm", "PTm", "attnT", "Xb0", "Xb1", "XTb0", "XTb1", "u_", "wT", "qgT", "vnew", "k_tm", "v_tm")
        B = [{nm: sbp("%s_%d" % (nm, h), [128, 128]) for nm in BN} for h in range(4)]
        gcsb = [sbp("gcs%d" % h, [128, 4]) for h in range(4)]
        S32 = [sbp("S32_%d" % h, [128, 128]) for h in range(4)]
        ob = sbp("ob", [128, 4, 512])
        ebuf = (sbp("e_sq", [128, 512], BF16), sbp("e_rs", [128, 512]), sbp("e_on", [128, 512]), sbp("e_sz", [128, 512]))
        obs = sbp("obs", [128, 4, 512], BF16)
        for h in range(4):
            S.memset("pool", S32[h][:], 0.0, W=["S32_%d" % h])
        for c in range(12):
            S.tr(P[0][:, 0:12], cst[0:12, c * 128:(c + 1) * 128], ident[0:12, 0:12], R=["cst", "ident"], W=["P0"])
            S.cp("dve", csin[:, c, :], P[0][:, 0:12], R=["P0"], W=["csin"])

        def tile_chain(Sx, h, tt_, tok0, sample, ty, nlev):
            bb, pb, pk = B[h], P[1 + h], "P%d" % (1 + h)
            k_ = lambda nm: "%s_%d" % (nm, h)
            qT, kT, vT = fT[h]["q"], fT[h]["k"], fT[h]["v"]
            gcs, gk = gcsb[h], "gcs%d" % h
            s32, s32k = S32[h], "S32_%d" % h
            tc = slice(tt_ * 128, (tt_ + 1) * 128)
            be = bgt[:, tt_, h:h + 1]
            gg = bgt[:, tt_, 4 + h:5 + h]
            A_, B_, C_, D_ = pb[:, 0:128], pb[:, 128:256], pb[:, 256:384], pb[:, 384:512]
            Sx.tr(A_, kT[:, tc], ident[:], R=[k_("kT"), "ident"], W=[pk])
            Sx.tr(B_, vT[:, tc], ident[:], R=[k_("vT"), "ident"], A=[pk])
            Sx.cp("act", bb["k_tm"][:], A_, R=[pk], W=[k_("k_tm")])
            Sx.cp("dve", bb["v_tm"][:], B_, R=[pk], W=[k_("v_tm")])
            Sx.ts("dve", bb["sA"][:], ones[:], gg, None, ALU.mult, R=["ones", "bgt"], W=[k_("sA")])
            Sx.mm(A_, bb["sA"][:], tab["tri" + ty][:], R=[k_("sA"), "tab"], W=[pk])
            Sx.mm(pb[:, 128:129], tab["tri" + ty][:], gg, R=["tab", "bgt"], A=[pk])
            Sx.mm(pb[:, 129:130], tab["blk" + ty][:], gg, R=["tab", "bgt"], A=[pk])
            Sx.cp("dve", gcs[:, 0:2], pb[:, 128:130], R=[pk], W=[gk])
            Sx.stt("dve", bb["sA"][:], A_, gcs[:, 0:1], tab["negS" + ty][:], ALU.subtract, ALU.subtract, R=[pk, gk, "tab"], W=[k_("sA")])
            Sx.act(bb["sB"][:], bb["sA"][:], AF.Exp, scale=-1.0, R=[k_("sA")], W=[k_("sB")])
            Sx.stt("dve", bb["sA"][:], A_, gcs[:, 0:1], tab["negIT" + ty][:], ALU.subtract, ALU.add, R=[pk, gk, "tab", k_("sB")], W=[k_("sA")])
            Sx.act(bb["sC"][:], bb["sA"][:], AF.Exp, R=[k_("sA")], W=[k_("sC")])
            Sx.act(bb["egc"][:], A_, AF.Exp, R=[pk], W=[k_("egc")])
            Sx.tt("dve", gcs[:, 2:3], gcs[:, 1:2], gcs[:, 0:1], ALU.subtract, R=[gk], W=[gk])
            Sx.act(gcs[:, 2:3], gcs[:, 2:3], AF.Exp, R=[gk], W=[gk])
            Sx.tt("dve", gcs[:, 2:3], gcs[:, 2:3], tab["valid" + ty][:, 0:1], ALU.mult, R=[gk, "tab"], W=[gk])
            Sx.act(gcs[:, 3:4], gcs[:, 0:1], AF.Exp, R=[gk], W=[gk])
            Sx.tt("dve", gcs[:, 3:4], gcs[:, 3:4], be, ALU.mult, R=[gk, "bgt"], W=[gk])
            Sx.mm(C_, kT[:, tc], kT[:, tc], R=[k_("kT")], W=[pk])
            Sx.mm(D_, kT[:, tc], qT[:, tc], R=[k_("kT"), k_("qT")], A=[pk])
            Sx.stt("dve", bb["Nm"][:], C_, nbe[:, tt_, h:h + 1], bb["sB"][:], ALU.mult, ALU.mult, R=[pk, "nbe", k_("sB")], W=[k_("Nm")])
            Sx.tt("dve", bb["attnT"][:], D_, bb["sC"][:], ALU.mult, R=[pk, k_("sC")], W=[k_("attnT")])
            Sx.tr(A_, bb["Nm"][:], ident[:], R=[k_("Nm"), "ident"], W=[pk])
            Sx.cp("act", bb["NTm"][:], A_, R=[pk], W=[k_("NTm")])
            Sx.tt("pool", bb["PTm"][:], bb["NTm"][:], ident[:], ALU.add, R=[k_("NTm"), "ident"], W=[k_("PTm")])
            Xc, Xk, XTc, XTk = bb["Nm"], k_("Nm"), bb["NTm"], k_("NTm")
            for m in range(1, nlev + 1):
                X2, X2k = bb["Xb%d" % (m % 2)], k_("Xb%d" % (m % 2))
                XT2, XT2k = bb["XTb%d" % (m % 2)], k_("XTb%d" % (m % 2))
                Sx.mm(B_, XTc[:], Xc[:], R=[Xk, XTk], W=[pk])
                if m < nlev:
                    Sx.mm(C_, Xc[:], XTc[:], R=[Xk, XTk], A=[pk])
                Sx.cp("act", X2[:], B_, R=[pk], W=[X2k])
                if m < nlev:
                    Sx.cp("dve", XT2[:], C_, R=[pk], W=[XT2k])
                Sx.mm(D_, X2[:], bb["PTm"][:], R=[X2k, k_("PTm")], W=[pk])
                Sx.tt("dve", bb["PTm"][:], bb["PTm"][:], D_, ALU.add, R=[k_("PTm"), pk], W=[k_("PTm")])
                Xc, Xk, XTc, XTk = X2, X2k, XT2, XT2k
            Sx.ts("dve", bb["sA"][:], bb["v_tm"][:], be, None, ALU.mult, R=[k_("v_tm"), "bgt"], W=[k_("sA")])
            Sx.ts("pool", bb["sB"][:], bb["k_tm"][:], gcs[:, 3:4], None, ALU.mult, R=[k_("k_tm"), gk], W=[k_("sB")])
            Sx.mm(A_, bb["PTm"][:], bb["sA"][:], R=[k_("PTm"), k_("sA")], W=[pk])
            Sx.mm(B_, bb["sB"][:], bb["PTm"][:], R=[k_("PTm"), k_("sB")], A=[pk])
            Sx.cp("act", bb["u_"][:], A_, R=[pk], W=[k_("u_")])
            Sx.cp("dve", bb["wT"][:], B_, R=[pk], W=[k_("wT")])
            Sx.tt("pool", bb["qgT"][:], qT[:, tc], bb["egc"][:], ALU.mult, R=[k_("qT"), k_("egc")], W=[k_("qgT")])
            Sx.ts("pool", bb["sC"][:], bb["k_tm"][:], gcs[:, 2:3], None, ALU.mult, R=[k_("k_tm"), gk], W=[k_("sC")])
            for b in range(2):
                rows = slice(64 * b, 64 * b + 64)
                sq_ = 2 * (tok0 // 128 - NTT) + b
                if sample:
                    Sx.dma("sp", s32[:], I["sS"][l, sq_, h], W=[s32k])
                Sx.mm(C_, bb["wT"][:], s32[:], R=[k_("wT"), s32k], W=[pk])
                Sx.tt("dve", bb["vnew"][rows, :], bb["u_"][rows, :], pb[rows, 256:384], ALU.subtract, R=[k_("u_"), pk], W=[k_("vnew")])
                oc = slice(384 + 64 * b, 384 + 64 * b + 64)
                Sx.mm(pb[:, oc], s32[:], bb["qgT"][:, rows], start=True, stop=False, R=[s32k, k_("qgT")], W=[pk])
                Sx.mm(pb[:, oc], bb["vnew"][rows, :], bb["attnT"][rows, rows], start=False, stop=True, R=[k_("vnew"), k_("attnT")], A=[pk])
                Sx.cp("act", ob[:, h, tt_ * 128 + 64 * b:tt_ * 128 + 64 * b + 64], pb[:, oc], R=[pk], W=["ob%d" % h])
                Sx.mm(A_, bb["sC"][rows, :], bb["vnew"][rows, :], R=[k_("sC"), k_("vnew")], W=[pk])
                Sx.stt("dve", s32[:], s32[:], bb["egc"][:, 64 * b + 63:64 * b + 64], A_, ALU.mult, ALU.add,
                       R=[s32k, k_("egc"), pk], W=[s32k])
                if sample:
                    Sx.dma("sp", O["Ss"][l, sq_, h], s32[:], R=[s32k])
                elif tok0 + 128 == T and b == 1:
                    Sx.dma("sp", O["Sp"][l, h], s32[:], R=[s32k])

        for (t0, n) in X["blocks"]:
            ntl = n // 128
            sample = (t0 // 128 >= NTT)
            ty = "s" if sample else "p"
            nlev = 2 if sample else 5
            for tt_ in range(ntl):
                tok0 = t0 + tt_ * 128
                for kc in range(KD):
                    S.mm(P[0][:, 0:8], hT[:, kc, tok0:tok0 + 128], wbg[:, kc, :], start=(kc == 0), stop=(kc == KD - 1),
                         R=["wb", "hT"], W=["P0"] if kc == 0 else [], A=["P0"] if kc > 0 else [])
                S.act(bgt[:, tt_, 0:4], P[0][:, 0:4], AF.Sigmoid, R=["P0"], W=["bgt"])
                S.tt("dve", bgt[:, tt_, 4:8], P[0][:, 4:8], dtb[:], ALU.add, R=["P0", "dtb"], W=["bgt"])
                S.act(bgt[:, tt_, 4:8], bgt[:, tt_, 4:8], AF.Exp, R=["bgt"], W=["bgt"])
                S.act(bgt[:, tt_, 4:8], bgt[:, tt_, 4:8], AF.Ln, bias=1.0, R=["bgt"], W=["bgt"])
                S.stt("dve", bgt[:, tt_, 4:8], bgt[:, tt_, 4:8], -1.0, alog[:], ALU.mult, ALU.mult, R=["bgt", "alog"], W=["bgt"])
                S.ts("dve", bgt[:, tt_, 4:8], bgt[:, tt_, 4:8], tab["valid" + ty][:, 0:1], None, ALU.mult, R=["bgt", "tab"], W=["bgt"])
                S.ts("dve", nbe[:, tt_, :], bgt[:, tt_, 0:4], -1.0, None, ALU.mult, R=["bgt"], W=["nbe"])
            for h in range(4):
                wq_t, wqk = wqh[h % 2], "wqh%d" % (h % 2)
                for ri in range(3):
                    S.dma("pool", wq_t[:, :, ri, :], wview(w_in, BQKV + (ri * 4 + h) * 128, 128), W=[wqk])
                for ri, (role, c) in enumerate((("q", h), ("k", 4 + h), ("v", 8 + h))):
                    pp, ppk = (P[0], "P0") if (ri % 2 == 0) else (P[6], "P6")
                    for kc in range(KD):
                        S.mm(pp[:, 0:n], wq_t[:, kc, ri, :], hT[:, kc, t0:t0 + n], start=(kc == 0), stop=(kc == KD - 1),
                             R=[wqk, "hT"], W=[ppk] if kc == 0 else [], A=[ppk] if kc > 0 else [])
                    if not sample:
                        S.cp("pool", pre[:, 0:3], carry[:, c, :], R=["carry"], W=["pre"])
                        S.cp("act", pre[:, 3:3 + n], pp[:, 0:n], R=[ppk], W=["pre"])
                        S.ts("dve", acc[:, 0:n], pre[:, 0:n], cw[:, c, 0:1], None, ALU.mult, R=["pre", "cw"], W=["acc"])
                        for i in range(1, 4):
                            S.stt("dve", acc[:, 0:n], pre[:, i:i + n], cw[:, c, i:i + 1], acc[:, 0:n], ALU.mult, ALU.add,
                                  R=["pre", "cw", "acc"], W=["acc"])
                        if t0 + n == T:
                            S.cp("pool", cpo[:, c, :], pre[:, n:n + 3], R=["pre"], W=["cpo"])
                        S.cp("pool", carry[:, c, :], pre[:, n:n + 3], R=["pre"], W=["carry"])
                        accv = acc[:, 0:n]
                    else:
                        S.cp("pool", pres[:, :, 0:3], csin[:, c, :].rearrange("p (s k) -> p s k", s=4), R=["csin"], W=["pres"])
                        S.cp("act", pres[:, :, 3:67], pp[:, 0:256].rearrange("p (s k) -> p s k", s=4), R=[ppk], W=["pres"])
                        a3 = acc[:, 0:256].rearrange("p (s k) -> p s k", s=4)
                        S.ts("dve", a3, pres[:, :, 0:64], cw[:, c, 0:1], None, ALU.mult, R=["pres", "cw"], W=["acc"])
                        for i in range(1, 4):
                            S.stt("dve", a3, pres[:, :, i:i + 64], cw[:, c, i:i + 1], a3, ALU.mult, ALU.add, R=["pres", "cw", "acc"], W=["acc"])
                        S.cp("pool", cso[:, c, :].rearrange("p (s k) -> p s k", s=4), pres[:, :, 8:11], R=["pres"], W=["cso"])
                        accv = acc[:, 0:n]
                    fk = "%sT_%d" % (role, h)
                    if role == "v":
                        S.act(fT[h]["v"][:, 0:n], accv, AF.Silu, R=["acc"], W=[fk])
                    else:
                        S.act(yv[:, 0:n], accv, AF.Silu, R=["acc"], W=["yv"])
                        S.act(sqb[:, 0:n], yv[:, 0:n], AF.Square, R=["yv"], W=["sqb"])
                        S.mm(P[5][:, 0:n], onesb[:], sqb[:, 0:n], R=["onesb", "sqb"], W=["P5"])
                        X["rstd"](rn[:, 0:n], P[5][:, 0:n], 1.0, ["P5"], ["rn"])
                        sc = 128.0 ** -0.5 if role == "q" else 1.0
                        S.stt("dve", fT[h][role][:, 0:n], yv[:, 0:n], sc, rn[:, 0:n], ALU.mult, ALU.mult, R=["yv", "rn"], W=[fk])
            recs = []
            for h in range(4):
                r = Rec()
                for tt_ in range(ntl):
                    tile_chain(r, h, tt_, t0 + tt_ * 128, sample, ty, nlev)
                recs.append(r)
            replay_interleaved(S, recs)
            for tt_ in range(ntl):
                tok0 = t0 + tt_ * 128
                tc = slice(tt_ * 128, (tt_ + 1) * 128)
                _gated_norm_epilogue(X, ebuf, ob[:, :, tc], ["ob0", "ob1", "ob2", "ob3"], 0, bog[:, 0:1], "bog", tok0, obs, "obs", tc,
                                     P[0], "P0", P[5], "P5", wz, "wb")
            S.dma("sp", oT_d[4 * 128:8 * 128, t0:t0 + n].rearrange("(c p) n -> p c n", p=128), obs[:, :, 0:n], R=["obs"])
        for (src, sk, nrow, dst) in ((cpo, "cpo", 3, O["convp"][l]), (cso, "cso", 12, O["convs"][l])):
            for g4 in range(3):
                pb, pk = P[g4], "P%d" % g4
                for c4 in range(4):
                    c = g4 * 4 + c4
                    S.tr(pb[0:nrow, c4 * 128:(c4 + 1) * 128], src[:, c, 0:nrow], ident[:], R=[sk, "ident"],
                         W=[pk] if c4 == 0 else [], A=[pk] if c4 else [])
                S.cp("dve", cout[0:nrow, g4 * 512:(g4 + 1) * 512], pb[0:nrow, :], R=[pk], W=["cout"])
            S.dma("sp", dst, cout[0:nrow, :], R=["cout"])


def mixer_a(X):
    nc, S, cfg, l, I, O, P, PT, hT = X["nc"], X["S"], X["cfg"], X["l"], X["I"], X["O"], X["P"], X["PT"], X["hT"]
    KD, NTT, NS, T, NTOK = cfg.KD, cfg.NTT, cfg.NS, cfg.T, cfg.NTOK
    w_in, oT_d, ident, identb, onesb, wview = X["w_in"], X["oT_d"], X["ident"], X["identb"], X["onesb"], X["wview"]
    KEEP = (128, 512, min(2048, T))
    DIL = (1, 4, 16)
    sc_ = 128.0 ** -0.5
    with contextlib.ExitStack() as ph:
        sbp = lambda n, s, d=F32: ph.enter_context(nc.sbuf_tensor("a%d_%s" % (l, n), list(s), d))
        rotA = sbp("rotA", [128, 128], BF16); S.dma("pool", rotA[:], I["c_rotA"], W=["rotA"])
        mcur = sbp("mcur", [128, 128], BF16); S.dma("pool", mcur[:], I["c_mcur"][:, 0:128], W=["mask"])
        mprev = sbp("mprev", [128, 128], BF16); S.dma("pool", mprev[:], I["c_mprev"][:, 0:128], W=["mask"])
        msc = sbp("msc", [128, 104], BF16); S.dma("pool", msc[:], I["c_msc"], W=["mask"])
        msn = sbp("msn", [128, 48], BF16); S.dma("pool", msn[:], I["c_msn"], W=["mask"])
        aqg = sbp("aqg", [128, 1]); S.dma("sp", aqg[:], I["aqg"][l], W=["aqg"])
        akg = sbp("akg", [128, 1]); S.dma("sp", akg[:], I["akg"][l], W=["aqg"])
        nshift = sbp("nshift", [128, 1]); S.memset("pool", nshift[:], -SHIFT, W=["nshift"])
        wq = sbp("wq", [128, KD, 128], BF16); wk = sbp("wk", [128, KD, 128], BF16); wv = sbp("wv", [128, KD, 128], BF16)
        ct = sbp("ct", [128, 512]); sn = sbp("sn", [128, 512])
        QT = sbp("QT", [128, NTOK], BF16); KT = sbp("KT", [128, NTOK], BF16); K32 = sbp("K32", [128, NTOK])
        V = sbp("V", [128, cfg.NT, 128], BF16)
        Oacc = sbp("Oacc", [128, NTOK]); Dacc = sbp("Dacc", [128, NTOK])
        rn = sbp("rn", [128, 512])
        sqb2 = [sbp("sqb%d" % i, [128, 512], BF16) for i in range(2)]; rn2 = [sbp("rn%d" % i, [128, 512]) for i in range(2)]
        qb2 = [sbp("qb%d" % i, [128, 512], BF16) for i in range(2)]
        t12 = [sbp("t1%d" % i, [128, 512]) for i in range(2)]; t22 = [sbp("t2%d" % i, [128, 512]) for i in range(2)]
        pT2 = [sbp("pT%d" % i, [128, 128], BF16) for i in range(2)]
        v32 = sbp("v32", [128, 128]); k32 = sbp("k32", [128, 128])
        pT = sbp("pT", [128, 128], BF16)
        kvc = sbp("kvc", [128, 2, 128]); kcT = sbp("kcT", [128, 128], BF16); vcb = sbp("vcb", [128, 128], BF16)
        ost = sbp("ost", [128, 512], BF16)
        for h in range(4):
            S.memset("pool", Oacc[:], 0.0, W=["Oacc"])
            S.memset("pool", Dacc[:], 0.0, W=["Dacc"])
            for g in range(3):
                dil, keep = DIL[g], KEEP[g]
                kvp, kvs, kvin = O["kvp%d" % g], O["kvs%d" % g], I["kv%d" % g]
                for t_, i3 in ((wq, 0), (wk, 1), (wv, 2)):
                    S.dma("pool", t_[:], wview(w_in, ((i3 * 3 + g) * 4 + h) * 128, 128), W=["wa"])
                for (t0, n) in X["blocks"]:
                    S.dma("sp", ct[:, 0:n], I["c_cosA"][:, t0:t0 + n], W=["ct"])
                    S.dma("sp", sn[:, 0:n], I["c_sinA"][:, t0:t0 + n], W=["sn"])
                    recs = []
                    for si, (which, w_t, gcol) in enumerate((("q", wq, aqg), ("k", wk, akg))):
                        Sx = Rec()
                        Pa, Pb_, Pc = (P[0], P[1], P[2]) if si == 0 else (P[4], P[5], P[6])
                        ka, kb_, kc_ = ("P0", "P1", "P2") if si == 0 else ("P4", "P5", "P6")
                        sq_, rn_, qb_, t1_, t2_ = sqb2[si], rn2[si], qb2[si], t12[si], t22[si]
                        sfx = str(si)
                        for kc in range(KD):
                            Sx.mm(Pa[:, 0:n], w_t[:, kc, :], hT[:, kc, t0:t0 + n], start=(kc == 0), stop=(kc == KD - 1),
                                  R=["wa", "hT"], W=[ka] if kc == 0 else [], A=[ka] if kc > 0 else [])
                        Sx.act(sq_[:, 0:n], Pa[:, 0:n], AF.Square, R=[ka], W=["sqb" + sfx])
                        Sx.mm(Pb_[:, 0:n], onesb[:], sq_[:, 0:n], R=["onesb", "sqb" + sfx], W=[kb_])
                        Sx.act(rn_[:, 0:n], Pb_[:, 0:n], AF.Ln, bias=EPS, scale=1.0 / 128, R=[kb_], W=["rn" + sfx])
                        Sx.act(rn_[:, 0:n], rn_[:, 0:n], AF.Exp, scale=-0.5, R=["rn" + sfx], W=["rn" + sfx])
                        Sx.stt("dve", qb_[:, 0:n], Pa[:, 0:n], gcol[:, 0:1], rn_[:, 0:n], ALU.mult, ALU.mult, R=[ka, "aqg", "rn" + sfx], W=["qb" + sfx])
                        Sx.mm(Pc[:, 0:n], rotA[:], qb_[:, 0:n], R=["rotA", "qb" + sfx], W=[kc_])
                        Sx.tt("pool", t1_[:, 0:n], qb_[:, 0:n], ct[:, 0:n], ALU.mult, R=["qb" + sfx, "ct"], W=["t1" + sfx])
                        Sx.tt("dve", t2_[:, 0:n], Pc[:, 0:n], sn[:, 0:n], ALU.mult, R=[kc_, "sn"], W=["t2" + sfx])
                        if which == "q":
                            Sx.tt("pool", QT[:, t0:t0 + n], t1_[:, 0:n], t2_[:, 0:n], ALU.add, R=["t1" + sfx, "t2" + sfx], W=["QT"])
                        else:
                            Sx.tt("pool", K32[:, t0:t0 + n], t1_[:, 0:n], t2_[:, 0:n], ALU.add, R=["t1" + sfx, "t2" + sfx], W=["K32"])
                            Sx.cp("act", KT[:, t0:t0 + n], K32[:, t0:t0 + n], R=["K32"], W=["KT"])
                        recs.append(Sx)
                    replay_interleaved(S, recs)
                nbk = T // (128 * dil)
                def tcols(r, nb):
                    st0 = dil * 128 * nb + r
                    return slice(st0, st0 + dil * 127 + 1, dil)
                tiles = [(r, nb, tcols(r, nb), r * nbk + nb) for r in range(dil) for nb in range(nbk)]
                tiles += [(None, s3, slice(T + 128 * s3, T + 128 * s3 + 128), NTT + s3) for s3 in range(2)]
                for vi, (r, nb, cols, ti) in enumerate(tiles):
                    pv, pvk = P[(0, 1, 2, 3)[vi % 4]], "P%d" % (vi % 4)
                    for kc in range(KD):
                        S.mm(pv[:, 0:128], hT[:, kc, cols], wv[:, kc, :], start=(kc == 0), stop=(kc == KD - 1),
                             R=["wa", "hT"], W=[pvk] if kc == 0 else [], A=[pvk] if kc > 0 else [])
                    S.cp("act", V[:, ti, :], pv[:, 0:128], R=[pvk], W=["V"])
                    if r is None:
                        S.cp("dve", v32[:], pv[:, 0:128], R=[pvk], W=["v32"])
                        for s2 in range(2):
                            S.dma("sp", kvs[l, 2 * nb + s2, :, 1, h, :], v32[64 * s2:64 * s2 + 8, :], R=["v32"])
                    elif dil * 128 * nb >= T - keep:
                        S.cp("dve", v32[:], pv[:, 0:128], R=[pvk], W=["v32"])
                        r0 = dil * 128 * nb + r - (T - keep)
                        S.dma("sp", kvp[l, r0:r0 + dil * 127 + 1:dil, 1, h, :], v32[:], R=["v32"])
                for ti in list(range((T - keep) // 128, NTT)) + [NTT, NTT + 1]:
                    S.tr(P[3][:, 128:256], K32[:, ti * 128:(ti + 1) * 128], ident[:], R=["K32", "ident"], W=["P3"])
                    S.cp("dve", k32[:], P[3][:, 128:256], R=["P3"], W=["k32"])
                    if ti < NTT:
                        r0 = ti * 128 - (T - keep)
                        S.dma("sp", kvp[l, r0:r0 + 128, 0, h, :], k32[:], R=["k32"])
                    else:
                        for s2 in range(2):
                            S.dma("sp", kvs[l, 2 * (ti - NTT) + s2, :, 0, h, :], k32[64 * s2:64 * s2 + 8, :], R=["k32"])
                recs = [Rec(), Rec()]
                bi = 0
                for r in range(dil):
                    for nb in range(nbk):
                        si = bi % 2
                        bi += 1
                        Sx = recs[si]
                        Ps, Po, Pd = (P[0], P[1], P[2]) if si == 0 else (P[4], P[5], P[6])
                        ks, ko, kd = ("P0", "P1", "P2") if si == 0 else ("P4", "P5", "P6")
                        pT_, pTk = pT2[si], "pT%d" % si
                        qc = tcols(r, nb)
                        kbs = [kb for kb in (nb - 1, nb) if kb >= 0]
                        for i, kb in enumerate(kbs):
                            Sx.mm(Ps[:, 0:128], KT[:, tcols(r, kb)], QT[:, qc], R=["KT", "QT"], W=[ks])
                            Sx.act(pT_[:], Ps[:, 0:128], AF.Exp, bias=nshift[:, 0:1], scale=sc_, R=[ks, "nshift"], W=[pTk])
                            Sx.tt("pool", pT_[:], pT_[:], (mcur if kb == nb else mprev)[:], ALU.mult, R=[pTk, "mask"], W=[pTk])
                            Sx.mm(Po[:, 0:128], V[:, r * nbk + kb, :], pT_[:], start=(i == 0), stop=(i == len(kbs) - 1),
                                  R=["V", pTk], W=[ko] if i == 0 else [], A=[ko] if i > 0 else [])
                            Sx.mm(Pd[:, 0:128], onesb[:], pT_[:], start=(i == 0), stop=(i == len(kbs) - 1),
                                  R=["onesb", pTk], W=[kd] if i == 0 else [], A=[kd] if i > 0 else [])
                        Sx.tt("dve", Oacc[:, qc], Oacc[:, qc], Po[:, 0:128], ALU.add, R=["Oacc", ko], W=["Oacc"])
                        Sx.tt("dve", Dacc[:, qc], Dacc[:, qc], Pd[:, 0:128], ALU.add, R=["Dacc", kd], W=["Dacc"])
                replay_interleaved(S, recs)
                idx0 = (0, 1, 5)[g]
                for s in range(NS):
                    qs = slice(T + 64 * s, T + 64 * s + 8)
                    tl = NTT + s // 2
                    nr = min(dil, 8)
                    for r in range(nr + 1):
                        if r < nr:
                            S.dma("sp", kvc[:], kvin[l, s, r:r + dil * 127 + 1:dil, :].rearrange("p (a hh d) -> p a hh d", a=2, hh=4)[:, :, h, :], W=["kvc"])
                            S.tr(P[3][:, 256:384], kvc[:, 0, :], ident[:], R=["kvc", "ident"], W=["P3"])
                            S.cp("act", kcT[:], P[3][:, 256:384], R=["P3"], W=["kcT"])
                            S.cp("dve", vcb[:], kvc[:, 1, :], R=["kvc"], W=["vcb"])
                            S.mm(P[4][:, 0:8], kcT[:], QT[:, qs], R=["kcT", "QT"], W=["P4"])
                            mk = msc[:, (idx0 + r) * 8:(idx0 + r) * 8 + 8]
                            vv = vcb[:]
                        else:
                            S.mm(P[4][:, 0:8], KT[:, tl * 128:(tl + 1) * 128], QT[:, qs], R=["KT", "QT"], W=["P4"])
                            mk = msn[:, (g * 2 + s % 2) * 8:(g * 2 + s % 2) * 8 + 8]
                            vv = V[:, tl, :]
                        S.act(pT[:, 0:8], P[4][:, 0:8], AF.Exp, bias=nshift[:, 0:1], scale=sc_, R=["P4", "nshift"], W=["pT"])
                        S.tt("pool", pT[:, 0:8], pT[:, 0:8], mk, ALU.mult, R=["pT", "mask"], W=["pT"])
                        S.mm(P[5][:, 0:8], vv, pT[:, 0:8], start=(r == 0), stop=(r == nr), R=["vcb", "V", "pT"],
                             W=["P5"] if r == 0 else [], A=["P5"] if r > 0 else [])
                        S.mm(P[6][:, 0:8], onesb[:], pT[:, 0:8], start=(r == 0), stop=(r == nr), R=["onesb", "pT"],
                             W=["P6"] if r == 0 else [], A=["P6"] if r > 0 else [])
                    S.tt("dve", Oacc[:, qs], Oacc[:, qs], P[5][:, 0:8], ALU.add, R=["Oacc", "P5"], W=["Oacc"])
                    S.tt("dve", Dacc[:, qs], Dacc[:, qs], P[6][:, 0:8], ALU.add, R=["Dacc", "P6"], W=["Dacc"])
            for (t0, n) in X["blocks"]:
                S.ts("dve", rn[:, 0:n], Dacc[:, t0:t0 + n], 1e-30, None, ALU.max, R=["Dacc"], W=["rn"])
                S.op("dve", (lambda a, b: (lambda e: e.reciprocal(a, b)))(rn[:, 0:n], rn[:, 0:n]), ["rn"], ["rn"])
                S.tt("dve", ost[:, 0:n], Oacc[:, t0:t0 + n], rn[:, 0:n], ALU.mult, R=["Oacc", "rn"], W=["ost"])
                S.dma("sp", oT_d[h * 128:(h + 1) * 128, t0:t0 + n], ost[:, 0:n], R=["ost"])
```

```python
import contextlib
import numpy as np
import concourse.bass as bass
import concourse.mybir as mybir
from concourse.bass_utils import run_bass_kernel_spmd

F32 = mybir.dt.float32
BF16 = mybir.dt.bfloat16
AF = mybir.ActivationFunctionType
ALU = mybir.AluOpType

PAST_LEN = 8192
EPS = 1e-6
A_OFF, BQKV, BZ, BBETA, BA, CQ, CK, CV, CZ, GOFF = 0, 4608, 6144, 6656, 6660, 6664, 6920, 7176, 7688, 8200
NEG = -30000.0
SHIFT = 12.0


class Cfg:
    def __init__(self, D=1024, DFF=2816, T=4096, NS=4, DEPTH=2, dbg=False, mixers="abc"):
        self.D, self.DFF, self.T, self.NS, self.DEPTH, self.dbg = D, DFF, T, NS, DEPTH, dbg
        self.mixers = mixers
        self.KD = D // 128
        self.KF = DFF // 128
        self.NTT = T // 128
        self.NT = self.NTT + 2
        self.NTOK = self.NT * 128
        self.NIN = GOFF + 3 * D


class Sched:
    ENGS = ("pe", "act", "dve", "pool", "sp")
    DMAQ = ("sp", "act", "pool")

    def __init__(self, nc, stack, nslots=6):
        self.nc = nc
        self.prog = {e: [] for e in self.ENGS}
        self.cnt = {e: 0 for e in self.ENGS}
        self.sem = {e: stack.enter_context(nc.semaphore("s_" + e)) for e in self.ENGS}
        self.nslots = nslots
        self.dsem = {q: [stack.enter_context(nc.semaphore("d_%s%d" % (q, i))) for i in range(nslots)] for q in self.DMAQ}
        self.dval = {q: [0] * nslots for q in self.DMAQ}
        self.dnext = {q: 0 for q in self.DMAQ}
        self.seen = {e: {} for e in self.ENGS}
        self.lastw = {}
        self.readers = {}

    def _semof(self, ev):
        if ev[0] == "E":
            return ("E", ev[1]), self.sem[ev[1]], ev[2]
        return ("D", ev[1], ev[2]), self.dsem[ev[1]][ev[2]], ev[3]

    def _emit_waits(self, e, waits):
        best = {}
        for ev in waits:
            key, sem, val = self._semof(ev)
            if self.seen[e].get(key, 0) >= val:
                continue
            if key not in best or best[key][1] < val:
                best[key] = (sem, val)
        for key, (sem, val) in best.items():
            self.seen[e][key] = val
            self.prog[e].append(("w", sem, val))

    def _deps(self, R, W):
        waits = []
        for k in R:
            w = self.lastw.get(k)
            if w is not None:
                waits.append(w)
            if k in PSUM_KEYS:
                rd = self.readers.get(k)
                if rd:
                    waits.extend(rd.values())
        for k in W:
            w = self.lastw.get(k)
            if w is not None:
                waits.append(w)
            rd = self.readers.get(k)
            if rd:
                waits.extend(rd.values())
        return waits

    def _record(self, ev, R, W):
        key = self._semof(ev)[0]
        for k in R:
            self.readers.setdefault(k, {})[key] = ev
        for k in W:
            self.lastw[k] = ev
            self.readers[k] = {}

    def op(self, e, fn, R=(), W=(), A=()):
        NOPS[0] += 1
        if NOPS[0] > LIMIT:
            return None
        self._emit_waits(e, self._deps(R, W))
        self.cnt[e] += 1
        self.prog[e].append(("i", fn, self.sem[e]))
        ev = ("E", e, self.cnt[e])
        self._record(ev, R, list(W) + list(A))
        return ev

    def dma(self, q, out, in_, R=(), W=(), **kw):
        NOPS[0] += 1
        if NOPS[0] > LIMIT:
            return None
        waits = self._deps(R, W)
        s = self.dnext[q]
        self.dnext[q] = (s + 1) % self.nslots
        if self.dval[q][s] > 0:
            waits.append(("D", q, s, self.dval[q][s]))
        self._emit_waits(q, waits)
        self.dval[q][s] += 16
        self.prog[q].append(("d", out, in_, kw, self.dsem[q][s]))
        ev = ("D", q, s, self.dval[q][s])
        self._record(ev, R, W)
        return ev

    def barrier(self):
        evs = [("E", e, self.cnt[e]) for e in self.ENGS if self.cnt[e] > 0]
        for q in self.DMAQ:
            for s in range(self.nslots):
                if self.dval[q][s] > 0:
                    evs.append(("D", q, s, self.dval[q][s]))
        for e in self.ENGS:
            self._emit_waits(e, evs)
        self.lastw.clear()
        self.readers.clear()

    def emit(self):
        nc = self.nc
        for q in self.DMAQ:
            self._emit_waits(q, [("D", q, s, self.dval[q][s]) for s in range(self.nslots) if self.dval[q][s] > 0])
        self._emit_waits("sp", [("E", e, self.cnt[e]) for e in self.ENGS if e != "sp" and self.cnt[e] > 0])

        def run(e):
            def body(eng):
                for it in self.prog[e]:
                    if it[0] == "w":
                        eng.wait_ge(it[1], it[2])
                    elif it[0] == "i":
                        it[1](eng).then_inc(it[2], 1)
                    else:
                        eng.dma_start(out=it[1], in_=it[2], **it[3]).then_inc(it[4], 16)
            return body

        with nc.Block() as block:
            block.tensor(run("pe"))
            block.scalar(run("act"))
            block.vector(run("dve"))
            block.gpsimd(run("pool"))
            block.sync(run("sp"))

    def mm(self, out, lhsT, rhs, start=True, stop=True, R=(), W=(), A=()):
        return self.op("pe", lambda e: e.matmul(out, lhsT, rhs, start=start, stop=stop), R, W, A)

    def tr(self, out, in_, ident, R=(), W=(), A=()):
        return self.op("pe", lambda e: e.transpose(out, in_, ident), R, W, A)

    def act(self, out, in_, func, bias=0.0, scale=1.0, accum_out=None, R=(), W=()):
        if accum_out is None:
            return self.op("act", lambda e: e.activation(out, in_, func, bias=bias, scale=scale), R, W)
        return self.op("act", lambda e: e.activation(out, in_, func, bias=bias, scale=scale, accum_out=accum_out), R, W)

    def tt(self, eng, out, in0, in1, op, R=(), W=()):
        return self.op(eng, lambda e: e.tensor_tensor(out, in0, in1, op), R, W)

    def ts(self, eng, out, in0, s1, s2, op0, op1=None, R=(), W=()):
        if op1 is None:
            return self.op(eng, lambda e: e.tensor_scalar(out, in0, s1, None, op0), R, W)
        return self.op(eng, lambda e: e.tensor_scalar(out, in0, s1, s2, op0, op1), R, W)

    def stt(self, eng, out, in0, scalar, in1, op0, op1, R=(), W=()):
        return self.op(eng, lambda e: e.scalar_tensor_tensor(out, in0, scalar, in1, op0, op1), R, W)

    def cp(self, eng, out, in_, R=(), W=()):
        if eng == "act":
            return self.op("act", lambda e: e.copy(out, in_), R, W)
        return self.op(eng, lambda e: e.tensor_copy(out, in_), R, W)

    def memset(self, eng, ap, val, W=()):
        return self.op(eng, lambda e: e.memset(ap, val), (), W)


class Rec:
    def __init__(self):
        self.calls = []

    def __getattr__(self, name):
        def f(*a, **k):
            self.calls.append((name, a, k))
        return f


def replay_interleaved(S, recs):
    n = max(len(r.calls) for r in recs)
    for i in range(n):
        for r in recs:
            if i < len(r.calls):
                name, a, k = r.calls[i]
                getattr(S, name)(*a, **k)


def _tile_consts(bs, nvalid, gam):
    p = np.arange(128)
    blk, loc = p // bs, p % bs
    same = blk[:, None] == blk[None, :]
    val = loc < nvalid
    c = {}
    c["tri"] = (same & (p[:, None] <= p[None, :])).astype(np.float32)
    c["blk"] = same.astype(np.float32)
    c["negS"] = np.where(same & (p[:, None] > p[None, :]), 0.0, NEG).astype(np.float32)
    c["negIT"] = np.where(same & (p[None, :] >= p[:, None]), 0.0, NEG).astype(np.float32)
    c["valid"] = val.astype(np.float32)[:, None].copy()
    dtc = np.zeros((128, 4, 128), np.float64)
    for h in range(4):
        d = (p[None, :] - p[:, None]).astype(np.float64)
        dtc[:, h, :] = np.where(same & (d >= 0) & val[:, None] & val[None, :], gam[h] ** np.maximum(d, 0), 0.0)
    c["dtc"] = dtc.reshape(128, 512).astype(np.float32)
    qd = np.zeros((128, 2, 128), np.float64)
    gcol = np.zeros((128, 2), np.float64)
    for pr in range(2):
        for half in range(2):
            h = 2 * pr + half
            qd[64 * half:64 * half + 64, pr, :] = (gam[h] ** (loc + 1.0))[None, :]
            gcol[64 * half:64 * half + 64, pr] = gam[h] ** nvalid
    c["qdec"] = qd.reshape(128, 256).astype(np.float32)
    c["gC"] = gcol.astype(np.float32)
    kd = np.zeros((128, 4, 64), np.float64)
    for h in range(4):
        kd[:, h, :] = np.where(val, gam[h] ** np.maximum(nvalid - 1.0 - loc, 0), 0.0)[:, None]
    c["kdec"] = kd.reshape(128, 256).astype(np.float32)
    return c


def make_consts(cfg):
    T, NTOK = cfg.T, cfg.NTOK
    pos = np.zeros(NTOK, np.float32)
    pos[:T] = np.arange(T)
    for s in range(4):
        pos[T + 64 * s:T + 64 * s + 64] = PAST_LEN + np.arange(64)
    C = {}
    C["ident"] = np.eye(128, dtype=np.float32)
    C["ones"] = np.ones((128, 128), np.float32)
    d = np.arange(128)
    invA = (np.float32(10000.0) ** (-np.arange(0, 128, 2, dtype=np.float32) / np.float32(128))).astype(np.float32)
    angA = (pos[None, :] * invA[d % 64][:, None]).astype(np.float32).astype(np.float64)
    C["cosA"] = np.cos(angA).astype(np.float32)
    C["sinA"] = (np.sin(angA) * np.where(d < 64, -1.0, 1.0)[:, None]).astype(np.float32)
    rotA = np.zeros((128, 128), np.float32)
    rotA[(d + 64) % 128, d] = 1.0
    C["rotA"] = rotA
    invC = (np.float32(10000.0) ** (-np.arange(0, 64, 2, dtype=np.float32) / np.float32(64))).astype(np.float32)
    dd = d % 64
    angC = (pos[None, :] * invC[dd % 32][:, None]).astype(np.float32).astype(np.float64)
    C["cosC"] = np.cos(angC).astype(np.float32)
    C["sinC"] = (np.sin(angC) * np.where(dd < 32, -1.0, 1.0)[:, None]).astype(np.float32)
    rotC = np.zeros((128, 128), np.float32)
    rotC[(d // 64) * 64 + (dd + 32) % 64, d] = 1.0
    C["rotC"] = rotC
    k = np.arange(128)
    C["mcur"] = np.tile((k[:, None] <= k[None, :]).astype(np.float32), (1, 4))
    C["mprev"] = np.tile((k[:, None] >= k[None, :]).astype(np.float32), (1, 4))
    msc = np.zeros((13, 128, 8), np.float32)
    msn = np.zeros((3, 2, 128, 8), np.float32)
    idx = 0
    for g, dil in enumerate((1, 4, 16)):
        for r in range(min(dil, 8)):
            for i in range(8):
                if i % dil == r:
                    msc[idx, :, i] = (k >= i // dil)
            idx += 1
        for s2 in range(2):
            for i in range(8):
                for j in range(8):
                    if j <= i and (i - j) % dil == 0:
                        msn[g, s2, 64 * s2 + j, i] = 1.0
    C["msc"] = np.ascontiguousarray(msc.transpose(1, 0, 2)).reshape(128, 13 * 8)
    C["msn"] = np.ascontiguousarray(msn.transpose(2, 0, 1, 3)).reshape(128, 48)
    gam = [1.0 - 2.0 ** (-5.0 - h) for h in range(4)]
    for nm, (bs, nv) in (("p", (64, 64)), ("s", (64, 8))):
        for kk, v in _tile_consts(bs, nv, gam).items():
            C[kk + "_" + nm] = v
    return C


CUT = 99
LIMIT = 10 ** 9
PADOPS = 0
PSUM_KEYS = frozenset(["P%d" % i for i in range(7)] + ["PT"])
VAR = 0
NOPS = [0]


def build(cfg):
    D, DFF, T, NS, DEPTH = cfg.D, cfg.DFF, cfg.T, cfg.NS, cfg.DEPTH
    KD, KF, NTT, NT, NTOK, NIN = cfg.KD, cfg.KF, cfg.NTT, cfg.NT, cfg.NTOK, cfg.NIN
    assert NS == 4 and T % 2048 == 0
    nc = bass.Bass("TRN2", target_bir_lowering=False)
    consts = make_consts(cfg)

    def din(name, shape, dt=F32):
        return nc.dram_tensor(name, list(shape), dt, kind="ExternalInput").ap()

    def dout(name, shape, dt=F32):
        return nc.dram_tensor(name, list(shape), dt, kind="ExternalOutput").ap()

    def dscr(name, shape, dt):
        return nc.dram_tensor(name, list(shape), dt, kind="ExternalOutput" if cfg.dbg else "Internal").ap()

    I = {}
    I["xp"] = din("xp", [T, D])
    I["xs"] = din("xs", [NS * 8, D])
    I["kv0"] = din("kv0", [DEPTH, NS, 128, 1024])
    I["kv1"] = din("kv1", [DEPTH, NS, 512, 1024])
    I["kv2"] = din("kv2", [DEPTH, NS, 2048, 1024])
    I["sconv"] = din("sconv", [DEPTH, NS * 3, 1536])
    I["sS"] = din("sS", [DEPTH, NS, 4, 128, 128])
    I["sR"] = din("sR", [DEPTH, NS, 2, 128, 128])
    I["g1B"] = din("g1B", [DEPTH, 128, D])
    I["g2B"] = din("g2B", [DEPTH, 128, D])
    I["w_in"] = din("w_in", [DEPTH, D, NIN])
    I["aqg"] = din("aqg", [DEPTH, 128, 1])
    I["akg"] = din("akg", [DEPTH, 128, 1])
    I["convw"] = din("convw", [DEPTH, 128, 12, 4])
    I["alogB"] = din("alogB", [DEPTH, 128, 4])
    I["dtbB"] = din("dtbB", [DEPTH, 128, 4])
    I["bog"] = din("bog", [DEPTH, 128, 1])
    I["cog"] = din("cog", [DEPTH, 128, 1])
    I["w_oa"] = din("w_oa", [DEPTH, 512, D])
    I["w_ob"] = din("w_ob", [DEPTH, 512, D])
    I["w_oc"] = din("w_oc", [DEPTH, 512, D])
    I["w_out"] = din("w_out", [DEPTH, D, D])
    I["w_fi"] = din("w_fi", [DEPTH, D, 2 * DFF])
    I["w_fo"] = din("w_fo", [DEPTH, DFF, D])
    for k, v in consts.items():
        I["c_" + k] = din("c_" + k, v.shape)

    O = {}
    O["yp"] = dout("yp", [T, D])
    O["ys"] = dout("ys", [NS * 8, D])
    KEEP = (128, 512, min(2048, T))
    for g in range(3):
        O["kvp%d" % g] = dout("kvp%d" % g, [DEPTH, KEEP[g], 2, 4, 128])
        O["kvs%d" % g] = dout("kvs%d" % g, [DEPTH, NS, 8, 2, 4, 128])
    O["convp"] = dout("convp", [DEPTH, 3, 1536])
    O["Sp"] = dout("Sp", [DEPTH, 4, 128, 128])
    O["Rp"] = dout("Rp", [DEPTH, 2, 128, 128])
    O["convs"] = dout("convs", [DEPTH, NS * 3, 1536])
    O["Ss"] = dout("Ss", [DEPTH, NS, 4, 128, 128])
    O["Rs"] = dout("Rs", [DEPTH, NS, 2, 128, 128])
    xres = dscr("xres", [NTOK, D], F32)
    oT_d = dscr("oT_d", [12 * 128, NTOK], BF16)
    sg_d = dscr("sg_d", [3 * D, NTOK], BF16)
    WB = {}
    for nm, shp in (("w_oa", [512, D]), ("w_ob", [512, D]), ("w_oc", [512, D]), ("w_out", [D, D]), ("w_fi", [D, 2 * DFF]), ("w_fo", [DFF, D])):
        WB[nm] = nc.dram_tensor("wbf_" + nm, [DEPTH] + shp, BF16, kind="Internal").ap()

    with contextlib.ExitStack() as st:
        S = Sched(nc, st)
        sb = lambda n, s, d=F32: st.enter_context(nc.sbuf_tensor(n, list(s), d))
        P = [st.enter_context(nc.psum_tensor("p%d" % i, [128, 512], F32)) for i in range(7)]
        PT = st.enter_context(nc.psum_tensor("pT", [128, 1024], BF16))

        ident = sb("ident", [128, 128])
        identb = sb("identb", [128, 128], BF16)
        ones = sb("ones", [128, 128])
        onesb = sb("onesb", [128, 128], BF16)
        S.dma("sp", ident[:], I["c_ident"], W=["ident"])
        S.dma("pool", identb[:], I["c_ident"], W=["identb"])
        S.dma("sp", ones[:], I["c_ones"], W=["ones"])
        S.dma("pool", onesb[:], I["c_ones"], W=["onesb"])
        for l_ in range(DEPTH):
            for nm in ("w_oa", "w_ob", "w_oc", "w_out", "w_fi", "w_fo"):
                rows = WB[nm].shape[1]
                for r0 in range(0, rows, 128):
                    S.dma("pool", WB[nm][l_, r0:r0 + 128, :], I[nm][l_, r0:r0 + 128, :], W=["wbf_%s_%d_%d" % (nm, l_, r0)])
        NBLK = (NTOK + 511) // 512
        blocks = [(b * 512, min(512, NTOK - b * 512)) for b in range(NBLK)]

        def wview(ap2d, c0, ncols):
            return ap2d.rearrange("(kc p) n -> p kc n", p=128)[:, :, c0:c0 + ncols]

        def rstd_from_ss(dst, src, scale, R, W):
            S.act(dst, src, AF.Ln, bias=EPS, scale=scale, R=R, W=W)
            S.act(dst, dst, AF.Exp, scale=-0.5, R=W, W=W)

        def norm_tile(x_t, xkey, gB, h_t, hkey, junk, ss, hTdst, tcols, hTkey):
            S.memset("dve", ss[:, 0:1], 0.0, W=["ss"])
            S.act(junk[:], x_t, AF.Square, accum_out=ss[:, 0:1], R=[xkey, "ss"], W=["junk", "ss"])
            rstd_from_ss(ss[:, 0:1], ss[:, 0:1], 1.0 / D, ["ss"], ["ss"])
            S.stt("dve", h_t, x_t, ss[:, 0:1], gB, ALU.mult, ALU.mult, R=[xkey, "ss", "gB"], W=[hkey])
            for kc in range(KD):
                S.tr(PT[:, kc * 128:(kc + 1) * 128], h_t[:, kc * 128:(kc + 1) * 128], identb[:],
                     R=[hkey, "identb"], W=["PT"] if kc == 0 else [], A=["PT"] if kc > 0 else [])
            S.cp("dve", hTdst[:, :, tcols], PT[:, 0:KD * 128].rearrange("p (k n) -> p k n", k=KD), R=["PT"], W=[hTkey])

        for l in range(DEPTH):
            w_in = I["w_in"][l]
            lay = contextlib.ExitStack()
            hT = lay.enter_context(nc.sbuf_tensor("hT%d" % l, [128, KD, NTOK], BF16))
            with contextlib.ExitStack() as ph:
                sbp = lambda n, s, d=F32: ph.enter_context(nc.sbuf_tensor("a%d_%s" % (l, n), list(s), d))
                gB = sbp("gB", [128, D])
                S.dma("sp", gB[:], I["g1B"][l], W=["gB"])
                xt = [sbp("xt%d" % i, [128, D]) for i in range(2)]
                ht = [sbp("ht%d" % i, [128, D], BF16) for i in range(2)]
                junk = sbp("junk", [128, D])
                ss = sbp("ss", [128, 1])
                for i in range(NT):
                    x_t, h_t = xt[i % 2], ht[i % 2]
                    xk, hk = "xt%d" % (i % 2), "ht%d" % (i % 2)
                    if l == 0:
                        if i < NTT:
                            S.dma("sp", x_t[:], I["xp"][i * 128:(i + 1) * 128, :], W=[xk])
                        else:
                            S.memset("dve", x_t[:], 0.0, W=[xk])
                            for s2 in range(2):
                                s = 2 * (i - NTT) + s2
                                S.dma("sp", x_t[64 * s2:64 * s2 + 8, :], I["xs"][8 * s:8 * s + 8, :], W=[xk])
                    else:
                        S.dma("sp", x_t[:], xres[i * 128:(i + 1) * 128, :], W=[xk])
                    norm_tile(x_t[:], xk, gB[:], h_t[:], hk, junk, ss, hT, slice(i * 128, (i + 1) * 128), "hT")
            S.barrier()
            with contextlib.ExitStack() as ph:
                sbp = lambda n, s, d=F32: ph.enter_context(nc.sbuf_tensor("g%d_%s" % (l, n), list(s), d))
                wg = [sbp("wg%d" % i, [128, KD, 512], BF16) for i in range(2)]
                sgo = [sbp("sgo%d" % i, [128, 4, 512], BF16) for i in range(2)]
                ncb = 3 * D // 512
                it = 0
                for cb in range(ncb):
                    w_t, wk = wg[cb % 2], "wg%d" % (cb % 2)
                    S.dma("pool", w_t[:], wview(w_in, GOFF + cb * 512, 512), W=[wk])
                    for (t0, n) in blocks:
                        so, sk = sgo[it % 2], "sgo%d" % (it % 2)
                        for j in range(4):
                            pp, pk = P[(it * 4 + j) % 4], "P%d" % ((it * 4 + j) % 4)
                            for kc in range(KD):
                                S.mm(pp[:, 0:n], w_t[:, kc, j * 128:(j + 1) * 128], hT[:, kc, t0:t0 + n],
                                     start=(kc == 0), stop=(kc == KD - 1), R=[wk, "hT"],
                                     W=[pk] if kc == 0 else [], A=[pk] if kc > 0 else [])
                            S.act(so[:, j, 0:n], pp[:, 0:n], AF.Sigmoid, R=[pk], W=[sk])
                        S.dma("sp", sg_d[cb * 512:(cb + 1) * 512, t0:t0 + n].rearrange("(j p) n -> p j n", p=128),
                              so[:, :, 0:n], R=[sk])
                        it += 1
            S.barrier()
            X = dict(nc=nc, S=S, cfg=cfg, l=l, I=I, O=O, P=P, PT=PT, hT=hT, oT_d=oT_d, ident=ident, identb=identb,
                     ones=ones, onesb=onesb, blocks=blocks, wview=wview, rstd=rstd_from_ss, w_in=w_in)
            for mi, ch in enumerate("abc"):
                if ch not in cfg.mixers:
                    zt = lay.enter_context(nc.sbuf_tensor("zt%d_%d" % (l, mi), [128, 4, NTOK], BF16))
                    S.memset("dve", zt[:], 0.0, W=["zt"])
                    S.dma("sp", oT_d[mi * 512:(mi + 1) * 512, :].rearrange("(c p) n -> p c n", p=128), zt[:], R=["zt"])
            S.barrier()
            if "c" in cfg.mixers:
                mixer_c(X)
                S.barrier()
            if "b" in cfg.mixers:
                mixer_b(X)
                S.barrier()
            if "a" in cfg.mixers:
                mixer_a(X)
                S.barrier()
            lay.close()
            with contextlib.ExitStack() as ph:
                sbp = lambda n, s, d=F32: ph.enter_context(nc.sbuf_tensor("d%d_%s" % (l, n), list(s), d))
                g2B = sbp("g2B", [128, D])
                S.dma("sp", g2B[:], I["g2B"][l], W=["gB"])
                oTb = sbp("oTb", [128, 12, 512], BF16)
                sgb = sbp("sgb", [128, 3 * KD, 512], BF16)
                mT = sbp("mT", [128, KD, 512], BF16)
                h2T = sbp("h2T", [128, KD, 512], BF16)
                aT = sbp("aT", [128, KF, 512], BF16)
                x1 = [sbp("x1_%d" % i, [128, D]) for i in range(4)]
                h2 = sbp("h2", [128, D], BF16)
                junk = sbp("junk", [128, D])
                ss = sbp("ss", [128, 1])
                tmp = [sbp("tmp%d" % i, [128, 512]) for i in range(3)]
                NW = 3
                wb = [sbp("wb%d" % i, [128, 8, 512], BF16) for i in range(NW)]
                wfo = sbp("wfo", [128, KF, 512], BF16)
                wctr = [0]

                def loadw(src2d, c0, ncols, k0, nk):
                    i = wctr[0] % NW
                    wctr[0] += 1
                    v = src2d.rearrange("(kc p) n -> p kc n", p=128)[:, k0:k0 + nk, c0:c0 + ncols]
                    S.dma("sp", wb[i][:, 0:nk, 0:ncols], v, W=["wb%d" % i])
                    return wb[i], "wb%d" % i

                pctr = [0]

                def nextp():
                    i = pctr[0] % 6
                    pctr[0] += 1
                    return P[i], "P%d" % i

                for (t0, n) in blocks:
                    ntl = n // 128
                    S.dma("sp", oTb[:, :, 0:n], oT_d[:, t0:t0 + n].rearrange("(c p) n -> p c n", p=128), W=["oTb"])
                    S.dma("sp", sgb[:, :, 0:n], sg_d[:, t0:t0 + n].rearrange("(c p) n -> p c n", p=128), W=["sgb"])
                    for cb in range(D // 512):
                        ws = []
                        for j, nm in enumerate(("w_oa", "w_ob", "w_oc")):
                            ws.append(loadw(WB[nm][l], cb * 512, 512, 0, 4))
                        for oc4 in range(4):
                            oc = cb * 4 + oc4
                            for j in range(3):
                                w_t, wk = ws[j]
                                pp, pk = nextp()
                                for kc in range(4):
                                    S.mm(pp[:, 0:n], w_t[:, kc, oc4 * 128:(oc4 + 1) * 128], oTb[:, 4 * j + kc, 0:n],
                                         start=(kc == 0), stop=(kc == 3), R=[wk, "oTb"],
                                         W=[pk] if kc == 0 else [], A=[pk] if kc > 0 else [])
                                S.tt("dve", tmp[j][:, 0:n], pp[:, 0:n], sgb[:, j * KD + oc, 0:n], ALU.mult,
                                     R=[pk, "sgb"], W=["tmp%d" % j])
                            S.tt("pool", tmp[0][:, 0:n], tmp[0][:, 0:n], tmp[1][:, 0:n], ALU.add, R=["tmp0", "tmp1"], W=["tmp0"])
                            S.tt("pool", mT[:, oc, 0:n], tmp[0][:, 0:n], tmp[2][:, 0:n], ALU.add, R=["tmp0", "tmp2"], W=["mT"])
                    for tt_ in range(ntl):
                        tok0 = t0 + tt_ * 128
                        ti = tok0 // 128
                        xk = "x1_%d" % tt_
                        x_t = x1[tt_]
                        if l == 0:
                            if ti < NTT:
                                S.dma("sp", x_t[:], I["xp"][tok0:tok0 + 128, :], W=[xk])
                            else:
                                S.memset("dve", x_t[:], 0.0, W=[xk])
                                for s2 in range(2):
                                    s = 2 * (ti - NTT) + s2
                                    S.dma("sp", x_t[64 * s2:64 * s2 + 8, :], I["xs"][8 * s:8 * s + 8, :], W=[xk])
                        else:
                            S.dma("sp", x_t[:], xres[tok0:tok0 + 128, :], W=[xk])
                    for cb in range(D // 512):
                        w_t, wk = loadw(WB["w_out"][l], cb * 512, 512, 0, KD)
                        for tt_ in range(ntl):
                            pp, pk = nextp()
                            xk = "x1_%d" % tt_
                            for kc in range(KD):
                                S.mm(pp[:], mT[:, kc, tt_ * 128:(tt_ + 1) * 128], w_t[:, kc, :], start=(kc == 0), stop=(kc == KD - 1),
                                     R=[wk, "mT"], W=[pk] if kc == 0 else [], A=[pk] if kc > 0 else [])
                            S.tt("dve", x1[tt_][:, cb * 512:(cb + 1) * 512], x1[tt_][:, cb * 512:(cb + 1) * 512], pp[:], ALU.add,
                                 R=[pk, xk], W=[xk])
                    for tt_ in range(ntl):
                        norm_tile(x1[tt_][:], "x1_%d" % tt_, g2B[:], h2[:], "h2", junk, ss, h2T,
                                  slice(tt_ * 128, (tt_ + 1) * 128), "h2T")
                    for fb in range((DFF + 511) // 512):
                        f0 = fb * 512
                        fn_ = min(512, DFF - f0)
                        wg_t, wgk = loadw(WB["w_fi"][l], f0, fn_, 0, KD)
                        wu_t, wuk = loadw(WB["w_fi"][l], DFF + f0, fn_, 0, KD)
                        for fc in range(fn_ // 128):
                            pg, pgk = nextp()
                            pu, puk = nextp()
                            for kc in range(KD):
                                S.mm(pg[:, 0:n], wg_t[:, kc, fc * 128:(fc + 1) * 128], h2T[:, kc, 0:n], start=(kc == 0), stop=(kc == KD - 1),
                                     R=[wgk, "h2T"], W=[pgk] if kc == 0 else [], A=[pgk] if kc > 0 else [])
                            for kc in range(KD):
                                S.mm(pu[:, 0:n], wu_t[:, kc, fc * 128:(fc + 1) * 128], h2T[:, kc, 0:n], start=(kc == 0), stop=(kc == KD - 1),
                                     R=[wuk, "h2T"], W=[puk] if kc == 0 else [], A=[puk] if kc > 0 else [])
                            S.act(tmp[0][:, 0:n], pg[:, 0:n], AF.Silu, R=[pgk], W=["tmp0"])
                            S.tt("dve", aT[:, f0 // 128 + fc, 0:n], tmp[0][:, 0:n], pu[:, 0:n], ALU.mult, R=["tmp0", puk], W=["aT"])
                    for cb in range(D // 512):
                        S.dma("sp", wfo[:], WB["w_fo"][l].rearrange("(kc p) n -> p kc n", p=128)[:, :, cb * 512:(cb + 1) * 512], W=["wfo"])
                        for tt_ in range(ntl):
                            pp, pk = nextp()
                            xk = "x1_%d" % tt_
                            for fc in range(KF):
                                S.mm(pp[:], aT[:, fc, tt_ * 128:(tt_ + 1) * 128], wfo[:, fc, :], start=(fc == 0), stop=(fc == KF - 1),
                                     R=["wfo", "aT"], W=[pk] if fc == 0 else [], A=[pk] if fc > 0 else [])
                            S.tt("dve", x1[tt_][:, cb * 512:(cb + 1) * 512], x1[tt_][:, cb * 512:(cb + 1) * 512], pp[:], ALU.add,
                                 R=[pk, xk], W=[xk])
                    dst = xres if l < DEPTH - 1 else None
                    for tt_ in range(ntl):
                        tok0 = t0 + tt_ * 128
                        ti = tok0 // 128
                        xk = "x1_%d" % tt_
                        if dst is not None:
                            S.dma("sp", xres[tok0:tok0 + 128, :], x1[tt_][:], R=[xk])
                        else:
                            if ti < NTT:
                                S.dma("sp", O["yp"][tok0:tok0 + 128, :], x1[tt_][:], R=[xk])
                            else:
                                for s2 in range(2):
                                    s = 2 * (ti - NTT) + s2
                                    S.dma("sp", O["ys"][8 * s:8 * s + 8, :], x1[tt_][64 * s2:64 * s2 + 8, :], R=[xk])
            S.barrier()
        for _ in range(PADOPS):
            S.memset("dve", ones[:], 1.0, W=["ones"])
        S.emit()
    return nc


def prep_core_inputs(cfg, inp, core, consts):
    D, NS, DEPTH = cfg.D, cfg.NS, cfg.DEPTH
    f = lambda a: np.ascontiguousarray(np.asarray(a, dtype=np.float32))
    nb = inp["x_prompt"].shape[0]
    ss = slice(core * NS, (core + 1) * NS)
    m = {}
    m["xp"] = f(inp["x_prompt"][core % nb])
    m["xs"] = f(inp["x_sample"][ss]).reshape(NS * 8, D)
    m["kv0"] = f(inp["cache_a_kv0"][:, ss]).reshape(DEPTH, NS, -1, 1024)
    m["kv1"] = f(inp["cache_a_kv1"][:, ss]).reshape(DEPTH, NS, -1, 1024)
    m["kv2"] = f(inp["cache_a_kv2"][:, ss]).reshape(DEPTH, NS, -1, 1024)
    m["sconv"] = f(inp["state_b_conv"][:, ss]).reshape(DEPTH, NS * 3, 1536)
    m["sS"] = f(inp["state_b_S"][:, ss])
    m["sR"] = f(inp["state_c_R"][:, ss]).reshape(DEPTH, NS, 2, 128, 128)
    m["g1B"] = f(np.broadcast_to(np.asarray(inp["norm1_g"])[:, None, :], (DEPTH, 128, D)))
    m["g2B"] = f(np.broadcast_to(np.asarray(inp["norm2_g"])[:, None, :], (DEPTH, 128, D)))
    m["w_in"] = f(inp["w_in"])
    m["aqg"] = f(inp["a_q_norm_g"]).reshape(DEPTH, 128, 1)
    m["akg"] = f(inp["a_k_norm_g"]).reshape(DEPTH, 128, 1)
    m["convw"] = f(np.asarray(inp["b_conv_w"]).reshape(DEPTH, 4, 12, 128).transpose(0, 3, 2, 1))
    m["alogB"] = f(np.broadcast_to(np.asarray(inp["b_a_log"])[:, None, :], (DEPTH, 128, 4)))
    m["dtbB"] = f(np.broadcast_to(np.asarray(inp["b_dt_bias"])[:, None, :], (DEPTH, 128, 4)))
    m["bog"] = f(inp["b_out_norm_g"]).reshape(DEPTH, 128, 1)
    m["cog"] = f(inp["c_out_norm_g"]).reshape(DEPTH, 128, 1)
    m["w_oa"] = f(inp["w_out_a"])
    m["w_ob"] = f(inp["w_out_b"])
    m["w_oc"] = f(inp["w_out_c"])
    m["w_out"] = f(inp["w_out"])
    m["w_fi"] = f(inp["w_ffn_in"])
    m["w_fo"] = f(inp["w_ffn_out"])
    for k, v in consts.items():
        m["c_" + k] = v
    return m


def assemble(cfg, res, nb, ncores):
    DEPTH, NS = cfg.DEPTH, cfg.NS
    r = res
    st = lambda k, cores: np.stack([np.asarray(r[c][k]) for c in cores])
    pc = list(range(nb))
    ac = list(range(ncores))
    yp = st("yp", pc)
    ys = np.concatenate([np.asarray(r[c]["ys"]).reshape(NS, 8, cfg.D) for c in ac], axis=0)
    outs = [yp, ys]
    for g in range(3):
        outs.append(st("kvp%d" % g, pc).transpose(1, 0, 2, 3, 4, 5))
    outs.append(st("convp", pc).transpose(1, 0, 2, 3))
    outs.append(st("Sp", pc).transpose(1, 0, 2, 3, 4))
    outs.append(st("Rp", pc).transpose(1, 0, 2, 3, 4).reshape(DEPTH, nb, 4, 64, 128))
    for g in range(3):
        outs.append(np.concatenate([np.asarray(r[c]["kvs%d" % g]) for c in ac], axis=1))
    outs.append(np.concatenate([np.asarray(r[c]["convs"]).reshape(DEPTH, NS, 3, 1536) for c in ac], axis=1))
    outs.append(np.concatenate([np.asarray(r[c]["Ss"]) for c in ac], axis=1))
    outs.append(np.concatenate([np.asarray(r[c]["Rs"]).reshape(DEPTH, NS, 4, 64, 128) for c in ac], axis=1))
    return tuple(np.ascontiguousarray(o, dtype=np.float32) for o in outs)


def kernel(**inputs):
    cfg = Cfg()
    ncores = 8
    consts = make_consts(cfg)
    nc = build(cfg)
    in_maps = [prep_core_inputs(cfg, inputs, c, consts) for c in range(ncores)]
    res = run_bass_kernel_spmd(nc, in_maps, core_ids=list(range(ncores)))
    return assemble(cfg, res.results, inputs["x_prompt"].shape[0], ncores)


def _gated_norm_epilogue(X, sbp_bufs, P_o, pok, zcol0, gcol, gkey, tok0, out_stage, okey, stage_cols, Pz, pzk, Pss, pssk, wz, wzk):
    S, hT, KD, onesb = X["S"], X["hT"], X["cfg"].KD, X["onesb"]
    sq, rs, on, sz = sbp_bufs
    poks = list(pok) if isinstance(pok, (list, tuple)) else [pok]
    S.act(sq[:], P_o[:], AF.Square, R=poks, W=["e_sq"])
    S.mm(Pss[:], onesb[:], sq[:], R=["onesb", "e_sq"], W=[pssk])
    X["rstd"](rs[:], Pss[:], 1.0 / 128, [pssk], ["e_rs"])
    S.stt("dve", on[:], P_o[:], gcol, rs[:], ALU.mult, ALU.mult, R=poks + [gkey, "e_rs"], W=["e_on"])
    for h in range(4):
        for kc in range(KD):
            S.mm(Pz[:, h * 128:(h + 1) * 128], wz[:, kc, zcol0 + h * 128:zcol0 + (h + 1) * 128], hT[:, kc, tok0:tok0 + 128],
                 start=(kc == 0), stop=(kc == KD - 1), R=[wzk, "hT"],
                 W=[pzk] if (h == 0 and kc == 0) else [], A=[pzk] if not (h == 0 and kc == 0) else [])
    S.act(sz[:], Pz[:], AF.Silu, R=[pzk], W=["e_sz"])
    S.tt("dve", out_stage[:, :, stage_cols], on[:].rearrange("p (h n) -> p h n", h=4), sz[:].rearrange("p (h n) -> p h n", h=4),
         ALU.mult, R=["e_on", "e_sz"], W=[okey])


def mixer_c(X):
    nc, S, cfg, l, I, O, P, PT, hT = X["nc"], X["S"], X["cfg"], X["l"], X["I"], X["O"], X["P"], X["PT"], X["hT"]
    KD, NTT, NS = cfg.KD, cfg.NTT, cfg.NS
    w_in, oT_d, identb, wview = X["w_in"], X["oT_d"], X["identb"], X["wview"]
    with contextlib.ExitStack() as ph:
        sbp = lambda n, s, d=F32: ph.enter_context(nc.sbuf_tensor("c%d_%s" % (l, n), list(s), d))
        rotC = sbp("rotC", [128, 128], BF16)
        S.dma("pool", rotC[:], I["c_rotC"], W=["rotC"])
        tab = {}
        for ty in "ps":
            for nm, w in (("dtc", 512), ("qdec", 256), ("gC", 2), ("kdec", 256)):
                tab[nm + ty] = sbp(nm + ty, [128, w])
                S.dma("sp", tab[nm + ty][:], I["c_%s_%s" % (nm, ty)], W=["tab"])
        cog = sbp("cog", [128, 1])
        S.dma("sp", cog[:], I["cog"][l], W=["cog"])
        wq = sbp("wq", [128, KD, 256], BF16)
        wk = sbp("wk", [128, KD, 256], BF16)
        wv = sbp("wv", [128, KD, 512], BF16)
        wz = sbp("wz", [128, KD, 512], BF16)
        for t_, c0, w_ in ((wq, CQ, 256), (wk, CK, 256), (wv, CV, 512), (wz, CZ, 512)):
            S.dma("pool", t_[:], wview(w_in, c0, w_), W=["wc"])
        ct = sbp("ct", [128, 512])
        sn = sbp("sn", [128, 512])
        qT = sbp("qT", [128, 2, 2, 512], BF16)
        S.memset("pool", qT[:], 0.0, W=["qT"])
        qdT = sbp("qdT", [128, 2, 512], BF16)
        kT = sbp("kT", [128, 2, 512], BF16)
        qb = sbp("qb", [128, 512], BF16)
        t1 = sbp("t1", [128, 512])
        t2 = sbp("t2", [128, 512])
        v_t = sbp("v_t", [128, 512], BF16)
        kd_t = sbp("kd_t", [128, 256], BF16)
        attn = sbp("attn", [128, 512], BF16)
        R32 = sbp("R32", [128, 2, 128])
        Rb = [sbp("Rb%d" % i, [128, 2, 128], BF16) for i in range(2)]
        ebuf = (sbp("e_sq", [128, 512], BF16), sbp("e_rs", [128, 512]), sbp("e_on", [128, 512]), sbp("e_sz", [128, 512]))
        ocs = sbp("ocs", [128, 4, 512], BF16)
        S.memset("dve", R32[:], 0.0, W=["R32"])
        rbi = 0
        for (t0, n) in X["blocks"]:
            ntl = n // 128
            sample = (t0 // 128 >= NTT)
            ty = "s" if sample else "p"
            if CUT <= 0:
                continue
            S.dma("sp", ct[:, 0:n], I["c_cosC"][:, t0:t0 + n], W=["ct"])
            S.dma("sp", sn[:, 0:n], I["c_sinC"][:, t0:t0 + n], W=["sn"])
            for which, w_t, dst in (("q", wq, qT), ("k", wk, kT)):
                sc = 1.0 if which == "q" else 0.125
                for c in range(2):
                    for kc in range(KD):
                        S.mm(P[0][:, 0:n], w_t[:, kc, c * 128:(c + 1) * 128], hT[:, kc, t0:t0 + n], start=(kc == 0), stop=(kc == KD - 1),
                             R=["wc", "hT"], W=["P0"] if kc == 0 else [], A=["P0"] if kc > 0 else [])
                    S.cp("act", qb[:, 0:n], P[0][:, 0:n], R=["P0"], W=["qb"])
                    S.mm(P[1][:, 0:n], rotC[:], qb[:, 0:n], R=["rotC", "qb"], W=["P1"])
                    if VAR == 1:
                        S.cp("dve", t1[:, 0:n], ct[:, 0:n], R=["ct"], W=["t1"])
                    elif VAR == 2:
                        S.cp("dve", t1[:, 0:n], P[0][:, 0:n], R=["P0"], W=["t1"])
                    elif VAR == 3:
                        S.cp("dve", t1[:, 0:n], X["ones"][:, 0:1].broadcast_to([128, n]) if False else t2[:, 0:n], R=[], W=["t1"])
                    else:
                        S.tt("dve", t1[:, 0:n], P[0][:, 0:n], ct[:, 0:n], ALU.mult, R=["P0", "ct"], W=["t1"])
                    S.tt("dve", t2[:, 0:n], P[1][:, 0:n], sn[:, 0:n], ALU.mult, R=["P1", "sn"], W=["t2"])
                    S.tt("pool", t1[:, 0:n], t1[:, 0:n], t2[:, 0:n], ALU.add, R=["t1", "t2"], W=["t1"])
                    if which == "k":
                        S.act(dst[:, c, 0:n], t1[:, 0:n], AF.Copy, scale=sc, R=["t1"], W=[which + "T"])
                    else:
                        for half in range(2):
                            hr = slice(64 * half, 64 * half + 64)
                            S.act(dst[hr, c, half, 0:n], t1[hr, 0:n], AF.Copy, R=["t1"], W=["qT"])
                    if which == "q":
                        for tt_ in range(ntl):
                            S.tt("pool", qdT[:, c, tt_ * 128:(tt_ + 1) * 128], t1[:, tt_ * 128:(tt_ + 1) * 128],
                                 tab["qdec" + ty][:, c * 128:(c + 1) * 128], ALU.mult, R=["t1", "tab"], W=["qdT"])
            if CUT <= 1:
                continue
            for tt_ in range(ntl):
                tok0 = t0 + tt_ * 128
                tc = slice(tt_ * 128, (tt_ + 1) * 128)
                for kc in range(KD):
                    S.mm(P[2][:], hT[:, kc, tok0:tok0 + 128], wv[:, kc, :], start=(kc == 0), stop=(kc == KD - 1),
                         R=["wc", "hT"], W=["P2"] if kc == 0 else [], A=["P2"] if kc > 0 else [])
                S.cp("act", v_t[:], P[2][:], R=["P2"], W=["v_t"])
                for c in range(2):
                    S.tr(PT[:, c * 128:(c + 1) * 128], kT[:, c, tc], identb[:], R=["kT", "identb"],
                         W=["PT"] if c == 0 else [], A=["PT"] if c > 0 else [])
                S.tt("dve", kd_t[:], PT[:, 0:256], tab["kdec" + ty][:], ALU.mult, R=["PT", "tab"], W=["kd_t"])
                if CUT <= 2:
                    continue
                for h in range(4):
                    c, rows = h // 2, slice(64 * (h % 2), 64 * (h % 2) + 64)
                    S.mm(P[3][:, h * 128:(h + 1) * 128], kT[:, c, tc], qT[:, c, h % 2, tc], R=["kT", "qT"],
                         W=["P3"] if h == 0 else [], A=["P3"] if h > 0 else [])
                S.tt("dve", attn[:], P[3][:], tab["dtc" + ty][:], ALU.mult, R=["P3", "tab"], W=["attn"])
                if CUT <= 3:
                    continue
                bs = 64
                rbs = []
                for b in range(2):
                    br = slice(b * bs, (b + 1) * bs)
                    sq_ = 2 * (tok0 // 128 - NTT) + b
                    if sample:
                        S.dma("sp", R32[:], I["sR"][l, sq_].rearrange("pr p d -> p pr d"), W=["R32"])
                    rb, rbk = Rb[b], "Rb%d" % b
                    rbs.append((rb, rbk))
                    S.cp("act", rb[:], R32[:], R=["R32"], W=[rbk])
                    for pr in range(2):
                        S.mm(P[4][:, pr * 256:(pr + 1) * 256], kd_t[br, pr * 128:(pr + 1) * 128], v_t[br, pr * 256:(pr + 1) * 256],
                             R=["kd_t", "v_t"], W=["P4"] if pr == 0 else [], A=["P4"] if pr > 0 else [])
                    for pr in range(2):
                        for half in range(2):
                            rows = slice(64 * half, 64 * half + 64)
                            S.stt("dve", R32[rows, pr, :], R32[rows, pr, :], tab["gC" + ty][rows, pr:pr + 1],
                                  P[4][rows, pr * 256 + half * 128:pr * 256 + half * 128 + 128], ALU.mult, ALU.add,
                                  R=["R32", "P4", "tab"], W=["R32"])
                    if sample:
                        S.dma("sp", O["Rs"][l, sq_].rearrange("pr p d -> p pr d"), R32[:], R=["R32"])
                    elif tok0 + 128 == cfg.T and b == 1:
                        S.dma("sp", O["Rp"][l].rearrange("pr p d -> p pr d"), R32[:], R=["R32"])
                if CUT <= 4:
                    continue
                for h in range(4):
                    c, rows = h // 2, slice(64 * (h % 2), 64 * (h % 2) + 64)
                    S.mm(P[5][:, h * 128:(h + 1) * 128], v_t[:, h * 128:(h + 1) * 128], attn[:, h * 128:(h + 1) * 128],
                         start=True, stop=False, R=["v_t", "attn"], W=["P5"] if h == 0 else [], A=["P5"] if h > 0 else [])
                    for b in range(2):
                        rb, rbk = rbs[b]
                        S.mm(P[5][:, h * 128 + b * bs:h * 128 + (b + 1) * bs], rb[rows, c, :], qdT[rows, c, tt_ * 128 + b * bs:tt_ * 128 + (b + 1) * bs],
                             start=False, stop=(b == 1), R=[rbk, "qdT"], A=["P5"])
                if CUT <= 5:
                    continue
                _gated_norm_epilogue(X, ebuf, P[5], "P5", 0, cog[:, 0:1], "cog", tok0, ocs, "ocs", tc, P[2], "P2", P[6], "P6", wz, "wc")
            if CUT <= 6:
                continue
            S.dma("sp", oT_d[8 * 128:12 * 128, t0:t0 + n].rearrange("(c p) n -> p c n", p=128), ocs[:, :, 0:n], R=["ocs"])


def mixer_b(X):
    nc, S, cfg, l, I, O, P, PT, hT = X["nc"], X["S"], X["cfg"], X["l"], X["I"], X["O"], X["P"], X["PT"], X["hT"]
    KD, NTT, NS, T = cfg.KD, cfg.NTT, cfg.NS, cfg.T
    w_in, oT_d, ident, ones, onesb, wview = X["w_in"], X["oT_d"], X["ident"], X["ones"], X["onesb"], X["wview"]
    with contextlib.ExitStack() as ph:
        sbp = lambda n, s, d=F32: ph.enter_context(nc.sbuf_tensor("b%d_%s" % (l, n), list(s), d))
        tab = {}
        for ty in "ps":
            for nm, w in (("tri", 128), ("blk", 128), ("negS", 128), ("negIT", 128), ("valid", 1)):
                tab[nm + ty] = sbp(nm + ty, [128, w])
                S.dma("sp", tab[nm + ty][:], I["c_%s_%s" % (nm, ty)], W=["tab"])
        bog = sbp("bog", [128, 1]); S.dma("sp", bog[:], I["bog"][l], W=["bog"])
        cw = sbp("cw", [128, 12, 4]); S.dma("sp", cw[:], I["convw"][l], W=["cw"])
        alog = sbp("alog", [128, 4]); S.dma("sp", alog[:], I["alogB"][l], W=["alog"])
        dtb = sbp("dtb", [128, 4]); S.dma("sp", dtb[:], I["dtbB"][l], W=["dtb"])
        S.act(alog[:], alog[:], AF.Exp, R=["alog"], W=["alog"])
        wqh = [sbp("wqh%d" % i, [128, KD, 3, 128], BF16) for i in range(2)]
        wz = sbp("wz", [128, KD, 512], BF16)
        wbg = sbp("wbg", [128, KD, 8], BF16)
        S.dma("pool", wz[:], wview(w_in, BZ, 512), W=["wb"])
        S.dma("pool", wbg[:], wview(w_in, BBETA, 8), W=["wb"])
        pre = sbp("pre", [128, 3 + 512])
        carry = sbp("carry", [128, 12, 3])
        S.memset("pool", carry[:], 0.0, W=["carry"])
        pres = sbp("pres", [128, 4, 67])
        cst = sbp("cst", [12, 1536]); S.dma("sp", cst[:], I["sconv"][l], W=["cst"])
        cso = sbp("cso", [128, 12, 12])
        csin = sbp("csin", [128, 12, 12])
        cpo = sbp("cpo", [128, 12, 3])
        cout = sbp("cout", [12, 1536])
        acc = sbp("acc", [128, 512])
        yv = sbp("yv", [128, 512])
        sqb = sbp("sqb", [128, 512], BF16)
        rn = sbp("rn", [128, 512])
        fT = [{r: sbp("%sT%d" % (r, h), [128, 512]) for r in "qkv"} for h in range(4)]
        bgt = sbp("bgt", [128, 4, 8])
        nbe = sbp("nbe", [128, 4, 4])
        BN = ("sA", "sB", "sC", "egc", "Nm", "NTm", "PTm", "attnT", "Xb0", "Xb1", "XTb0", "XTb1", "u_", "wT", "qgT", "vnew", "k_tm", "v_tm")
        B = [{nm: sbp("%s_%d" % (nm, h), [128, 128]) for nm in BN} for h in range(4)]
        gcsb = [sbp("gcs%d" % h, [128, 4]) for h in range(4)]
        S32 = [sbp("S32_%d" % h, [128, 128]) for h in range(4)]
        ob = sbp("ob", [128, 4, 512])
        ebuf = (sbp("e_sq", [128, 512], BF16), sbp("e_rs", [128, 512]), sbp("e_on", [128, 512]), sbp("e_sz", [128, 512]))
        obs = sbp("obs", [128, 4, 512], BF16)
        for h in range(4):
            S.memset("pool", S32[h][:], 0.0, W=["S32_%d" % h])
        for c in range(12):
            S.tr(P[0][:, 0:12], cst[0:12, c * 128:(c + 1) * 128], ident[0:12, 0:12], R=["cst", "ident"], W=["P0"])
            S.cp("dve", csin[:, c, :], P[0][:, 0:12], R=["P0"], W=["csin"])

        def tile_chain(Sx, h, tt_, tok0, sample, ty, nlev):
            bb, pb, pk = B[h], P[1 + h], "P%d" % (1 + h)
            k_ = lambda nm: "%s_%d" % (nm, h)
            qT, kT, vT = fT[h]["q"], fT[h]["k"], fT[h]["v"]
            gcs, gk = gcsb[h], "gcs%d" % h
            s32, s32k = S32[h], "S32_%d" % h
            tc = slice(tt_ * 128, (tt_ + 1) * 128)
            be = bgt[:, tt_, h:h + 1]
            gg = bgt[:, tt_, 4 + h:5 + h]
            A_, B_, C_, D_ = pb[:, 0:128], pb[:, 128:256], pb[:, 256:384], pb[:, 384:512]
            Sx.tr(A_, kT[:, tc], ident[:], R=[k_("kT"), "ident"], W=[pk])
            Sx.tr(B_, vT[:, tc], ident[:], R=[k_("vT"), "ident"], A=[pk])
            Sx.cp("act", bb["k_tm"][:], A_, R=[pk], W=[k_("k_tm")])
            Sx.cp("dve", bb["v_tm"][:], B_, R=[pk], W=[k_("v_tm")])
            Sx.ts("dve", bb["sA"][:], ones[:], gg, None, ALU.mult, R=["ones", "bgt"], W=[k_("sA")])
            Sx.mm(A_, bb["sA"][:], tab["tri" + ty][:], R=[k_("sA"), "tab"], W=[pk])
            Sx.mm(pb[:, 128:129], tab["tri" + ty][:], gg, R=["tab", "bgt"], A=[pk])
            Sx.mm(pb[:, 129:130], tab["blk" + ty][:], gg, R=["tab", "bgt"], A=[pk])
            Sx.cp("dve", gcs[:, 0:2], pb[:, 128:130], R=[pk], W=[gk])
            Sx.stt("dve", bb["sA"][:], A_, gcs[:, 0:1], tab["negS" + ty][:], ALU.subtract, ALU.subtract, R=[pk, gk, "tab"], W=[k_("sA")])
            Sx.act(bb["sB"][:], bb["sA"][:], AF.Exp, scale=-1.0, R=[k_("sA")], W=[k_("sB")])
            Sx.stt("dve", bb["sA"][:], A_, gcs[:, 0:1], tab["negIT" + ty][:], ALU.subtract, ALU.add, R=[pk, gk, "tab", k_("sB")], W=[k_("sA")])
            Sx.act(bb["sC"][:], bb["sA"][:], AF.Exp, R=[k_("sA")], W=[k_("sC")])
            Sx.act(bb["egc"][:], A_, AF.Exp, R=[pk], W=[k_("egc")])
            Sx.tt("dve", gcs[:, 2:3], gcs[:, 1:2], gcs[:, 0:1], ALU.subtract, R=[gk], W=[gk])
            Sx.act(gcs[:, 2:3], gcs[:, 2:3], AF.Exp, R=[gk], W=[gk])
            Sx.tt("dve", gcs[:, 2:3], gcs[:, 2:3], tab["valid" + ty][:, 0:1], ALU.mult, R=[gk, "tab"], W=[gk])
            Sx.act(gcs[:, 3:4], gcs[:, 0:1], AF.Exp, R=[gk], W=[gk])
            Sx.tt("dve", gcs[:, 3:4], gcs[:, 3:4], be, ALU.mult, R=[gk, "bgt"], W=[gk])
            Sx.mm(C_, kT[:, tc], kT[:, tc], R=[k_("kT")], W=[pk])
            Sx.mm(D_, kT[:, tc], qT[:, tc], R=[k_("kT"), k_("qT")], A=[pk])
            Sx.stt("dve", bb["Nm"][:], C_, nbe[:, tt_, h:h + 1], bb["sB"][:], ALU.mult, ALU.mult, R=[pk, "nbe", k_("sB")], W=[k_("Nm")])
            Sx.tt("dve", bb["attnT"][:], D_, bb["sC"][:], ALU.mult, R=[pk, k_("sC")], W=[k_("attnT")])
            Sx.tr(A_, bb["Nm"][:], ident[:], R=[k_("Nm"), "ident"], W=[pk])
            Sx.cp("act", bb["NTm"][:], A_, R=[pk], W=[k_("NTm")])
            Sx.tt("pool", bb["PTm"][:], bb["NTm"][:], ident[:], ALU.add, R=[k_("NTm"), "ident"], W=[k_("PTm")])
            Xc, Xk, XTc, XTk = bb["Nm"], k_("Nm"), bb["NTm"], k_("NTm")
            for m in range(1, nlev + 1):
                X2, X2k = bb["Xb%d" % (m % 2)], k_("Xb%d" % (m % 2))
                XT2, XT2k = bb["XTb%d" % (m % 2)], k_("XTb%d" % (m % 2))
                Sx.mm(B_, XTc[:], Xc[:], R=[Xk, XTk], W=[pk])
                if m < nlev:
                    Sx.mm(C_, Xc[:], XTc[:], R=[Xk, XTk], A=[pk])
                Sx.cp("act", X2[:], B_, R=[pk], W=[X2k])
                if m < nlev:
                    Sx.cp("dve", XT2[:], C_, R=[pk], W=[XT2k])
                Sx.mm(D_, X2[:], bb["PTm"][:], R=[X2k, k_("PTm")], W=[pk])
                Sx.tt("dve", bb["PTm"][:], bb["PTm"][:], D_, ALU.add, R=[k_("PTm"), pk], W=[k_("PTm")])
                Xc, Xk, XTc, XTk = X2, X2k, XT2, XT2k
            Sx.ts("dve", bb["sA"][:], bb["v_tm"][:], be, None, ALU.mult, R=[k_("v_tm"), "bgt"], W=[k_("sA")])
            Sx.ts("pool", bb["sB"][:], bb["k_tm"][:], gcs[:, 3:4], None, ALU.mult, R=[k_("k_tm"), gk], W=[k_("sB")])
            Sx.mm(A_, bb["PTm"][:], bb["sA"][:], R=[k_("PTm"), k_("sA")], W=[pk])
            Sx.mm(B_, bb["sB"][:], bb["PTm"][:], R=[k_("PTm"), k_("sB")], A=[pk])
            Sx.cp("act", bb["u_"][:], A_, R=[pk], W=[k_("u_")])
            Sx.cp("dve", bb["wT"][:], B_, R=[pk], W=[k_("wT")])
            Sx.tt("pool", bb["qgT"][:], qT[:, tc], bb["egc"][:], ALU.mult, R=[k_("qT"), k_("egc")], W=[k_("qgT")])
            Sx.ts("pool", bb["sC"][:], bb["k_tm"][:], gcs[:, 2:3], None, ALU.mult, R=[k_("k_tm"), gk], W=[k_("sC")])
            for b in range(2):
                rows = slice(64 * b, 64 * b + 64)
                sq_ = 2 * (tok0 // 128 - NTT) + b
                if sample:
                    Sx.dma("sp", s32[:], I["sS"][l, sq_, h], W=[s32k])
                Sx.mm(C_, bb["wT"][:], s32[:], R=[k_("wT"), s32k], W=[pk])
                Sx.tt("dve", bb["vnew"][rows, :], bb["u_"][rows, :], pb[rows, 256:384], ALU.subtract, R=[k_("u_"), pk], W=[k_("vnew")])
                oc = slice(384 + 64 * b, 384 + 64 * b + 64)
                Sx.mm(pb[:, oc], s32[:], bb["qgT"][:, rows], start=True, stop=False, R=[s32k, k_("qgT")], W=[pk])
                Sx.mm(pb[:, oc], bb["vnew"][rows, :], bb["attnT"][rows, rows], start=False, stop=True, R=[k_("vnew"), k_("attnT")], A=[pk])
                Sx.cp("act", ob[:, h, tt_ * 128 + 64 * b:tt_ * 128 + 64 * b + 64], pb[:, oc], R=[pk], W=["ob%d" % h])
                Sx.mm(A_, bb["sC"][rows, :], bb["vnew"][rows, :], R=[k_("sC"), k_("vnew")], W=[pk])
                Sx.stt("dve", s32[:], s32[:], bb["egc"][:, 64 * b + 63:64 * b + 64], A_, ALU.mult, ALU.add,
                       R=[s32k, k_("egc"), pk], W=[s32k])
                if sample:
                    Sx.dma("sp", O["Ss"][l, sq_, h], s32[:], R=[s32k])
                elif tok0 + 128 == T and b == 1:
                    Sx.dma("sp", O["Sp"][l, h], s32[:], R=[s32k])

        for (t0, n) in X["blocks"]:
            ntl = n // 128
            sample = (t0 // 128 >= NTT)
            ty = "s" if sample else "p"
            nlev = 2 if sample else 5
            for tt_ in range(ntl):
                tok0 = t0 + tt_ * 128
                for kc in range(KD):
                    S.mm(P[0][:, 0:8], hT[:, kc, tok0:tok0 + 128], wbg[:, kc, :], start=(kc == 0), stop=(kc == KD - 1),
                         R=["wb", "hT"], W=["P0"] if kc == 0 else [], A=["P0"] if kc > 0 else [])
                S.act(bgt[:, tt_, 0:4], P[0][:, 0:4], AF.Sigmoid, R=["P0"], W=["bgt"])
                S.tt("dve", bgt[:, tt_, 4:8], P[0][:, 4:8], dtb[:], ALU.add, R=["P0", "dtb"], W=["bgt"])
                S.act(bgt[:, tt_, 4:8], bgt[:, tt_, 4:8], AF.Exp, R=["bgt"], W=["bgt"])
                S.act(bgt[:, tt_, 4:8], bgt[:, tt_, 4:8], AF.Ln, bias=1.0, R=["bgt"], W=["bgt"])
                S.stt("dve", bgt[:, tt_, 4:8], bgt[:, tt_, 4:8], -1.0, alog[:], ALU.mult, ALU.mult, R=["bgt", "alog"], W=["bgt"])
                S.ts("dve", bgt[:, tt_, 4:8], bgt[:, tt_, 4:8], tab["valid" + ty][:, 0:1], None, ALU.mult, R=["bgt", "tab"], W=["bgt"])
                S.ts("dve", nbe[:, tt_, :], bgt[:, tt_, 0:4], -1.0, None, ALU.mult, R=["bgt"], W=["nbe"])
            for h in range(4):
                wq_t, wqk = wqh[h % 2], "wqh%d" % (h % 2)
                for ri in range(3):
                    S.dma("pool", wq_t[:, :, ri, :], wview(w_in, BQKV + (ri * 4 + h) * 128, 128), W=[wqk])
                for ri, (role, c) in enumerate((("q", h), ("k", 4 + h), ("v", 8 + h))):
                    pp, ppk = (P[0], "P0") if (ri % 2 == 0) else (P[6], "P6")
                    for kc in range(KD):
                        S.mm(pp[:, 0:n], wq_t[:, kc, ri, :], hT[:, kc, t0:t0 + n], start=(kc == 0), stop=(kc == KD - 1),
                             R=[wqk, "hT"], W=[ppk] if kc == 0 else [], A=[ppk] if kc > 0 else [])
                    if not sample:
                        S.cp("pool", pre[:, 0:3], carry[:, c, :], R=["carry"], W=["pre"])
                        S.cp("act", pre[:, 3:3 + n], pp[:, 0:n], R=[ppk], W=["pre"])
                        S.ts("dve", acc[:, 0:n], pre[:, 0:n], cw[:, c, 0:1], None, ALU.mult, R=["pre", "cw"], W=["acc"])
                        for i in range(1, 4):
                            S.stt("dve", acc[:, 0:n], pre[:, i:i + n], cw[:, c, i:i + 1], acc[:, 0:n], ALU.mult, ALU.add,
                                  R=["pre", "cw", "acc"], W=["acc"])
                        if t0 + n == T:
                            S.cp("pool", cpo[:, c, :], pre[:, n:n + 3], R=["pre"], W=["cpo"])
                        S.cp("pool", carry[:, c, :], pre[:, n:n + 3], R=["pre"], W=["carry"])
                        accv = acc[:, 0:n]
                    else:
                        S.cp("pool", pres[:, :, 0:3], csin[:, c, :].rearrange("p (s k) -> p s k", s=4), R=["csin"], W=["pres"])
                        S.cp("act", pres[:, :, 3:67], pp[:, 0:256].rearrange("p (s k) -> p s k", s=4), R=[ppk], W=["pres"])
                        a3 = acc[:, 0:256].rearrange("p (s k) -> p s k", s=4)
                        S.ts("dve", a3, pres[:, :, 0:64], cw[:, c, 0:1], None, ALU.mult, R=["pres", "cw"], W=["acc"])
                        for i in range(1, 4):
                            S.stt("dve", a3, pres[:, :, i:i + 64], cw[:, c, i:i + 1], a3, ALU.mult, ALU.add, R=["pres", "cw", "acc"], W=["acc"])
                        S.cp("pool", cso[:, c, :].rearrange("p (s k) -> p s k", s=4), pres[:, :, 8:11], R=["pres"], W=["cso"])
                        accv = acc[:, 0:n]
                    fk = "%sT_%d" % (role, h)
                    if role == "v":
                        S.act(fT[h]["v"][:, 0:n], accv, AF.Silu, R=["acc"], W=[fk])
                    else:
                        S.act(yv[:, 0:n], accv, AF.Silu, R=["acc"], W=["yv"])
                        S.act(sqb[:, 0:n], yv[:, 0:n], AF.Square, R=["yv"], W=["sqb"])
                        S.mm(P[5][:, 0:n], onesb[:], sqb[:, 0:n], R=["onesb", "sqb"], W=["P5"])
                        X["rstd"](rn[:, 0:n], P[5][:, 0:n], 1.0, ["P5"], ["rn"])
                        sc = 128.0 ** -0.5 if role == "q" else 1.0
                        S.stt("dve", fT[h][role][:, 0:n], yv[:, 0:n], sc, rn[:, 0:n], ALU.mult, ALU.mult, R=["yv", "rn"], W=[fk])
            recs = []
            for h in range(4):
                r = Rec()
                for tt_ in range(ntl):
                    tile_chain(r, h, tt_, t0 + tt_ * 128, sample, ty, nlev)
                recs.append(r)
            replay_interleaved(S, recs)
            for tt_ in range(ntl):
                tok0 = t0 + tt_ * 128
                tc = slice(tt_ * 128, (tt_ + 1) * 128)
                _gated_norm_epilogue(X, ebuf, ob[:, :, tc], ["ob0", "ob1", "ob2", "ob3"], 0, bog[:, 0:1], "bog", tok0, obs, "obs", tc,
                                     P[0], "P0", P[5], "P5", wz, "wb")
            S.dma("sp", oT_d[4 * 128:8 * 128, t0:t0 + n].rearrange("(c p) n -> p c n", p=128), obs[:, :, 0:n], R=["obs"])
        for (src, sk, nrow, dst) in ((cpo, "cpo", 3, O["convp"][l]), (cso, "cso", 12, O["convs"][l])):
            for g4 in range(3):
                pb, pk = P[g4], "P%d" % g4
                for c4 in range(4):
                    c = g4 * 4 + c4
                    S.tr(pb[0:nrow, c4 * 128:(c4 + 1) * 128], src[:, c, 0:nrow], ident[:], R=[sk, "ident"],
                         W=[pk] if c4 == 0 else [], A=[pk] if c4 else [])
                S.cp("dve", cout[0:nrow, g4 * 512:(g4 + 1) * 512], pb[0:nrow, :], R=[pk], W=["cout"])
            S.dma("sp", dst, cout[0:nrow, :], R=["cout"])


def mixer_a(X):
    nc, S, cfg, l, I, O, P, PT, hT = X["nc"], X["S"], X["cfg"], X["l"], X["I"], X["O"], X["P"], X["PT"], X["hT"]
    KD, NTT, NS, T, NTOK = cfg.KD, cfg.NTT, cfg.NS, cfg.T, cfg.NTOK
    w_in, oT_d, ident, identb, onesb, wview = X["w_in"], X["oT_d"], X["ident"], X["identb"], X["onesb"], X["wview"]
    KEEP = (128, 512, min(2048, T))
    DIL = (1, 4, 16)
    sc_ = 128.0 ** -0.5
    with contextlib.ExitStack() as ph:
        sbp = lambda n, s, d=F32: ph.enter_context(nc.sbuf_tensor("a%d_%s" % (l, n), list(s), d))
        rotA = sbp("rotA", [128, 128], BF16); S.dma("pool", rotA[:], I["c_rotA"], W=["rotA"])
        mcur = sbp("mcur", [128, 128], BF16); S.dma("pool", mcur[:], I["c_mcur"][:, 0:128], W=["mask"])
        mprev = sbp("mprev", [128, 128], BF16); S.dma("pool", mprev[:], I["c_mprev"][:, 0:128], W=["mask"])
        msc = sbp("msc", [128, 104], BF16); S.dma("pool", msc[:], I["c_msc"], W=["mask"])
        msn = sbp("msn", [128, 48], BF16); S.dma("pool", msn[:], I["c_msn"], W=["mask"])
        aqg = sbp("aqg", [128, 1]); S.dma("sp", aqg[:], I["aqg"][l], W=["aqg"])
        akg = sbp("akg", [128, 1]); S.dma("sp", akg[:], I["akg"][l], W=["aqg"])
        nshift = sbp("nshift", [128, 1]); S.memset("pool", nshift[:], -SHIFT, W=["nshift"])
        wq = sbp("wq", [128, KD, 128], BF16); wk = sbp("wk", [128, KD, 128], BF16); wv = sbp("wv", [128, KD, 128], BF16)
        ct = sbp("ct", [128, 512]); sn = sbp("sn", [128, 512])
        QT = sbp("QT", [128, NTOK], BF16); KT = sbp("KT", [128, NTOK], BF16); K32 = sbp("K32", [128, NTOK])
        V = sbp("V", [128, cfg.NT, 128], BF16)
        Oacc = sbp("Oacc", [128, NTOK]); Dacc = sbp("Dacc", [128, NTOK])
        rn = sbp("rn", [128, 512])
        sqb2 = [sbp("sqb%d" % i, [128, 512], BF16) for i in range(2)]; rn2 = [sbp("rn%d" % i, [128, 512]) for i in range(2)]
        qb2 = [sbp("qb%d" % i, [128, 512], BF16) for i in range(2)]
        t12 = [sbp("t1%d" % i, [128, 512]) for i in range(2)]; t22 = [sbp("t2%d" % i, [128, 512]) for i in range(2)]
        pT2 = [sbp("pT%d" % i, [128, 128], BF16) for i in range(2)]
        v32 = sbp("v32", [128, 128]); k32 = sbp("k32", [128, 128])
        pT = sbp("pT", [128, 128], BF16)
        kvc2 = [[sbp("kvc%d%d" % (i, j), [128, 2, 128]) for j in range(2)] for i in range(2)]
        kcT2 = [sbp("kcT%d" % i, [128, 128], BF16) for i in range(2)]
        vcb2 = [sbp("vcb%d" % i, [128, 128], BF16) for i in range(2)]
        ost = sbp("ost", [128, 512], BF16)
        for h in range(4):
            S.memset("pool", Oacc[:], 0.0, W=["Oacc"])
            S.memset("pool", Dacc[:], 0.0, W=["Dacc"])
            for g in range(3):
                dil, keep = DIL[g], KEEP[g]
                kvp, kvs, kvin = O["kvp%d" % g], O["kvs%d" % g], I["kv%d" % g]
                for t_, i3 in ((wq, 0), (wk, 1), (wv, 2)):
                    S.dma("pool", t_[:], wview(w_in, ((i3 * 3 + g) * 4 + h) * 128, 128), W=["wa"])
                for (t0, n) in X["blocks"]:
                    S.dma("sp", ct[:, 0:n], I["c_cosA"][:, t0:t0 + n], W=["ct"])
                    S.dma("sp", sn[:, 0:n], I["c_sinA"][:, t0:t0 + n], W=["sn"])
                    recs = []
                    for si, (which, w_t, gcol) in enumerate((("q", wq, aqg), ("k", wk, akg))):
                        Sx = Rec()
                        Pa, Pb_, Pc = (P[0], P[1], P[2]) if si == 0 else (P[4], P[5], P[6])
                        ka, kb_, kc_ = ("P0", "P1", "P2") if si == 0 else ("P4", "P5", "P6")
                        sq_, rn_, qb_, t1_, t2_ = sqb2[si], rn2[si], qb2[si], t12[si], t22[si]
                        sfx = str(si)
                        for kc in range(KD):
                            Sx.mm(Pa[:, 0:n], w_t[:, kc, :], hT[:, kc, t0:t0 + n], start=(kc == 0), stop=(kc == KD - 1),
                                  R=["wa", "hT"], W=[ka] if kc == 0 else [], A=[ka] if kc > 0 else [])
                        Sx.act(sq_[:, 0:n], Pa[:, 0:n], AF.Square, R=[ka], W=["sqb" + sfx])
                        Sx.mm(Pb_[:, 0:n], onesb[:], sq_[:, 0:n], R=["onesb", "sqb" + sfx], W=[kb_])
                        Sx.act(rn_[:, 0:n], Pb_[:, 0:n], AF.Ln, bias=EPS, scale=1.0 / 128, R=[kb_], W=["rn" + sfx])
                        Sx.act(rn_[:, 0:n], rn_[:, 0:n], AF.Exp, scale=-0.5, R=["rn" + sfx], W=["rn" + sfx])
                        Sx.stt("dve", qb_[:, 0:n], Pa[:, 0:n], gcol[:, 0:1], rn_[:, 0:n], ALU.mult, ALU.mult, R=[ka, "aqg", "rn" + sfx], W=["qb" + sfx])
                        Sx.mm(Pc[:, 0:n], rotA[:], qb_[:, 0:n], R=["rotA", "qb" + sfx], W=[kc_])
                        Sx.tt("pool", t1_[:, 0:n], qb_[:, 0:n], ct[:, 0:n], ALU.mult, R=["qb" + sfx, "ct"], W=["t1" + sfx])
                        Sx.tt("dve", t2_[:, 0:n], Pc[:, 0:n], sn[:, 0:n], ALU.mult, R=[kc_, "sn"], W=["t2" + sfx])
                        if which == "q":
                            Sx.tt("pool", QT[:, t0:t0 + n], t1_[:, 0:n], t2_[:, 0:n], ALU.add, R=["t1" + sfx, "t2" + sfx], W=["QT"])
                        else:
                            Sx.tt("pool", K32[:, t0:t0 + n], t1_[:, 0:n], t2_[:, 0:n], ALU.add, R=["t1" + sfx, "t2" + sfx], W=["K32"])
                            Sx.cp("act", KT[:, t0:t0 + n], K32[:, t0:t0 + n], R=["K32"], W=["KT"])
                        recs.append(Sx)
                    replay_interleaved(S, recs)
                nbk = T // (128 * dil)
                def tcols(r, nb):
                    st0 = dil * 128 * nb + r
                    return slice(st0, st0 + dil * 127 + 1, dil)
                tiles = [(r, nb, tcols(r, nb), r * nbk + nb) for r in range(dil) for nb in range(nbk)]
                tiles += [(None, s3, slice(T + 128 * s3, T + 128 * s3 + 128), NTT + s3) for s3 in range(2)]
                for vi, (r, nb, cols, ti) in enumerate(tiles):
                    pv, pvk = P[(0, 1, 2, 3)[vi % 4]], "P%d" % (vi % 4)
                    for kc in range(KD):
                        S.mm(pv[:, 0:128], hT[:, kc, cols], wv[:, kc, :], start=(kc == 0), stop=(kc == KD - 1),
                             R=["wa", "hT"], W=[pvk] if kc == 0 else [], A=[pvk] if kc > 0 else [])
                    S.cp("act", V[:, ti, :], pv[:, 0:128], R=[pvk], W=["V"])
                    if r is None:
                        S.cp("dve", v32[:], pv[:, 0:128], R=[pvk], W=["v32"])
                        for s2 in range(2):
                            S.dma("sp", kvs[l, 2 * nb + s2, :, 1, h, :], v32[64 * s2:64 * s2 + 8, :], R=["v32"])
                    elif dil * 128 * nb >= T - keep:
                        S.cp("dve", v32[:], pv[:, 0:128], R=[pvk], W=["v32"])
                        r0 = dil * 128 * nb + r - (T - keep)
                        S.dma("sp", kvp[l, r0:r0 + dil * 127 + 1:dil, 1, h, :], v32[:], R=["v32"])
                for ti in list(range((T - keep) // 128, NTT)) + [NTT, NTT + 1]:
                    S.tr(P[3][:, 128:256], K32[:, ti * 128:(ti + 1) * 128], ident[:], R=["K32", "ident"], W=["P3"])
                    S.cp("dve", k32[:], P[3][:, 128:256], R=["P3"], W=["k32"])
                    if ti < NTT:
                        r0 = ti * 128 - (T - keep)
                        S.dma("sp", kvp[l, r0:r0 + 128, 0, h, :], k32[:], R=["k32"])
                    else:
                        for s2 in range(2):
                            S.dma("sp", kvs[l, 2 * (ti - NTT) + s2, :, 0, h, :], k32[64 * s2:64 * s2 + 8, :], R=["k32"])
                recs = [Rec(), Rec()]
                bi = 0
                for r in range(dil):
                    for nb in range(nbk):
                        si = bi % 2
                        bi += 1
                        Sx = recs[si]
                        Ps, Po, Pd = (P[0], P[1], P[2]) if si == 0 else (P[4], P[5], P[6])
                        ks, ko, kd = ("P0", "P1", "P2") if si == 0 else ("P4", "P5", "P6")
                        pT_, pTk = pT2[si], "pT%d" % si
                        qc = tcols(r, nb)
                        kbs = [kb for kb in (nb - 1, nb) if kb >= 0]
                        for i, kb in enumerate(kbs):
                            Sx.mm(Ps[:, 0:128], KT[:, tcols(r, kb)], QT[:, qc], R=["KT", "QT"], W=[ks])
                            Sx.act(pT_[:], Ps[:, 0:128], AF.Exp, bias=nshift[:, 0:1], scale=sc_, R=[ks, "nshift"], W=[pTk])
                            Sx.tt("pool", pT_[:], pT_[:], (mcur if kb == nb else mprev)[:], ALU.mult, R=[pTk, "mask"], W=[pTk])
                            Sx.mm(Po[:, 0:128], V[:, r * nbk + kb, :], pT_[:], start=(i == 0), stop=(i == len(kbs) - 1),
                                  R=["V", pTk], W=[ko] if i == 0 else [], A=[ko] if i > 0 else [])
                            Sx.mm(Pd[:, 0:128], onesb[:], pT_[:], start=(i == 0), stop=(i == len(kbs) - 1),
                                  R=["onesb", pTk], W=[kd] if i == 0 else [], A=[kd] if i > 0 else [])
                        Sx.tt("dve", Oacc[:, qc], Oacc[:, qc], Po[:, 0:128], ALU.add, R=["Oacc", ko], W=["Oacc"])
                        Sx.tt("dve", Dacc[:, qc], Dacc[:, qc], Pd[:, 0:128], ALU.add, R=["Dacc", kd], W=["Dacc"])
                replay_interleaved(S, recs)
                idx0 = (0, 1, 5)[g]
                recs = [Rec(), Rec()]
                for s in range(NS):
                    si = s % 2
                    Sx = recs[si]
                    Pt, Po, Pd = (P[0], P[1], P[2]) if si == 0 else (P[4], P[5], P[6])
                    kt, ko, kd = ("P0", "P1", "P2") if si == 0 else ("P4", "P5", "P6")
                    pT_, pTk = pT2[si], "pT%d" % si
                    kcT_, kcTk = kcT2[si], "kcT%d" % si
                    vcb_, vcbk = vcb2[si], "vcb%d" % si
                    qs = slice(T + 64 * s, T + 64 * s + 8)
                    tl = NTT + s // 2
                    nr = min(dil, 8)
                    for r in range(nr + 1):
                        if r < nr:
                            kv_, kvk = kvc2[si][r % 2], "kvc%d%d" % (si, r % 2)
                            Sx.dma("sp", kv_[:], kvin[l, s, r:r + dil * 127 + 1:dil, :].rearrange("p (a hh d) -> p a hh d", a=2, hh=4)[:, :, h, :], W=[kvk])
                            Sx.tr(Pt[:, 0:128], kv_[:, 0, :], ident[:], R=[kvk, "ident"], W=[kt])
                            Sx.cp("act", kcT_[:], Pt[:, 0:128], R=[kt], W=[kcTk])
                            Sx.cp("dve", vcb_[:], kv_[:, 1, :], R=[kvk], W=[vcbk])
                            Sx.mm(Pt[:, 128:136], kcT_[:], QT[:, qs], R=[kcTk, "QT"], W=[kt])
                            mk = msc[:, (idx0 + r) * 8:(idx0 + r) * 8 + 8]
                            vv = vcb_[:]
                        else:
                            Sx.mm(Pt[:, 128:136], KT[:, tl * 128:(tl + 1) * 128], QT[:, qs], R=["KT", "QT"], W=[kt])
                            mk = msn[:, (g * 2 + s % 2) * 8:(g * 2 + s % 2) * 8 + 8]
                            vv = V[:, tl, :]
                        Sx.act(pT_[:, 0:8], Pt[:, 128:136], AF.Exp, bias=nshift[:, 0:1], scale=sc_, R=[kt, "nshift"], W=[pTk])
                        Sx.tt("pool", pT_[:, 0:8], pT_[:, 0:8], mk, ALU.mult, R=[pTk, "mask"], W=[pTk])
                        Sx.mm(Po[:, 0:8], vv, pT_[:, 0:8], start=(r == 0), stop=(r == nr), R=[vcbk, "V", pTk],
                              W=[ko] if r == 0 else [], A=[ko] if r > 0 else [])
                        Sx.mm(Pd[:, 0:8], onesb[:], pT_[:, 0:8], start=(r == 0), stop=(r == nr), R=["onesb", pTk],
                              W=[kd] if r == 0 else [], A=[kd] if r > 0 else [])
                    Sx.tt("dve", Oacc[:, qs], Oacc[:, qs], Po[:, 0:8], ALU.add, R=["Oacc", ko], W=["Oacc"])
                    Sx.tt("dve", Dacc[:, qs], Dacc[:, qs], Pd[:, 0:8], ALU.add, R=["Dacc", kd], W=["Dacc"])
                replay_interleaved(S, recs)
            for (t0, n) in X["blocks"]:
                S.ts("dve", rn[:, 0:n], Dacc[:, t0:t0 + n], 1e-30, None, ALU.max, R=["Dacc"], W=["rn"])
                S.op("dve", (lambda a, b: (lambda e: e.reciprocal(a, b)))(rn[:, 0:n], rn[:, 0:n]), ["rn"], ["rn"])
                S.tt("dve", ost[:, 0:n], Oacc[:, t0:t0 + n], rn[:, 0:n], ALU.mult, R=["Oacc", "rn"], W=["ost"])
                S.dma("sp", oT_d[h * 128:(h + 1) * 128, t0:t0 + n], ost[:, 0:n], R=["ost"])
```

```python
import contextlib
import numpy as np
import concourse.bass as bass
import concourse.mybir as mybir
from concourse.bass_utils import run_bass_kernel_spmd

F32 = mybir.dt.float32
BF16 = mybir.dt.bfloat16
AF = mybir.ActivationFunctionType
ALU = mybir.AluOpType

PAST_LEN = 8192
EPS = 1e-6
A_OFF, BQKV, BZ, BBETA, BA, CQ, CK, CV, CZ, GOFF = 0, 4608, 6144, 6656, 6660, 6664, 6920, 7176, 7688, 8200
NEG = -30000.0
SHIFT = 12.0


class Cfg:
    def __init__(self, D=1024, DFF=2816, T=4096, NS=4, DEPTH=2, dbg=False, mixers="abc"):
        self.D, self.DFF, self.T, self.NS, self.DEPTH, self.dbg = D, DFF, T, NS, DEPTH, dbg
        self.mixers = mixers
        self.KD = D // 128
        self.KF = DFF // 128
        self.NTT = T // 128
        self.NT = self.NTT + 2
        self.NTOK = self.NT * 128
        self.NIN = GOFF + 3 * D


class Sched:
    ENGS = ("pe", "act", "dve", "pool", "sp")
    DMAQ = ("sp", "act", "pool")

    def __init__(self, nc, stack, nslots=8):
        self.nc = nc
        self.prog = {e: [] for e in self.ENGS}
        self.cnt = {e: 0 for e in self.ENGS}
        self.sem = {e: stack.enter_context(nc.semaphore("s_" + e)) for e in self.ENGS}
        self.nslots = nslots
        self.dsem = {q: [stack.enter_context(nc.semaphore("d_%s%d" % (q, i))) for i in range(nslots)] for q in self.DMAQ}
        self.dval = {q: [0] * nslots for q in self.DMAQ}
        self.dnext = {q: 0 for q in self.DMAQ}
        self.seen = {e: {} for e in self.ENGS}
        self.lastw = {}
        self.readers = {}

    def _semof(self, ev):
        if ev[0] == "E":
            return ("E", ev[1]), self.sem[ev[1]], ev[2]
        return ("D", ev[1], ev[2]), self.dsem[ev[1]][ev[2]], ev[3]

    def _emit_waits(self, e, waits):
        best = {}
        for ev in waits:
            key, sem, val = self._semof(ev)
            if self.seen[e].get(key, 0) >= val:
                continue
            if key not in best or best[key][1] < val:
                best[key] = (sem, val)
        for key, (sem, val) in best.items():
            self.seen[e][key] = val
            self.prog[e].append(("w", sem, val))

    def _deps(self, R, W):
        waits = []
        for k in R:
            w = self.lastw.get(k)
            if w is not None:
                waits.append(w)
            if k in PSUM_KEYS:
                rd = self.readers.get(k)
                if rd:
                    waits.extend(rd.values())
        for k in W:
            w = self.lastw.get(k)
            if w is not None:
                waits.append(w)
            rd = self.readers.get(k)
            if rd:
                waits.extend(rd.values())
        return waits

    def _record(self, ev, R, W):
        key = self._semof(ev)[0]
        for k in R:
            self.readers.setdefault(k, {})[key] = ev
        for k in W:
            self.lastw[k] = ev
            self.readers[k] = {}

    def op(self, e, fn, R=(), W=(), A=()):
        NOPS[0] += 1
        if NOPS[0] > LIMIT:
            return None
        self._emit_waits(e, self._deps(R, W))
        self.cnt[e] += 1
        self.prog[e].append(("i", fn, self.sem[e]))
        ev = ("E", e, self.cnt[e])
        self._record(ev, R, list(W) + list(A))
        return ev

    def dma(self, q, out, in_, R=(), W=(), **kw):
        NOPS[0] += 1
        if NOPS[0] > LIMIT:
            return None
        waits = self._deps(R, W)
        s = self.dnext[q]
        self.dnext[q] = (s + 1) % self.nslots
        if self.dval[q][s] > 0:
            waits.append(("D", q, s, self.dval[q][s]))
        self._emit_waits(q, waits)
        self.dval[q][s] += 16
        self.prog[q].append(("d", out, in_, kw, self.dsem[q][s]))
        ev = ("D", q, s, self.dval[q][s])
        self._record(ev, R, W)
        return ev

    def barrier(self):
        evs = [("E", e, self.cnt[e]) for e in self.ENGS if self.cnt[e] > 0]
        for q in self.DMAQ:
            for s in range(self.nslots):
                if self.dval[q][s] > 0:
                    evs.append(("D", q, s, self.dval[q][s]))
        for e in self.ENGS:
            self._emit_waits(e, evs)
        self.lastw.clear()
        self.readers.clear()

    def emit(self):
        nc = self.nc
        for q in self.DMAQ:
            self._emit_waits(q, [("D", q, s, self.dval[q][s]) for s in range(self.nslots) if self.dval[q][s] > 0])
        self._emit_waits("sp", [("E", e, self.cnt[e]) for e in self.ENGS if e != "sp" and self.cnt[e] > 0])

        def run(e):
            def body(eng):
                for it in self.prog[e]:
                    if it[0] == "w":
                        eng.wait_ge(it[1], it[2])
                    elif it[0] == "i":
                        it[1](eng).then_inc(it[2], 1)
                    else:
                        eng.dma_start(out=it[1], in_=it[2], **it[3]).then_inc(it[4], 16)
            return body

        with nc.Block() as block:
            block.tensor(run("pe"))
            block.scalar(run("act"))
            block.vector(run("dve"))
            block.gpsimd(run("pool"))
            block.sync(run("sp"))

    def mm(self, out, lhsT, rhs, start=True, stop=True, R=(), W=(), A=()):
        return self.op("pe", lambda e: e.matmul(out, lhsT, rhs, start=start, stop=stop), R, W, A)

    def tr(self, out, in_, ident, R=(), W=(), A=()):
        return self.op("pe", lambda e: e.transpose(out, in_, ident), R, W, A)

    def act(self, out, in_, func, bias=0.0, scale=1.0, accum_out=None, R=(), W=()):
        if accum_out is None:
            return self.op("act", lambda e: e.activation(out, in_, func, bias=bias, scale=scale), R, W)
        return self.op("act", lambda e: e.activation(out, in_, func, bias=bias, scale=scale, accum_out=accum_out), R, W)

    def tt(self, eng, out, in0, in1, op, R=(), W=()):
        return self.op(eng, lambda e: e.tensor_tensor(out, in0, in1, op), R, W)

    def ts(self, eng, out, in0, s1, s2, op0, op1=None, R=(), W=()):
        if op1 is None:
            return self.op(eng, lambda e: e.tensor_scalar(out, in0, s1, None, op0), R, W)
        return self.op(eng, lambda e: e.tensor_scalar(out, in0, s1, s2, op0, op1), R, W)

    def stt(self, eng, out, in0, scalar, in1, op0, op1, R=(), W=()):
        return self.op(eng, lambda e: e.scalar_tensor_tensor(out, in0, scalar, in1, op0, op1), R, W)

    def cp(self, eng, out, in_, R=(), W=()):
        if eng == "act":
            return self.op("act", lambda e: e.copy(out, in_), R, W)
        return self.op(eng, lambda e: e.tensor_copy(out, in_), R, W)

    def memset(self, eng, ap, val, W=()):
        return self.op(eng, lambda e: e.memset(ap, val), (), W)


class Rec:
    def __init__(self):
        self.calls = []

    def __getattr__(self, name):
        def f(*a, **k):
            self.calls.append((name, a, k))
        return f


def replay_interleaved(S, recs):
    n = max(len(r.calls) for r in recs)
    for i in range(n):
        for r in recs:
            if i < len(r.calls):
                name, a, k = r.calls[i]
                getattr(S, name)(*a, **k)


def _tile_consts(bs, nvalid, gam):
    p = np.arange(128)
    blk, loc = p // bs, p % bs
    same = blk[:, None] == blk[None, :]
    val = loc < nvalid
    c = {}
    c["tri"] = (same & (p[:, None] <= p[None, :])).astype(np.float32)
    c["blk"] = same.astype(np.float32)
    c["negS"] = np.where(same & (p[:, None] > p[None, :]), 0.0, NEG).astype(np.float32)
    c["negIT"] = np.where(same & (p[None, :] >= p[:, None]), 0.0, NEG).astype(np.float32)
    c["valid"] = val.astype(np.float32)[:, None].copy()
    dtc = np.zeros((128, 4, 128), np.float64)
    for h in range(4):
        d = (p[None, :] - p[:, None]).astype(np.float64)
        dtc[:, h, :] = np.where(same & (d >= 0) & val[:, None] & val[None, :], gam[h] ** np.maximum(d, 0), 0.0)
    c["dtc"] = dtc.reshape(128, 512).astype(np.float32)
    qd = np.zeros((128, 2, 128), np.float64)
    gcol = np.zeros((128, 2), np.float64)
    for pr in range(2):
        for half in range(2):
            h = 2 * pr + half
            qd[64 * half:64 * half + 64, pr, :] = (gam[h] ** (loc + 1.0))[None, :]
            gcol[64 * half:64 * half + 64, pr] = gam[h] ** nvalid
    c["qdec"] = qd.reshape(128, 256).astype(np.float32)
    c["gC"] = gcol.astype(np.float32)
    kd = np.zeros((128, 4, 64), np.float64)
    for h in range(4):
        kd[:, h, :] = np.where(val, gam[h] ** np.maximum(nvalid - 1.0 - loc, 0), 0.0)[:, None]
    c["kdec"] = kd.reshape(128, 256).astype(np.float32)
    return c


def make_consts(cfg):
    T, NTOK = cfg.T, cfg.NTOK
    pos = np.zeros(NTOK, np.float32)
    pos[:T] = np.arange(T)
    for s in range(4):
        pos[T + 64 * s:T + 64 * s + 64] = PAST_LEN + np.arange(64)
    C = {}
    C["ident"] = np.eye(128, dtype=np.float32)
    C["ones"] = np.ones((128, 128), np.float32)
    d = np.arange(128)
    invA = (np.float32(10000.0) ** (-np.arange(0, 128, 2, dtype=np.float32) / np.float32(128))).astype(np.float32)
    angA = (pos[None, :] * invA[d % 64][:, None]).astype(np.float32).astype(np.float64)
    C["cosA"] = np.cos(angA).astype(np.float32)
    C["sinA"] = (np.sin(angA) * np.where(d < 64, -1.0, 1.0)[:, None]).astype(np.float32)
    rotA = np.zeros((128, 128), np.float32)
    rotA[(d + 64) % 128, d] = 1.0
    C["rotA"] = rotA
    invC = (np.float32(10000.0) ** (-np.arange(0, 64, 2, dtype=np.float32) / np.float32(64))).astype(np.float32)
    dd = d % 64
    angC = (pos[None, :] * invC[dd % 32][:, None]).astype(np.float32).astype(np.float64)
    C["cosC"] = np.cos(angC).astype(np.float32)
    C["sinC"] = (np.sin(angC) * np.where(dd < 32, -1.0, 1.0)[:, None]).astype(np.float32)
    rotC = np.zeros((128, 128), np.float32)
    rotC[(d // 64) * 64 + (dd + 32) % 64, d] = 1.0
    C["rotC"] = rotC
    k = np.arange(128)
    C["mcur"] = np.tile((k[:, None] <= k[None, :]).astype(np.float32), (1, 4))
    C["mprev"] = np.tile((k[:, None] >= k[None, :]).astype(np.float32), (1, 4))
    msc = np.zeros((13, 128, 8), np.float32)
    msn = np.zeros((3, 2, 128, 8), np.float32)
    idx = 0
    for g, dil in enumerate((1, 4, 16)):
        for r in range(min(dil, 8)):
            for i in range(8):
                if i % dil == r:
                    msc[idx, :, i] = (k >= i // dil)
            idx += 1
        for s2 in range(2):
            for i in range(8):
                for j in range(8):
                    if j <= i and (i - j) % dil == 0:
                        msn[g, s2, 64 * s2 + j, i] = 1.0
    C["msc"] = np.ascontiguousarray(msc.transpose(1, 0, 2)).reshape(128, 13 * 8)
    C["msn"] = np.ascontiguousarray(msn.transpose(2, 0, 1, 3)).reshape(128, 48)
    gam = [1.0 - 2.0 ** (-5.0 - h) for h in range(4)]
    for nm, (bs, nv) in (("p", (64, 64)), ("s", (64, 8))):
        for kk, v in _tile_consts(bs, nv, gam).items():
            C[kk + "_" + nm] = v
    return C


CUT = 99
LIMIT = 10 ** 9
PADOPS = 0
PSUM_KEYS = frozenset(["P%d" % i for i in range(7)] + ["PT"])
VAR = 0
NOPS = [0]


def build(cfg):
    D, DFF, T, NS, DEPTH = cfg.D, cfg.DFF, cfg.T, cfg.NS, cfg.DEPTH
    KD, KF, NTT, NT, NTOK, NIN = cfg.KD, cfg.KF, cfg.NTT, cfg.NT, cfg.NTOK, cfg.NIN
    assert NS == 4 and T % 2048 == 0
    nc = bass.Bass("TRN2", target_bir_lowering=False)
    consts = make_consts(cfg)

    def din(name, shape, dt=F32):
        return nc.dram_tensor(name, list(shape), dt, kind="ExternalInput").ap()

    def dout(name, shape, dt=F32):
        return nc.dram_tensor(name, list(shape), dt, kind="ExternalOutput").ap()

    def dscr(name, shape, dt):
        return nc.dram_tensor(name, list(shape), dt, kind="ExternalOutput" if cfg.dbg else "Internal").ap()

    I = {}
    I["xp"] = din("xp", [T, D])
    I["xs"] = din("xs", [NS * 8, D])
    I["kv0"] = din("kv0", [DEPTH, NS, 128, 1024])
    I["kv1"] = din("kv1", [DEPTH, NS, 512, 1024])
    I["kv2"] = din("kv2", [DEPTH, NS, 2048, 1024])
    I["sconv"] = din("sconv", [DEPTH, NS * 3, 1536])
    I["sS"] = din("sS", [DEPTH, NS, 4, 128, 128])
    I["sR"] = din("sR", [DEPTH, NS, 2, 128, 128])
    I["g1B"] = din("g1B", [DEPTH, 128, D])
    I["g2B"] = din("g2B", [DEPTH, 128, D])
    I["w_in"] = din("w_in", [DEPTH, D, NIN])
    I["aqg"] = din("aqg", [DEPTH, 128, 1])
    I["akg"] = din("akg", [DEPTH, 128, 1])
    I["convw"] = din("convw", [DEPTH, 128, 12, 4])
    I["alogB"] = din("alogB", [DEPTH, 128, 4])
    I["dtbB"] = din("dtbB", [DEPTH, 128, 4])
    I["bog"] = din("bog", [DEPTH, 128, 1])
    I["cog"] = din("cog", [DEPTH, 128, 1])
    I["w_oa"] = din("w_oa", [DEPTH, 512, D])
    I["w_ob"] = din("w_ob", [DEPTH, 512, D])
    I["w_oc"] = din("w_oc", [DEPTH, 512, D])
    I["w_out"] = din("w_out", [DEPTH, D, D])
    I["w_fi"] = din("w_fi", [DEPTH, D, 2 * DFF])
    I["w_fo"] = din("w_fo", [DEPTH, DFF, D])
    for k, v in consts.items():
        I["c_" + k] = din("c_" + k, v.shape)

    O = {}
    O["yp"] = dout("yp", [T, D])
    O["ys"] = dout("ys", [NS * 8, D])
    KEEP = (128, 512, min(2048, T))
    for g in range(3):
        O["kvp%d" % g] = dout("kvp%d" % g, [DEPTH, KEEP[g], 2, 4, 128])
        O["kvs%d" % g] = dout("kvs%d" % g, [DEPTH, NS, 8, 2, 4, 128])
    O["convp"] = dout("convp", [DEPTH, 3, 1536])
    O["Sp"] = dout("Sp", [DEPTH, 4, 128, 128])
    O["Rp"] = dout("Rp", [DEPTH, 2, 128, 128])
    O["convs"] = dout("convs", [DEPTH, NS * 3, 1536])
    O["Ss"] = dout("Ss", [DEPTH, NS, 4, 128, 128])
    O["Rs"] = dout("Rs", [DEPTH, NS, 2, 128, 128])
    xres = dscr("xres", [NTOK, D], F32)
    oT_d = dscr("oT_d", [12 * 128, NTOK], BF16)
    sg_d = dscr("sg_d", [3 * D, NTOK], BF16)
    WB = {}
    for nm, shp in (("w_oa", [512, D]), ("w_ob", [512, D]), ("w_oc", [512, D]), ("w_out", [D, D]), ("w_fi", [D, 2 * DFF]), ("w_fo", [DFF, D])):
        WB[nm] = nc.dram_tensor("wbf_" + nm, [DEPTH] + shp, BF16, kind="Internal").ap()

    with contextlib.ExitStack() as st:
        S = Sched(nc, st)
        sb = lambda n, s, d=F32: st.enter_context(nc.sbuf_tensor(n, list(s), d))
        P = [st.enter_context(nc.psum_tensor("p%d" % i, [128, 512], F32)) for i in range(7)]
        PT = st.enter_context(nc.psum_tensor("pT", [128, 1024], BF16))

        ident = sb("ident", [128, 128])
        identb = sb("identb", [128, 128], BF16)
        ones = sb("ones", [128, 128])
        onesb = sb("onesb", [128, 128], BF16)
        S.dma("sp", ident[:], I["c_ident"], W=["ident"])
        S.dma("pool", identb[:], I["c_ident"], W=["identb"])
        S.dma("sp", ones[:], I["c_ones"], W=["ones"])
        S.dma("pool", onesb[:], I["c_ones"], W=["onesb"])
        for l_ in range(DEPTH):
            for nm in ("w_oa", "w_ob", "w_oc", "w_out", "w_fi", "w_fo"):
                rows = WB[nm].shape[1]
                for r0 in range(0, rows, 128):
                    S.dma("pool", WB[nm][l_, r0:r0 + 128, :], I[nm][l_, r0:r0 + 128, :], W=["wbf_%s_%d_%d" % (nm, l_, r0)])
        NBLK = (NTOK + 511) // 512
        blocks = [(b * 512, min(512, NTOK - b * 512)) for b in range(NBLK)]

        def wview(ap2d, c0, ncols):
            return ap2d.rearrange("(kc p) n -> p kc n", p=128)[:, :, c0:c0 + ncols]

        def rstd_from_ss(dst, src, scale, R, W):
            S.act(dst, src, AF.Ln, bias=EPS, scale=scale, R=R, W=W)
            S.act(dst, dst, AF.Exp, scale=-0.5, R=W, W=W)

        def norm_tile(x_t, xkey, gB, h_t, hkey, junk, ss, hTdst, tcols, hTkey):
            S.memset("dve", ss[:, 0:1], 0.0, W=["ss"])
            S.act(junk[:], x_t, AF.Square, accum_out=ss[:, 0:1], R=[xkey, "ss"], W=["junk", "ss"])
            rstd_from_ss(ss[:, 0:1], ss[:, 0:1], 1.0 / D, ["ss"], ["ss"])
            S.stt("dve", h_t, x_t, ss[:, 0:1], gB, ALU.mult, ALU.mult, R=[xkey, "ss", "gB"], W=[hkey])
            for kc in range(KD):
                S.tr(PT[:, kc * 128:(kc + 1) * 128], h_t[:, kc * 128:(kc + 1) * 128], identb[:],
                     R=[hkey, "identb"], W=["PT"] if kc == 0 else [], A=["PT"] if kc > 0 else [])
            S.cp("dve", hTdst[:, :, tcols], PT[:, 0:KD * 128].rearrange("p (k n) -> p k n", k=KD), R=["PT"], W=[hTkey])

        for l in range(DEPTH):
            w_in = I["w_in"][l]
            lay = contextlib.ExitStack()
            hT = lay.enter_context(nc.sbuf_tensor("hT%d" % l, [128, KD, NTOK], BF16))
            with contextlib.ExitStack() as ph:
                sbp = lambda n, s, d=F32: ph.enter_context(nc.sbuf_tensor("a%d_%s" % (l, n), list(s), d))
                gB = sbp("gB", [128, D])
                S.dma("sp", gB[:], I["g1B"][l], W=["gB"])
                xt = [sbp("xt%d" % i, [128, D]) for i in range(2)]
                ht = [sbp("ht%d" % i, [128, D], BF16) for i in range(2)]
                junk = sbp("junk", [128, D])
                ss = sbp("ss", [128, 1])
                for i in range(NT):
                    x_t, h_t = xt[i % 2], ht[i % 2]
                    xk, hk = "xt%d" % (i % 2), "ht%d" % (i % 2)
                    if l == 0:
                        if i < NTT:
                            S.dma("sp", x_t[:], I["xp"][i * 128:(i + 1) * 128, :], W=[xk])
                        else:
                            S.memset("dve", x_t[:], 0.0, W=[xk])
                            for s2 in range(2):
                                s = 2 * (i - NTT) + s2
                                S.dma("sp", x_t[64 * s2:64 * s2 + 8, :], I["xs"][8 * s:8 * s + 8, :], W=[xk])
                    else:
                        S.dma("sp", x_t[:], xres[i * 128:(i + 1) * 128, :], W=[xk])
                    norm_tile(x_t[:], xk, gB[:], h_t[:], hk, junk, ss, hT, slice(i * 128, (i + 1) * 128), "hT")
            S.barrier()
            with contextlib.ExitStack() as ph:
                sbp = lambda n, s, d=F32: ph.enter_context(nc.sbuf_tensor("g%d_%s" % (l, n), list(s), d))
                wg = [sbp("wg%d" % i, [128, KD, 512], BF16) for i in range(2)]
                sgo = [sbp("sgo%d" % i, [128, 4, 512], BF16) for i in range(2)]
                ncb = 3 * D // 512
                it = 0
                for cb in range(ncb):
                    w_t, wk = wg[cb % 2], "wg%d" % (cb % 2)
                    S.dma("pool", w_t[:], wview(w_in, GOFF + cb * 512, 512), W=[wk])
                    for (t0, n) in blocks:
                        so, sk = sgo[it % 2], "sgo%d" % (it % 2)
                        for j in range(4):
                            pp, pk = P[(it * 4 + j) % 4], "P%d" % ((it * 4 + j) % 4)
                            for kc in range(KD):
                                S.mm(pp[:, 0:n], w_t[:, kc, j * 128:(j + 1) * 128], hT[:, kc, t0:t0 + n],
                                     start=(kc == 0), stop=(kc == KD - 1), R=[wk, "hT"],
                                     W=[pk] if kc == 0 else [], A=[pk] if kc > 0 else [])
                            S.act(so[:, j, 0:n], pp[:, 0:n], AF.Sigmoid, R=[pk], W=[sk])
                        S.dma("sp", sg_d[cb * 512:(cb + 1) * 512, t0:t0 + n].rearrange("(j p) n -> p j n", p=128),
                              so[:, :, 0:n], R=[sk])
                        it += 1
            S.barrier()
            X = dict(nc=nc, S=S, cfg=cfg, l=l, I=I, O=O, P=P, PT=PT, hT=hT, oT_d=oT_d, ident=ident, identb=identb,
                     ones=ones, onesb=onesb, blocks=blocks, wview=wview, rstd=rstd_from_ss, w_in=w_in)
            for mi, ch in enumerate("abc"):
                if ch not in cfg.mixers:
                    zt = lay.enter_context(nc.sbuf_tensor("zt%d_%d" % (l, mi), [128, 4, NTOK], BF16))
                    S.memset("dve", zt[:], 0.0, W=["zt"])
                    S.dma("sp", oT_d[mi * 512:(mi + 1) * 512, :].rearrange("(c p) n -> p c n", p=128), zt[:], R=["zt"])
            S.barrier()
            if "c" in cfg.mixers:
                mixer_c(X)
                S.barrier()
            if "b" in cfg.mixers:
                mixer_b(X)
                S.barrier()
            if "a" in cfg.mixers:
                mixer_a(X)
                S.barrier()
            lay.close()
            with contextlib.ExitStack() as ph:
                sbp = lambda n, s, d=F32: ph.enter_context(nc.sbuf_tensor("d%d_%s" % (l, n), list(s), d))
                g2B = sbp("g2B", [128, D])
                S.dma("sp", g2B[:], I["g2B"][l], W=["gB"])
                oTb = sbp("oTb", [128, 12, 512], BF16)
                sgb = sbp("sgb", [128, 3 * KD, 512], BF16)
                mT = sbp("mT", [128, KD, 512], BF16)
                h2T = sbp("h2T", [128, KD, 512], BF16)
                aT = sbp("aT", [128, KF, 512], BF16)
                x1 = [sbp("x1_%d" % i, [128, D]) for i in range(4)]
                h2 = sbp("h2", [128, D], BF16)
                junk = sbp("junk", [128, D])
                ss = sbp("ss", [128, 1])
                tmp = [sbp("tmp%d" % i, [128, 512]) for i in range(3)]
                NW = 3
                wb = [sbp("wb%d" % i, [128, 8, 512], BF16) for i in range(NW)]
                wfo = sbp("wfo", [128, KF, 512], BF16)
                wctr = [0]

                def loadw(src2d, c0, ncols, k0, nk):
                    i = wctr[0] % NW
                    wctr[0] += 1
                    v = src2d.rearrange("(kc p) n -> p kc n", p=128)[:, k0:k0 + nk, c0:c0 + ncols]
                    S.dma("sp", wb[i][:, 0:nk, 0:ncols], v, W=["wb%d" % i])
                    return wb[i], "wb%d" % i

                pctr = [0]

                def nextp():
                    i = pctr[0] % 6
                    pctr[0] += 1
                    return P[i], "P%d" % i

                for (t0, n) in blocks:
                    ntl = n // 128
                    S.dma("sp", oTb[:, :, 0:n], oT_d[:, t0:t0 + n].rearrange("(c p) n -> p c n", p=128), W=["oTb"])
                    S.dma("sp", sgb[:, :, 0:n], sg_d[:, t0:t0 + n].rearrange("(c p) n -> p c n", p=128), W=["sgb"])
                    for cb in range(D // 512):
                        ws = []
                        for j, nm in enumerate(("w_oa", "w_ob", "w_oc")):
                            ws.append(loadw(WB[nm][l], cb * 512, 512, 0, 4))
                        for oc4 in range(4):
                            oc = cb * 4 + oc4
                            for j in range(3):
                                w_t, wk = ws[j]
                                pp, pk = nextp()
                                for kc in range(4):
                                    S.mm(pp[:, 0:n], w_t[:, kc, oc4 * 128:(oc4 + 1) * 128], oTb[:, 4 * j + kc, 0:n],
                                         start=(kc == 0), stop=(kc == 3), R=[wk, "oTb"],
                                         W=[pk] if kc == 0 else [], A=[pk] if kc > 0 else [])
                                S.tt("dve", tmp[j][:, 0:n], pp[:, 0:n], sgb[:, j * KD + oc, 0:n], ALU.mult,
                                     R=[pk, "sgb"], W=["tmp%d" % j])
                            S.tt("pool", tmp[0][:, 0:n], tmp[0][:, 0:n], tmp[1][:, 0:n], ALU.add, R=["tmp0", "tmp1"], W=["tmp0"])
                            S.tt("pool", mT[:, oc, 0:n], tmp[0][:, 0:n], tmp[2][:, 0:n], ALU.add, R=["tmp0", "tmp2"], W=["mT"])
                    for tt_ in range(ntl):
                        tok0 = t0 + tt_ * 128
                        ti = tok0 // 128
                        xk = "x1_%d" % tt_
                        x_t = x1[tt_]
                        if l == 0:
                            if ti < NTT:
                                S.dma("sp", x_t[:], I["xp"][tok0:tok0 + 128, :], W=[xk])
                            else:
                                S.memset("dve", x_t[:], 0.0, W=[xk])
                                for s2 in range(2):
                                    s = 2 * (ti - NTT) + s2
                                    S.dma("sp", x_t[64 * s2:64 * s2 + 8, :], I["xs"][8 * s:8 * s + 8, :], W=[xk])
                        else:
                            S.dma("sp", x_t[:], xres[tok0:tok0 + 128, :], W=[xk])
                    for cb in range(D // 512):
                        w_t, wk = loadw(WB["w_out"][l], cb * 512, 512, 0, KD)
                        for tt_ in range(ntl):
                            pp, pk = nextp()
                            xk = "x1_%d" % tt_
                            for kc in range(KD):
                                S.mm(pp[:], mT[:, kc, tt_ * 128:(tt_ + 1) * 128], w_t[:, kc, :], start=(kc == 0), stop=(kc == KD - 1),
                                     R=[wk, "mT"], W=[pk] if kc == 0 else [], A=[pk] if kc > 0 else [])
                            S.tt("dve", x1[tt_][:, cb * 512:(cb + 1) * 512], x1[tt_][:, cb * 512:(cb + 1) * 512], pp[:], ALU.add,
                                 R=[pk, xk], W=[xk])
                    for tt_ in range(ntl):
                        norm_tile(x1[tt_][:], "x1_%d" % tt_, g2B[:], h2[:], "h2", junk, ss, h2T,
                                  slice(tt_ * 128, (tt_ + 1) * 128), "h2T")
                    for fb in range((DFF + 511) // 512):
                        f0 = fb * 512
                        fn_ = min(512, DFF - f0)
                        wg_t, wgk = loadw(WB["w_fi"][l], f0, fn_, 0, KD)
                        wu_t, wuk = loadw(WB["w_fi"][l], DFF + f0, fn_, 0, KD)
                        for fc in range(fn_ // 128):
                            pg, pgk = nextp()
                            pu, puk = nextp()
                            for kc in range(KD):
                                S.mm(pg[:, 0:n], wg_t[:, kc, fc * 128:(fc + 1) * 128], h2T[:, kc, 0:n], start=(kc == 0), stop=(kc == KD - 1),
                                     R=[wgk, "h2T"], W=[pgk] if kc == 0 else [], A=[pgk] if kc > 0 else [])
                            for kc in range(KD):
                                S.mm(pu[:, 0:n], wu_t[:, kc, fc * 128:(fc + 1) * 128], h2T[:, kc, 0:n], start=(kc == 0), stop=(kc == KD - 1),
                                     R=[wuk, "h2T"], W=[puk] if kc == 0 else [], A=[puk] if kc > 0 else [])
                            S.act(tmp[0][:, 0:n], pg[:, 0:n], AF.Silu, R=[pgk], W=["tmp0"])
                            S.tt("dve", aT[:, f0 // 128 + fc, 0:n], tmp[0][:, 0:n], pu[:, 0:n], ALU.mult, R=["tmp0", puk], W=["aT"])
                    for cb in range(D // 512):
                        S.dma("sp", wfo[:], WB["w_fo"][l].rearrange("(kc p) n -> p kc n", p=128)[:, :, cb * 512:(cb + 1) * 512], W=["wfo"])
                        for tt_ in range(ntl):
                            pp, pk = nextp()
                            xk = "x1_%d" % tt_
                            for fc in range(KF):
                                S.mm(pp[:], aT[:, fc, tt_ * 128:(tt_ + 1) * 128], wfo[:, fc, :], start=(fc == 0), stop=(fc == KF - 1),
                                     R=["wfo", "aT"], W=[pk] if fc == 0 else [], A=[pk] if fc > 0 else [])
                            S.tt("dve", x1[tt_][:, cb * 512:(cb + 1) * 512], x1[tt_][:, cb * 512:(cb + 1) * 512], pp[:], ALU.add,
                                 R=[pk, xk], W=[xk])
                    dst = xres if l < DEPTH - 1 else None
                    for tt_ in range(ntl):
                        tok0 = t0 + tt_ * 128
                        ti = tok0 // 128
                        xk = "x1_%d" % tt_
                        if dst is not None:
                            S.dma("sp", xres[tok0:tok0 + 128, :], x1[tt_][:], R=[xk])
                        else:
                            if ti < NTT:
                                S.dma("sp", O["yp"][tok0:tok0 + 128, :], x1[tt_][:], R=[xk])
                            else:
                                for s2 in range(2):
                                    s = 2 * (ti - NTT) + s2
                                    S.dma("sp", O["ys"][8 * s:8 * s + 8, :], x1[tt_][64 * s2:64 * s2 + 8, :], R=[xk])
            S.barrier()
        for _ in range(PADOPS):
            S.memset("dve", ones[:], 1.0, W=["ones"])
        S.emit()
    return nc


def prep_core_inputs(cfg, inp, core, consts):
    D, NS, DEPTH = cfg.D, cfg.NS, cfg.DEPTH
    f = lambda a: np.ascontiguousarray(np.asarray(a, dtype=np.float32))
    nb = inp["x_prompt"].shape[0]
    ss = slice(core * NS, (core + 1) * NS)
    m = {}
    m["xp"] = f(inp["x_prompt"][core % nb])
    m["xs"] = f(inp["x_sample"][ss]).reshape(NS * 8, D)
    m["kv0"] = f(inp["cache_a_kv0"][:, ss]).reshape(DEPTH, NS, -1, 1024)
    m["kv1"] = f(inp["cache_a_kv1"][:, ss]).reshape(DEPTH, NS, -1, 1024)
    m["kv2"] = f(inp["cache_a_kv2"][:, ss]).reshape(DEPTH, NS, -1, 1024)
    m["sconv"] = f(inp["state_b_conv"][:, ss]).reshape(DEPTH, NS * 3, 1536)
    m["sS"] = f(inp["state_b_S"][:, ss])
    m["sR"] = f(inp["state_c_R"][:, ss]).reshape(DEPTH, NS, 2, 128, 128)
    m["g1B"] = f(np.broadcast_to(np.asarray(inp["norm1_g"])[:, None, :], (DEPTH, 128, D)))
    m["g2B"] = f(np.broadcast_to(np.asarray(inp["norm2_g"])[:, None, :], (DEPTH, 128, D)))
    m["w_in"] = f(inp["w_in"])
    m["aqg"] = f(inp["a_q_norm_g"]).reshape(DEPTH, 128, 1)
    m["akg"] = f(inp["a_k_norm_g"]).reshape(DEPTH, 128, 1)
    m["convw"] = f(np.asarray(inp["b_conv_w"]).reshape(DEPTH, 4, 12, 128).transpose(0, 3, 2, 1))
    m["alogB"] = f(np.broadcast_to(np.asarray(inp["b_a_log"])[:, None, :], (DEPTH, 128, 4)))
    m["dtbB"] = f(np.broadcast_to(np.asarray(inp["b_dt_bias"])[:, None, :], (DEPTH, 128, 4)))
    m["bog"] = f(inp["b_out_norm_g"]).reshape(DEPTH, 128, 1)
    m["cog"] = f(inp["c_out_norm_g"]).reshape(DEPTH, 128, 1)
    m["w_oa"] = f(inp["w_out_a"])
    m["w_ob"] = f(inp["w_out_b"])
    m["w_oc"] = f(inp["w_out_c"])
    m["w_out"] = f(inp["w_out"])
    m["w_fi"] = f(inp["w_ffn_in"])
    m["w_fo"] = f(inp["w_ffn_out"])
    for k, v in consts.items():
        m["c_" + k] = v
    return m


def assemble(cfg, res, nb, ncores):
    DEPTH, NS = cfg.DEPTH, cfg.NS
    r = res
    st = lambda k, cores: np.stack([np.asarray(r[c][k]) for c in cores])
    pc = list(range(nb))
    ac = list(range(ncores))
    yp = st("yp", pc)
    ys = np.concatenate([np.asarray(r[c]["ys"]).reshape(NS, 8, cfg.D) for c in ac], axis=0)
    outs = [yp, ys]
    for g in range(3):
        outs.append(st("kvp%d" % g, pc).transpose(1, 0, 2, 3, 4, 5))
    outs.append(st("convp", pc).transpose(1, 0, 2, 3))
    outs.append(st("Sp", pc).transpose(1, 0, 2, 3, 4))
    outs.append(st("Rp", pc).transpose(1, 0, 2, 3, 4).reshape(DEPTH, nb, 4, 64, 128))
    for g in range(3):
        outs.append(np.concatenate([np.asarray(r[c]["kvs%d" % g]) for c in ac], axis=1))
    outs.append(np.concatenate([np.asarray(r[c]["convs"]).reshape(DEPTH, NS, 3, 1536) for c in ac], axis=1))
    outs.append(np.concatenate([np.asarray(r[c]["Ss"]) for c in ac], axis=1))
    outs.append(np.concatenate([np.asarray(r[c]["Rs"]).reshape(DEPTH, NS, 4, 64, 128) for c in ac], axis=1))
    return tuple(np.ascontiguousarray(o, dtype=np.float32) for o in outs)


def kernel(**inputs):
    cfg = Cfg()
    ncores = 8
    consts = make_consts(cfg)
    nc = build(cfg)
    in_maps = [prep_core_inputs(cfg, inputs, c, consts) for c in range(ncores)]
    res = run_bass_kernel_spmd(nc, in_maps, core_ids=list(range(ncores)))
    return assemble(cfg, res.results, inputs["x_prompt"].shape[0], ncores)


def _gated_norm_epilogue(X, sbp_bufs, P_o, pok, zcol0, gcol, gkey, tok0, out_stage, okey, stage_cols, Pz, pzk, Pss, pssk, wz, wzk):
    S, hT, KD, onesb = X["S"], X["hT"], X["cfg"].KD, X["onesb"]
    sq, rs, on, sz = sbp_bufs
    poks = list(pok) if isinstance(pok, (list, tuple)) else [pok]
    S.act(sq[:], P_o[:], AF.Square, R=poks, W=["e_sq"])
    S.mm(Pss[:], onesb[:], sq[:], R=["onesb", "e_sq"], W=[pssk])
    X["rstd"](rs[:], Pss[:], 1.0 / 128, [pssk], ["e_rs"])
    S.stt("dve", on[:], P_o[:], gcol, rs[:], ALU.mult, ALU.mult, R=poks + [gkey, "e_rs"], W=["e_on"])
    for h in range(4):
        for kc in range(KD):
            S.mm(Pz[:, h * 128:(h + 1) * 128], wz[:, kc, zcol0 + h * 128:zcol0 + (h + 1) * 128], hT[:, kc, tok0:tok0 + 128],
                 start=(kc == 0), stop=(kc == KD - 1), R=[wzk, "hT"],
                 W=[pzk] if (h == 0 and kc == 0) else [], A=[pzk] if not (h == 0 and kc == 0) else [])
    S.act(sz[:], Pz[:], AF.Silu, R=[pzk], W=["e_sz"])
    S.tt("dve", out_stage[:, :, stage_cols], on[:].rearrange("p (h n) -> p h n", h=4), sz[:].rearrange("p (h n) -> p h n", h=4),
         ALU.mult, R=["e_on", "e_sz"], W=[okey])


def mixer_c(X):
    nc, S, cfg, l, I, O, P, PT, hT = X["nc"], X["S"], X["cfg"], X["l"], X["I"], X["O"], X["P"], X["PT"], X["hT"]
    KD, NTT, NS = cfg.KD, cfg.NTT, cfg.NS
    w_in, oT_d, identb, wview = X["w_in"], X["oT_d"], X["identb"], X["wview"]
    with contextlib.ExitStack() as ph:
        sbp = lambda n, s, d=F32: ph.enter_context(nc.sbuf_tensor("c%d_%s" % (l, n), list(s), d))
        rotC = sbp("rotC", [128, 128], BF16)
        S.dma("pool", rotC[:], I["c_rotC"], W=["rotC"])
        tab = {}
        for ty in "ps":
            for nm, w in (("dtc", 512), ("qdec", 256), ("gC", 2), ("kdec", 256)):
                tab[nm + ty] = sbp(nm + ty, [128, w])
                S.dma("sp", tab[nm + ty][:], I["c_%s_%s" % (nm, ty)], W=["tab"])
        cog = sbp("cog", [128, 1])
        S.dma("sp", cog[:], I["cog"][l], W=["cog"])
        wq = sbp("wq", [128, KD, 256], BF16)
        wk = sbp("wk", [128, KD, 256], BF16)
        wv = sbp("wv", [128, KD, 512], BF16)
        wz = sbp("wz", [128, KD, 512], BF16)
        for t_, c0, w_ in ((wq, CQ, 256), (wk, CK, 256), (wv, CV, 512), (wz, CZ, 512)):
            S.dma("pool", t_[:], wview(w_in, c0, w_), W=["wc"])
        ct = sbp("ct", [128, 512])
        sn = sbp("sn", [128, 512])
        qT = sbp("qT", [128, 2, 2, 512], BF16)
        S.memset("pool", qT[:], 0.0, W=["qT"])
        qdT = sbp("qdT", [128, 2, 512], BF16)
        kT = sbp("kT", [128, 2, 512], BF16)
        qb = sbp("qb", [128, 512], BF16)
        t1 = sbp("t1", [128, 512])
        t2 = sbp("t2", [128, 512])
        v_t = sbp("v_t", [128, 512], BF16)
        kd_t = sbp("kd_t", [128, 256], BF16)
        attn = sbp("attn", [128, 512], BF16)
        R32 = sbp("R32", [128, 2, 128])
        Rb = [sbp("Rb%d" % i, [128, 2, 128], BF16) for i in range(2)]
        ebuf = (sbp("e_sq", [128, 512], BF16), sbp("e_rs", [128, 512]), sbp("e_on", [128, 512]), sbp("e_sz", [128, 512]))
        ocs = sbp("ocs", [128, 4, 512], BF16)
        S.memset("dve", R32[:], 0.0, W=["R32"])
        rbi = 0
        for (t0, n) in X["blocks"]:
            ntl = n // 128
            sample = (t0 // 128 >= NTT)
            ty = "s" if sample else "p"
            if CUT <= 0:
                continue
            S.dma("sp", ct[:, 0:n], I["c_cosC"][:, t0:t0 + n], W=["ct"])
            S.dma("sp", sn[:, 0:n], I["c_sinC"][:, t0:t0 + n], W=["sn"])
            for which, w_t, dst in (("q", wq, qT), ("k", wk, kT)):
                sc = 1.0 if which == "q" else 0.125
                for c in range(2):
                    for kc in range(KD):
                        S.mm(P[0][:, 0:n], w_t[:, kc, c * 128:(c + 1) * 128], hT[:, kc, t0:t0 + n], start=(kc == 0), stop=(kc == KD - 1),
                             R=["wc", "hT"], W=["P0"] if kc == 0 else [], A=["P0"] if kc > 0 else [])
                    S.cp("act", qb[:, 0:n], P[0][:, 0:n], R=["P0"], W=["qb"])
                    S.mm(P[1][:, 0:n], rotC[:], qb[:, 0:n], R=["rotC", "qb"], W=["P1"])
                    if VAR == 1:
                        S.cp("dve", t1[:, 0:n], ct[:, 0:n], R=["ct"], W=["t1"])
                    elif VAR == 2:
                        S.cp("dve", t1[:, 0:n], P[0][:, 0:n], R=["P0"], W=["t1"])
                    elif VAR == 3:
                        S.cp("dve", t1[:, 0:n], X["ones"][:, 0:1].broadcast_to([128, n]) if False else t2[:, 0:n], R=[], W=["t1"])
                    else:
                        S.tt("dve", t1[:, 0:n], P[0][:, 0:n], ct[:, 0:n], ALU.mult, R=["P0", "ct"], W=["t1"])
                    S.tt("dve", t2[:, 0:n], P[1][:, 0:n], sn[:, 0:n], ALU.mult, R=["P1", "sn"], W=["t2"])
                    S.tt("pool", t1[:, 0:n], t1[:, 0:n], t2[:, 0:n], ALU.add, R=["t1", "t2"], W=["t1"])
                    if which == "k":
                        S.act(dst[:, c, 0:n], t1[:, 0:n], AF.Copy, scale=sc, R=["t1"], W=[which + "T"])
                    else:
                        for half in range(2):
                            hr = slice(64 * half, 64 * half + 64)
                            S.act(dst[hr, c, half, 0:n], t1[hr, 0:n], AF.Copy, R=["t1"], W=["qT"])
                    if which == "q":
                        for tt_ in range(ntl):
                            S.tt("pool", qdT[:, c, tt_ * 128:(tt_ + 1) * 128], t1[:, tt_ * 128:(tt_ + 1) * 128],
                                 tab["qdec" + ty][:, c * 128:(c + 1) * 128], ALU.mult, R=["t1", "tab"], W=["qdT"])
            if CUT <= 1:
                continue
            for tt_ in range(ntl):
                tok0 = t0 + tt_ * 128
                tc = slice(tt_ * 128, (tt_ + 1) * 128)
                for kc in range(KD):
                    S.mm(P[2][:], hT[:, kc, tok0:tok0 + 128], wv[:, kc, :], start=(kc == 0), stop=(kc == KD - 1),
                         R=["wc", "hT"], W=["P2"] if kc == 0 else [], A=["P2"] if kc > 0 else [])
                S.cp("act", v_t[:], P[2][:], R=["P2"], W=["v_t"])
                for c in range(2):
                    S.tr(PT[:, c * 128:(c + 1) * 128], kT[:, c, tc], identb[:], R=["kT", "identb"],
                         W=["PT"] if c == 0 else [], A=["PT"] if c > 0 else [])
                S.tt("dve", kd_t[:], PT[:, 0:256], tab["kdec" + ty][:], ALU.mult, R=["PT", "tab"], W=["kd_t"])
                if CUT <= 2:
                    continue
                for h in range(4):
                    c, rows = h // 2, slice(64 * (h % 2), 64 * (h % 2) + 64)
                    S.mm(P[3][:, h * 128:(h + 1) * 128], kT[:, c, tc], qT[:, c, h % 2, tc], R=["kT", "qT"],
                         W=["P3"] if h == 0 else [], A=["P3"] if h > 0 else [])
                S.tt("dve", attn[:], P[3][:], tab["dtc" + ty][:], ALU.mult, R=["P3", "tab"], W=["attn"])
                if CUT <= 3:
                    continue
                bs = 64
                rbs = []
                for b in range(2):
                    br = slice(b * bs, (b + 1) * bs)
                    sq_ = 2 * (tok0 // 128 - NTT) + b
                    if sample:
                        S.dma("sp", R32[:], I["sR"][l, sq_].rearrange("pr p d -> p pr d"), W=["R32"])
                    rb, rbk = Rb[b], "Rb%d" % b
                    rbs.append((rb, rbk))
                    S.cp("act", rb[:], R32[:], R=["R32"], W=[rbk])
                    for pr in range(2):
                        S.mm(P[4][:, pr * 256:(pr + 1) * 256], kd_t[br, pr * 128:(pr + 1) * 128], v_t[br, pr * 256:(pr + 1) * 256],
                             R=["kd_t", "v_t"], W=["P4"] if pr == 0 else [], A=["P4"] if pr > 0 else [])
                    for pr in range(2):
                        for half in range(2):
                            rows = slice(64 * half, 64 * half + 64)
                            S.stt("dve", R32[rows, pr, :], R32[rows, pr, :], tab["gC" + ty][rows, pr:pr + 1],
                                  P[4][rows, pr * 256 + half * 128:pr * 256 + half * 128 + 128], ALU.mult, ALU.add,
                                  R=["R32", "P4", "tab"], W=["R32"])
                    if sample:
                        S.dma("sp", O["Rs"][l, sq_].rearrange("pr p d -> p pr d"), R32[:], R=["R32"])
                    elif tok0 + 128 == cfg.T and b == 1:
                        S.dma("sp", O["Rp"][l].rearrange("pr p d -> p pr d"), R32[:], R=["R32"])
                if CUT <= 4:
                    continue
                for h in range(4):
                    c, rows = h // 2, slice(64 * (h % 2), 64 * (h % 2) + 64)
                    S.mm(P[5][:, h * 128:(h + 1) * 128], v_t[:, h * 128:(h + 1) * 128], attn[:, h * 128:(h + 1) * 128],
                         start=True, stop=False, R=["v_t", "attn"], W=["P5"] if h == 0 else [], A=["P5"] if h > 0 else [])
                    for b in range(2):
                        rb, rbk = rbs[b]
                        S.mm(P[5][:, h * 128 + b * bs:h * 128 + (b + 1) * bs], rb[rows, c, :], qdT[rows, c, tt_ * 128 + b * bs:tt_ * 128 + (b + 1) * bs],
                             start=False, stop=(b == 1), R=[rbk, "qdT"], A=["P5"])
                if CUT <= 5:
                    continue
                _gated_norm_epilogue(X, ebuf, P[5], "P5", 0, cog[:, 0:1], "cog", tok0, ocs, "ocs", tc, P[2], "P2", P[6], "P6", wz, "wc")
            if CUT <= 6:
                continue
            S.dma("sp", oT_d[8 * 128:12 * 128, t0:t0 + n].rearrange("(c p) n -> p c n", p=128), ocs[:, :, 0:n], R=["ocs"])


def mixer_b(X):
    nc, S, cfg, l, I, O, P, PT, hT = X["nc"], X["S"], X["cfg"], X["l"], X["I"], X["O"], X["P"], X["PT"], X["hT"]
    KD, NTT, NS, T = cfg.KD, cfg.NTT, cfg.NS, cfg.T
    w_in, oT_d, ident, ones, onesb, wview = X["w_in"], X["oT_d"], X["ident"], X["ones"], X["onesb"], X["wview"]
    with contextlib.ExitStack() as ph:
        sbp = lambda n, s, d=F32: ph.enter_context(nc.sbuf_tensor("b%d_%s" % (l, n), list(s), d))
        tab = {}
        for ty in "ps":
            for nm, w in (("tri", 128), ("blk", 128), ("negS", 128), ("negIT", 128), ("valid", 1)):
                tab[nm + ty] = sbp(nm + ty, [128, w])
                S.dma("sp", tab[nm + ty][:], I["c_%s_%s" % (nm, ty)], W=["tab"])
        bog = sbp("bog", [128, 1]); S.dma("sp", bog[:], I["bog"][l], W=["bog"])
        cw = sbp("cw", [128, 12, 4]); S.dma("sp", cw[:], I["convw"][l], W=["cw"])
        alog = sbp("alog", [128, 4]); S.dma("sp", alog[:], I["alogB"][l], W=["alog"])
        dtb = sbp("dtb", [128, 4]); S.dma("sp", dtb[:], I["dtbB"][l], W=["dtb"])
        S.act(alog[:], alog[:], AF.Exp, R=["alog"], W=["alog"])
        wqh = [sbp("wqh%d" % i, [128, KD, 3, 128], BF16) for i in range(2)]
        wz = sbp("wz", [128, KD, 512], BF16)
        wbg = sbp("wbg", [128, KD, 8], BF16)
        S.dma("pool", wz[:], wview(w_in, BZ, 512), W=["wb"])
        S.dma("pool", wbg[:], wview(w_in, BBETA, 8), W=["wb"])
        pre = sbp("pre", [128, 3 + 512])
        carry = sbp("carry", [128, 12, 3])
        S.memset("pool", carry[:], 0.0, W=["carry"])
        pres = sbp("pres", [128, 4, 67])
        cst = sbp("cst", [12, 1536]); S.dma("sp", cst[:], I["sconv"][l], W=["cst"])
        cso = sbp("cso", [128, 12, 12])
        csin = sbp("csin", [128, 12, 12])
        cpo = sbp("cpo", [128, 12, 3])
        cout = sbp("cout", [12, 1536])
        acc = sbp("acc", [128, 512])
        yv = sbp("yv", [128, 512])
        sqb = sbp("sqb", [128, 512], BF16)
        rn = sbp("rn", [128, 512])
        fT = [{r: sbp("%sT%d" % (r, h), [128, 512]) for r in "qkv"} for h in range(4)]
        bgt = sbp("bgt", [128, 4, 8])
        nbe = sbp("nbe", [128, 4, 4])
        BN = ("sA", "sB", "sC", "egc", "Nm", "NTm", "PTm", "attnT", "Xb0", "Xb1", "XTb0", "XTb1", "u_", "wT", "qgT", "vnew", "k_tm", "v_tm")
        B = [{nm: sbp("%s_%d" % (nm, h), [128, 128]) for nm in BN} for h in range(4)]
        gcsb = [sbp("gcs%d" % h, [128, 4]) for h in range(4)]
        S32 = [sbp("S32_%d" % h, [128, 128]) for h in range(4)]
        ob = sbp("ob", [128, 4, 512])
        ebuf = (sbp("e_sq", [128, 512], BF16), sbp("e_rs", [128, 512]), sbp("e_on", [128, 512]), sbp("e_sz", [128, 512]))
        obs = sbp("obs", [128, 4, 512], BF16)
        for h in range(4):
            S.memset("pool", S32[h][:], 0.0, W=["S32_%d" % h])
        for c in range(12):
            S.tr(P[0][:, 0:12], cst[0:12, c * 128:(c + 1) * 128], ident[0:12, 0:12], R=["cst", "ident"], W=["P0"])
            S.cp("dve", csin[:, c, :], P[0][:, 0:12], R=["P0"], W=["csin"])

        def tile_chain(Sx, h, tt_, tok0, sample, ty, nlev):
            bb, pb, pk = B[h], P[1 + h], "P%d" % (1 + h)
            k_ = lambda nm: "%s_%d" % (nm, h)
            qT, kT, vT = fT[h]["q"], fT[h]["k"], fT[h]["v"]
            gcs, gk = gcsb[h], "gcs%d" % h
            s32, s32k = S32[h], "S32_%d" % h
            tc = slice(tt_ * 128, (tt_ + 1) * 128)
            be = bgt[:, tt_, h:h + 1]
            gg = bgt[:, tt_, 4 + h:5 + h]
            A_, B_, C_, D_ = pb[:, 0:128], pb[:, 128:256], pb[:, 256:384], pb[:, 384:512]
            Sx.tr(A_, kT[:, tc], ident[:], R=[k_("kT"), "ident"], W=[pk])
            Sx.tr(B_, vT[:, tc], ident[:], R=[k_("vT"), "ident"], A=[pk])
            Sx.cp("act", bb["k_tm"][:], A_, R=[pk], W=[k_("k_tm")])
            Sx.cp("dve", bb["v_tm"][:], B_, R=[pk], W=[k_("v_tm")])
            Sx.ts("dve", bb["sA"][:], ones[:], gg, None, ALU.mult, R=["ones", "bgt"], W=[k_("sA")])
            Sx.mm(A_, bb["sA"][:], tab["tri" + ty][:], R=[k_("sA"), "tab"], W=[pk])
            Sx.mm(pb[:, 128:129], tab["tri" + ty][:], gg, R=["tab", "bgt"], A=[pk])
            Sx.mm(pb[:, 129:130], tab["blk" + ty][:], gg, R=["tab", "bgt"], A=[pk])
            Sx.cp("dve", gcs[:, 0:2], pb[:, 128:130], R=[pk], W=[gk])
            Sx.stt("dve", bb["sA"][:], A_, gcs[:, 0:1], tab["negS" + ty][:], ALU.subtract, ALU.subtract, R=[pk, gk, "tab"], W=[k_("sA")])
            Sx.act(bb["sB"][:], bb["sA"][:], AF.Exp, scale=-1.0, R=[k_("sA")], W=[k_("sB")])
            Sx.stt("dve", bb["sA"][:], A_, gcs[:, 0:1], tab["negIT" + ty][:], ALU.subtract, ALU.add, R=[pk, gk, "tab", k_("sB")], W=[k_("sA")])
            Sx.act(bb["sC"][:], bb["sA"][:], AF.Exp, R=[k_("sA")], W=[k_("sC")])
            Sx.act(bb["egc"][:], A_, AF.Exp, R=[pk], W=[k_("egc")])
            Sx.tt("dve", gcs[:, 2:3], gcs[:, 1:2], gcs[:, 0:1], ALU.subtract, R=[gk], W=[gk])
            Sx.act(gcs[:, 2:3], gcs[:, 2:3], AF.Exp, R=[gk], W=[gk])
            Sx.tt("dve", gcs[:, 2:3], gcs[:, 2:3], tab["valid" + ty][:, 0:1], ALU.mult, R=[gk, "tab"], W=[gk])
            Sx.act(gcs[:, 3:4], gcs[:, 0:1], AF.Exp, R=[gk], W=[gk])
            Sx.tt("dve", gcs[:, 3:4], gcs[:, 3:4], be, ALU.mult, R=[gk, "bgt"], W=[gk])
            Sx.mm(C_, kT[:, tc], kT[:, tc], R=[k_("kT")], W=[pk])
            Sx.mm(D_, kT[:, tc], qT[:, tc], R=[k_("kT"), k_("qT")], A=[pk])
            Sx.stt("dve", bb["Nm"][:], C_, nbe[:, tt_, h:h + 1], bb["sB"][:], ALU.mult, ALU.mult, R=[pk, "nbe", k_("sB")], W=[k_("Nm")])
            Sx.tt("dve", bb["attnT"][:], D_, bb["sC"][:], ALU.mult, R=[pk, k_("sC")], W=[k_("attnT")])
            Sx.tr(A_, bb["Nm"][:], ident[:], R=[k_("Nm"), "ident"], W=[pk])
            Sx.cp("act", bb["NTm"][:], A_, R=[pk], W=[k_("NTm")])
            Sx.tt("pool", bb["PTm"][:], bb["NTm"][:], ident[:], ALU.add, R=[k_("NTm"), "ident"], W=[k_("PTm")])
            Xc, Xk, XTc, XTk = bb["Nm"], k_("Nm"), bb["NTm"], k_("NTm")
            for m in range(1, nlev + 1):
                X2, X2k = bb["Xb%d" % (m % 2)], k_("Xb%d" % (m % 2))
                XT2, XT2k = bb["XTb%d" % (m % 2)], k_("XTb%d" % (m % 2))
                Sx.mm(B_, XTc[:], Xc[:], R=[Xk, XTk], W=[pk])
                if m < nlev:
                    Sx.mm(C_, Xc[:], XTc[:], R=[Xk, XTk], A=[pk])
                Sx.cp("act", X2[:], B_, R=[pk], W=[X2k])
                if m < nlev:
                    Sx.cp("dve", XT2[:], C_, R=[pk], W=[XT2k])
                Sx.mm(D_, X2[:], bb["PTm"][:], R=[X2k, k_("PTm")], W=[pk])
                Sx.tt("dve", bb["PTm"][:], bb["PTm"][:], D_, ALU.add, R=[k_("PTm"), pk], W=[k_("PTm")])
                Xc, Xk, XTc, XTk = X2, X2k, XT2, XT2k
            Sx.ts("dve", bb["sA"][:], bb["v_tm"][:], be, None, ALU.mult, R=[k_("v_tm"), "bgt"], W=[k_("sA")])
            Sx.ts("pool", bb["sB"][:], bb["k_tm"][:], gcs[:, 3:4], None, ALU.mult, R=[k_("k_tm"), gk], W=[k_("sB")])
            Sx.mm(A_, bb["PTm"][:], bb["sA"][:], R=[k_("PTm"), k_("sA")], W=[pk])
            Sx.mm(B_, bb["sB"][:], bb["PTm"][:], R=[k_("PTm"), k_("sB")], A=[pk])
            Sx.cp("act", bb["u_"][:], A_, R=[pk], W=[k_("u_")])
            Sx.cp("dve", bb["wT"][:], B_, R=[pk], W=[k_("wT")])
            Sx.tt("pool", bb["qgT"][:], qT[:, tc], bb["egc"][:], ALU.mult, R=[k_("qT"), k_("egc")], W=[k_("qgT")])
            Sx.ts("pool", bb["sC"][:], bb["k_tm"][:], gcs[:, 2:3], None, ALU.mult, R=[k_("k_tm"), gk], W=[k_("sC")])
            for b in range(2):
                rows = slice(64 * b, 64 * b + 64)
                sq_ = 2 * (tok0 // 128 - NTT) + b
                if sample:
                    Sx.dma("sp", s32[:], I["sS"][l, sq_, h], W=[s32k])
                Sx.mm(C_, bb["wT"][:], s32[:], R=[k_("wT"), s32k], W=[pk])
                Sx.tt("dve", bb["vnew"][rows, :], bb["u_"][rows, :], pb[rows, 256:384], ALU.subtract, R=[k_("u_"), pk], W=[k_("vnew")])
                oc = slice(384 + 64 * b, 384 + 64 * b + 64)
                Sx.mm(pb[:, oc], s32[:], bb["qgT"][:, rows], start=True, stop=False, R=[s32k, k_("qgT")], W=[pk])
                Sx.mm(pb[:, oc], bb["vnew"][rows, :], bb["attnT"][rows, rows], start=False, stop=True, R=[k_("vnew"), k_("attnT")], A=[pk])
                Sx.cp("act", ob[:, h, tt_ * 128 + 64 * b:tt_ * 128 + 64 * b + 64], pb[:, oc], R=[pk], W=["ob%d" % h])
                Sx.mm(A_, bb["sC"][rows, :], bb["vnew"][rows, :], R=[k_("sC"), k_("vnew")], W=[pk])
                Sx.stt("dve", s32[:], s32[:], bb["egc"][:, 64 * b + 63:64 * b + 64], A_, ALU.mult, ALU.add,
                       R=[s32k, k_("egc"), pk], W=[s32k])
                if sample:
                    Sx.dma("sp", O["Ss"][l, sq_, h], s32[:], R=[s32k])
                elif tok0 + 128 == T and b == 1:
                    Sx.dma("sp", O["Sp"][l, h], s32[:], R=[s32k])

        for (t0, n) in X["blocks"]:
            ntl = n // 128
            sample = (t0 // 128 >= NTT)
            ty = "s" if sample else "p"
            nlev = 2 if sample else 5
            for tt_ in range(ntl):
                tok0 = t0 + tt_ * 128
                for kc in range(KD):
                    S.mm(P[0][:, 0:8], hT[:, kc, tok0:tok0 + 128], wbg[:, kc, :], start=(kc == 0), stop=(kc == KD - 1),
                         R=["wb", "hT"], W=["P0"] if kc == 0 else [], A=["P0"] if kc > 0 else [])
                S.act(bgt[:, tt_, 0:4], P[0][:, 0:4], AF.Sigmoid, R=["P0"], W=["bgt"])
                S.tt("dve", bgt[:, tt_, 4:8], P[0][:, 4:8], dtb[:], ALU.add, R=["P0", "dtb"], W=["bgt"])
                S.act(bgt[:, tt_, 4:8], bgt[:, tt_, 4:8], AF.Exp, R=["bgt"], W=["bgt"])
                S.act(bgt[:, tt_, 4:8], bgt[:, tt_, 4:8], AF.Ln, bias=1.0, R=["bgt"], W=["bgt"])
                S.stt("dve", bgt[:, tt_, 4:8], bgt[:, tt_, 4:8], -1.0, alog[:], ALU.mult, ALU.mult, R=["bgt", "alog"], W=["bgt"])
                S.ts("dve", bgt[:, tt_, 4:8], bgt[:, tt_, 4:8], tab["valid" + ty][:, 0:1], None, ALU.mult, R=["bgt", "tab"], W=["bgt"])
                S.ts("dve", nbe[:, tt_, :], bgt[:, tt_, 0:4], -1.0, None, ALU.mult, R=["bgt"], W=["nbe"])
            for h in range(4):
                wq_t, wqk = wqh[h % 2], "wqh%d" % (h % 2)
                for ri in range(3):
                    S.dma("pool", wq_t[:, :, ri, :], wview(w_in, BQKV + (ri * 4 + h) * 128, 128), W=[wqk])
                for ri, (role, c) in enumerate((("q", h), ("k", 4 + h), ("v", 8 + h))):
                    pp, ppk = (P[0], "P0") if (ri % 2 == 0) else (P[6], "P6")
                    for kc in range(KD):
                        S.mm(pp[:, 0:n], wq_t[:, kc, ri, :], hT[:, kc, t0:t0 + n], start=(kc == 0), stop=(kc == KD - 1),
                             R=[wqk, "hT"], W=[ppk] if kc == 0 else [], A=[ppk] if kc > 0 else [])
                    if not sample:
                        S.cp("pool", pre[:, 0:3], carry[:, c, :], R=["carry"], W=["pre"])
                        S.cp("act", pre[:, 3:3 + n], pp[:, 0:n], R=[ppk], W=["pre"])
                        S.ts("dve", acc[:, 0:n], pre[:, 0:n], cw[:, c, 0:1], None, ALU.mult, R=["pre", "cw"], W=["acc"])
                        for i in range(1, 4):
                            S.stt("dve", acc[:, 0:n], pre[:, i:i + n], cw[:, c, i:i + 1], acc[:, 0:n], ALU.mult, ALU.add,
                                  R=["pre", "cw", "acc"], W=["acc"])
                        if t0 + n == T:
                            S.cp("pool", cpo[:, c, :], pre[:, n:n + 3], R=["pre"], W=["cpo"])
                        S.cp("pool", carry[:, c, :], pre[:, n:n + 3], R=["pre"], W=["carry"])
                        accv = acc[:, 0:n]
                    else:
                        S.cp("pool", pres[:, :, 0:3], csin[:, c, :].rearrange("p (s k) -> p s k", s=4), R=["csin"], W=["pres"])
                        S.cp("act", pres[:, :, 3:67], pp[:, 0:256].rearrange("p (s k) -> p s k", s=4), R=[ppk], W=["pres"])
                        a3 = acc[:, 0:256].rearrange("p (s k) -> p s k", s=4)
                        S.ts("dve", a3, pres[:, :, 0:64], cw[:, c, 0:1], None, ALU.mult, R=["pres", "cw"], W=["acc"])
                        for i in range(1, 4):
                            S.stt("dve", a3, pres[:, :, i:i + 64], cw[:, c, i:i + 1], a3, ALU.mult, ALU.add, R=["pres", "cw", "acc"], W=["acc"])
                        S.cp("pool", cso[:, c, :].rearrange("p (s k) -> p s k", s=4), pres[:, :, 8:11], R=["pres"], W=["cso"])
                        accv = acc[:, 0:n]
                    fk = "%sT_%d" % (role, h)
                    if role == "v":
                        S.act(fT[h]["v"][:, 0:n], accv, AF.Silu, R=["acc"], W=[fk])
                    else:
                        S.act(yv[:, 0:n], accv, AF.Silu, R=["acc"], W=["yv"])
                        S.act(sqb[:, 0:n], yv[:, 0:n], AF.Square, R=["yv"], W=["sqb"])
                        S.mm(P[5][:, 0:n], onesb[:], sqb[:, 0:n], R=["onesb", "sqb"], W=["P5"])
                        X["rstd"](rn[:, 0:n], P[5][:, 0:n], 1.0, ["P5"], ["rn"])
                        sc = 128.0 ** -0.5 if role == "q" else 1.0
                        S.stt("dve", fT[h][role][:, 0:n], yv[:, 0:n], sc, rn[:, 0:n], ALU.mult, ALU.mult, R=["yv", "rn"], W=[fk])
            recs = []
            for h in range(4):
                r = Rec()
                for tt_ in range(ntl):
                    tile_chain(r, h, tt_, t0 + tt_ * 128, sample, ty, nlev)
                recs.append(r)
            replay_interleaved(S, recs)
            for tt_ in range(ntl):
                tok0 = t0 + tt_ * 128
                tc = slice(tt_ * 128, (tt_ + 1) * 128)
                _gated_norm_epilogue(X, ebuf, ob[:, :, tc], ["ob0", "ob1", "ob2", "ob3"], 0, bog[:, 0:1], "bog", tok0, obs, "obs", tc,
                                     P[0], "P0", P[5], "P5", wz, "wb")
            S.dma("sp", oT_d[4 * 128:8 * 128, t0:t0 + n].rearrange("(c p) n -> p c n", p=128), obs[:, :, 0:n], R=["obs"])
        for (src, sk, nrow, dst) in ((cpo, "cpo", 3, O["convp"][l]), (cso, "cso", 12, O["convs"][l])):
            for g4 in range(3):
                pb, pk = P[g4], "P%d" % g4
                for c4 in range(4):
                    c = g4 * 4 + c4
                    S.tr(pb[0:nrow, c4 * 128:(c4 + 1) * 128], src[:, c, 0:nrow], ident[:], R=[sk, "ident"],
                         W=[pk] if c4 == 0 else [], A=[pk] if c4 else [])
                S.cp("dve", cout[0:nrow, g4 * 512:(g4 + 1) * 512], pb[0:nrow, :], R=[pk], W=["cout"])
            S.dma("sp", dst, cout[0:nrow, :], R=["cout"])


def mixer_a(X):
    nc, S, cfg, l, I, O, P, PT, hT = X["nc"], X["S"], X["cfg"], X["l"], X["I"], X["O"], X["P"], X["PT"], X["hT"]
    KD, NTT, NS, T, NTOK = cfg.KD, cfg.NTT, cfg.NS, cfg.T, cfg.NTOK
    w_in, oT_d, ident, identb, onesb, wview = X["w_in"], X["oT_d"], X["ident"], X["identb"], X["onesb"], X["wview"]
    KEEP = (128, 512, min(2048, T))
    DIL = (1, 4, 16)
    sc_ = 128.0 ** -0.5
    with contextlib.ExitStack() as ph:
        sbp = lambda n, s, d=F32: ph.enter_context(nc.sbuf_tensor("a%d_%s" % (l, n), list(s), d))
        rotA = sbp("rotA", [128, 128], BF16); S.dma("pool", rotA[:], I["c_rotA"], W=["rotA"])
        mcur = sbp("mcur", [128, 128], BF16); S.dma("pool", mcur[:], I["c_mcur"][:, 0:128], W=["mask"])
        mprev = sbp("mprev", [128, 128], BF16); S.dma("pool", mprev[:], I["c_mprev"][:, 0:128], W=["mask"])
        msc = sbp("msc", [128, 104], BF16); S.dma("pool", msc[:], I["c_msc"], W=["mask"])
        msn = sbp("msn", [128, 48], BF16); S.dma("pool", msn[:], I["c_msn"], W=["mask"])
        aqg = sbp("aqg", [128, 1]); S.dma("sp", aqg[:], I["aqg"][l], W=["aqg"])
        akg = sbp("akg", [128, 1]); S.dma("sp", akg[:], I["akg"][l], W=["aqg"])
        nshift = sbp("nshift", [128, 1]); S.memset("pool", nshift[:], -SHIFT, W=["nshift"])
        wset = [[sbp("w%s%d" % (nm, i), [128, KD, 128], BF16) for nm in "qkv"] for i in range(2)]
        ct = sbp("ct", [128, 512]); sn = sbp("sn", [128, 512])
        QT = sbp("QT", [128, NTOK], BF16); KT = sbp("KT", [128, NTOK], BF16); K32 = sbp("K32", [128, NTOK])
        V = sbp("V", [128, cfg.NT, 128], BF16)
        Oacc = sbp("Oacc", [128, NTOK]); Dacc = sbp("Dacc", [128, NTOK])
        rn = sbp("rn", [128, 512])
        sqb2 = [sbp("sqb%d" % i, [128, 512], BF16) for i in range(2)]; rn2 = [sbp("rn%d" % i, [128, 512]) for i in range(2)]
        qb2 = [sbp("qb%d" % i, [128, 512], BF16) for i in range(2)]
        t12 = [sbp("t1%d" % i, [128, 512]) for i in range(2)]; t22 = [sbp("t2%d" % i, [128, 512]) for i in range(2)]
        pT2 = [sbp("pT%d" % i, [128, 128], BF16) for i in range(2)]
        v32 = sbp("v32", [128, 128]); k32 = sbp("k32", [128, 128])
        pT = sbp("pT", [128, 128], BF16)
        kvc2 = [[sbp("kvc%d%d" % (i, j), [128, 2, 128]) for j in range(2)] for i in range(2)]
        kcT2 = [sbp("kcT%d" % i, [128, 128], BF16) for i in range(2)]
        vcb2 = [sbp("vcb%d" % i, [128, 128], BF16) for i in range(2)]
        ost = sbp("ost", [128, 512], BF16)
        for h in range(4):
            S.memset("pool", Oacc[:], 0.0, W=["Oacc"])
            S.memset("pool", Dacc[:], 0.0, W=["Dacc"])
            for g in range(3):
                dil, keep = DIL[g], KEEP[g]
                kvp, kvs, kvin = O["kvp%d" % g], O["kvs%d" % g], I["kv%d" % g]
                wi = (h * 3 + g) % 2
                wq, wk, wv = wset[wi]
                wak = "wa%d" % wi
                for t_, i3 in ((wq, 0), (wk, 1), (wv, 2)):
                    S.dma("pool", t_[:], wview(w_in, ((i3 * 3 + g) * 4 + h) * 128, 128), W=[wak])
                for (t0, n) in X["blocks"]:
                    S.dma("sp", ct[:, 0:n], I["c_cosA"][:, t0:t0 + n], W=["ct"])
                    S.dma("sp", sn[:, 0:n], I["c_sinA"][:, t0:t0 + n], W=["sn"])
                    recs = []
                    for si, (which, w_t, gcol) in enumerate((("q", wq, aqg), ("k", wk, akg))):
                        Sx = Rec()
                        Pa, Pb_, Pc = (P[0], P[1], P[2]) if si == 0 else (P[4], P[5], P[6])
                        ka, kb_, kc_ = ("P0", "P1", "P2") if si == 0 else ("P4", "P5", "P6")
                        sq_, rn_, qb_, t1_, t2_ = sqb2[si], rn2[si], qb2[si], t12[si], t22[si]
                        sfx = str(si)
                        for kc in range(KD):
                            Sx.mm(Pa[:, 0:n], w_t[:, kc, :], hT[:, kc, t0:t0 + n], start=(kc == 0), stop=(kc == KD - 1),
                                  R=[wak, "hT"], W=[ka] if kc == 0 else [], A=[ka] if kc > 0 else [])
                        Sx.act(sq_[:, 0:n], Pa[:, 0:n], AF.Square, R=[ka], W=["sqb" + sfx])
                        Sx.mm(Pb_[:, 0:n], onesb[:], sq_[:, 0:n], R=["onesb", "sqb" + sfx], W=[kb_])
                        Sx.act(rn_[:, 0:n], Pb_[:, 0:n], AF.Ln, bias=EPS, scale=1.0 / 128, R=[kb_], W=["rn" + sfx])
                        Sx.act(rn_[:, 0:n], rn_[:, 0:n], AF.Exp, scale=-0.5, R=["rn" + sfx], W=["rn" + sfx])
                        Sx.stt("dve", qb_[:, 0:n], Pa[:, 0:n], gcol[:, 0:1], rn_[:, 0:n], ALU.mult, ALU.mult, R=[ka, "aqg", "rn" + sfx], W=["qb" + sfx])
                        Sx.mm(Pc[:, 0:n], rotA[:], qb_[:, 0:n], R=["rotA", "qb" + sfx], W=[kc_])
                        Sx.tt("dve", t1_[:, 0:n], qb_[:, 0:n], ct[:, 0:n], ALU.mult, R=["qb" + sfx, "ct"], W=["t1" + sfx])
                        Sx.tt("dve", t2_[:, 0:n], Pc[:, 0:n], sn[:, 0:n], ALU.mult, R=[kc_, "sn"], W=["t2" + sfx])
                        if which == "q":
                            Sx.tt("pool", QT[:, t0:t0 + n], t1_[:, 0:n], t2_[:, 0:n], ALU.add, R=["t1" + sfx, "t2" + sfx], W=["QT"])
                        else:
                            Sx.tt("pool", K32[:, t0:t0 + n], t1_[:, 0:n], t2_[:, 0:n], ALU.add, R=["t1" + sfx, "t2" + sfx], W=["K32"])
                            Sx.cp("act", KT[:, t0:t0 + n], K32[:, t0:t0 + n], R=["K32"], W=["KT"])
                        recs.append(Sx)
                    replay_interleaved(S, recs)
                nbk = T // (128 * dil)
                def tcols(r, nb):
                    st0 = dil * 128 * nb + r
                    return slice(st0, st0 + dil * 127 + 1, dil)
                tiles = [(r, nb, tcols(r, nb), r * nbk + nb) for r in range(dil) for nb in range(nbk)]
                tiles += [(None, s3, slice(T + 128 * s3, T + 128 * s3 + 128), NTT + s3) for s3 in range(2)]
                for vi, (r, nb, cols, ti) in enumerate(tiles):
                    pv, pvk = P[(0, 1, 2, 3)[vi % 4]], "P%d" % (vi % 4)
                    for kc in range(KD):
                        S.mm(pv[:, 0:128], hT[:, kc, cols], wv[:, kc, :], start=(kc == 0), stop=(kc == KD - 1),
                             R=[wak, "hT"], W=[pvk] if kc == 0 else [], A=[pvk] if kc > 0 else [])
                    S.cp("act", V[:, ti, :], pv[:, 0:128], R=[pvk], W=["V"])
                    if r is None:
                        S.cp("dve", v32[:], pv[:, 0:128], R=[pvk], W=["v32"])
                        for s2 in range(2):
                            S.dma("sp", kvs[l, 2 * nb + s2, :, 1, h, :], v32[64 * s2:64 * s2 + 8, :], R=["v32"])
                    elif dil * 128 * nb >= T - keep:
                        S.cp("dve", v32[:], pv[:, 0:128], R=[pvk], W=["v32"])
                        r0 = dil * 128 * nb + r - (T - keep)
                        S.dma("sp", kvp[l, r0:r0 + dil * 127 + 1:dil, 1, h, :], v32[:], R=["v32"])
                for ti in list(range((T - keep) // 128, NTT)) + [NTT, NTT + 1]:
                    S.tr(P[3][:, 128:256], K32[:, ti * 128:(ti + 1) * 128], ident[:], R=["K32", "ident"], W=["P3"])
                    S.cp("dve", k32[:], P[3][:, 128:256], R=["P3"], W=["k32"])
                    if ti < NTT:
                        r0 = ti * 128 - (T - keep)
                        S.dma("sp", kvp[l, r0:r0 + 128, 0, h, :], k32[:], R=["k32"])
                    else:
                        for s2 in range(2):
                            S.dma("sp", kvs[l, 2 * (ti - NTT) + s2, :, 0, h, :], k32[64 * s2:64 * s2 + 8, :], R=["k32"])
                recs = [Rec(), Rec()]
                bi = 0
                for r in range(dil):
                    for nb in range(nbk):
                        si = bi % 2
                        bi += 1
                        Sx = recs[si]
                        Ps, Po, Pd = (P[0], P[1], P[2]) if si == 0 else (P[4], P[5], P[6])
                        ks, ko, kd = ("P0", "P1", "P2") if si == 0 else ("P4", "P5", "P6")
                        pT_, pTk = pT2[si], "pT%d" % si
                        qc = tcols(r, nb)
                        kbs = [kb for kb in (nb - 1, nb) if kb >= 0]
                        for i, kb in enumerate(kbs):
                            Sx.mm(Ps[:, 0:128], KT[:, tcols(r, kb)], QT[:, qc], R=["KT", "QT"], W=[ks])
                            Sx.act(pT_[:], Ps[:, 0:128], AF.Exp, bias=nshift[:, 0:1], scale=sc_, R=[ks, "nshift"], W=[pTk])
                            Sx.tt("pool", pT_[:], pT_[:], (mcur if kb == nb else mprev)[:], ALU.mult, R=[pTk, "mask"], W=[pTk])
                            Sx.mm(Po[:, 0:128], V[:, r * nbk + kb, :], pT_[:], start=(i == 0), stop=(i == len(kbs) - 1),
                                  R=["V", pTk], W=[ko] if i == 0 else [], A=[ko] if i > 0 else [])
                            Sx.mm(Pd[:, 0:128], onesb[:], pT_[:], start=(i == 0), stop=(i == len(kbs) - 1),
                                  R=["onesb", pTk], W=[kd] if i == 0 else [], A=[kd] if i > 0 else [])
                        Sx.tt("dve", Oacc[:, qc], Oacc[:, qc], Po[:, 0:128], ALU.add, R=["Oacc", ko], W=["Oacc"])
                        Sx.tt("dve", Dacc[:, qc], Dacc[:, qc], Pd[:, 0:128], ALU.add, R=["Dacc", kd], W=["Dacc"])
                replay_interleaved(S, recs)
                idx0 = (0, 1, 5)[g]
                recs = [Rec(), Rec()]
                for s in range(NS):
                    si = s % 2
                    Sx = recs[si]
                    Pt, Po, Pd = (P[0], P[1], P[2]) if si == 0 else (P[4], P[5], P[6])
                    kt, ko, kd = ("P0", "P1", "P2") if si == 0 else ("P4", "P5", "P6")
                    pT_, pTk = pT2[si], "pT%d" % si
                    kcT_, kcTk = kcT2[si], "kcT%d" % si
                    vcb_, vcbk = vcb2[si], "vcb%d" % si
                    qs = slice(T + 64 * s, T + 64 * s + 8)
                    tl = NTT + s // 2
                    nr = min(dil, 8)
                    for r in range(nr + 1):
                        if r < nr:
                            kv_, kvk = kvc2[si][r % 2], "kvc%d%d" % (si, r % 2)
                            Sx.dma("sp", kv_[:], kvin[l, s, r:r + dil * 127 + 1:dil, :].rearrange("p (a hh d) -> p a hh d", a=2, hh=4)[:, :, h, :], W=[kvk])
                            Sx.tr(Pt[:, 0:128], kv_[:, 0, :], ident[:], R=[kvk, "ident"], W=[kt])
                            Sx.cp("act", kcT_[:], Pt[:, 0:128], R=[kt], W=[kcTk])
                            Sx.cp("dve", vcb_[:], kv_[:, 1, :], R=[kvk], W=[vcbk])
                            Sx.mm(Pt[:, 128:136], kcT_[:], QT[:, qs], R=[kcTk, "QT"], W=[kt])
                            mk = msc[:, (idx0 + r) * 8:(idx0 + r) * 8 + 8]
                            vv = vcb_[:]
                        else:
                            Sx.mm(Pt[:, 128:136], KT[:, tl * 128:(tl + 1) * 128], QT[:, qs], R=["KT", "QT"], W=[kt])
                            mk = msn[:, (g * 2 + s % 2) * 8:(g * 2 + s % 2) * 8 + 8]
                            vv = V[:, tl, :]
                        Sx.act(pT_[:, 0:8], Pt[:, 128:136], AF.Exp, bias=nshift[:, 0:1], scale=sc_, R=[kt, "nshift"], W=[pTk])
                        Sx.tt("pool", pT_[:, 0:8], pT_[:, 0:8], mk, ALU.mult, R=[pTk, "mask"], W=[pTk])
                        Sx.mm(Po[:, 0:8], vv, pT_[:, 0:8], start=(r == 0), stop=(r == nr), R=[vcbk, "V", pTk],
                              W=[ko] if r == 0 else [], A=[ko] if r > 0 else [])
                        Sx.mm(Pd[:, 0:8], onesb[:], pT_[:, 0:8], start=(r == 0), stop=(r == nr), R=["onesb", pTk],
                              W=[kd] if r == 0 else [], A=[kd] if r > 0 else [])
                    Sx.tt("dve", Oacc[:, qs], Oacc[:, qs], Po[:, 0:8], ALU.add, R=["Oacc", ko], W=["Oacc"])
                    Sx.tt("dve", Dacc[:, qs], Dacc[:, qs], Pd[:, 0:8], ALU.add, R=["Dacc", kd], W=["Dacc"])
                replay_interleaved(S, recs)
            for (t0, n) in X["blocks"]:
                S.ts("dve", rn[:, 0:n], Dacc[:, t0:t0 + n], 1e-30, None, ALU.max, R=["Dacc"], W=["rn"])
                S.op("dve", (lambda a, b: (lambda e: e.reciprocal(a, b)))(rn[:, 0:n], rn[:, 0:n]), ["rn"], ["rn"])
                S.tt("dve", ost[:, 0:n], Oacc[:, t0:t0 + n], rn[:, 0:n], ALU.mult, R=["Oacc", "rn"], W=["ost"])
                S.dma("sp", oT_d[h * 128:(h + 1) * 128, t0:t0 + n], ost[:, 0:n], R=["ost"])
```

```python
import contextlib
import numpy as np
import concourse.bass as bass
import concourse.mybir as mybir
from concourse.bass_utils import run_bass_kernel_spmd

F32 = mybir.dt.float32
BF16 = mybir.dt.bfloat16
AF = mybir.ActivationFunctionType
ALU = mybir.AluOpType

PAST_LEN = 8192
EPS = 1e-6
A_OFF, BQKV, BZ, BBETA, BA, CQ, CK, CV, CZ, GOFF = 0, 4608, 6144, 6656, 6660, 6664, 6920, 7176, 7688, 8200
NEG = -30000.0
SHIFT = 12.0


class Cfg:
    def __init__(self, D=1024, DFF=2816, T=4096, NS=4, DEPTH=2, dbg=False, mixers="abc"):
        self.D, self.DFF, self.T, self.NS, self.DEPTH, self.dbg = D, DFF, T, NS, DEPTH, dbg
        self.mixers = mixers
        self.KD = D // 128
        self.KF = DFF // 128
        self.NTT = T // 128
        self.NT = self.NTT + 2
        self.NTOK = self.NT * 128
        self.NIN = GOFF + 3 * D


class Sched:
    ENGS = ("pe", "act", "dve", "pool", "sp")
    DMAQ = ("sp", "act", "pool")

    def __init__(self, nc, stack, nslots=8):
        self.nc = nc
        self.prog = {e: [] for e in self.ENGS}
        self.cnt = {e: 0 for e in self.ENGS}
        self.sem = {e: stack.enter_context(nc.semaphore("s_" + e)) for e in self.ENGS}
        self.nslots = nslots
        self.dsem = {q: [stack.enter_context(nc.semaphore("d_%s%d" % (q, i))) for i in range(nslots)] for q in self.DMAQ}
        self.dval = {q: [0] * nslots for q in self.DMAQ}
        self.dnext = {q: 0 for q in self.DMAQ}
        self.seen = {e: {} for e in self.ENGS}
        self.lastw = {}
        self.readers = {}

    def _semof(self, ev):
        if ev[0] == "E":
            return ("E", ev[1]), self.sem[ev[1]], ev[2]
        return ("D", ev[1], ev[2]), self.dsem[ev[1]][ev[2]], ev[3]

    def _emit_waits(self, e, waits):
        best = {}
        for ev in waits:
            key, sem, val = self._semof(ev)
            if self.seen[e].get(key, 0) >= val:
                continue
            if key not in best or best[key][1] < val:
                best[key] = (sem, val)
        for key, (sem, val) in best.items():
            self.seen[e][key] = val
            self.prog[e].append(("w", sem, val))

    def _deps(self, R, W):
        waits = []
        for k in R:
            w = self.lastw.get(k)
            if w is not None:
                waits.append(w)
            if k in PSUM_KEYS:
                rd = self.readers.get(k)
                if rd:
                    waits.extend(rd.values())
        for k in W:
            w = self.lastw.get(k)
            if w is not None:
                waits.append(w)
            rd = self.readers.get(k)
            if rd:
                waits.extend(rd.values())
        return waits

    def _record(self, ev, R, W):
        key = self._semof(ev)[0]
        for k in R:
            self.readers.setdefault(k, {})[key] = ev
        for k in W:
            self.lastw[k] = ev
            self.readers[k] = {}

    def op(self, e, fn, R=(), W=(), A=()):
        NOPS[0] += 1
        if NOPS[0] > LIMIT:
            return None
        self._emit_waits(e, self._deps(R, W))
        self.cnt[e] += 1
        self.prog[e].append(("i", fn, self.sem[e]))
        ev = ("E", e, self.cnt[e])
        self._record(ev, R, list(W) + list(A))
        return ev

    def dma(self, q, out, in_, R=(), W=(), **kw):
        NOPS[0] += 1
        if NOPS[0] > LIMIT:
            return None
        waits = self._deps(R, W)
        s = self.dnext[q]
        self.dnext[q] = (s + 1) % self.nslots
        if self.dval[q][s] > 0:
            waits.append(("D", q, s, self.dval[q][s]))
        self._emit_waits(q, waits)
        self.dval[q][s] += 16
        self.prog[q].append(("d", out, in_, kw, self.dsem[q][s]))
        ev = ("D", q, s, self.dval[q][s])
        self._record(ev, R, W)
        return ev

    def barrier(self):
        evs = [("E", e, self.cnt[e]) for e in self.ENGS if self.cnt[e] > 0]
        for q in self.DMAQ:
            for s in range(self.nslots):
                if self.dval[q][s] > 0:
                    evs.append(("D", q, s, self.dval[q][s]))
        for e in self.ENGS:
            self._emit_waits(e, evs)
        self.lastw.clear()
        self.readers.clear()

    def emit(self):
        nc = self.nc
        for q in self.DMAQ:
            self._emit_waits(q, [("D", q, s, self.dval[q][s]) for s in range(self.nslots) if self.dval[q][s] > 0])
        self._emit_waits("sp", [("E", e, self.cnt[e]) for e in self.ENGS if e != "sp" and self.cnt[e] > 0])

        def run(e):
            def body(eng):
                for it in self.prog[e]:
                    if it[0] == "w":
                        eng.wait_ge(it[1], it[2])
                    elif it[0] == "i":
                        it[1](eng).then_inc(it[2], 1)
                    else:
                        eng.dma_start(out=it[1], in_=it[2], **it[3]).then_inc(it[4], 16)
            return body

        with nc.Block() as block:
            block.tensor(run("pe"))
            block.scalar(run("act"))
            block.vector(run("dve"))
            block.gpsimd(run("pool"))
            block.sync(run("sp"))

    def mm(self, out, lhsT, rhs, start=True, stop=True, R=(), W=(), A=()):
        return self.op("pe", lambda e: e.matmul(out, lhsT, rhs, start=start, stop=stop), R, W, A)

    def tr(self, out, in_, ident, R=(), W=(), A=()):
        return self.op("pe", lambda e: e.transpose(out, in_, ident), R, W, A)

    def act(self, out, in_, func, bias=0.0, scale=1.0, accum_out=None, R=(), W=()):
        if accum_out is None:
            return self.op("act", lambda e: e.activation(out, in_, func, bias=bias, scale=scale), R, W)
        return self.op("act", lambda e: e.activation(out, in_, func, bias=bias, scale=scale, accum_out=accum_out), R, W)

    def tt(self, eng, out, in0, in1, op, R=(), W=()):
        return self.op(eng, lambda e: e.tensor_tensor(out, in0, in1, op), R, W)

    def ts(self, eng, out, in0, s1, s2, op0, op1=None, R=(), W=()):
        if op1 is None:
            return self.op(eng, lambda e: e.tensor_scalar(out, in0, s1, None, op0), R, W)
        return self.op(eng, lambda e: e.tensor_scalar(out, in0, s1, s2, op0, op1), R, W)

    def stt(self, eng, out, in0, scalar, in1, op0, op1, R=(), W=()):
        return self.op(eng, lambda e: e.scalar_tensor_tensor(out, in0, scalar, in1, op0, op1), R, W)

    def cp(self, eng, out, in_, R=(), W=()):
        if eng == "act":
            return self.op("act", lambda e: e.copy(out, in_), R, W)
        return self.op(eng, lambda e: e.tensor_copy(out, in_), R, W)

    def memset(self, eng, ap, val, W=()):
        return self.op(eng, lambda e: e.memset(ap, val), (), W)


class Rec:
    def __init__(self):
        self.calls = []

    def __getattr__(self, name):
        def f(*a, **k):
            self.calls.append((name, a, k))
        return f


def replay_interleaved(S, recs):
    n = max(len(r.calls) for r in recs)
    for i in range(n):
        for r in recs:
            if i < len(r.calls):
                name, a, k = r.calls[i]
                getattr(S, name)(*a, **k)


def _tile_consts(bs, nvalid, gam):
    p = np.arange(128)
    blk, loc = p // bs, p % bs
    same = blk[:, None] == blk[None, :]
    val = loc < nvalid
    c = {}
    c["tri"] = (same & (p[:, None] <= p[None, :])).astype(np.float32)
    c["blk"] = same.astype(np.float32)
    c["negS"] = np.where(same & (p[:, None] > p[None, :]), 0.0, NEG).astype(np.float32)
    c["negIT"] = np.where(same & (p[None, :] >= p[:, None]), 0.0, NEG).astype(np.float32)
    c["valid"] = val.astype(np.float32)[:, None].copy()
    dtc = np.zeros((128, 4, 128), np.float64)
    for h in range(4):
        d = (p[None, :] - p[:, None]).astype(np.float64)
        dtc[:, h, :] = np.where(same & (d >= 0) & val[:, None] & val[None, :], gam[h] ** np.maximum(d, 0), 0.0)
    c["dtc"] = dtc.reshape(128, 512).astype(np.float32)
    qd = np.zeros((128, 2, 128), np.float64)
    gcol = np.zeros((128, 2), np.float64)
    for pr in range(2):
        for half in range(2):
            h = 2 * pr + half
            qd[64 * half:64 * half + 64, pr, :] = (gam[h] ** (loc + 1.0))[None, :]
            gcol[64 * half:64 * half + 64, pr] = gam[h] ** nvalid
    c["qdec"] = qd.reshape(128, 256).astype(np.float32)
    c["gC"] = gcol.astype(np.float32)
    kd = np.zeros((128, 4, 64), np.float64)
    for h in range(4):
        kd[:, h, :] = np.where(val, gam[h] ** np.maximum(nvalid - 1.0 - loc, 0), 0.0)[:, None]
    c["kdec"] = kd.reshape(128, 256).astype(np.float32)
    return c


def make_consts(cfg):
    T, NTOK = cfg.T, cfg.NTOK
    pos = np.zeros(NTOK, np.float32)
    pos[:T] = np.arange(T)
    for s in range(4):
        pos[T + 64 * s:T + 64 * s + 64] = PAST_LEN + np.arange(64)
    C = {}
    C["ident"] = np.eye(128, dtype=np.float32)
    C["ones"] = np.ones((128, 128), np.float32)
    d = np.arange(128)
    invA = (np.float32(10000.0) ** (-np.arange(0, 128, 2, dtype=np.float32) / np.float32(128))).astype(np.float32)
    angA = (pos[None, :] * invA[d % 64][:, None]).astype(np.float32).astype(np.float64)
    C["cosA"] = np.cos(angA).astype(np.float32)
    C["sinA"] = (np.sin(angA) * np.where(d < 64, -1.0, 1.0)[:, None]).astype(np.float32)
    rotA = np.zeros((128, 128), np.float32)
    rotA[(d + 64) % 128, d] = 1.0
    C["rotA"] = rotA
    invC = (np.float32(10000.0) ** (-np.arange(0, 64, 2, dtype=np.float32) / np.float32(64))).astype(np.float32)
    dd = d % 64
    angC = (pos[None, :] * invC[dd % 32][:, None]).astype(np.float32).astype(np.float64)
    C["cosC"] = np.cos(angC).astype(np.float32)
    C["sinC"] = (np.sin(angC) * np.where(dd < 32, -1.0, 1.0)[:, None]).astype(np.float32)
    rotC = np.zeros((128, 128), np.float32)
    rotC[(d // 64) * 64 + (dd + 32) % 64, d] = 1.0
    C["rotC"] = rotC
    k = np.arange(128)
    C["mcur"] = np.tile((k[:, None] <= k[None, :]).astype(np.float32), (1, 4))
    C["mprev"] = np.tile((k[:, None] >= k[None, :]).astype(np.float32), (1, 4))
    msc = np.zeros((13, 128, 8), np.float32)
    msn = np.zeros((3, 2, 128, 8), np.float32)
    idx = 0
    for g, dil in enumerate((1, 4, 16)):
        for r in range(min(dil, 8)):
            for i in range(8):
                if i % dil == r:
                    msc[idx, :, i] = (k >= i // dil)
            idx += 1
        for s2 in range(2):
            for i in range(8):
                for j in range(8):
                    if j <= i and (i - j) % dil == 0:
                        msn[g, s2, 64 * s2 + j, i] = 1.0
    C["msc"] = np.ascontiguousarray(np.repeat(msc.transpose(1, 0, 2)[:, :, None, :], 4, axis=2)).reshape(128, 13 * 32)
    C["msn"] = np.ascontiguousarray(np.repeat(msn.transpose(2, 0, 1, 3).reshape(128, 6, 1, 8), 4, axis=2)).reshape(128, 6 * 32)
    gam = [1.0 - 2.0 ** (-5.0 - h) for h in range(4)]
    for nm, (bs, nv) in (("p", (64, 64)), ("s", (64, 8))):
        for kk, v in _tile_consts(bs, nv, gam).items():
            C[kk + "_" + nm] = v
    return C


CUT = 99
LIMIT = 10 ** 9
PADOPS = 0
PSUM_KEYS = frozenset(["P%d" % i for i in range(7)] + ["PT"])
VAR = 0
NOPS = [0]


def build(cfg):
    D, DFF, T, NS, DEPTH = cfg.D, cfg.DFF, cfg.T, cfg.NS, cfg.DEPTH
    KD, KF, NTT, NT, NTOK, NIN = cfg.KD, cfg.KF, cfg.NTT, cfg.NT, cfg.NTOK, cfg.NIN
    assert NS == 4 and T % 2048 == 0
    nc = bass.Bass("TRN2", target_bir_lowering=False)
    consts = make_consts(cfg)

    def din(name, shape, dt=F32):
        return nc.dram_tensor(name, list(shape), dt, kind="ExternalInput").ap()

    def dout(name, shape, dt=F32):
        return nc.dram_tensor(name, list(shape), dt, kind="ExternalOutput").ap()

    def dscr(name, shape, dt):
        return nc.dram_tensor(name, list(shape), dt, kind="ExternalOutput" if cfg.dbg else "Internal").ap()

    I = {}
    I["xp"] = din("xp", [T, D])
    I["xs"] = din("xs", [NS * 8, D])
    I["kv0"] = din("kv0", [DEPTH, NS, 128, 1024])
    I["kv1"] = din("kv1", [DEPTH, NS, 512, 1024])
    I["kv2"] = din("kv2", [DEPTH, NS, 2048, 1024])
    I["sconv"] = din("sconv", [DEPTH, NS * 3, 1536])
    I["sS"] = din("sS", [DEPTH, NS, 4, 128, 128])
    I["sR"] = din("sR", [DEPTH, NS, 2, 128, 128])
    I["g1B"] = din("g1B", [DEPTH, 128, D])
    I["g2B"] = din("g2B", [DEPTH, 128, D])
    I["w_in"] = din("w_in", [DEPTH, D, NIN])
    I["aqg"] = din("aqg", [DEPTH, 128, 1])
    I["akg"] = din("akg", [DEPTH, 128, 1])
    I["convw"] = din("convw", [DEPTH, 128, 12, 4])
    I["alogB"] = din("alogB", [DEPTH, 128, 4])
    I["dtbB"] = din("dtbB", [DEPTH, 128, 4])
    I["bog"] = din("bog", [DEPTH, 128, 1])
    I["cog"] = din("cog", [DEPTH, 128, 1])
    I["w_oa"] = din("w_oa", [DEPTH, 512, D])
    I["w_ob"] = din("w_ob", [DEPTH, 512, D])
    I["w_oc"] = din("w_oc", [DEPTH, 512, D])
    I["w_out"] = din("w_out", [DEPTH, D, D])
    I["w_fi"] = din("w_fi", [DEPTH, D, 2 * DFF])
    I["w_fo"] = din("w_fo", [DEPTH, DFF, D])
    for k, v in consts.items():
        I["c_" + k] = din("c_" + k, v.shape)

    O = {}
    O["yp"] = dout("yp", [T, D])
    O["ys"] = dout("ys", [NS * 8, D])
    KEEP = (128, 512, min(2048, T))
    for g in range(3):
        O["kvp%d" % g] = dout("kvp%d" % g, [DEPTH, KEEP[g], 2, 4, 128])
        O["kvs%d" % g] = dout("kvs%d" % g, [DEPTH, NS, 8, 2, 4, 128])
    O["convp"] = dout("convp", [DEPTH, 3, 1536])
    O["Sp"] = dout("Sp", [DEPTH, 4, 128, 128])
    O["Rp"] = dout("Rp", [DEPTH, 2, 128, 128])
    O["convs"] = dout("convs", [DEPTH, NS * 3, 1536])
    O["Ss"] = dout("Ss", [DEPTH, NS, 4, 128, 128])
    O["Rs"] = dout("Rs", [DEPTH, NS, 2, 128, 128])
    xres = dscr("xres", [NTOK, D], F32)
    oT_d = dscr("oT_d", [12 * 128, NTOK], BF16)
    sg_d = dscr("sg_d", [3 * D, NTOK], BF16)
    WB = {}
    for nm, shp in (("w_oa", [512, D]), ("w_ob", [512, D]), ("w_oc", [512, D]), ("w_out", [D, D]), ("w_fi", [D, 2 * DFF]), ("w_fo", [DFF, D])):
        WB[nm] = nc.dram_tensor("wbf_" + nm, [DEPTH] + shp, BF16, kind="Internal").ap()

    with contextlib.ExitStack() as st:
        S = Sched(nc, st)
        sb = lambda n, s, d=F32: st.enter_context(nc.sbuf_tensor(n, list(s), d))
        P = [st.enter_context(nc.psum_tensor("p%d" % i, [128, 512], F32)) for i in range(7)]
        PT = st.enter_context(nc.psum_tensor("pT", [128, 1024], BF16))

        ident = sb("ident", [128, 128])
        identb = sb("identb", [128, 128], BF16)
        ones = sb("ones", [128, 128])
        onesb = sb("onesb", [128, 128], BF16)
        S.dma("sp", ident[:], I["c_ident"], W=["ident"])
        S.dma("pool", identb[:], I["c_ident"], W=["identb"])
        S.dma("sp", ones[:], I["c_ones"], W=["ones"])
        S.dma("pool", onesb[:], I["c_ones"], W=["onesb"])
        for l_ in range(DEPTH):
            for nm in ("w_oa", "w_ob", "w_oc", "w_out", "w_fi", "w_fo"):
                rows = WB[nm].shape[1]
                for r0 in range(0, rows, 128):
                    S.dma("pool", WB[nm][l_, r0:r0 + 128, :], I[nm][l_, r0:r0 + 128, :], W=["wbf_%s_%d_%d" % (nm, l_, r0)])
        NBLK = (NTOK + 511) // 512
        blocks = [(b * 512, min(512, NTOK - b * 512)) for b in range(NBLK)]

        def wview(ap2d, c0, ncols):
            return ap2d.rearrange("(kc p) n -> p kc n", p=128)[:, :, c0:c0 + ncols]

        def rstd_from_ss(dst, src, scale, R, W):
            S.act(dst, src, AF.Ln, bias=EPS, scale=scale, R=R, W=W)
            S.act(dst, dst, AF.Exp, scale=-0.5, R=W, W=W)

        def norm_tile(x_t, xkey, gB, h_t, hkey, junk, ss, hTdst, tcols, hTkey):
            S.memset("dve", ss[:, 0:1], 0.0, W=["ss"])
            S.act(junk[:], x_t, AF.Square, accum_out=ss[:, 0:1], R=[xkey, "ss"], W=["junk", "ss"])
            rstd_from_ss(ss[:, 0:1], ss[:, 0:1], 1.0 / D, ["ss"], ["ss"])
            S.stt("dve", h_t, x_t, ss[:, 0:1], gB, ALU.mult, ALU.mult, R=[xkey, "ss", "gB"], W=[hkey])
            for kc in range(KD):
                S.tr(PT[:, kc * 128:(kc + 1) * 128], h_t[:, kc * 128:(kc + 1) * 128], identb[:],
                     R=[hkey, "identb"], W=["PT"] if kc == 0 else [], A=["PT"] if kc > 0 else [])
            S.cp("dve", hTdst[:, :, tcols], PT[:, 0:KD * 128].rearrange("p (k n) -> p k n", k=KD), R=["PT"], W=[hTkey])

        for l in range(DEPTH):
            w_in = I["w_in"][l]
            lay = contextlib.ExitStack()
            hT = lay.enter_context(nc.sbuf_tensor("hT%d" % l, [128, KD, NTOK], BF16))
            with contextlib.ExitStack() as ph:
                sbp = lambda n, s, d=F32: ph.enter_context(nc.sbuf_tensor("a%d_%s" % (l, n), list(s), d))
                gB = sbp("gB", [128, D])
                S.dma("sp", gB[:], I["g1B"][l], W=["gB"])
                xt = [sbp("xt%d" % i, [128, D]) for i in range(2)]
                ht = [sbp("ht%d" % i, [128, D], BF16) for i in range(2)]
                junk = sbp("junk", [128, D])
                ss = sbp("ss", [128, 1])
                for i in range(NT):
                    x_t, h_t = xt[i % 2], ht[i % 2]
                    xk, hk = "xt%d" % (i % 2), "ht%d" % (i % 2)
                    if l == 0:
                        if i < NTT:
                            S.dma("sp", x_t[:], I["xp"][i * 128:(i + 1) * 128, :], W=[xk])
                        else:
                            S.memset("dve", x_t[:], 0.0, W=[xk])
                            for s2 in range(2):
                                s = 2 * (i - NTT) + s2
                                S.dma("sp", x_t[64 * s2:64 * s2 + 8, :], I["xs"][8 * s:8 * s + 8, :], W=[xk])
                    else:
                        S.dma("sp", x_t[:], xres[i * 128:(i + 1) * 128, :], W=[xk])
                    norm_tile(x_t[:], xk, gB[:], h_t[:], hk, junk, ss, hT, slice(i * 128, (i + 1) * 128), "hT")
            S.barrier()
            with contextlib.ExitStack() as ph:
                sbp = lambda n, s, d=F32: ph.enter_context(nc.sbuf_tensor("g%d_%s" % (l, n), list(s), d))
                wg = [sbp("wg%d" % i, [128, KD, 512], BF16) for i in range(2)]
                sgo = [sbp("sgo%d" % i, [128, 4, 512], BF16) for i in range(2)]
                ncb = 3 * D // 512
                it = 0
                for cb in range(ncb):
                    w_t, wk = wg[cb % 2], "wg%d" % (cb % 2)
                    S.dma("pool", w_t[:], wview(w_in, GOFF + cb * 512, 512), W=[wk])
                    for (t0, n) in blocks:
                        so, sk = sgo[it % 2], "sgo%d" % (it % 2)
                        for j in range(4):
                            pp, pk = P[(it * 4 + j) % 4], "P%d" % ((it * 4 + j) % 4)
                            for kc in range(KD):
                                S.mm(pp[:, 0:n], w_t[:, kc, j * 128:(j + 1) * 128], hT[:, kc, t0:t0 + n],
                                     start=(kc == 0), stop=(kc == KD - 1), R=[wk, "hT"],
                                     W=[pk] if kc == 0 else [], A=[pk] if kc > 0 else [])
                            S.act(so[:, j, 0:n], pp[:, 0:n], AF.Sigmoid, R=[pk], W=[sk])
                        S.dma("sp", sg_d[cb * 512:(cb + 1) * 512, t0:t0 + n].rearrange("(j p) n -> p j n", p=128),
                              so[:, :, 0:n], R=[sk])
                        it += 1
            S.barrier()
            X = dict(nc=nc, S=S, cfg=cfg, l=l, I=I, O=O, P=P, PT=PT, hT=hT, oT_d=oT_d, ident=ident, identb=identb,
                     ones=ones, onesb=onesb, blocks=blocks, wview=wview, rstd=rstd_from_ss, w_in=w_in)
            for mi, ch in enumerate("abc"):
                if ch not in cfg.mixers:
                    zt = lay.enter_context(nc.sbuf_tensor("zt%d_%d" % (l, mi), [128, 4, NTOK], BF16))
                    S.memset("dve", zt[:], 0.0, W=["zt"])
                    S.dma("sp", oT_d[mi * 512:(mi + 1) * 512, :].rearrange("(c p) n -> p c n", p=128), zt[:], R=["zt"])
            S.barrier()
            if "c" in cfg.mixers:
                mixer_c(X)
                S.barrier()
            if "b" in cfg.mixers:
                mixer_b(X)
                S.barrier()
            if "a" in cfg.mixers:
                mixer_a(X)
                S.barrier()
            lay.close()
            with contextlib.ExitStack() as ph:
                sbp = lambda n, s, d=F32: ph.enter_context(nc.sbuf_tensor("d%d_%s" % (l, n), list(s), d))
                g2B = sbp("g2B", [128, D])
                S.dma("sp", g2B[:], I["g2B"][l], W=["gB"])
                oTb = sbp("oTb", [128, 12, 512], BF16)
                sgb = sbp("sgb", [128, 3 * KD, 512], BF16)
                mT = sbp("mT", [128, KD, 512], BF16)
                h2T = sbp("h2T", [128, KD, 512], BF16)
                aT = sbp("aT", [128, KF, 512], BF16)
                x1 = [sbp("x1_%d" % i, [128, D]) for i in range(4)]
                h2 = sbp("h2", [128, D], BF16)
                junk = sbp("junk", [128, D])
                ss = sbp("ss", [128, 1])
                tmp = [sbp("tmp%d" % i, [128, 512]) for i in range(3)]
                NW = 3
                wb = [sbp("wb%d" % i, [128, 8, 512], BF16) for i in range(NW)]
                wfo = sbp("wfo", [128, KF, 512], BF16)
                wctr = [0]

                def loadw(src2d, c0, ncols, k0, nk):
                    i = wctr[0] % NW
                    wctr[0] += 1
                    v = src2d.rearrange("(kc p) n -> p kc n", p=128)[:, k0:k0 + nk, c0:c0 + ncols]
                    S.dma("sp", wb[i][:, 0:nk, 0:ncols], v, W=["wb%d" % i])
                    return wb[i], "wb%d" % i

                pctr = [0]

                def nextp():
                    i = pctr[0] % 6
                    pctr[0] += 1
                    return P[i], "P%d" % i

                for (t0, n) in blocks:
                    ntl = n // 128
                    S.dma("sp", oTb[:, :, 0:n], oT_d[:, t0:t0 + n].rearrange("(c p) n -> p c n", p=128), W=["oTb"])
                    S.dma("sp", sgb[:, :, 0:n], sg_d[:, t0:t0 + n].rearrange("(c p) n -> p c n", p=128), W=["sgb"])
                    for cb in range(D // 512):
                        ws = []
                        for j, nm in enumerate(("w_oa", "w_ob", "w_oc")):
                            ws.append(loadw(WB[nm][l], cb * 512, 512, 0, 4))
                        for oc4 in range(4):
                            oc = cb * 4 + oc4
                            for j in range(3):
                                w_t, wk = ws[j]
                                pp, pk = nextp()
                                for kc in range(4):
                                    S.mm(pp[:, 0:n], w_t[:, kc, oc4 * 128:(oc4 + 1) * 128], oTb[:, 4 * j + kc, 0:n],
                                         start=(kc == 0), stop=(kc == 3), R=[wk, "oTb"],
                                         W=[pk] if kc == 0 else [], A=[pk] if kc > 0 else [])
                                S.tt("dve", tmp[j][:, 0:n], pp[:, 0:n], sgb[:, j * KD + oc, 0:n], ALU.mult,
                                     R=[pk, "sgb"], W=["tmp%d" % j])
                            S.tt("pool", tmp[0][:, 0:n], tmp[0][:, 0:n], tmp[1][:, 0:n], ALU.add, R=["tmp0", "tmp1"], W=["tmp0"])
                            S.tt("pool", mT[:, oc, 0:n], tmp[0][:, 0:n], tmp[2][:, 0:n], ALU.add, R=["tmp0", "tmp2"], W=["mT"])
                    for tt_ in range(ntl):
                        tok0 = t0 + tt_ * 128
                        ti = tok0 // 128
                        xk = "x1_%d" % tt_
                        x_t = x1[tt_]
                        if l == 0:
                            if ti < NTT:
                                S.dma("sp", x_t[:], I["xp"][tok0:tok0 + 128, :], W=[xk])
                            else:
                                S.memset("dve", x_t[:], 0.0, W=[xk])
                                for s2 in range(2):
                                    s = 2 * (ti - NTT) + s2
                                    S.dma("sp", x_t[64 * s2:64 * s2 + 8, :], I["xs"][8 * s:8 * s + 8, :], W=[xk])
                        else:
                            S.dma("sp", x_t[:], xres[tok0:tok0 + 128, :], W=[xk])
                    for cb in range(D // 512):
                        w_t, wk = loadw(WB["w_out"][l], cb * 512, 512, 0, KD)
                        for tt_ in range(ntl):
                            pp, pk = nextp()
                            xk = "x1_%d" % tt_
                            for kc in range(KD):
                                S.mm(pp[:], mT[:, kc, tt_ * 128:(tt_ + 1) * 128], w_t[:, kc, :], start=(kc == 0), stop=(kc == KD - 1),
                                     R=[wk, "mT"], W=[pk] if kc == 0 else [], A=[pk] if kc > 0 else [])
                            S.tt("dve", x1[tt_][:, cb * 512:(cb + 1) * 512], x1[tt_][:, cb * 512:(cb + 1) * 512], pp[:], ALU.add,
                                 R=[pk, xk], W=[xk])
                    for tt_ in range(ntl):
                        norm_tile(x1[tt_][:], "x1_%d" % tt_, g2B[:], h2[:], "h2", junk, ss, h2T,
                                  slice(tt_ * 128, (tt_ + 1) * 128), "h2T")
                    for fb in range((DFF + 511) // 512):
                        f0 = fb * 512
                        fn_ = min(512, DFF - f0)
                        wg_t, wgk = loadw(WB["w_fi"][l], f0, fn_, 0, KD)
                        wu_t, wuk = loadw(WB["w_fi"][l], DFF + f0, fn_, 0, KD)
                        for fc in range(fn_ // 128):
                            pg, pgk = nextp()
                            pu, puk = nextp()
                            for kc in range(KD):
                                S.mm(pg[:, 0:n], wg_t[:, kc, fc * 128:(fc + 1) * 128], h2T[:, kc, 0:n], start=(kc == 0), stop=(kc == KD - 1),
                                     R=[wgk, "h2T"], W=[pgk] if kc == 0 else [], A=[pgk] if kc > 0 else [])
                            for kc in range(KD):
                                S.mm(pu[:, 0:n], wu_t[:, kc, fc * 128:(fc + 1) * 128], h2T[:, kc, 0:n], start=(kc == 0), stop=(kc == KD - 1),
                                     R=[wuk, "h2T"], W=[puk] if kc == 0 else [], A=[puk] if kc > 0 else [])
                            S.act(tmp[0][:, 0:n], pg[:, 0:n], AF.Silu, R=[pgk], W=["tmp0"])
                            S.tt("dve", aT[:, f0 // 128 + fc, 0:n], tmp[0][:, 0:n], pu[:, 0:n], ALU.mult, R=["tmp0", puk], W=["aT"])
                    for cb in range(D // 512):
                        S.dma("sp", wfo[:], WB["w_fo"][l].rearrange("(kc p) n -> p kc n", p=128)[:, :, cb * 512:(cb + 1) * 512], W=["wfo"])
                        for tt_ in range(ntl):
                            pp, pk = nextp()
                            xk = "x1_%d" % tt_
                            for fc in range(KF):
                                S.mm(pp[:], aT[:, fc, tt_ * 128:(tt_ + 1) * 128], wfo[:, fc, :], start=(fc == 0), stop=(fc == KF - 1),
                                     R=["wfo", "aT"], W=[pk] if fc == 0 else [], A=[pk] if fc > 0 else [])
                            S.tt("dve", x1[tt_][:, cb * 512:(cb + 1) * 512], x1[tt_][:, cb * 512:(cb + 1) * 512], pp[:], ALU.add,
                                 R=[pk, xk], W=[xk])
                    dst = xres if l < DEPTH - 1 else None
                    for tt_ in range(ntl):
                        tok0 = t0 + tt_ * 128
                        ti = tok0 // 128
                        xk = "x1_%d" % tt_
                        if dst is not None:
                            S.dma("sp", xres[tok0:tok0 + 128, :], x1[tt_][:], R=[xk])
                        else:
                            if ti < NTT:
                                S.dma("sp", O["yp"][tok0:tok0 + 128, :], x1[tt_][:], R=[xk])
                            else:
                                for s2 in range(2):
                                    s = 2 * (ti - NTT) + s2
                                    S.dma("sp", O["ys"][8 * s:8 * s + 8, :], x1[tt_][64 * s2:64 * s2 + 8, :], R=[xk])
            S.barrier()
        for _ in range(PADOPS):
            S.memset("dve", ones[:], 1.0, W=["ones"])
        S.emit()
    return nc


def prep_core_inputs(cfg, inp, core, consts):
    D, NS, DEPTH = cfg.D, cfg.NS, cfg.DEPTH
    f = lambda a: np.ascontiguousarray(np.asarray(a, dtype=np.float32))
    nb = inp["x_prompt"].shape[0]
    ss = slice(core * NS, (core + 1) * NS)
    m = {}
    m["xp"] = f(inp["x_prompt"][core % nb])
    m["xs"] = f(inp["x_sample"][ss]).reshape(NS * 8, D)
    m["kv0"] = f(inp["cache_a_kv0"][:, ss]).reshape(DEPTH, NS, -1, 1024)
    m["kv1"] = f(inp["cache_a_kv1"][:, ss]).reshape(DEPTH, NS, -1, 1024)
    m["kv2"] = f(inp["cache_a_kv2"][:, ss]).reshape(DEPTH, NS, -1, 1024)
    m["sconv"] = f(inp["state_b_conv"][:, ss]).reshape(DEPTH, NS * 3, 1536)
    m["sS"] = f(inp["state_b_S"][:, ss])
    m["sR"] = f(inp["state_c_R"][:, ss]).reshape(DEPTH, NS, 2, 128, 128)
    m["g1B"] = f(np.broadcast_to(np.asarray(inp["norm1_g"])[:, None, :], (DEPTH, 128, D)))
    m["g2B"] = f(np.broadcast_to(np.asarray(inp["norm2_g"])[:, None, :], (DEPTH, 128, D)))
    m["w_in"] = f(inp["w_in"])
    m["aqg"] = f(inp["a_q_norm_g"]).reshape(DEPTH, 128, 1)
    m["akg"] = f(inp["a_k_norm_g"]).reshape(DEPTH, 128, 1)
    m["convw"] = f(np.asarray(inp["b_conv_w"]).reshape(DEPTH, 4, 12, 128).transpose(0, 3, 2, 1))
    m["alogB"] = f(np.broadcast_to(np.asarray(inp["b_a_log"])[:, None, :], (DEPTH, 128, 4)))
    m["dtbB"] = f(np.broadcast_to(np.asarray(inp["b_dt_bias"])[:, None, :], (DEPTH, 128, 4)))
    m["bog"] = f(inp["b_out_norm_g"]).reshape(DEPTH, 128, 1)
    m["cog"] = f(inp["c_out_norm_g"]).reshape(DEPTH, 128, 1)
    m["w_oa"] = f(inp["w_out_a"])
    m["w_ob"] = f(inp["w_out_b"])
    m["w_oc"] = f(inp["w_out_c"])
    m["w_out"] = f(inp["w_out"])
    m["w_fi"] = f(inp["w_ffn_in"])
    m["w_fo"] = f(inp["w_ffn_out"])
    for k, v in consts.items():
        m["c_" + k] = v
    return m


def assemble(cfg, res, nb, ncores):
    DEPTH, NS = cfg.DEPTH, cfg.NS
    r = res
    st = lambda k, cores: np.stack([np.asarray(r[c][k]) for c in cores])
    pc = list(range(nb))
    ac = list(range(ncores))
    yp = st("yp", pc)
    ys = np.concatenate([np.asarray(r[c]["ys"]).reshape(NS, 8, cfg.D) for c in ac], axis=0)
    outs = [yp, ys]
    for g in range(3):
        outs.append(st("kvp%d" % g, pc).transpose(1, 0, 2, 3, 4, 5))
    outs.append(st("convp", pc).transpose(1, 0, 2, 3))
    outs.append(st("Sp", pc).transpose(1, 0, 2, 3, 4))
    outs.append(st("Rp", pc).transpose(1, 0, 2, 3, 4).reshape(DEPTH, nb, 4, 64, 128))
    for g in range(3):
        outs.append(np.concatenate([np.asarray(r[c]["kvs%d" % g]) for c in ac], axis=1))
    outs.append(np.concatenate([np.asarray(r[c]["convs"]).reshape(DEPTH, NS, 3, 1536) for c in ac], axis=1))
    outs.append(np.concatenate([np.asarray(r[c]["Ss"]) for c in ac], axis=1))
    outs.append(np.concatenate([np.asarray(r[c]["Rs"]).reshape(DEPTH, NS, 4, 64, 128) for c in ac], axis=1))
    return tuple(np.ascontiguousarray(o, dtype=np.float32) for o in outs)


def kernel(**inputs):
    cfg = Cfg()
    ncores = 8
    consts = make_consts(cfg)
    nc = build(cfg)
    in_maps = [prep_core_inputs(cfg, inputs, c, consts) for c in range(ncores)]
    res = run_bass_kernel_spmd(nc, in_maps, core_ids=list(range(ncores)))
    return assemble(cfg, res.results, inputs["x_prompt"].shape[0], ncores)


def _gated_norm_epilogue(X, sbp_bufs, P_o, pok, zcol0, gcol, gkey, tok0, out_stage, okey, stage_cols, Pz, pzk, Pss, pssk, wz, wzk):
    S, hT, KD, onesb = X["S"], X["hT"], X["cfg"].KD, X["onesb"]
    sq, rs, on, sz = sbp_bufs
    poks = list(pok) if isinstance(pok, (list, tuple)) else [pok]
    S.act(sq[:], P_o[:], AF.Square, R=poks, W=["e_sq"])
    S.mm(Pss[:], onesb[:], sq[:], R=["onesb", "e_sq"], W=[pssk])
    X["rstd"](rs[:], Pss[:], 1.0 / 128, [pssk], ["e_rs"])
    S.stt("dve", on[:], P_o[:], gcol, rs[:], ALU.mult, ALU.mult, R=poks + [gkey, "e_rs"], W=["e_on"])
    for h in range(4):
        for kc in range(KD):
            S.mm(Pz[:, h * 128:(h + 1) * 128], wz[:, kc, zcol0 + h * 128:zcol0 + (h + 1) * 128], hT[:, kc, tok0:tok0 + 128],
                 start=(kc == 0), stop=(kc == KD - 1), R=[wzk, "hT"],
                 W=[pzk] if (h == 0 and kc == 0) else [], A=[pzk] if not (h == 0 and kc == 0) else [])
    S.act(sz[:], Pz[:], AF.Silu, R=[pzk], W=["e_sz"])
    S.tt("dve", out_stage[:, :, stage_cols], on[:].rearrange("p (h n) -> p h n", h=4), sz[:].rearrange("p (h n) -> p h n", h=4),
         ALU.mult, R=["e_on", "e_sz"], W=[okey])


def mixer_c(X):
    nc, S, cfg, l, I, O, P, PT, hT = X["nc"], X["S"], X["cfg"], X["l"], X["I"], X["O"], X["P"], X["PT"], X["hT"]
    KD, NTT, NS = cfg.KD, cfg.NTT, cfg.NS
    w_in, oT_d, identb, wview = X["w_in"], X["oT_d"], X["identb"], X["wview"]
    with contextlib.ExitStack() as ph:
        sbp = lambda n, s, d=F32: ph.enter_context(nc.sbuf_tensor("c%d_%s" % (l, n), list(s), d))
        rotC = sbp("rotC", [128, 128], BF16)
        S.dma("pool", rotC[:], I["c_rotC"], W=["rotC"])
        tab = {}
        for ty in "ps":
            for nm, w in (("dtc", 512), ("qdec", 256), ("gC", 2), ("kdec", 256)):
                tab[nm + ty] = sbp(nm + ty, [128, w])
                S.dma("sp", tab[nm + ty][:], I["c_%s_%s" % (nm, ty)], W=["tab"])
        cog = sbp("cog", [128, 1])
        S.dma("sp", cog[:], I["cog"][l], W=["cog"])
        wq = sbp("wq", [128, KD, 256], BF16)
        wk = sbp("wk", [128, KD, 256], BF16)
        wv = sbp("wv", [128, KD, 512], BF16)
        wz = sbp("wz", [128, KD, 512], BF16)
        for t_, c0, w_ in ((wq, CQ, 256), (wk, CK, 256), (wv, CV, 512), (wz, CZ, 512)):
            S.dma("pool", t_[:], wview(w_in, c0, w_), W=["wc"])
        ct = sbp("ct", [128, 512])
        sn = sbp("sn", [128, 512])
        qT = sbp("qT", [128, 2, 2, 512], BF16)
        S.memset("pool", qT[:], 0.0, W=["qT"])
        qdT = sbp("qdT", [128, 2, 512], BF16)
        kT = sbp("kT", [128, 2, 512], BF16)
        qb = sbp("qb", [128, 512], BF16)
        t1 = sbp("t1", [128, 512])
        t2 = sbp("t2", [128, 512])
        v_t = sbp("v_t", [128, 512], BF16)
        kd_t = sbp("kd_t", [128, 256], BF16)
        attn = sbp("attn", [128, 512], BF16)
        R32 = sbp("R32", [128, 2, 128])
        Rb = [sbp("Rb%d" % i, [128, 2, 128], BF16) for i in range(2)]
        ebuf = (sbp("e_sq", [128, 512], BF16), sbp("e_rs", [128, 512]), sbp("e_on", [128, 512]), sbp("e_sz", [128, 512]))
        ocs = sbp("ocs", [128, 4, 512], BF16)
        S.memset("dve", R32[:], 0.0, W=["R32"])
        rbi = 0
        for (t0, n) in X["blocks"]:
            ntl = n // 128
            sample = (t0 // 128 >= NTT)
            ty = "s" if sample else "p"
            if CUT <= 0:
                continue
            S.dma("sp", ct[:, 0:n], I["c_cosC"][:, t0:t0 + n], W=["ct"])
            S.dma("sp", sn[:, 0:n], I["c_sinC"][:, t0:t0 + n], W=["sn"])
            for which, w_t, dst in (("q", wq, qT), ("k", wk, kT)):
                sc = 1.0 if which == "q" else 0.125
                for c in range(2):
                    for kc in range(KD):
                        S.mm(P[0][:, 0:n], w_t[:, kc, c * 128:(c + 1) * 128], hT[:, kc, t0:t0 + n], start=(kc == 0), stop=(kc == KD - 1),
                             R=["wc", "hT"], W=["P0"] if kc == 0 else [], A=["P0"] if kc > 0 else [])
                    S.cp("act", qb[:, 0:n], P[0][:, 0:n], R=["P0"], W=["qb"])
                    S.mm(P[1][:, 0:n], rotC[:], qb[:, 0:n], R=["rotC", "qb"], W=["P1"])
                    if VAR == 1:
                        S.cp("dve", t1[:, 0:n], ct[:, 0:n], R=["ct"], W=["t1"])
                    elif VAR == 2:
                        S.cp("dve", t1[:, 0:n], P[0][:, 0:n], R=["P0"], W=["t1"])
                    elif VAR == 3:
                        S.cp("dve", t1[:, 0:n], X["ones"][:, 0:1].broadcast_to([128, n]) if False else t2[:, 0:n], R=[], W=["t1"])
                    else:
                        S.tt("dve", t1[:, 0:n], P[0][:, 0:n], ct[:, 0:n], ALU.mult, R=["P0", "ct"], W=["t1"])
                    S.tt("dve", t2[:, 0:n], P[1][:, 0:n], sn[:, 0:n], ALU.mult, R=["P1", "sn"], W=["t2"])
                    S.tt("pool", t1[:, 0:n], t1[:, 0:n], t2[:, 0:n], ALU.add, R=["t1", "t2"], W=["t1"])
                    if which == "k":
                        S.act(dst[:, c, 0:n], t1[:, 0:n], AF.Copy, scale=sc, R=["t1"], W=[which + "T"])
                    else:
                        for half in range(2):
                            hr = slice(64 * half, 64 * half + 64)
                            S.act(dst[hr, c, half, 0:n], t1[hr, 0:n], AF.Copy, R=["t1"], W=["qT"])
                    if which == "q":
                        for tt_ in range(ntl):
                            S.tt("pool", qdT[:, c, tt_ * 128:(tt_ + 1) * 128], t1[:, tt_ * 128:(tt_ + 1) * 128],
                                 tab["qdec" + ty][:, c * 128:(c + 1) * 128], ALU.mult, R=["t1", "tab"], W=["qdT"])
            if CUT <= 1:
                continue
            for tt_ in range(ntl):
                tok0 = t0 + tt_ * 128
                tc = slice(tt_ * 128, (tt_ + 1) * 128)
                for kc in range(KD):
                    S.mm(P[2][:], hT[:, kc, tok0:tok0 + 128], wv[:, kc, :], start=(kc == 0), stop=(kc == KD - 1),
                         R=["wc", "hT"], W=["P2"] if kc == 0 else [], A=["P2"] if kc > 0 else [])
                S.cp("act", v_t[:], P[2][:], R=["P2"], W=["v_t"])
                for c in range(2):
                    S.tr(PT[:, c * 128:(c + 1) * 128], kT[:, c, tc], identb[:], R=["kT", "identb"],
                         W=["PT"] if c == 0 else [], A=["PT"] if c > 0 else [])
                S.tt("dve", kd_t[:], PT[:, 0:256], tab["kdec" + ty][:], ALU.mult, R=["PT", "tab"], W=["kd_t"])
                if CUT <= 2:
                    continue
                for h in range(4):
                    c, rows = h // 2, slice(64 * (h % 2), 64 * (h % 2) + 64)
                    S.mm(P[3][:, h * 128:(h + 1) * 128], kT[:, c, tc], qT[:, c, h % 2, tc], R=["kT", "qT"],
                         W=["P3"] if h == 0 else [], A=["P3"] if h > 0 else [])
                S.tt("dve", attn[:], P[3][:], tab["dtc" + ty][:], ALU.mult, R=["P3", "tab"], W=["attn"])
                if CUT <= 3:
                    continue
                bs = 64
                rbs = []
                for b in range(2):
                    br = slice(b * bs, (b + 1) * bs)
                    sq_ = 2 * (tok0 // 128 - NTT) + b
                    if sample:
                        S.dma("sp", R32[:], I["sR"][l, sq_].rearrange("pr p d -> p pr d"), W=["R32"])
                    rb, rbk = Rb[b], "Rb%d" % b
                    rbs.append((rb, rbk))
                    S.cp("act", rb[:], R32[:], R=["R32"], W=[rbk])
                    for pr in range(2):
                        S.mm(P[4][:, pr * 256:(pr + 1) * 256], kd_t[br, pr * 128:(pr + 1) * 128], v_t[br, pr * 256:(pr + 1) * 256],
                             R=["kd_t", "v_t"], W=["P4"] if pr == 0 else [], A=["P4"] if pr > 0 else [])
                    for pr in range(2):
                        for half in range(2):
                            rows = slice(64 * half, 64 * half + 64)
                            S.stt("dve", R32[rows, pr, :], R32[rows, pr, :], tab["gC" + ty][rows, pr:pr + 1],
                                  P[4][rows, pr * 256 + half * 128:pr * 256 + half * 128 + 128], ALU.mult, ALU.add,
                                  R=["R32", "P4", "tab"], W=["R32"])
                    if sample:
                        S.dma("sp", O["Rs"][l, sq_].rearrange("pr p d -> p pr d"), R32[:], R=["R32"])
                    elif tok0 + 128 == cfg.T and b == 1:
                        S.dma("sp", O["Rp"][l].rearrange("pr p d -> p pr d"), R32[:], R=["R32"])
                if CUT <= 4:
                    continue
                for h in range(4):
                    c, rows = h // 2, slice(64 * (h % 2), 64 * (h % 2) + 64)
                    S.mm(P[5][:, h * 128:(h + 1) * 128], v_t[:, h * 128:(h + 1) * 128], attn[:, h * 128:(h + 1) * 128],
                         start=True, stop=False, R=["v_t", "attn"], W=["P5"] if h == 0 else [], A=["P5"] if h > 0 else [])
                    for b in range(2):
                        rb, rbk = rbs[b]
                        S.mm(P[5][:, h * 128 + b * bs:h * 128 + (b + 1) * bs], rb[rows, c, :], qdT[rows, c, tt_ * 128 + b * bs:tt_ * 128 + (b + 1) * bs],
                             start=False, stop=(b == 1), R=[rbk, "qdT"], A=["P5"])
                if CUT <= 5:
                    continue
                _gated_norm_epilogue(X, ebuf, P[5], "P5", 0, cog[:, 0:1], "cog", tok0, ocs, "ocs", tc, P[2], "P2", P[6], "P6", wz, "wc")
            if CUT <= 6:
                continue
            S.dma("sp", oT_d[8 * 128:12 * 128, t0:t0 + n].rearrange("(c p) n -> p c n", p=128), ocs[:, :, 0:n], R=["ocs"])


def mixer_b(X):
    nc, S, cfg, l, I, O, P, PT, hT = X["nc"], X["S"], X["cfg"], X["l"], X["I"], X["O"], X["P"], X["PT"], X["hT"]
    KD, NTT, NS, T = cfg.KD, cfg.NTT, cfg.NS, cfg.T
    w_in, oT_d, ident, ones, onesb, wview = X["w_in"], X["oT_d"], X["ident"], X["ones"], X["onesb"], X["wview"]
    with contextlib.ExitStack() as ph:
        sbp = lambda n, s, d=F32: ph.enter_context(nc.sbuf_tensor("b%d_%s" % (l, n), list(s), d))
        tab = {}
        for ty in "ps":
            for nm, w in (("tri", 128), ("blk", 128), ("negS", 128), ("negIT", 128), ("valid", 1)):
                tab[nm + ty] = sbp(nm + ty, [128, w])
                S.dma("sp", tab[nm + ty][:], I["c_%s_%s" % (nm, ty)], W=["tab"])
        bog = sbp("bog", [128, 1]); S.dma("sp", bog[:], I["bog"][l], W=["bog"])
        cw = sbp("cw", [128, 12, 4]); S.dma("sp", cw[:], I["convw"][l], W=["cw"])
        alog = sbp("alog", [128, 4]); S.dma("sp", alog[:], I["alogB"][l], W=["alog"])
        dtb = sbp("dtb", [128, 4]); S.dma("sp", dtb[:], I["dtbB"][l], W=["dtb"])
        S.act(alog[:], alog[:], AF.Exp, R=["alog"], W=["alog"])
        wqh = [sbp("wqh%d" % i, [128, KD, 3, 128], BF16) for i in range(2)]
        wz = sbp("wz", [128, KD, 512], BF16)
        wbg = sbp("wbg", [128, KD, 8], BF16)
        S.dma("pool", wz[:], wview(w_in, BZ, 512), W=["wb"])
        S.dma("pool", wbg[:], wview(w_in, BBETA, 8), W=["wb"])
        pre = sbp("pre", [128, 3 + 512])
        carry = sbp("carry", [128, 12, 3])
        S.memset("pool", carry[:], 0.0, W=["carry"])
        pres = sbp("pres", [128, 4, 67])
        cst = sbp("cst", [12, 1536]); S.dma("sp", cst[:], I["sconv"][l], W=["cst"])
        cso = sbp("cso", [128, 12, 12])
        csin = sbp("csin", [128, 12, 12])
        cpo = sbp("cpo", [128, 12, 3])
        cout = sbp("cout", [12, 1536])
        acc = sbp("acc", [128, 512])
        yv = sbp("yv", [128, 512])
        sqb = sbp("sqb", [128, 512], BF16)
        rn = sbp("rn", [128, 512])
        fT = [{r: sbp("%sT%d" % (r, h), [128, 512]) for r in "qkv"} for h in range(4)]
        bgt = sbp("bgt", [128, 4, 8])
        nbe = sbp("nbe", [128, 4, 4])
        BN = ("sA", "sB", "sC", "egc", "Nm", "NTm", "PTm", "attnT", "Xb0", "Xb1", "XTb0", "XTb1", "u_", "wT", "qgT", "vnew", "k_tm", "v_tm")
        B = [{nm: sbp("%s_%d" % (nm, h), [128, 128]) for nm in BN} for h in range(4)]
        gcsb = [sbp("gcs%d" % h, [128, 4]) for h in range(4)]
        S32 = [sbp("S32_%d" % h, [128, 128]) for h in range(4)]
        ob = sbp("ob", [128, 4, 512])
        ebuf = (sbp("e_sq", [128, 512], BF16), sbp("e_rs", [128, 512]), sbp("e_on", [128, 512]), sbp("e_sz", [128, 512]))
        obs = sbp("obs", [128, 4, 512], BF16)
        for h in range(4):
            S.memset("pool", S32[h][:], 0.0, W=["S32_%d" % h])
        for c in range(12):
            S.tr(P[0][:, 0:12], cst[0:12, c * 128:(c + 1) * 128], ident[0:12, 0:12], R=["cst", "ident"], W=["P0"])
            S.cp("dve", csin[:, c, :], P[0][:, 0:12], R=["P0"], W=["csin"])

        def tile_chain(Sx, h, tt_, tok0, sample, ty, nlev):
            bb, pb, pk = B[h], P[1 + h], "P%d" % (1 + h)
            k_ = lambda nm: "%s_%d" % (nm, h)
            qT, kT, vT = fT[h]["q"], fT[h]["k"], fT[h]["v"]
            gcs, gk = gcsb[h], "gcs%d" % h
            s32, s32k = S32[h], "S32_%d" % h
            tc = slice(tt_ * 128, (tt_ + 1) * 128)
            be = bgt[:, tt_, h:h + 1]
            gg = bgt[:, tt_, 4 + h:5 + h]
            A_, B_, C_, D_ = pb[:, 0:128], pb[:, 128:256], pb[:, 256:384], pb[:, 384:512]
            Sx.tr(A_, kT[:, tc], ident[:], R=[k_("kT"), "ident"], W=[pk])
            Sx.tr(B_, vT[:, tc], ident[:], R=[k_("vT"), "ident"], A=[pk])
            Sx.cp("act", bb["k_tm"][:], A_, R=[pk], W=[k_("k_tm")])
            Sx.cp("dve", bb["v_tm"][:], B_, R=[pk], W=[k_("v_tm")])
            Sx.ts("dve", bb["sA"][:], ones[:], gg, None, ALU.mult, R=["ones", "bgt"], W=[k_("sA")])
            Sx.mm(A_, bb["sA"][:], tab["tri" + ty][:], R=[k_("sA"), "tab"], W=[pk])
            Sx.mm(pb[:, 128:129], tab["tri" + ty][:], gg, R=["tab", "bgt"], A=[pk])
            Sx.mm(pb[:, 129:130], tab["blk" + ty][:], gg, R=["tab", "bgt"], A=[pk])
            Sx.cp("dve", gcs[:, 0:2], pb[:, 128:130], R=[pk], W=[gk])
            Sx.stt("dve", bb["sA"][:], A_, gcs[:, 0:1], tab["negS" + ty][:], ALU.subtract, ALU.subtract, R=[pk, gk, "tab"], W=[k_("sA")])
            Sx.act(bb["sB"][:], bb["sA"][:], AF.Exp, scale=-1.0, R=[k_("sA")], W=[k_("sB")])
            Sx.stt("dve", bb["sA"][:], A_, gcs[:, 0:1], tab["negIT" + ty][:], ALU.subtract, ALU.add, R=[pk, gk, "tab", k_("sB")], W=[k_("sA")])
            Sx.act(bb["sC"][:], bb["sA"][:], AF.Exp, R=[k_("sA")], W=[k_("sC")])
            Sx.act(bb["egc"][:], A_, AF.Exp, R=[pk], W=[k_("egc")])
            Sx.tt("dve", gcs[:, 2:3], gcs[:, 1:2], gcs[:, 0:1], ALU.subtract, R=[gk], W=[gk])
            Sx.act(gcs[:, 2:3], gcs[:, 2:3], AF.Exp, R=[gk], W=[gk])
            Sx.tt("dve", gcs[:, 2:3], gcs[:, 2:3], tab["valid" + ty][:, 0:1], ALU.mult, R=[gk, "tab"], W=[gk])
            Sx.act(gcs[:, 3:4], gcs[:, 0:1], AF.Exp, R=[gk], W=[gk])
            Sx.tt("dve", gcs[:, 3:4], gcs[:, 3:4], be, ALU.mult, R=[gk, "bgt"], W=[gk])
            Sx.mm(C_, kT[:, tc], kT[:, tc], R=[k_("kT")], W=[pk])
            Sx.mm(D_, kT[:, tc], qT[:, tc], R=[k_("kT"), k_("qT")], A=[pk])
            Sx.stt("dve", bb["Nm"][:], C_, nbe[:, tt_, h:h + 1], bb["sB"][:], ALU.mult, ALU.mult, R=[pk, "nbe", k_("sB")], W=[k_("Nm")])
            Sx.tt("dve", bb["attnT"][:], D_, bb["sC"][:], ALU.mult, R=[pk, k_("sC")], W=[k_("attnT")])
            Sx.tr(A_, bb["Nm"][:], ident[:], R=[k_("Nm"), "ident"], W=[pk])
            Sx.cp("act", bb["NTm"][:], A_, R=[pk], W=[k_("NTm")])
            Sx.tt("pool", bb["PTm"][:], bb["NTm"][:], ident[:], ALU.add, R=[k_("NTm"), "ident"], W=[k_("PTm")])
            Xc, Xk, XTc, XTk = bb["Nm"], k_("Nm"), bb["NTm"], k_("NTm")
            for m in range(1, nlev + 1):
                X2, X2k = bb["Xb%d" % (m % 2)], k_("Xb%d" % (m % 2))
                XT2, XT2k = bb["XTb%d" % (m % 2)], k_("XTb%d" % (m % 2))
                Sx.mm(B_, XTc[:], Xc[:], R=[Xk, XTk], W=[pk])
                if m < nlev:
                    Sx.mm(C_, Xc[:], XTc[:], R=[Xk, XTk], A=[pk])
                Sx.cp("act", X2[:], B_, R=[pk], W=[X2k])
                if m < nlev:
                    Sx.cp("dve", XT2[:], C_, R=[pk], W=[XT2k])
                Sx.mm(D_, X2[:], bb["PTm"][:], R=[X2k, k_("PTm")], W=[pk])
                Sx.tt("dve", bb["PTm"][:], bb["PTm"][:], D_, ALU.add, R=[k_("PTm"), pk], W=[k_("PTm")])
                Xc, Xk, XTc, XTk = X2, X2k, XT2, XT2k
            Sx.ts("dve", bb["sA"][:], bb["v_tm"][:], be, None, ALU.mult, R=[k_("v_tm"), "bgt"], W=[k_("sA")])
            Sx.ts("pool", bb["sB"][:], bb["k_tm"][:], gcs[:, 3:4], None, ALU.mult, R=[k_("k_tm"), gk], W=[k_("sB")])
            Sx.mm(A_, bb["PTm"][:], bb["sA"][:], R=[k_("PTm"), k_("sA")], W=[pk])
            Sx.mm(B_, bb["sB"][:], bb["PTm"][:], R=[k_("PTm"), k_("sB")], A=[pk])
            Sx.cp("act", bb["u_"][:], A_, R=[pk], W=[k_("u_")])
            Sx.cp("dve", bb["wT"][:], B_, R=[pk], W=[k_("wT")])
            Sx.tt("pool", bb["qgT"][:], qT[:, tc], bb["egc"][:], ALU.mult, R=[k_("qT"), k_("egc")], W=[k_("qgT")])
            Sx.ts("pool", bb["sC"][:], bb["k_tm"][:], gcs[:, 2:3], None, ALU.mult, R=[k_("k_tm"), gk], W=[k_("sC")])
            for b in range(2):
                rows = slice(64 * b, 64 * b + 64)
                sq_ = 2 * (tok0 // 128 - NTT) + b
                if sample:
                    Sx.dma("sp", s32[:], I["sS"][l, sq_, h], W=[s32k])
                Sx.mm(C_, bb["wT"][:], s32[:], R=[k_("wT"), s32k], W=[pk])
                Sx.tt("dve", bb["vnew"][rows, :], bb["u_"][rows, :], pb[rows, 256:384], ALU.subtract, R=[k_("u_"), pk], W=[k_("vnew")])
                oc = slice(384 + 64 * b, 384 + 64 * b + 64)
                Sx.mm(pb[:, oc], s32[:], bb["qgT"][:, rows], start=True, stop=False, R=[s32k, k_("qgT")], W=[pk])
                Sx.mm(pb[:, oc], bb["vnew"][rows, :], bb["attnT"][rows, rows], start=False, stop=True, R=[k_("vnew"), k_("attnT")], A=[pk])
                Sx.cp("act", ob[:, h, tt_ * 128 + 64 * b:tt_ * 128 + 64 * b + 64], pb[:, oc], R=[pk], W=["ob%d" % h])
                Sx.mm(A_, bb["sC"][rows, :], bb["vnew"][rows, :], R=[k_("sC"), k_("vnew")], W=[pk])
                Sx.stt("dve", s32[:], s32[:], bb["egc"][:, 64 * b + 63:64 * b + 64], A_, ALU.mult, ALU.add,
                       R=[s32k, k_("egc"), pk], W=[s32k])
                if sample:
                    Sx.dma("sp", O["Ss"][l, sq_, h], s32[:], R=[s32k])
                elif tok0 + 128 == T and b == 1:
                    Sx.dma("sp", O["Sp"][l, h], s32[:], R=[s32k])

        for (t0, n) in X["blocks"]:
            ntl = n // 128
            sample = (t0 // 128 >= NTT)
            ty = "s" if sample else "p"
            nlev = 2 if sample else 5
            for tt_ in range(ntl):
                tok0 = t0 + tt_ * 128
                for kc in range(KD):
                    S.mm(P[0][:, 0:8], hT[:, kc, tok0:tok0 + 128], wbg[:, kc, :], start=(kc == 0), stop=(kc == KD - 1),
                         R=["wb", "hT"], W=["P0"] if kc == 0 else [], A=["P0"] if kc > 0 else [])
                S.act(bgt[:, tt_, 0:4], P[0][:, 0:4], AF.Sigmoid, R=["P0"], W=["bgt"])
                S.tt("dve", bgt[:, tt_, 4:8], P[0][:, 4:8], dtb[:], ALU.add, R=["P0", "dtb"], W=["bgt"])
                S.act(bgt[:, tt_, 4:8], bgt[:, tt_, 4:8], AF.Exp, R=["bgt"], W=["bgt"])
                S.act(bgt[:, tt_, 4:8], bgt[:, tt_, 4:8], AF.Ln, bias=1.0, R=["bgt"], W=["bgt"])
                S.stt("dve", bgt[:, tt_, 4:8], bgt[:, tt_, 4:8], -1.0, alog[:], ALU.mult, ALU.mult, R=["bgt", "alog"], W=["bgt"])
                S.ts("dve", bgt[:, tt_, 4:8], bgt[:, tt_, 4:8], tab["valid" + ty][:, 0:1], None, ALU.mult, R=["bgt", "tab"], W=["bgt"])
                S.ts("dve", nbe[:, tt_, :], bgt[:, tt_, 0:4], -1.0, None, ALU.mult, R=["bgt"], W=["nbe"])
            for h in range(4):
                wq_t, wqk = wqh[h % 2], "wqh%d" % (h % 2)
                for ri in range(3):
                    S.dma("pool", wq_t[:, :, ri, :], wview(w_in, BQKV + (ri * 4 + h) * 128, 128), W=[wqk])
                for ri, (role, c) in enumerate((("q", h), ("k", 4 + h), ("v", 8 + h))):
                    pp, ppk = (P[0], "P0") if (ri % 2 == 0) else (P[6], "P6")
                    for kc in range(KD):
                        S.mm(pp[:, 0:n], wq_t[:, kc, ri, :], hT[:, kc, t0:t0 + n], start=(kc == 0), stop=(kc == KD - 1),
                             R=[wqk, "hT"], W=[ppk] if kc == 0 else [], A=[ppk] if kc > 0 else [])
                    if not sample:
                        S.cp("pool", pre[:, 0:3], carry[:, c, :], R=["carry"], W=["pre"])
                        S.cp("act", pre[:, 3:3 + n], pp[:, 0:n], R=[ppk], W=["pre"])
                        S.ts("dve", acc[:, 0:n], pre[:, 0:n], cw[:, c, 0:1], None, ALU.mult, R=["pre", "cw"], W=["acc"])
                        for i in range(1, 4):
                            S.stt("dve", acc[:, 0:n], pre[:, i:i + n], cw[:, c, i:i + 1], acc[:, 0:n], ALU.mult, ALU.add,
                                  R=["pre", "cw", "acc"], W=["acc"])
                        if t0 + n == T:
                            S.cp("pool", cpo[:, c, :], pre[:, n:n + 3], R=["pre"], W=["cpo"])
                        S.cp("pool", carry[:, c, :], pre[:, n:n + 3], R=["pre"], W=["carry"])
                        accv = acc[:, 0:n]
                    else:
                        S.cp("pool", pres[:, :, 0:3], csin[:, c, :].rearrange("p (s k) -> p s k", s=4), R=["csin"], W=["pres"])
                        S.cp("act", pres[:, :, 3:67], pp[:, 0:256].rearrange("p (s k) -> p s k", s=4), R=[ppk], W=["pres"])
                        a3 = acc[:, 0:256].rearrange("p (s k) -> p s k", s=4)
                        S.ts("dve", a3, pres[:, :, 0:64], cw[:, c, 0:1], None, ALU.mult, R=["pres", "cw"], W=["acc"])
                        for i in range(1, 4):
                            S.stt("dve", a3, pres[:, :, i:i + 64], cw[:, c, i:i + 1], a3, ALU.mult, ALU.add, R=["pres", "cw", "acc"], W=["acc"])
                        S.cp("pool", cso[:, c, :].rearrange("p (s k) -> p s k", s=4), pres[:, :, 8:11], R=["pres"], W=["cso"])
                        accv = acc[:, 0:n]
                    fk = "%sT_%d" % (role, h)
                    if role == "v":
                        S.act(fT[h]["v"][:, 0:n], accv, AF.Silu, R=["acc"], W=[fk])
                    else:
                        S.act(yv[:, 0:n], accv, AF.Silu, R=["acc"], W=["yv"])
                        S.act(sqb[:, 0:n], yv[:, 0:n], AF.Square, R=["yv"], W=["sqb"])
                        S.mm(P[5][:, 0:n], onesb[:], sqb[:, 0:n], R=["onesb", "sqb"], W=["P5"])
                        X["rstd"](rn[:, 0:n], P[5][:, 0:n], 1.0, ["P5"], ["rn"])
                        sc = 128.0 ** -0.5 if role == "q" else 1.0
                        S.stt("dve", fT[h][role][:, 0:n], yv[:, 0:n], sc, rn[:, 0:n], ALU.mult, ALU.mult, R=["yv", "rn"], W=[fk])
            recs = []
            for h in range(4):
                r = Rec()
                for tt_ in range(ntl):
                    tile_chain(r, h, tt_, t0 + tt_ * 128, sample, ty, nlev)
                recs.append(r)
            replay_interleaved(S, recs)
            for tt_ in range(ntl):
                tok0 = t0 + tt_ * 128
                tc = slice(tt_ * 128, (tt_ + 1) * 128)
                _gated_norm_epilogue(X, ebuf, ob[:, :, tc], ["ob0", "ob1", "ob2", "ob3"], 0, bog[:, 0:1], "bog", tok0, obs, "obs", tc,
                                     P[0], "P0", P[5], "P5", wz, "wb")
            S.dma("sp", oT_d[4 * 128:8 * 128, t0:t0 + n].rearrange("(c p) n -> p c n", p=128), obs[:, :, 0:n], R=["obs"])
        for (src, sk, nrow, dst) in ((cpo, "cpo", 3, O["convp"][l]), (cso, "cso", 12, O["convs"][l])):
            for g4 in range(3):
                pb, pk = P[g4], "P%d" % g4
                for c4 in range(4):
                    c = g4 * 4 + c4
                    S.tr(pb[0:nrow, c4 * 128:(c4 + 1) * 128], src[:, c, 0:nrow], ident[:], R=[sk, "ident"],
                         W=[pk] if c4 == 0 else [], A=[pk] if c4 else [])
                S.cp("dve", cout[0:nrow, g4 * 512:(g4 + 1) * 512], pb[0:nrow, :], R=[pk], W=["cout"])
            S.dma("sp", dst, cout[0:nrow, :], R=["cout"])


def mixer_a(X):
    nc, S, cfg, l, I, O, P, PT, hT = X["nc"], X["S"], X["cfg"], X["l"], X["I"], X["O"], X["P"], X["PT"], X["hT"]
    KD, NTT, NS, T, NTOK = cfg.KD, cfg.NTT, cfg.NS, cfg.T, cfg.NTOK
    w_in, oT_d, ident, identb, onesb, wview = X["w_in"], X["oT_d"], X["ident"], X["identb"], X["onesb"], X["wview"]
    KEEP = (128, 512, min(2048, T))
    DIL = (1, 4, 16)
    sc_ = 128.0 ** -0.5
    with contextlib.ExitStack() as ph:
        sbp = lambda n, s, d=F32: ph.enter_context(nc.sbuf_tensor("a%d_%s" % (l, n), list(s), d))
        rotA = sbp("rotA", [128, 128], BF16); S.dma("pool", rotA[:], I["c_rotA"], W=["rotA"])
        mcur = sbp("mcur", [128, 128], BF16); S.dma("pool", mcur[:], I["c_mcur"][:, 0:128], W=["mask"])
        mprev = sbp("mprev", [128, 128], BF16); S.dma("pool", mprev[:], I["c_mprev"][:, 0:128], W=["mask"])
        msc = sbp("msc", [128, 416], BF16); S.dma("pool", msc[:], I["c_msc"], W=["mask"])
        msn = sbp("msn", [128, 192], BF16); S.dma("pool", msn[:], I["c_msn"], W=["mask"])
        QTs = sbp("QTs", [128, 4, 3, 4, 8], BF16); KTs = sbp("KTs", [128, 4, 3, 256], BF16)
        Vs = sbp("Vs", [128, 2, 12, 128], BF16)
        OaccS = sbp("OaccS", [128, 4, 256]); DaccS = sbp("DaccS", [128, 4, 256])
        S.memset("pool", OaccS[:], 0.0, W=["OaccS"]); S.memset("pool", DaccS[:], 0.0, W=["DaccS"])
        aqg = sbp("aqg", [128, 1]); S.dma("sp", aqg[:], I["aqg"][l], W=["aqg"])
        akg = sbp("akg", [128, 1]); S.dma("sp", akg[:], I["akg"][l], W=["aqg"])
        nshift = sbp("nshift", [128, 1]); S.memset("pool", nshift[:], -SHIFT, W=["nshift"])
        rn = sbp("rn", [128, 512]); ost = sbp("ost", [128, 512], BF16)
        hs = contextlib.ExitStack()
        sbo = sbp
        sbp = lambda n, s, d=F32: hs.enter_context(nc.sbuf_tensor("a%d_%s" % (l, n), list(s), d))
        wset = [[sbp("w%s%d" % (nm, i), [128, KD, 128], BF16) for nm in "qkv"] for i in range(2)]
        ct = sbp("ct", [128, 512]); sn = sbp("sn", [128, 512])
        QT = sbp("QT", [128, NTOK], BF16); KT = sbp("KT", [128, NTOK], BF16); K32 = sbp("K32", [128, NTOK])
        V = sbp("V", [128, cfg.NT, 128], BF16)
        Oacc = sbp("Oacc", [128, NTOK]); Dacc = sbp("Dacc", [128, NTOK])
        sqb2 = [sbp("sqb%d" % i, [128, 512], BF16) for i in range(2)]; rn2 = [sbp("rn%d" % i, [128, 512]) for i in range(2)]
        qb2 = [sbp("qb%d" % i, [128, 512], BF16) for i in range(2)]
        t12 = [sbp("t1%d" % i, [128, 512]) for i in range(2)]; t22 = [sbp("t2%d" % i, [128, 512]) for i in range(2)]
        pT2 = [sbp("pT%d" % i, [128, 128], BF16) for i in range(2)]
        v32 = sbp("v32", [128, 128]); k32 = sbp("k32", [128, 128])
        for h in range(4):
            S.memset("pool", Oacc[:], 0.0, W=["Oacc"])
            S.memset("pool", Dacc[:], 0.0, W=["Dacc"])
            for g in range(3):
                dil, keep = DIL[g], KEEP[g]
                kvp, kvs, kvin = O["kvp%d" % g], O["kvs%d" % g], I["kv%d" % g]
                wi = (h * 3 + g) % 2
                wq, wk, wv = wset[wi]
                wak = "wa%d" % wi
                for t_, i3 in ((wq, 0), (wk, 1), (wv, 2)):
                    S.dma("pool", t_[:], wview(w_in, ((i3 * 3 + g) * 4 + h) * 128, 128), W=[wak])
                for (t0, n) in X["blocks"]:
                    S.dma("sp", ct[:, 0:n], I["c_cosA"][:, t0:t0 + n], W=["ct"])
                    S.dma("sp", sn[:, 0:n], I["c_sinA"][:, t0:t0 + n], W=["sn"])
                    recs = []
                    for si, (which, w_t, gcol) in enumerate((("q", wq, aqg), ("k", wk, akg))):
                        Sx = Rec()
                        Pa, Pb_, Pc = (P[0], P[1], P[2]) if si == 0 else (P[4], P[5], P[6])
                        ka, kb_, kc_ = ("P0", "P1", "P2") if si == 0 else ("P4", "P5", "P6")
                        sq_, rn_, qb_, t1_, t2_ = sqb2[si], rn2[si], qb2[si], t12[si], t22[si]
                        sfx = str(si)
                        for kc in range(KD):
                            Sx.mm(Pa[:, 0:n], w_t[:, kc, :], hT[:, kc, t0:t0 + n], start=(kc == 0), stop=(kc == KD - 1),
                                  R=[wak, "hT"], W=[ka] if kc == 0 else [], A=[ka] if kc > 0 else [])
                        Sx.act(sq_[:, 0:n], Pa[:, 0:n], AF.Square, R=[ka], W=["sqb" + sfx])
                        Sx.mm(Pb_[:, 0:n], onesb[:], sq_[:, 0:n], R=["onesb", "sqb" + sfx], W=[kb_])
                        Sx.act(rn_[:, 0:n], Pb_[:, 0:n], AF.Ln, bias=EPS, scale=1.0 / 128, R=[kb_], W=["rn" + sfx])
                        Sx.act(rn_[:, 0:n], rn_[:, 0:n], AF.Exp, scale=-0.5, R=["rn" + sfx], W=["rn" + sfx])
                        Sx.stt("dve", qb_[:, 0:n], Pa[:, 0:n], gcol[:, 0:1], rn_[:, 0:n], ALU.mult, ALU.mult, R=[ka, "aqg", "rn" + sfx], W=["qb" + sfx])
                        Sx.mm(Pc[:, 0:n], rotA[:], qb_[:, 0:n], R=["rotA", "qb" + sfx], W=[kc_])
                        Sx.tt("dve", t1_[:, 0:n], qb_[:, 0:n], ct[:, 0:n], ALU.mult, R=["qb" + sfx, "ct"], W=["t1" + sfx])
                        Sx.tt("dve", t2_[:, 0:n], Pc[:, 0:n], sn[:, 0:n], ALU.mult, R=[kc_, "sn"], W=["t2" + sfx])
                        if which == "q":
                            Sx.tt("pool", QT[:, t0:t0 + n], t1_[:, 0:n], t2_[:, 0:n], ALU.add, R=["t1" + sfx, "t2" + sfx], W=["QT"])
                        else:
                            Sx.tt("pool", K32[:, t0:t0 + n], t1_[:, 0:n], t2_[:, 0:n], ALU.add, R=["t1" + sfx, "t2" + sfx], W=["K32"])
                            Sx.cp("act", KT[:, t0:t0 + n], K32[:, t0:t0 + n], R=["K32"], W=["KT"])
                        recs.append(Sx)
                    replay_interleaved(S, recs)
                S.cp("pool", QTs[:, h, g, :, :], QT[:, T:T + 256].rearrange("p (s k) -> p s k", s=4)[:, :, 0:8], R=["QT"], W=["QTs"])
                S.cp("pool", KTs[:, h, g, :], KT[:, T:T + 256], R=["KT"], W=["KTs"])
                nbk = T // (128 * dil)
                def tcols(r, nb):
                    st0 = dil * 128 * nb + r
                    return slice(st0, st0 + dil * 127 + 1, dil)
                tiles = [(r, nb, tcols(r, nb), r * nbk + nb) for r in range(dil) for nb in range(nbk)]
                tiles += [(None, s3, slice(T + 128 * s3, T + 128 * s3 + 128), NTT + s3) for s3 in range(2)]
                for vi, (r, nb, cols, ti) in enumerate(tiles):
                    pv, pvk = P[(0, 1, 2, 3)[vi % 4]], "P%d" % (vi % 4)
                    for kc in range(KD):
                        S.mm(pv[:, 0:128], hT[:, kc, cols], wv[:, kc, :], start=(kc == 0), stop=(kc == KD - 1),
                             R=[wak, "hT"], W=[pvk] if kc == 0 else [], A=[pvk] if kc > 0 else [])
                    S.cp("act", V[:, ti, :], pv[:, 0:128], R=[pvk], W=["V"])
                    if r is None:
                        S.cp("dve", Vs[:, nb, g * 4 + h, :], pv[:, 0:128], R=[pvk], W=["Vs"])
                        S.cp("dve", v32[:], pv[:, 0:128], R=[pvk], W=["v32"])
                        for s2 in range(2):
                            S.dma("sp", kvs[l, 2 * nb + s2, :, 1, h, :], v32[64 * s2:64 * s2 + 8, :], R=["v32"])
                    elif dil * 128 * nb >= T - keep:
                        S.cp("dve", v32[:], pv[:, 0:128], R=[pvk], W=["v32"])
                        r0 = dil * 128 * nb + r - (T - keep)
                        S.dma("sp", kvp[l, r0:r0 + dil * 127 + 1:dil, 1, h, :], v32[:], R=["v32"])
                for ti in list(range((T - keep) // 128, NTT)) + [NTT, NTT + 1]:
                    S.tr(P[3][:, 128:256], K32[:, ti * 128:(ti + 1) * 128], ident[:], R=["K32", "ident"], W=["P3"])
                    S.cp("dve", k32[:], P[3][:, 128:256], R=["P3"], W=["k32"])
                    if ti < NTT:
                        r0 = ti * 128 - (T - keep)
                        S.dma("sp", kvp[l, r0:r0 + 128, 0, h, :], k32[:], R=["k32"])
                    else:
                        for s2 in range(2):
                            S.dma("sp", kvs[l, 2 * (ti - NTT) + s2, :, 0, h, :], k32[64 * s2:64 * s2 + 8, :], R=["k32"])
                recs = [Rec(), Rec()]
                bi = 0
                for r in range(dil):
                    for nb in range(nbk):
                        si = bi % 2
                        bi += 1
                        Sx = recs[si]
                        Ps, Po, Pd = (P[0], P[1], P[2]) if si == 0 else (P[4], P[5], P[6])
                        ks, ko, kd = ("P0", "P1", "P2") if si == 0 else ("P4", "P5", "P6")
                        pT_, pTk = pT2[si], "pT%d" % si
                        qc = tcols(r, nb)
                        kbs = [kb for kb in (nb - 1, nb) if kb >= 0]
                        for i, kb in enumerate(kbs):
                            Sx.mm(Ps[:, 0:128], KT[:, tcols(r, kb)], QT[:, qc], R=["KT", "QT"], W=[ks])
                            Sx.act(pT_[:], Ps[:, 0:128], AF.Exp, bias=nshift[:, 0:1], scale=sc_, R=[ks, "nshift"], W=[pTk])
                            Sx.tt("pool", pT_[:], pT_[:], (mcur if kb == nb else mprev)[:], ALU.mult, R=[pTk, "mask"], W=[pTk])
                            Sx.mm(Po[:, 0:128], V[:, r * nbk + kb, :], pT_[:], start=(i == 0), stop=(i == len(kbs) - 1),
                                  R=["V", pTk], W=[ko] if i == 0 else [], A=[ko] if i > 0 else [])
                            Sx.mm(Pd[:, 0:128], onesb[:], pT_[:], start=(i == 0), stop=(i == len(kbs) - 1),
                                  R=["onesb", pTk], W=[kd] if i == 0 else [], A=[kd] if i > 0 else [])
                        Sx.tt("dve", Oacc[:, qc], Oacc[:, qc], Po[:, 0:128], ALU.add, R=["Oacc", ko], W=["Oacc"])
                        Sx.tt("dve", Dacc[:, qc], Dacc[:, qc], Pd[:, 0:128], ALU.add, R=["Dacc", kd], W=["Dacc"])
                replay_interleaved(S, recs)
            for (t0, n) in X["blocks"]:
                if t0 >= T:
                    continue
                S.ts("dve", rn[:, 0:n], Dacc[:, t0:t0 + n], 1e-30, None, ALU.max, R=["Dacc"], W=["rn"])
                S.op("dve", (lambda a, b: (lambda e: e.reciprocal(a, b)))(rn[:, 0:n], rn[:, 0:n]), ["rn"], ["rn"])
                S.tt("dve", ost[:, 0:n], Oacc[:, t0:t0 + n], rn[:, 0:n], ALU.mult, R=["Oacc", "rn"], W=["ost"])
                S.dma("sp", oT_d[h * 128:(h + 1) * 128, t0:t0 + n], ost[:, 0:n], R=["ost"])
        S.barrier()
        hs.close()
        sbp = sbo
        kvc2 = [[sbp("kvc%d%d" % (i, j), [128, 1024]) for j in range(3)] for i in range(2)]
        kcT2 = [sbp("kcT%d" % i, [128, 512], BF16) for i in range(2)]
        vcb2 = [sbp("vcb%d" % i, [128, 512], BF16) for i in range(2)]
        pS2 = [sbp("pS%d" % i, [128, 32], BF16) for i in range(2)]
        for g in range(3):
            dil = DIL[g]
            kvin = I["kv%d" % g]
            idx0 = (0, 1, 5)[g]
            nr = min(dil, 8)
            recs = [Rec(), Rec()]
            for s in range(NS):
                si = s % 2
                Sx = recs[si]
                Pt, Ps, Po = (P[0], P[1], P[2]) if si == 0 else (P[4], P[5], P[6])
                kt, ks, ko = ("P0", "P1", "P2") if si == 0 else ("P4", "P5", "P6")
                pS_, pSk = pS2[si], "pS%d" % si
                kcT_, kcTk = kcT2[si], "kcT%d" % si
                vcb_, vcbk = vcb2[si], "vcb%d" % si
                q0 = 64 * s
                tl = s // 2
                for r in range(nr + 1):
                    if r < nr:
                        kv_, kvk = kvc2[si][r % 3], "kvc%d%d" % (si, r % 3)
                        Sx.dma("sp", kv_[:], kvin[l, s, r:r + dil * 127 + 1:dil, :], W=[kvk])
                        for hh in range(4):
                            Sx.tr(Pt[:, hh * 128:(hh + 1) * 128], kv_[:, hh * 128:(hh + 1) * 128], ident[:], R=[kvk, "ident"],
                                  W=[kt] if hh == 0 else [], A=[kt] if hh else [])
                        Sx.cp("act", kcT_[:], Pt[:], R=[kt], W=[kcTk])
                        Sx.cp("dve", vcb_[:], kv_[:, 512:1024], R=[kvk], W=[vcbk])
                        for hh in range(4):
                            Sx.mm(Ps[:, hh * 8:(hh + 1) * 8], kcT_[:, hh * 128:(hh + 1) * 128], QTs[:, hh, g, s, :], R=[kcTk, "QTs"],
                                  W=[ks] if hh == 0 else [], A=[ks] if hh else [])
                        mk = msc[:, (idx0 + r) * 32:(idx0 + r) * 32 + 32]
                        vvs = [vcb_[:, hh * 128:(hh + 1) * 128] for hh in range(4)]
                    else:
                        for hh in range(4):
                            Sx.mm(Ps[:, hh * 8:(hh + 1) * 8], KTs[:, hh, g, tl * 128:(tl + 1) * 128], QTs[:, hh, g, s, :], R=["KTs", "QTs"],
                                  W=[ks] if hh == 0 else [], A=[ks] if hh else [])
                        mk = msn[:, (g * 2 + s % 2) * 32:(g * 2 + s % 2) * 32 + 32]
                        vvs = [Vs[:, tl, g * 4 + hh, :] for hh in range(4)]
                    Sx.act(pS_[:], Ps[:, 0:32], AF.Exp, bias=nshift[:, 0:1], scale=sc_, R=[ks, "nshift"], W=[pSk])
                    Sx.tt("pool", pS_[:], pS_[:], mk, ALU.mult, R=[pSk, "mask"], W=[pSk])
                    for hh in range(4):
                        Sx.mm(Po[:, hh * 8:(hh + 1) * 8], vvs[hh], pS_[:, hh * 8:(hh + 1) * 8], R=[vcbk, "Vs", pSk],
                              W=[ko] if hh == 0 else [], A=[ko] if hh else [])
                    Sx.mm(Po[:, 32:64], onesb[:], pS_[:], R=["onesb", pSk], A=[ko])
                    Sx.tt("dve", OaccS[:, :, q0:q0 + 8], OaccS[:, :, q0:q0 + 8], Po[:, 0:32].rearrange("p (hh q) -> p hh q", hh=4), ALU.add,
                          R=["OaccS", ko], W=["OaccS"])
                    Sx.tt("dve", DaccS[:, :, q0:q0 + 8], DaccS[:, :, q0:q0 + 8], Po[:, 32:64].rearrange("p (hh q) -> p hh q", hh=4), ALU.add,
                          R=["DaccS", ko], W=["DaccS"])
            replay_interleaved(S, recs)
        for h in range(4):
            S.ts("dve", rn[:, 0:256], DaccS[:, h, :], 1e-30, None, ALU.max, R=["DaccS"], W=["rn"])
            S.op("dve", (lambda a, b: (lambda e: e.reciprocal(a, b)))(rn[:, 0:256], rn[:, 0:256]), ["rn"], ["rn"])
            S.tt("dve", ost[:, 0:256], OaccS[:, h, :], rn[:, 0:256], ALU.mult, R=["OaccS", "rn"], W=["ost"])
            S.dma("sp", oT_d[h * 128:(h + 1) * 128, T:T + 256], ost[:, 0:256], R=["ost"])
```

```python
import contextlib
import numpy as np
import concourse.bass as bass
import concourse.mybir as mybir
from concourse.bass_utils import run_bass_kernel_spmd

F32 = mybir.dt.float32
BF16 = mybir.dt.bfloat16
AF = mybir.ActivationFunctionType
ALU = mybir.AluOpType

PAST_LEN = 8192
EPS = 1e-6
A_OFF, BQKV, BZ, BBETA, BA, CQ, CK, CV, CZ, GOFF = 0, 4608, 6144, 6656, 6660, 6664, 6920, 7176, 7688, 8200
NEG = -30000.0
SHIFT = 12.0


class Cfg:
    def __init__(self, D=1024, DFF=2816, T=4096, NS=4, DEPTH=2, dbg=False, mixers="abc"):
        self.D, self.DFF, self.T, self.NS, self.DEPTH, self.dbg = D, DFF, T, NS, DEPTH, dbg
        self.mixers = mixers
        self.KD = D // 128
        self.KF = DFF // 128
        self.NTT = T // 128
        self.NT = self.NTT + 2
        self.NTOK = self.NT * 128
        self.NIN = GOFF + 3 * D


class Sched:
    ENGS = ("pe", "act", "dve", "pool", "sp")
    DMAQ = ("sp", "act", "pool")

    def __init__(self, nc, stack, nslots=12):
        self.nc = nc
        self.prog = {e: [] for e in self.ENGS}
        self.cnt = {e: 0 for e in self.ENGS}
        self.sem = {e: stack.enter_context(nc.semaphore("s_" + e)) for e in self.ENGS}
        self.nslots = nslots
        self.dsem = {q: [stack.enter_context(nc.semaphore("d_%s%d" % (q, i))) for i in range(nslots)] for q in self.DMAQ}
        self.dval = {q: [0] * nslots for q in self.DMAQ}
        self.dnext = {q: 0 for q in self.DMAQ}
        self.seen = {e: {} for e in self.ENGS}
        self.lastw = {}
        self.readers = {}

    def _semof(self, ev):
        if ev[0] == "E":
            return ("E", ev[1]), self.sem[ev[1]], ev[2]
        return ("D", ev[1], ev[2]), self.dsem[ev[1]][ev[2]], ev[3]

    def _emit_waits(self, e, waits):
        best = {}
        for ev in waits:
            key, sem, val = self._semof(ev)
            if self.seen[e].get(key, 0) >= val:
                continue
            if key not in best or best[key][1] < val:
                best[key] = (sem, val)
        for key, (sem, val) in best.items():
            self.seen[e][key] = val
            self.prog[e].append(("w", sem, val))

    def _deps(self, R, W):
        waits = []
        for k in R:
            w = self.lastw.get(k)
            if w is not None:
                waits.append(w)
            if k in PSUM_KEYS:
                rd = self.readers.get(k)
                if rd:
                    waits.extend(rd.values())
        for k in W:
            w = self.lastw.get(k)
            if w is not None:
                waits.append(w)
            rd = self.readers.get(k)
            if rd:
                waits.extend(rd.values())
        return waits

    def _record(self, ev, R, W):
        key = self._semof(ev)[0]
        for k in R:
            self.readers.setdefault(k, {})[key] = ev
        for k in W:
            self.lastw[k] = ev
            self.readers[k] = {}

    def op(self, e, fn, R=(), W=(), A=()):
        NOPS[0] += 1
        if NOPS[0] > LIMIT:
            return None
        self._emit_waits(e, self._deps(R, W))
        self.cnt[e] += 1
        self.prog[e].append(("i", fn, self.sem[e]))
        ev = ("E", e, self.cnt[e])
        self._record(ev, R, list(W) + list(A))
        return ev

    def dma(self, q, out, in_, R=(), W=(), **kw):
        NOPS[0] += 1
        if NOPS[0] > LIMIT:
            return None
        waits = self._deps(R, W)
        s = self.dnext[q]
        self.dnext[q] = (s + 1) % self.nslots
        if self.dval[q][s] > 0:
            waits.append(("D", q, s, self.dval[q][s]))
        self._emit_waits(q, waits)
        self.dval[q][s] += 16
        self.prog[q].append(("d", out, in_, kw, self.dsem[q][s]))
        ev = ("D", q, s, self.dval[q][s])
        self._record(ev, R, W)
        return ev

    def barrier(self):
        evs = [("E", e, self.cnt[e]) for e in self.ENGS if self.cnt[e] > 0]
        for q in self.DMAQ:
            for s in range(self.nslots):
                if self.dval[q][s] > 0:
                    evs.append(("D", q, s, self.dval[q][s]))
        for e in self.ENGS:
            self._emit_waits(e, evs)
        self.lastw.clear()
        self.readers.clear()

    def emit(self):
        nc = self.nc
        for q in self.DMAQ:
            self._emit_waits(q, [("D", q, s, self.dval[q][s]) for s in range(self.nslots) if self.dval[q][s] > 0])
        self._emit_waits("sp", [("E", e, self.cnt[e]) for e in self.ENGS if e != "sp" and self.cnt[e] > 0])

        def run(e):
            def body(eng):
                for it in self.prog[e]:
                    if it[0] == "w":
                        eng.wait_ge(it[1], it[2])
                    elif it[0] == "i":
                        it[1](eng).then_inc(it[2], 1)
                    else:
                        eng.dma_start(out=it[1], in_=it[2], **it[3]).then_inc(it[4], 16)
            return body

        with nc.Block() as block:
            block.tensor(run("pe"))
            block.scalar(run("act"))
            block.vector(run("dve"))
            block.gpsimd(run("pool"))
            block.sync(run("sp"))

    def mm(self, out, lhsT, rhs, start=True, stop=True, R=(), W=(), A=()):
        return self.op("pe", lambda e: e.matmul(out, lhsT, rhs, start=start, stop=stop), R, W, A)

    def tr(self, out, in_, ident, R=(), W=(), A=()):
        return self.op("pe", lambda e: e.transpose(out, in_, ident), R, W, A)

    def act(self, out, in_, func, bias=0.0, scale=1.0, accum_out=None, R=(), W=()):
        if accum_out is None:
            return self.op("act", lambda e: e.activation(out, in_, func, bias=bias, scale=scale), R, W)
        return self.op("act", lambda e: e.activation(out, in_, func, bias=bias, scale=scale, accum_out=accum_out), R, W)

    def tt(self, eng, out, in0, in1, op, R=(), W=()):
        return self.op(eng, lambda e: e.tensor_tensor(out, in0, in1, op), R, W)

    def ts(self, eng, out, in0, s1, s2, op0, op1=None, R=(), W=()):
        if op1 is None:
            return self.op(eng, lambda e: e.tensor_scalar(out, in0, s1, None, op0), R, W)
        return self.op(eng, lambda e: e.tensor_scalar(out, in0, s1, s2, op0, op1), R, W)

    def stt(self, eng, out, in0, scalar, in1, op0, op1, R=(), W=()):
        return self.op(eng, lambda e: e.scalar_tensor_tensor(out, in0, scalar, in1, op0, op1), R, W)

    def cp(self, eng, out, in_, R=(), W=()):
        if eng == "act":
            return self.op("act", lambda e: e.copy(out, in_), R, W)
        return self.op(eng, lambda e: e.tensor_copy(out, in_), R, W)

    def memset(self, eng, ap, val, W=()):
        return self.op(eng, lambda e: e.memset(ap, val), (), W)


class Rec:
    def __init__(self):
        self.calls = []

    def __getattr__(self, name):
        def f(*a, **k):
            self.calls.append((name, a, k))
        return f


def replay_interleaved(S, recs):
    n = max(len(r.calls) for r in recs)
    for i in range(n):
        for r in recs:
            if i < len(r.calls):
                name, a, k = r.calls[i]
                getattr(S, name)(*a, **k)


def _tile_consts(bs, nvalid, gam):
    p = np.arange(128)
    blk, loc = p // bs, p % bs
    same = blk[:, None] == blk[None, :]
    val = loc < nvalid
    c = {}
    c["tri"] = (same & (p[:, None] <= p[None, :])).astype(np.float32)
    c["blk"] = same.astype(np.float32)
    c["negS"] = np.where(same & (p[:, None] > p[None, :]), 0.0, NEG).astype(np.float32)
    c["negIT"] = np.where(same & (p[None, :] >= p[:, None]), 0.0, NEG).astype(np.float32)
    c["valid"] = val.astype(np.float32)[:, None].copy()
    dtc = np.zeros((128, 4, 128), np.float64)
    for h in range(4):
        d = (p[None, :] - p[:, None]).astype(np.float64)
        dtc[:, h, :] = np.where(same & (d >= 0) & val[:, None] & val[None, :], gam[h] ** np.maximum(d, 0), 0.0)
    c["dtc"] = dtc.reshape(128, 512).astype(np.float32)
    qd = np.zeros((128, 2, 128), np.float64)
    gcol = np.zeros((128, 2), np.float64)
    for pr in range(2):
        for half in range(2):
            h = 2 * pr + half
            qd[64 * half:64 * half + 64, pr, :] = (gam[h] ** (loc + 1.0))[None, :]
            gcol[64 * half:64 * half + 64, pr] = gam[h] ** nvalid
    c["qdec"] = qd.reshape(128, 256).astype(np.float32)
    c["gC"] = gcol.astype(np.float32)
    kd = np.zeros((128, 4, 64), np.float64)
    for h in range(4):
        kd[:, h, :] = np.where(val, gam[h] ** np.maximum(nvalid - 1.0 - loc, 0), 0.0)[:, None]
    c["kdec"] = kd.reshape(128, 256).astype(np.float32)
    return c


def make_consts(cfg):
    T, NTOK = cfg.T, cfg.NTOK
    pos = np.zeros(NTOK, np.float32)
    pos[:T] = np.arange(T)
    for s in range(4):
        pos[T + 64 * s:T + 64 * s + 64] = PAST_LEN + np.arange(64)
    C = {}
    C["ident"] = np.eye(128, dtype=np.float32)
    C["ones"] = np.ones((128, 128), np.float32)
    d = np.arange(128)
    invA = (np.float32(10000.0) ** (-np.arange(0, 128, 2, dtype=np.float32) / np.float32(128))).astype(np.float32)
    angA = (pos[None, :] * invA[d % 64][:, None]).astype(np.float32).astype(np.float64)
    C["cosA"] = np.cos(angA).astype(np.float32)
    C["sinA"] = (np.sin(angA) * np.where(d < 64, -1.0, 1.0)[:, None]).astype(np.float32)
    rotA = np.zeros((128, 128), np.float32)
    rotA[(d + 64) % 128, d] = 1.0
    C["rotA"] = rotA
    invC = (np.float32(10000.0) ** (-np.arange(0, 64, 2, dtype=np.float32) / np.float32(64))).astype(np.float32)
    dd = d % 64
    angC = (pos[None, :] * invC[dd % 32][:, None]).astype(np.float32).astype(np.float64)
    C["cosC"] = np.cos(angC).astype(np.float32)
    C["sinC"] = (np.sin(angC) * np.where(dd < 32, -1.0, 1.0)[:, None]).astype(np.float32)
    rotC = np.zeros((128, 128), np.float32)
    rotC[(d // 64) * 64 + (dd + 32) % 64, d] = 1.0
    C["rotC"] = rotC
    k = np.arange(128)
    C["mcur"] = np.tile((k[:, None] <= k[None, :]).astype(np.float32), (1, 4))
    C["mprev"] = np.tile((k[:, None] >= k[None, :]).astype(np.float32), (1, 4))
    msc = np.zeros((13, 128, 8), np.float32)
    msn = np.zeros((3, 2, 128, 8), np.float32)
    idx = 0
    for g, dil in enumerate((1, 4, 16)):
        for r in range(min(dil, 8)):
            for i in range(8):
                if i % dil == r:
                    msc[idx, :, i] = (k >= i // dil)
            idx += 1
        for s2 in range(2):
            for i in range(8):
                for j in range(8):
                    if j <= i and (i - j) % dil == 0:
                        msn[g, s2, 64 * s2 + j, i] = 1.0
    C["msc"] = np.ascontiguousarray(np.repeat(msc.transpose(1, 0, 2)[:, :, None, :], 4, axis=2)).reshape(128, 13 * 32)
    C["msn"] = np.ascontiguousarray(np.repeat(msn.transpose(2, 0, 1, 3).reshape(128, 6, 1, 8), 4, axis=2)).reshape(128, 6 * 32)
    gam = [1.0 - 2.0 ** (-5.0 - h) for h in range(4)]
    for nm, (bs, nv) in (("p", (64, 64)), ("s", (64, 8))):
        for kk, v in _tile_consts(bs, nv, gam).items():
            C[kk + "_" + nm] = v
    return C


CUT = 99
LIMIT = 10 ** 9
PADOPS = 0
PSUM_KEYS = frozenset(["P%d" % i for i in range(7)] + ["PT"])
VAR = 0
NOPS = [0]


def build(cfg):
    D, DFF, T, NS, DEPTH = cfg.D, cfg.DFF, cfg.T, cfg.NS, cfg.DEPTH
    KD, KF, NTT, NT, NTOK, NIN = cfg.KD, cfg.KF, cfg.NTT, cfg.NT, cfg.NTOK, cfg.NIN
    assert NS == 4 and T % 2048 == 0
    nc = bass.Bass("TRN2", target_bir_lowering=False)
    consts = make_consts(cfg)

    def din(name, shape, dt=F32):
        return nc.dram_tensor(name, list(shape), dt, kind="ExternalInput").ap()

    def dout(name, shape, dt=F32):
        return nc.dram_tensor(name, list(shape), dt, kind="ExternalOutput").ap()

    def dscr(name, shape, dt):
        return nc.dram_tensor(name, list(shape), dt, kind="ExternalOutput" if cfg.dbg else "Internal").ap()

    I = {}
    I["xp"] = din("xp", [T, D])
    I["xs"] = din("xs", [NS * 8, D])
    I["kv0"] = din("kv0", [DEPTH, NS, 128, 1024])
    I["kv1"] = din("kv1", [DEPTH, NS, 512, 1024])
    I["kv2"] = din("kv2", [DEPTH, NS, 2048, 1024])
    I["sconv"] = din("sconv", [DEPTH, NS * 3, 1536])
    I["sS"] = din("sS", [DEPTH, NS, 4, 128, 128])
    I["sR"] = din("sR", [DEPTH, NS, 2, 128, 128])
    I["g1B"] = din("g1B", [DEPTH, 128, D])
    I["g2B"] = din("g2B", [DEPTH, 128, D])
    I["w_in"] = din("w_in", [DEPTH, D, NIN])
    I["aqg"] = din("aqg", [DEPTH, 128, 1])
    I["akg"] = din("akg", [DEPTH, 128, 1])
    I["convw"] = din("convw", [DEPTH, 128, 12, 4])
    I["alogB"] = din("alogB", [DEPTH, 128, 4])
    I["dtbB"] = din("dtbB", [DEPTH, 128, 4])
    I["bog"] = din("bog", [DEPTH, 128, 1])
    I["cog"] = din("cog", [DEPTH, 128, 1])
    I["w_oa"] = din("w_oa", [DEPTH, 512, D])
    I["w_ob"] = din("w_ob", [DEPTH, 512, D])
    I["w_oc"] = din("w_oc", [DEPTH, 512, D])
    I["w_out"] = din("w_out", [DEPTH, D, D])
    I["w_fi"] = din("w_fi", [DEPTH, D, 2 * DFF])
    I["w_fo"] = din("w_fo", [DEPTH, DFF, D])
    for k, v in consts.items():
        I["c_" + k] = din("c_" + k, v.shape)

    O = {}
    O["yp"] = dout("yp", [T, D])
    O["ys"] = dout("ys", [NS * 8, D])
    KEEP = (128, 512, min(2048, T))
    for g in range(3):
        O["kvp%d" % g] = dout("kvp%d" % g, [DEPTH, KEEP[g], 2, 4, 128])
        O["kvs%d" % g] = dout("kvs%d" % g, [DEPTH, NS, 8, 2, 4, 128])
    O["convp"] = dout("convp", [DEPTH, 3, 1536])
    O["Sp"] = dout("Sp", [DEPTH, 4, 128, 128])
    O["Rp"] = dout("Rp", [DEPTH, 2, 128, 128])
    O["convs"] = dout("convs", [DEPTH, NS * 3, 1536])
    O["Ss"] = dout("Ss", [DEPTH, NS, 4, 128, 128])
    O["Rs"] = dout("Rs", [DEPTH, NS, 2, 128, 128])
    xres = dscr("xres", [NTOK, D], F32)
    oT_d = dscr("oT_d", [12 * 128, NTOK], BF16)
    sg_d = dscr("sg_d", [3 * D, NTOK], BF16)
    WB = {}
    for nm, shp in (("w_oa", [512, D]), ("w_ob", [512, D]), ("w_oc", [512, D]), ("w_out", [D, D]), ("w_fi", [D, 2 * DFF]), ("w_fo", [DFF, D])):
        WB[nm] = nc.dram_tensor("wbf_" + nm, [DEPTH] + shp, BF16, kind="Internal").ap()

    with contextlib.ExitStack() as st:
        S = Sched(nc, st)
        sb = lambda n, s, d=F32: st.enter_context(nc.sbuf_tensor(n, list(s), d))
        P = [st.enter_context(nc.psum_tensor("p%d" % i, [128, 512], F32)) for i in range(7)]
        PT = st.enter_context(nc.psum_tensor("pT", [128, 1024], BF16))

        ident = sb("ident", [128, 128])
        identb = sb("identb", [128, 128], BF16)
        ones = sb("ones", [128, 128])
        onesb = sb("onesb", [128, 128], BF16)
        S.dma("sp", ident[:], I["c_ident"], W=["ident"])
        S.dma("pool", identb[:], I["c_ident"], W=["identb"])
        S.dma("sp", ones[:], I["c_ones"], W=["ones"])
        S.dma("pool", onesb[:], I["c_ones"], W=["onesb"])
        def convert_layer(l_):
            for nm in ("w_oa", "w_ob", "w_oc", "w_out", "w_fi", "w_fo"):
                rows = WB[nm].shape[1]
                for r0 in range(0, rows, 512):
                    r1 = min(rows, r0 + 512)
                    S.dma("pool", WB[nm][l_, r0:r1, :], I[nm][l_, r0:r1, :], W=["wbf_%s_%d_%d" % (nm, l_, r0)])
        NBLK = (NTOK + 511) // 512
        blocks = [(b * 512, min(512, NTOK - b * 512)) for b in range(NBLK)]

        def wview(ap2d, c0, ncols):
            return ap2d.rearrange("(kc p) n -> p kc n", p=128)[:, :, c0:c0 + ncols]

        def rstd_from_ss(dst, src, scale, R, W):
            S.act(dst, src, AF.Ln, bias=EPS, scale=scale, R=R, W=W)
            S.act(dst, dst, AF.Exp, scale=-0.5, R=W, W=W)

        def norm_tile(x_t, xkey, gB, h_t, hkey, junk, ss, hTdst, tcols, hTkey):
            S.memset("dve", ss[:, 0:1], 0.0, W=["ss"])
            S.act(junk[:], x_t, AF.Square, accum_out=ss[:, 0:1], R=[xkey, "ss"], W=["junk", "ss"])
            rstd_from_ss(ss[:, 0:1], ss[:, 0:1], 1.0 / D, ["ss"], ["ss"])
            S.stt("dve", h_t, x_t, ss[:, 0:1], gB, ALU.mult, ALU.mult, R=[xkey, "ss", "gB"], W=[hkey])
            for kc in range(KD):
                S.tr(PT[:, kc * 128:(kc + 1) * 128], h_t[:, kc * 128:(kc + 1) * 128], identb[:],
                     R=[hkey, "identb"], W=["PT"] if kc == 0 else [], A=["PT"] if kc > 0 else [])
            S.cp("dve", hTdst[:, :, tcols], PT[:, 0:KD * 128].rearrange("p (k n) -> p k n", k=KD), R=["PT"], W=[hTkey])

        for l in range(DEPTH):
            w_in = I["w_in"][l]
            lay = contextlib.ExitStack()
            hT = lay.enter_context(nc.sbuf_tensor("hT%d" % l, [128, KD, NTOK], BF16))
            with contextlib.ExitStack() as ph:
                sbp = lambda n, s, d=F32: ph.enter_context(nc.sbuf_tensor("a%d_%s" % (l, n), list(s), d))
                gB = sbp("gB", [128, D])
                S.dma("sp", gB[:], I["g1B"][l], W=["gB"])
                xt = [sbp("xt%d" % i, [128, D]) for i in range(2)]
                ht = [sbp("ht%d" % i, [128, D], BF16) for i in range(2)]
                junk = sbp("junk", [128, D])
                ss = sbp("ss", [128, 1])
                for i in range(NT):
                    x_t, h_t = xt[i % 2], ht[i % 2]
                    xk, hk = "xt%d" % (i % 2), "ht%d" % (i % 2)
                    if l == 0:
                        if i < NTT:
                            S.dma("sp", x_t[:], I["xp"][i * 128:(i + 1) * 128, :], W=[xk])
                        else:
                            S.memset("dve", x_t[:], 0.0, W=[xk])
                            for s2 in range(2):
                                s = 2 * (i - NTT) + s2
                                S.dma("sp", x_t[64 * s2:64 * s2 + 8, :], I["xs"][8 * s:8 * s + 8, :], W=[xk])
                    else:
                        S.dma("sp", x_t[:], xres[i * 128:(i + 1) * 128, :], W=[xk])
                    norm_tile(x_t[:], xk, gB[:], h_t[:], hk, junk, ss, hT, slice(i * 128, (i + 1) * 128), "hT")
            S.barrier()
            with contextlib.ExitStack() as ph:
                sbp = lambda n, s, d=F32: ph.enter_context(nc.sbuf_tensor("g%d_%s" % (l, n), list(s), d))
                wg = [sbp("wg%d" % i, [128, KD, 512], BF16) for i in range(2)]
                sgo = [sbp("sgo%d" % i, [128, 4, 512], BF16) for i in range(2)]
                ncb = 3 * D // 512
                it = 0
                for cb in range(ncb):
                    w_t, wk = wg[cb % 2], "wg%d" % (cb % 2)
                    S.dma("pool", w_t[:], wview(w_in, GOFF + cb * 512, 512), W=[wk])
                    for (t0, n) in blocks:
                        so, sk = sgo[it % 2], "sgo%d" % (it % 2)
                        for j in range(4):
                            pp, pk = P[(it * 4 + j) % 4], "P%d" % ((it * 4 + j) % 4)
                            for kc in range(KD):
                                S.mm(pp[:, 0:n], w_t[:, kc, j * 128:(j + 1) * 128], hT[:, kc, t0:t0 + n],
                                     start=(kc == 0), stop=(kc == KD - 1), R=[wk, "hT"],
                                     W=[pk] if kc == 0 else [], A=[pk] if kc > 0 else [])
                            S.act(so[:, j, 0:n], pp[:, 0:n], AF.Sigmoid, R=[pk], W=[sk])
                        S.dma("sp", sg_d[cb * 512:(cb + 1) * 512, t0:t0 + n].rearrange("(j p) n -> p j n", p=128),
                              so[:, :, 0:n], R=[sk])
                        it += 1
            S.barrier()
            X = dict(nc=nc, S=S, cfg=cfg, l=l, I=I, O=O, P=P, PT=PT, hT=hT, oT_d=oT_d, ident=ident, identb=identb,
                     ones=ones, onesb=onesb, blocks=blocks, wview=wview, rstd=rstd_from_ss, w_in=w_in)
            for mi, ch in enumerate("abc"):
                if ch not in cfg.mixers:
                    zt = lay.enter_context(nc.sbuf_tensor("zt%d_%d" % (l, mi), [128, 4, NTOK], BF16))
                    S.memset("dve", zt[:], 0.0, W=["zt"])
                    S.dma("sp", oT_d[mi * 512:(mi + 1) * 512, :].rearrange("(c p) n -> p c n", p=128), zt[:], R=["zt"])
            S.barrier()
            X["bgwork"] = lambda l_=l: convert_layer(l_)
            if "c" in cfg.mixers:
                mixer_c(X)
                S.barrier()
            else:
                X["bgwork"]()
            if "b" in cfg.mixers:
                mixer_b(X)
                S.barrier()
            if "a" in cfg.mixers:
                mixer_a(X)
                S.barrier()
            lay.close()
            with contextlib.ExitStack() as ph:
                sbp = lambda n, s, d=F32: ph.enter_context(nc.sbuf_tensor("d%d_%s" % (l, n), list(s), d))
                g2B = sbp("g2B", [128, D])
                S.dma("sp", g2B[:], I["g2B"][l], W=["gB"])
                oTb = sbp("oTb", [128, 12, 512], BF16)
                sgb = sbp("sgb", [128, 3 * KD, 512], BF16)
                mT = sbp("mT", [128, KD, 512], BF16)
                h2T = sbp("h2T", [128, KD, 512], BF16)
                aT = sbp("aT", [128, KF, 512], BF16)
                x1 = [sbp("x1_%d" % i, [128, D]) for i in range(4)]
                h2 = sbp("h2", [128, D], BF16)
                junk = sbp("junk", [128, D])
                ss = sbp("ss", [128, 1])
                tmp = [sbp("tmp%d" % i, [128, 512]) for i in range(3)]
                NW = 3
                wb = [sbp("wb%d" % i, [128, 8, 512], BF16) for i in range(NW)]
                wfo = sbp("wfo", [128, KF, 512], BF16)
                wctr = [0]

                def loadw(src2d, c0, ncols, k0, nk):
                    i = wctr[0] % NW
                    wctr[0] += 1
                    v = src2d.rearrange("(kc p) n -> p kc n", p=128)[:, k0:k0 + nk, c0:c0 + ncols]
                    S.dma("sp", wb[i][:, 0:nk, 0:ncols], v, W=["wb%d" % i])
                    return wb[i], "wb%d" % i

                pctr = [0]

                def nextp():
                    i = pctr[0] % 6
                    pctr[0] += 1
                    return P[i], "P%d" % i

                for (t0, n) in blocks:
                    ntl = n // 128
                    S.dma("sp", oTb[:, :, 0:n], oT_d[:, t0:t0 + n].rearrange("(c p) n -> p c n", p=128), W=["oTb"])
                    S.dma("sp", sgb[:, :, 0:n], sg_d[:, t0:t0 + n].rearrange("(c p) n -> p c n", p=128), W=["sgb"])
                    for cb in range(D // 512):
                        ws = []
                        for j, nm in enumerate(("w_oa", "w_ob", "w_oc")):
                            ws.append(loadw(WB[nm][l], cb * 512, 512, 0, 4))
                        for oc4 in range(4):
                            oc = cb * 4 + oc4
                            for j in range(3):
                                w_t, wk = ws[j]
                                pp, pk = nextp()
                                for kc in range(4):
                                    S.mm(pp[:, 0:n], w_t[:, kc, oc4 * 128:(oc4 + 1) * 128], oTb[:, 4 * j + kc, 0:n],
                                         start=(kc == 0), stop=(kc == 3), R=[wk, "oTb"],
                                         W=[pk] if kc == 0 else [], A=[pk] if kc > 0 else [])
                                S.tt("dve", tmp[j][:, 0:n], pp[:, 0:n], sgb[:, j * KD + oc, 0:n], ALU.mult,
                                     R=[pk, "sgb"], W=["tmp%d" % j])
                            S.tt("pool", tmp[0][:, 0:n], tmp[0][:, 0:n], tmp[1][:, 0:n], ALU.add, R=["tmp0", "tmp1"], W=["tmp0"])
                            S.tt("pool", mT[:, oc, 0:n], tmp[0][:, 0:n], tmp[2][:, 0:n], ALU.add, R=["tmp0", "tmp2"], W=["mT"])
                    for tt_ in range(ntl):
                        tok0 = t0 + tt_ * 128
                        ti = tok0 // 128
                        xk = "x1_%d" % tt_
                        x_t = x1[tt_]
                        if l == 0:
                            if ti < NTT:
                                S.dma("sp", x_t[:], I["xp"][tok0:tok0 + 128, :], W=[xk])
                            else:
                                S.memset("dve", x_t[:], 0.0, W=[xk])
                                for s2 in range(2):
                                    s = 2 * (ti - NTT) + s2
                                    S.dma("sp", x_t[64 * s2:64 * s2 + 8, :], I["xs"][8 * s:8 * s + 8, :], W=[xk])
                        else:
                            S.dma("sp", x_t[:], xres[tok0:tok0 + 128, :], W=[xk])
                    for cb in range(D // 512):
                        w_t, wk = loadw(WB["w_out"][l], cb * 512, 512, 0, KD)
                        for tt_ in range(ntl):
                            pp, pk = nextp()
                            xk = "x1_%d" % tt_
                            for kc in range(KD):
                                S.mm(pp[:], mT[:, kc, tt_ * 128:(tt_ + 1) * 128], w_t[:, kc, :], start=(kc == 0), stop=(kc == KD - 1),
                                     R=[wk, "mT"], W=[pk] if kc == 0 else [], A=[pk] if kc > 0 else [])
                            S.tt("dve", x1[tt_][:, cb * 512:(cb + 1) * 512], x1[tt_][:, cb * 512:(cb + 1) * 512], pp[:], ALU.add,
                                 R=[pk, xk], W=[xk])
                    for tt_ in range(ntl):
                        norm_tile(x1[tt_][:], "x1_%d" % tt_, g2B[:], h2[:], "h2", junk, ss, h2T,
                                  slice(tt_ * 128, (tt_ + 1) * 128), "h2T")
                    for fb in range((DFF + 511) // 512):
                        f0 = fb * 512
                        fn_ = min(512, DFF - f0)
                        wg_t, wgk = loadw(WB["w_fi"][l], f0, fn_, 0, KD)
                        wu_t, wuk = loadw(WB["w_fi"][l], DFF + f0, fn_, 0, KD)
                        for fc in range(fn_ // 128):
                            pg, pgk = nextp()
                            pu, puk = nextp()
                            for kc in range(KD):
                                S.mm(pg[:, 0:n], wg_t[:, kc, fc * 128:(fc + 1) * 128], h2T[:, kc, 0:n], start=(kc == 0), stop=(kc == KD - 1),
                                     R=[wgk, "h2T"], W=[pgk] if kc == 0 else [], A=[pgk] if kc > 0 else [])
                            for kc in range(KD):
                                S.mm(pu[:, 0:n], wu_t[:, kc, fc * 128:(fc + 1) * 128], h2T[:, kc, 0:n], start=(kc == 0), stop=(kc == KD - 1),
                                     R=[wuk, "h2T"], W=[puk] if kc == 0 else [], A=[puk] if kc > 0 else [])
                            S.act(tmp[0][:, 0:n], pg[:, 0:n], AF.Silu, R=[pgk], W=["tmp0"])
                            S.tt("dve", aT[:, f0 // 128 + fc, 0:n], tmp[0][:, 0:n], pu[:, 0:n], ALU.mult, R=["tmp0", puk], W=["aT"])
                    for cb in range(D // 512):
                        S.dma("sp", wfo[:], WB["w_fo"][l].rearrange("(kc p) n -> p kc n", p=128)[:, :, cb * 512:(cb + 1) * 512], W=["wfo"])
                        for tt_ in range(ntl):
                            pp, pk = nextp()
                            xk = "x1_%d" % tt_
                            for fc in range(KF):
                                S.mm(pp[:], aT[:, fc, tt_ * 128:(tt_ + 1) * 128], wfo[:, fc, :], start=(fc == 0), stop=(fc == KF - 1),
                                     R=["wfo", "aT"], W=[pk] if fc == 0 else [], A=[pk] if fc > 0 else [])
                            S.tt("dve", x1[tt_][:, cb * 512:(cb + 1) * 512], x1[tt_][:, cb * 512:(cb + 1) * 512], pp[:], ALU.add,
                                 R=[pk, xk], W=[xk])
                    dst = xres if l < DEPTH - 1 else None
                    for tt_ in range(ntl):
                        tok0 = t0 + tt_ * 128
                        ti = tok0 // 128
                        xk = "x1_%d" % tt_
                        if dst is not None:
                            S.dma("sp", xres[tok0:tok0 + 128, :], x1[tt_][:], R=[xk])
                        else:
                            if ti < NTT:
                                S.dma("sp", O["yp"][tok0:tok0 + 128, :], x1[tt_][:], R=[xk])
                            else:
                                for s2 in range(2):
                                    s = 2 * (ti - NTT) + s2
                                    S.dma("sp", O["ys"][8 * s:8 * s + 8, :], x1[tt_][64 * s2:64 * s2 + 8, :], R=[xk])
            S.barrier()
        for _ in range(PADOPS):
            S.memset("dve", ones[:], 1.0, W=["ones"])
        S.emit()
    return nc


def prep_core_inputs(cfg, inp, core, consts):
    D, NS, DEPTH = cfg.D, cfg.NS, cfg.DEPTH
    f = lambda a: np.ascontiguousarray(np.asarray(a, dtype=np.float32))
    nb = inp["x_prompt"].shape[0]
    ss = slice(core * NS, (core + 1) * NS)
    m = {}
    m["xp"] = f(inp["x_prompt"][core % nb])
    m["xs"] = f(inp["x_sample"][ss]).reshape(NS * 8, D)
    m["kv0"] = f(inp["cache_a_kv0"][:, ss]).reshape(DEPTH, NS, -1, 1024)
    m["kv1"] = f(inp["cache_a_kv1"][:, ss]).reshape(DEPTH, NS, -1, 1024)
    m["kv2"] = f(inp["cache_a_kv2"][:, ss]).reshape(DEPTH, NS, -1, 1024)
    m["sconv"] = f(inp["state_b_conv"][:, ss]).reshape(DEPTH, NS * 3, 1536)
    m["sS"] = f(inp["state_b_S"][:, ss])
    m["sR"] = f(inp["state_c_R"][:, ss]).reshape(DEPTH, NS, 2, 128, 128)
    m["g1B"] = f(np.broadcast_to(np.asarray(inp["norm1_g"])[:, None, :], (DEPTH, 128, D)))
    m["g2B"] = f(np.broadcast_to(np.asarray(inp["norm2_g"])[:, None, :], (DEPTH, 128, D)))
    m["w_in"] = f(inp["w_in"])
    m["aqg"] = f(inp["a_q_norm_g"]).reshape(DEPTH, 128, 1)
    m["akg"] = f(inp["a_k_norm_g"]).reshape(DEPTH, 128, 1)
    m["convw"] = f(np.asarray(inp["b_conv_w"]).reshape(DEPTH, 4, 12, 128).transpose(0, 3, 2, 1))
    m["alogB"] = f(np.broadcast_to(np.asarray(inp["b_a_log"])[:, None, :], (DEPTH, 128, 4)))
    m["dtbB"] = f(np.broadcast_to(np.asarray(inp["b_dt_bias"])[:, None, :], (DEPTH, 128, 4)))
    m["bog"] = f(inp["b_out_norm_g"]).reshape(DEPTH, 128, 1)
    m["cog"] = f(inp["c_out_norm_g"]).reshape(DEPTH, 128, 1)
    m["w_oa"] = f(inp["w_out_a"])
    m["w_ob"] = f(inp["w_out_b"])
    m["w_oc"] = f(inp["w_out_c"])
    m["w_out"] = f(inp["w_out"])
    m["w_fi"] = f(inp["w_ffn_in"])
    m["w_fo"] = f(inp["w_ffn_out"])
    for k, v in consts.items():
        m["c_" + k] = v
    return m


def assemble(cfg, res, nb, ncores):
    DEPTH, NS = cfg.DEPTH, cfg.NS
    r = res
    st = lambda k, cores: np.stack([np.asarray(r[c][k]) for c in cores])
    pc = list(range(nb))
    ac = list(range(ncores))
    yp = st("yp", pc)
    ys = np.concatenate([np.asarray(r[c]["ys"]).reshape(NS, 8, cfg.D) for c in ac], axis=0)
    outs = [yp, ys]
    for g in range(3):
        outs.append(st("kvp%d" % g, pc).transpose(1, 0, 2, 3, 4, 5))
    outs.append(st("convp", pc).transpose(1, 0, 2, 3))
    outs.append(st("Sp", pc).transpose(1, 0, 2, 3, 4))
    outs.append(st("Rp", pc).transpose(1, 0, 2, 3, 4).reshape(DEPTH, nb, 4, 64, 128))
    for g in range(3):
        outs.append(np.concatenate([np.asarray(r[c]["kvs%d" % g]) for c in ac], axis=1))
    outs.append(np.concatenate([np.asarray(r[c]["convs"]).reshape(DEPTH, NS, 3, 1536) for c in ac], axis=1))
    outs.append(np.concatenate([np.asarray(r[c]["Ss"]) for c in ac], axis=1))
    outs.append(np.concatenate([np.asarray(r[c]["Rs"]).reshape(DEPTH, NS, 4, 64, 128) for c in ac], axis=1))
    return tuple(np.ascontiguousarray(o, dtype=np.float32) for o in outs)


def kernel(**inputs):
    cfg = Cfg()
    ncores = 8
    consts = make_consts(cfg)
    nc = build(cfg)
    in_maps = [prep_core_inputs(cfg, inputs, c, consts) for c in range(ncores)]
    res = run_bass_kernel_spmd(nc, in_maps, core_ids=list(range(ncores)))
    return assemble(cfg, res.results, inputs["x_prompt"].shape[0], ncores)


def _gated_norm_epilogue(X, sbp_bufs, P_o, pok, zcol0, gcol, gkey, tok0, out_stage, okey, stage_cols, Pz, pzk, Pss, pssk, wz, wzk):
    S, hT, KD, onesb = X["S"], X["hT"], X["cfg"].KD, X["onesb"]
    sq, rs, on, sz = sbp_bufs
    poks = list(pok) if isinstance(pok, (list, tuple)) else [pok]
    S.act(sq[:], P_o[:], AF.Square, R=poks, W=["e_sq"])
    S.mm(Pss[:], onesb[:], sq[:], R=["onesb", "e_sq"], W=[pssk])
    X["rstd"](rs[:], Pss[:], 1.0 / 128, [pssk], ["e_rs"])
    S.stt("dve", on[:], P_o[:], gcol, rs[:], ALU.mult, ALU.mult, R=poks + [gkey, "e_rs"], W=["e_on"])
    for h in range(4):
        for kc in range(KD):
            S.mm(Pz[:, h * 128:(h + 1) * 128], wz[:, kc, zcol0 + h * 128:zcol0 + (h + 1) * 128], hT[:, kc, tok0:tok0 + 128],
                 start=(kc == 0), stop=(kc == KD - 1), R=[wzk, "hT"],
                 W=[pzk] if (h == 0 and kc == 0) else [], A=[pzk] if not (h == 0 and kc == 0) else [])
    S.act(sz[:], Pz[:], AF.Silu, R=[pzk], W=["e_sz"])
    S.tt("dve", out_stage[:, :, stage_cols], on[:].rearrange("p (h n) -> p h n", h=4), sz[:].rearrange("p (h n) -> p h n", h=4),
         ALU.mult, R=["e_on", "e_sz"], W=[okey])


def mixer_c(X):
    nc, S, cfg, l, I, O, P, PT, hT = X["nc"], X["S"], X["cfg"], X["l"], X["I"], X["O"], X["P"], X["PT"], X["hT"]
    KD, NTT, NS = cfg.KD, cfg.NTT, cfg.NS
    w_in, oT_d, identb, wview = X["w_in"], X["oT_d"], X["identb"], X["wview"]
    with contextlib.ExitStack() as ph:
        sbp = lambda n, s, d=F32: ph.enter_context(nc.sbuf_tensor("c%d_%s" % (l, n), list(s), d))
        rotC = sbp("rotC", [128, 128], BF16)
        S.dma("pool", rotC[:], I["c_rotC"], W=["rotC"])
        tab = {}
        for ty in "ps":
            for nm, w in (("dtc", 512), ("qdec", 256), ("gC", 2), ("kdec", 256)):
                tab[nm + ty] = sbp(nm + ty, [128, w])
                S.dma("sp", tab[nm + ty][:], I["c_%s_%s" % (nm, ty)], W=["tab"])
        cog = sbp("cog", [128, 1])
        S.dma("sp", cog[:], I["cog"][l], W=["cog"])
        wq = sbp("wq", [128, KD, 256], BF16)
        wk = sbp("wk", [128, KD, 256], BF16)
        wv = sbp("wv", [128, KD, 512], BF16)
        wz = sbp("wz", [128, KD, 512], BF16)
        for t_, c0, w_ in ((wq, CQ, 256), (wk, CK, 256), (wv, CV, 512), (wz, CZ, 512)):
            S.dma("pool", t_[:], wview(w_in, c0, w_), W=["wc"])
        X["bgwork"]()
        ct = sbp("ct", [128, 512])
        sn = sbp("sn", [128, 512])
        qT = sbp("qT", [128, 2, 2, 512], BF16)
        S.memset("pool", qT[:], 0.0, W=["qT"])
        qdT = sbp("qdT", [128, 2, 512], BF16)
        kT = sbp("kT", [128, 2, 512], BF16)
        qb = sbp("qb", [128, 512], BF16)
        t1 = sbp("t1", [128, 512])
        t2 = sbp("t2", [128, 512])
        v_t = sbp("v_t", [128, 512], BF16)
        kd_t = sbp("kd_t", [128, 256], BF16)
        attn = sbp("attn", [128, 512], BF16)
        R32 = sbp("R32", [128, 2, 128])
        Rb = [sbp("Rb%d" % i, [128, 2, 128], BF16) for i in range(2)]
        ebuf = (sbp("e_sq", [128, 512], BF16), sbp("e_rs", [128, 512]), sbp("e_on", [128, 512]), sbp("e_sz", [128, 512]))
        ocs = sbp("ocs", [128, 4, 512], BF16)
        S.memset("dve", R32[:], 0.0, W=["R32"])
        rbi = 0
        for (t0, n) in X["blocks"]:
            ntl = n // 128
            sample = (t0 // 128 >= NTT)
            ty = "s" if sample else "p"
            if CUT <= 0:
                continue
            S.dma("sp", ct[:, 0:n], I["c_cosC"][:, t0:t0 + n], W=["ct"])
            S.dma("sp", sn[:, 0:n], I["c_sinC"][:, t0:t0 + n], W=["sn"])
            for which, w_t, dst in (("q", wq, qT), ("k", wk, kT)):
                sc = 1.0 if which == "q" else 0.125
                for c in range(2):
                    for kc in range(KD):
                        S.mm(P[0][:, 0:n], w_t[:, kc, c * 128:(c + 1) * 128], hT[:, kc, t0:t0 + n], start=(kc == 0), stop=(kc == KD - 1),
                             R=["wc", "hT"], W=["P0"] if kc == 0 else [], A=["P0"] if kc > 0 else [])
                    S.cp("act", qb[:, 0:n], P[0][:, 0:n], R=["P0"], W=["qb"])
                    S.mm(P[1][:, 0:n], rotC[:], qb[:, 0:n], R=["rotC", "qb"], W=["P1"])
                    if VAR == 1:
                        S.cp("dve", t1[:, 0:n], ct[:, 0:n], R=["ct"], W=["t1"])
                    elif VAR == 2:
                        S.cp("dve", t1[:, 0:n], P[0][:, 0:n], R=["P0"], W=["t1"])
                    elif VAR == 3:
                        S.cp("dve", t1[:, 0:n], X["ones"][:, 0:1].broadcast_to([128, n]) if False else t2[:, 0:n], R=[], W=["t1"])
                    else:
                        S.tt("dve", t1[:, 0:n], P[0][:, 0:n], ct[:, 0:n], ALU.mult, R=["P0", "ct"], W=["t1"])
                    S.tt("dve", t2[:, 0:n], P[1][:, 0:n], sn[:, 0:n], ALU.mult, R=["P1", "sn"], W=["t2"])
                    S.tt("pool", t1[:, 0:n], t1[:, 0:n], t2[:, 0:n], ALU.add, R=["t1", "t2"], W=["t1"])
                    if which == "k":
                        S.act(dst[:, c, 0:n], t1[:, 0:n], AF.Copy, scale=sc, R=["t1"], W=[which + "T"])
                    else:
                        for half in range(2):
                            hr = slice(64 * half, 64 * half + 64)
                            S.act(dst[hr, c, half, 0:n], t1[hr, 0:n], AF.Copy, R=["t1"], W=["qT"])
                    if which == "q":
                        for tt_ in range(ntl):
                            S.tt("pool", qdT[:, c, tt_ * 128:(tt_ + 1) * 128], t1[:, tt_ * 128:(tt_ + 1) * 128],
                                 tab["qdec" + ty][:, c * 128:(c + 1) * 128], ALU.mult, R=["t1", "tab"], W=["qdT"])
            if CUT <= 1:
                continue
            for tt_ in range(ntl):
                tok0 = t0 + tt_ * 128
                tc = slice(tt_ * 128, (tt_ + 1) * 128)
                for kc in range(KD):
                    S.mm(P[2][:], hT[:, kc, tok0:tok0 + 128], wv[:, kc, :], start=(kc == 0), stop=(kc == KD - 1),
                         R=["wc", "hT"], W=["P2"] if kc == 0 else [], A=["P2"] if kc > 0 else [])
                S.cp("act", v_t[:], P[2][:], R=["P2"], W=["v_t"])
                for c in range(2):
                    S.tr(PT[:, c * 128:(c + 1) * 128], kT[:, c, tc], identb[:], R=["kT", "identb"],
                         W=["PT"] if c == 0 else [], A=["PT"] if c > 0 else [])
                S.tt("dve", kd_t[:], PT[:, 0:256], tab["kdec" + ty][:], ALU.mult, R=["PT", "tab"], W=["kd_t"])
                if CUT <= 2:
                    continue
                for h in range(4):
                    c, rows = h // 2, slice(64 * (h % 2), 64 * (h % 2) + 64)
                    S.mm(P[3][:, h * 128:(h + 1) * 128], kT[:, c, tc], qT[:, c, h % 2, tc], R=["kT", "qT"],
                         W=["P3"] if h == 0 else [], A=["P3"] if h > 0 else [])
                S.tt("dve", attn[:], P[3][:], tab["dtc" + ty][:], ALU.mult, R=["P3", "tab"], W=["attn"])
                if CUT <= 3:
                    continue
                bs = 64
                rbs = []
                for b in range(2):
                    br = slice(b * bs, (b + 1) * bs)
                    sq_ = 2 * (tok0 // 128 - NTT) + b
                    if sample:
                        S.dma("sp", R32[:], I["sR"][l, sq_].rearrange("pr p d -> p pr d"), W=["R32"])
                    rb, rbk = Rb[b], "Rb%d" % b
                    rbs.append((rb, rbk))
                    S.cp("act", rb[:], R32[:], R=["R32"], W=[rbk])
                    for pr in range(2):
                        S.mm(P[4][:, pr * 256:(pr + 1) * 256], kd_t[br, pr * 128:(pr + 1) * 128], v_t[br, pr * 256:(pr + 1) * 256],
                             R=["kd_t", "v_t"], W=["P4"] if pr == 0 else [], A=["P4"] if pr > 0 else [])
                    for pr in range(2):
                        for half in range(2):
                            rows = slice(64 * half, 64 * half + 64)
                            S.stt("dve", R32[rows, pr, :], R32[rows, pr, :], tab["gC" + ty][rows, pr:pr + 1],
                                  P[4][rows, pr * 256 + half * 128:pr * 256 + half * 128 + 128], ALU.mult, ALU.add,
                                  R=["R32", "P4", "tab"], W=["R32"])
                    if sample:
                        S.dma("sp", O["Rs"][l, sq_].rearrange("pr p d -> p pr d"), R32[:], R=["R32"])
                    elif tok0 + 128 == cfg.T and b == 1:
                        S.dma("sp", O["Rp"][l].rearrange("pr p d -> p pr d"), R32[:], R=["R32"])
                if CUT <= 4:
                    continue
                for h in range(4):
                    c, rows = h // 2, slice(64 * (h % 2), 64 * (h % 2) + 64)
                    S.mm(P[5][:, h * 128:(h + 1) * 128], v_t[:, h * 128:(h + 1) * 128], attn[:, h * 128:(h + 1) * 128],
                         start=True, stop=False, R=["v_t", "attn"], W=["P5"] if h == 0 else [], A=["P5"] if h > 0 else [])
                    for b in range(2):
                        rb, rbk = rbs[b]
                        S.mm(P[5][:, h * 128 + b * bs:h * 128 + (b + 1) * bs], rb[rows, c, :], qdT[rows, c, tt_ * 128 + b * bs:tt_ * 128 + (b + 1) * bs],
                             start=False, stop=(b == 1), R=[rbk, "qdT"], A=["P5"])
                if CUT <= 5:
                    continue
                _gated_norm_epilogue(X, ebuf, P[5], "P5", 0, cog[:, 0:1], "cog", tok0, ocs, "ocs", tc, P[2], "P2", P[6], "P6", wz, "wc")
            if CUT <= 6:
                continue
            S.dma("sp", oT_d[8 * 128:12 * 128, t0:t0 + n].rearrange("(c p) n -> p c n", p=128), ocs[:, :, 0:n], R=["ocs"])


def mixer_b(X):
    nc, S, cfg, l, I, O, P, PT, hT = X["nc"], X["S"], X["cfg"], X["l"], X["I"], X["O"], X["P"], X["PT"], X["hT"]
    KD, NTT, NS, T = cfg.KD, cfg.NTT, cfg.NS, cfg.T
    w_in, oT_d, ident, ones, onesb, wview = X["w_in"], X["oT_d"], X["ident"], X["ones"], X["onesb"], X["wview"]
    with contextlib.ExitStack() as ph:
        sbp = lambda n, s, d=F32: ph.enter_context(nc.sbuf_tensor("b%d_%s" % (l, n), list(s), d))
        tab = {}
        for ty in "ps":
            for nm, w in (("tri", 128), ("blk", 128), ("negS", 128), ("negIT", 128), ("valid", 1)):
                tab[nm + ty] = sbp(nm + ty, [128, w])
                S.dma("sp", tab[nm + ty][:], I["c_%s_%s" % (nm, ty)], W=["tab"])
        bog = sbp("bog", [128, 1]); S.dma("sp", bog[:], I["bog"][l], W=["bog"])
        cw = sbp("cw", [128, 12, 4]); S.dma("sp", cw[:], I["convw"][l], W=["cw"])
        alog = sbp("alog", [128, 4]); S.dma("sp", alog[:], I["alogB"][l], W=["alog"])
        dtb = sbp("dtb", [128, 4]); S.dma("sp", dtb[:], I["dtbB"][l], W=["dtb"])
        S.act(alog[:], alog[:], AF.Exp, R=["alog"], W=["alog"])
        wqh = [sbp("wqh%d" % i, [128, KD, 3, 128], BF16) for i in range(2)]
        wz = sbp("wz", [128, KD, 512], BF16)
        wbg = sbp("wbg", [128, KD, 8], BF16)
        S.dma("pool", wz[:], wview(w_in, BZ, 512), W=["wb"])
        S.dma("pool", wbg[:], wview(w_in, BBETA, 8), W=["wb"])
        pre = sbp("pre", [128, 3 + 512])
        carry = sbp("carry", [128, 12, 3])
        S.memset("pool", carry[:], 0.0, W=["carry"])
        pres = sbp("pres", [128, 4, 67])
        cst = sbp("cst", [12, 1536]); S.dma("sp", cst[:], I["sconv"][l], W=["cst"])
        cso = sbp("cso", [128, 12, 12])
        csin = sbp("csin", [128, 12, 12])
        cpo = sbp("cpo", [128, 12, 3])
        cout = sbp("cout", [12, 1536])
        acc = sbp("acc", [128, 512])
        yv = sbp("yv", [128, 512])
        sqb = sbp("sqb", [128, 512], BF16)
        rn = sbp("rn", [128, 512])
        fT = [{r: sbp("%sT%d" % (r, h), [128, 512]) for r in "qkv"} for h in range(4)]
        bgt = sbp("bgt", [128, 4, 8])
        nbe = sbp("nbe", [128, 4, 4])
        BN = ("sA", "sB", "sC", "egc", "Nm", "NTm", "PTm", "attnT", "Xb0", "Xb1", "XTb0", "XTb1", "u_", "wT", "qgT", "vnew", "k_tm", "v_tm")
        B = [{nm: sbp("%s_%d" % (nm, h), [128, 128]) for nm in BN} for h in range(4)]
        gcsb = [sbp("gcs%d" % h, [128, 4]) for h in range(4)]
        S32 = [sbp("S32_%d" % h, [128, 128]) for h in range(4)]
        ob = sbp("ob", [128, 4, 512])
        ebuf = (sbp("e_sq", [128, 512], BF16), sbp("e_rs", [128, 512]), sbp("e_on", [128, 512]), sbp("e_sz", [128, 512]))
        obs = sbp("obs", [128, 4, 512], BF16)
        for h in range(4):
            S.memset("pool", S32[h][:], 0.0, W=["S32_%d" % h])
        for c in range(12):
            S.tr(P[0][:, 0:12], cst[0:12, c * 128:(c + 1) * 128], ident[0:12, 0:12], R=["cst", "ident"], W=["P0"])
            S.cp("dve", csin[:, c, :], P[0][:, 0:12], R=["P0"], W=["csin"])

        def tile_chain(Sx, h, tt_, tok0, sample, ty, nlev):
            bb, pb, pk = B[h], P[1 + h], "P%d" % (1 + h)
            k_ = lambda nm: "%s_%d" % (nm, h)
            qT, kT, vT = fT[h]["q"], fT[h]["k"], fT[h]["v"]
            gcs, gk = gcsb[h], "gcs%d" % h
            s32, s32k = S32[h], "S32_%d" % h
            tc = slice(tt_ * 128, (tt_ + 1) * 128)
            be = bgt[:, tt_, h:h + 1]
            gg = bgt[:, tt_, 4 + h:5 + h]
            A_, B_, C_, D_ = pb[:, 0:128], pb[:, 128:256], pb[:, 256:384], pb[:, 384:512]
            Sx.tr(A_, kT[:, tc], ident[:], R=[k_("kT"), "ident"], W=[pk])
            Sx.tr(B_, vT[:, tc], ident[:], R=[k_("vT"), "ident"], A=[pk])
            Sx.cp("act", bb["k_tm"][:], A_, R=[pk], W=[k_("k_tm")])
            Sx.cp("dve", bb["v_tm"][:], B_, R=[pk], W=[k_("v_tm")])
            Sx.ts("dve", bb["sA"][:], ones[:], gg, None, ALU.mult, R=["ones", "bgt"], W=[k_("sA")])
            Sx.mm(A_, bb["sA"][:], tab["tri" + ty][:], R=[k_("sA"), "tab"], W=[pk])
            Sx.mm(pb[:, 128:129], tab["tri" + ty][:], gg, R=["tab", "bgt"], A=[pk])
            Sx.mm(pb[:, 129:130], tab["blk" + ty][:], gg, R=["tab", "bgt"], A=[pk])
            Sx.cp("dve", gcs[:, 0:2], pb[:, 128:130], R=[pk], W=[gk])
            Sx.stt("dve", bb["sA"][:], A_, gcs[:, 0:1], tab["negS" + ty][:], ALU.subtract, ALU.subtract, R=[pk, gk, "tab"], W=[k_("sA")])
            Sx.act(bb["sB"][:], bb["sA"][:], AF.Exp, scale=-1.0, R=[k_("sA")], W=[k_("sB")])
            Sx.stt("dve", bb["sA"][:], A_, gcs[:, 0:1], tab["negIT" + ty][:], ALU.subtract, ALU.add, R=[pk, gk, "tab", k_("sB")], W=[k_("sA")])
            Sx.act(bb["sC"][:], bb["sA"][:], AF.Exp, R=[k_("sA")], W=[k_("sC")])
            Sx.act(bb["egc"][:], A_, AF.Exp, R=[pk], W=[k_("egc")])
            Sx.tt("dve", gcs[:, 2:3], gcs[:, 1:2], gcs[:, 0:1], ALU.subtract, R=[gk], W=[gk])
            Sx.act(gcs[:, 2:3], gcs[:, 2:3], AF.Exp, R=[gk], W=[gk])
            Sx.tt("dve", gcs[:, 2:3], gcs[:, 2:3], tab["valid" + ty][:, 0:1], ALU.mult, R=[gk, "tab"], W=[gk])
            Sx.act(gcs[:, 3:4], gcs[:, 0:1], AF.Exp, R=[gk], W=[gk])
            Sx.tt("dve", gcs[:, 3:4], gcs[:, 3:4], be, ALU.mult, R=[gk, "bgt"], W=[gk])
            Sx.mm(C_, kT[:, tc], kT[:, tc], R=[k_("kT")], W=[pk])
            Sx.mm(D_, kT[:, tc], qT[:, tc], R=[k_("kT"), k_("qT")], A=[pk])
            Sx.stt("dve", bb["Nm"][:], C_, nbe[:, tt_, h:h + 1], bb["sB"][:], ALU.mult, ALU.mult, R=[pk, "nbe", k_("sB")], W=[k_("Nm")])
            Sx.tt("dve", bb["attnT"][:], D_, bb["sC"][:], ALU.mult, R=[pk, k_("sC")], W=[k_("attnT")])
            Sx.tr(A_, bb["Nm"][:], ident[:], R=[k_("Nm"), "ident"], W=[pk])
            Sx.cp("act", bb["NTm"][:], A_, R=[pk], W=[k_("NTm")])
            Sx.tt("pool", bb["PTm"][:], bb["NTm"][:], ident[:], ALU.add, R=[k_("NTm"), "ident"], W=[k_("PTm")])
            Xc, Xk, XTc, XTk = bb["Nm"], k_("Nm"), bb["NTm"], k_("NTm")
            for m in range(1, nlev + 1):
                X2, X2k = bb["Xb%d" % (m % 2)], k_("Xb%d" % (m % 2))
                XT2, XT2k = bb["XTb%d" % (m % 2)], k_("XTb%d" % (m % 2))
                Sx.mm(B_, XTc[:], Xc[:], R=[Xk, XTk], W=[pk])
                if m < nlev:
                    Sx.mm(C_, Xc[:], XTc[:], R=[Xk, XTk], A=[pk])
                Sx.cp("act", X2[:], B_, R=[pk], W=[X2k])
                if m < nlev:
                    Sx.cp("dve", XT2[:], C_, R=[pk], W=[XT2k])
                Sx.mm(D_, X2[:], bb["PTm"][:], R=[X2k, k_("PTm")], W=[pk])
                Sx.tt("dve", bb["PTm"][:], bb["PTm"][:], D_, ALU.add, R=[k_("PTm"), pk], W=[k_("PTm")])
                Xc, Xk, XTc, XTk = X2, X2k, XT2, XT2k
            Sx.ts("dve", bb["sA"][:], bb["v_tm"][:], be, None, ALU.mult, R=[k_("v_tm"), "bgt"], W=[k_("sA")])
            Sx.ts("pool", bb["sB"][:], bb["k_tm"][:], gcs[:, 3:4], None, ALU.mult, R=[k_("k_tm"), gk], W=[k_("sB")])
            Sx.mm(A_, bb["PTm"][:], bb["sA"][:], R=[k_("PTm"), k_("sA")], W=[pk])
            Sx.mm(B_, bb["sB"][:], bb["PTm"][:], R=[k_("PTm"), k_("sB")], A=[pk])
            Sx.cp("act", bb["u_"][:], A_, R=[pk], W=[k_("u_")])
            Sx.cp("dve", bb["wT"][:], B_, R=[pk], W=[k_("wT")])
            Sx.tt("pool", bb["qgT"][:], qT[:, tc], bb["egc"][:], ALU.mult, R=[k_("qT"), k_("egc")], W=[k_("qgT")])
            Sx.ts("pool", bb["sC"][:], bb["k_tm"][:], gcs[:, 2:3], None, ALU.mult, R=[k_("k_tm"), gk], W=[k_("sC")])
            for b in range(2):
                rows = slice(64 * b, 64 * b + 64)
                sq_ = 2 * (tok0 // 128 - NTT) + b
                if sample:
                    Sx.dma("sp", s32[:], I["sS"][l, sq_, h], W=[s32k])
                Sx.mm(C_, bb["wT"][:], s32[:], R=[k_("wT"), s32k], W=[pk])
                Sx.tt("dve", bb["vnew"][rows, :], bb["u_"][rows, :], pb[rows, 256:384], ALU.subtract, R=[k_("u_"), pk], W=[k_("vnew")])
                oc = slice(384 + 64 * b, 384 + 64 * b + 64)
                Sx.mm(pb[:, oc], s32[:], bb["qgT"][:, rows], start=True, stop=False, R=[s32k, k_("qgT")], W=[pk])
                Sx.mm(pb[:, oc], bb["vnew"][rows, :], bb["attnT"][rows, rows], start=False, stop=True, R=[k_("vnew"), k_("attnT")], A=[pk])
                Sx.cp("act", ob[:, h, tt_ * 128 + 64 * b:tt_ * 128 + 64 * b + 64], pb[:, oc], R=[pk], W=["ob%d" % h])
                Sx.mm(A_, bb["sC"][rows, :], bb["vnew"][rows, :], R=[k_("sC"), k_("vnew")], W=[pk])
                Sx.stt("dve", s32[:], s32[:], bb["egc"][:, 64 * b + 63:64 * b + 64], A_, ALU.mult, ALU.add,
                       R=[s32k, k_("egc"), pk], W=[s32k])
                if sample:
                    Sx.dma("sp", O["Ss"][l, sq_, h], s32[:], R=[s32k])
                elif tok0 + 128 == T and b == 1:
                    Sx.dma("sp", O["Sp"][l, h], s32[:], R=[s32k])

        for (t0, n) in X["blocks"]:
            ntl = n // 128
            sample = (t0 // 128 >= NTT)
            ty = "s" if sample else "p"
            nlev = 2 if sample else 5
            for tt_ in range(ntl):
                tok0 = t0 + tt_ * 128
                for kc in range(KD):
                    S.mm(P[0][:, 0:8], hT[:, kc, tok0:tok0 + 128], wbg[:, kc, :], start=(kc == 0), stop=(kc == KD - 1),
                         R=["wb", "hT"], W=["P0"] if kc == 0 else [], A=["P0"] if kc > 0 else [])
                S.act(bgt[:, tt_, 0:4], P[0][:, 0:4], AF.Sigmoid, R=["P0"], W=["bgt"])
                S.tt("dve", bgt[:, tt_, 4:8], P[0][:, 4:8], dtb[:], ALU.add, R=["P0", "dtb"], W=["bgt"])
                S.act(bgt[:, tt_, 4:8], bgt[:, tt_, 4:8], AF.Exp, R=["bgt"], W=["bgt"])
                S.act(bgt[:, tt_, 4:8], bgt[:, tt_, 4:8], AF.Ln, bias=1.0, R=["bgt"], W=["bgt"])
                S.stt("dve", bgt[:, tt_, 4:8], bgt[:, tt_, 4:8], -1.0, alog[:], ALU.mult, ALU.mult, R=["bgt", "alog"], W=["bgt"])
                S.ts("dve", bgt[:, tt_, 4:8], bgt[:, tt_, 4:8], tab["valid" + ty][:, 0:1], None, ALU.mult, R=["bgt", "tab"], W=["bgt"])
                S.ts("dve", nbe[:, tt_, :], bgt[:, tt_, 0:4], -1.0, None, ALU.mult, R=["bgt"], W=["nbe"])
            for h in range(4):
                wq_t, wqk = wqh[h % 2], "wqh%d" % (h % 2)
                for ri in range(3):
                    S.dma("pool", wq_t[:, :, ri, :], wview(w_in, BQKV + (ri * 4 + h) * 128, 128), W=[wqk])
                for ri, (role, c) in enumerate((("q", h), ("k", 4 + h), ("v", 8 + h))):
                    pp, ppk = (P[0], "P0") if (ri % 2 == 0) else (P[6], "P6")
                    for kc in range(KD):
                        S.mm(pp[:, 0:n], wq_t[:, kc, ri, :], hT[:, kc, t0:t0 + n], start=(kc == 0), stop=(kc == KD - 1),
                             R=[wqk, "hT"], W=[ppk] if kc == 0 else [], A=[ppk] if kc > 0 else [])
                    if not sample:
                        S.cp("pool", pre[:, 0:3], carry[:, c, :], R=["carry"], W=["pre"])
                        S.cp("act", pre[:, 3:3 + n], pp[:, 0:n], R=[ppk], W=["pre"])
                        S.ts("dve", acc[:, 0:n], pre[:, 0:n], cw[:, c, 0:1], None, ALU.mult, R=["pre", "cw"], W=["acc"])
                        for i in range(1, 4):
                            S.stt("dve", acc[:, 0:n], pre[:, i:i + n], cw[:, c, i:i + 1], acc[:, 0:n], ALU.mult, ALU.add,
                                  R=["pre", "cw", "acc"], W=["acc"])
                        if t0 + n == T:
                            S.cp("pool", cpo[:, c, :], pre[:, n:n + 3], R=["pre"], W=["cpo"])
                        S.cp("pool", carry[:, c, :], pre[:, n:n + 3], R=["pre"], W=["carry"])
                        accv = acc[:, 0:n]
                    else:
                        S.cp("pool", pres[:, :, 0:3], csin[:, c, :].rearrange("p (s k) -> p s k", s=4), R=["csin"], W=["pres"])
                        S.cp("act", pres[:, :, 3:67], pp[:, 0:256].rearrange("p (s k) -> p s k", s=4), R=[ppk], W=["pres"])
                        a3 = acc[:, 0:256].rearrange("p (s k) -> p s k", s=4)
                        S.ts("dve", a3, pres[:, :, 0:64], cw[:, c, 0:1], None, ALU.mult, R=["pres", "cw"], W=["acc"])
                        for i in range(1, 4):
                            S.stt("dve", a3, pres[:, :, i:i + 64], cw[:, c, i:i + 1], a3, ALU.mult, ALU.add, R=["pres", "cw", "acc"], W=["acc"])
                        S.cp("pool", cso[:, c, :].rearrange("p (s k) -> p s k", s=4), pres[:, :, 8:11], R=["pres"], W=["cso"])
                        accv = acc[:, 0:n]
                    fk = "%sT_%d" % (role, h)
                    if role == "v":
                        S.act(fT[h]["v"][:, 0:n], accv, AF.Silu, R=["acc"], W=[fk])
                    else:
                        S.act(yv[:, 0:n], accv, AF.Silu, R=["acc"], W=["yv"])
                        S.act(sqb[:, 0:n], yv[:, 0:n], AF.Square, R=["yv"], W=["sqb"])
                        S.mm(P[5][:, 0:n], onesb[:], sqb[:, 0:n], R=["onesb", "sqb"], W=["P5"])
                        X["rstd"](rn[:, 0:n], P[5][:, 0:n], 1.0, ["P5"], ["rn"])
                        sc = 128.0 ** -0.5 if role == "q" else 1.0
                        S.stt("dve", fT[h][role][:, 0:n], yv[:, 0:n], sc, rn[:, 0:n], ALU.mult, ALU.mult, R=["yv", "rn"], W=[fk])
            recs = []
            for h in range(4):
                r = Rec()
                for tt_ in range(ntl):
                    tile_chain(r, h, tt_, t0 + tt_ * 128, sample, ty, nlev)
                recs.append(r)
            replay_interleaved(S, recs)
            for tt_ in range(ntl):
                tok0 = t0 + tt_ * 128
                tc = slice(tt_ * 128, (tt_ + 1) * 128)
                _gated_norm_epilogue(X, ebuf, ob[:, :, tc], ["ob0", "ob1", "ob2", "ob3"], 0, bog[:, 0:1], "bog", tok0, obs, "obs", tc,
                                     P[0], "P0", P[5], "P5", wz, "wb")
            S.dma("sp", oT_d[4 * 128:8 * 128, t0:t0 + n].rearrange("(c p) n -> p c n", p=128), obs[:, :, 0:n], R=["obs"])
        for (src, sk, nrow, dst) in ((cpo, "cpo", 3, O["convp"][l]), (cso, "cso", 12, O["convs"][l])):
            for g4 in range(3):
                pb, pk = P[g4], "P%d" % g4
                for c4 in range(4):
                    c = g4 * 4 + c4
                    S.tr(pb[0:nrow, c4 * 128:(c4 + 1) * 128], src[:, c, 0:nrow], ident[:], R=[sk, "ident"],
                         W=[pk] if c4 == 0 else [], A=[pk] if c4 else [])
                S.cp("dve", cout[0:nrow, g4 * 512:(g4 + 1) * 512], pb[0:nrow, :], R=[pk], W=["cout"])
            S.dma("sp", dst, cout[0:nrow, :], R=["cout"])


def mixer_a(X):
    nc, S, cfg, l, I, O, P, PT, hT = X["nc"], X["S"], X["cfg"], X["l"], X["I"], X["O"], X["P"], X["PT"], X["hT"]
    KD, NTT, NS, T, NTOK = cfg.KD, cfg.NTT, cfg.NS, cfg.T, cfg.NTOK
    w_in, oT_d, ident, identb, onesb, wview = X["w_in"], X["oT_d"], X["ident"], X["identb"], X["onesb"], X["wview"]
    KEEP = (128, 512, min(2048, T))
    DIL = (1, 4, 16)
    sc_ = 128.0 ** -0.5
    with contextlib.ExitStack() as ph:
        sbp = lambda n, s, d=F32: ph.enter_context(nc.sbuf_tensor("a%d_%s" % (l, n), list(s), d))
        rotA = sbp("rotA", [128, 128], BF16); S.dma("pool", rotA[:], I["c_rotA"], W=["rotA"])
        mcur = sbp("mcur", [128, 128], BF16); S.dma("pool", mcur[:], I["c_mcur"][:, 0:128], W=["mask"])
        mprev = sbp("mprev", [128, 128], BF16); S.dma("pool", mprev[:], I["c_mprev"][:, 0:128], W=["mask"])
        msc = sbp("msc", [128, 416], BF16); S.dma("pool", msc[:], I["c_msc"], W=["mask"])
        msn = sbp("msn", [128, 192], BF16); S.dma("pool", msn[:], I["c_msn"], W=["mask"])
        QTs = sbp("QTs", [128, 4, 3, 4, 8], BF16); KTs = sbp("KTs", [128, 4, 3, 256], BF16)
        Vs = sbp("Vs", [128, 2, 12, 128], BF16)
        OaccS = sbp("OaccS", [128, 4, 256]); DaccS = sbp("DaccS", [128, 4, 256])
        S.memset("pool", OaccS[:], 0.0, W=["OaccS"]); S.memset("pool", DaccS[:], 0.0, W=["DaccS"])
        aqg = sbp("aqg", [128, 1]); S.dma("sp", aqg[:], I["aqg"][l], W=["aqg"])
        akg = sbp("akg", [128, 1]); S.dma("sp", akg[:], I["akg"][l], W=["aqg"])
        nshift = sbp("nshift", [128, 1]); S.memset("pool", nshift[:], -SHIFT, W=["nshift"])
        rn = sbp("rn", [128, 512]); ost = sbp("ost", [128, 512], BF16)
        hs = contextlib.ExitStack()
        sbo = sbp
        sbp = lambda n, s, d=F32: hs.enter_context(nc.sbuf_tensor("a%d_%s" % (l, n), list(s), d))
        wset = [[sbp("w%s%d" % (nm, i), [128, KD, 128], BF16) for nm in "qkv"] for i in range(2)]
        ct = sbp("ct", [128, 512]); sn = sbp("sn", [128, 512])
        QT = sbp("QT", [128, NTOK], BF16); KT = sbp("KT", [128, NTOK], BF16); K32 = sbp("K32", [128, NTOK])
        V = sbp("V", [128, cfg.NT, 128], BF16)
        Oacc = sbp("Oacc", [128, NTOK]); Dacc = sbp("Dacc", [128, NTOK])
        sqb2 = [sbp("sqb%d" % i, [128, 512], BF16) for i in range(2)]; rn2 = [sbp("rn%d" % i, [128, 512]) for i in range(2)]
        qb2 = [sbp("qb%d" % i, [128, 512], BF16) for i in range(2)]
        t12 = [sbp("t1%d" % i, [128, 512]) for i in range(2)]; t22 = [sbp("t2%d" % i, [128, 512]) for i in range(2)]
        pT2 = [sbp("pT%d" % i, [128, 128], BF16) for i in range(2)]
        v32 = sbp("v32", [128, 128]); k32 = sbp("k32", [128, 128])
        for h in range(4):
            S.memset("pool", Oacc[:], 0.0, W=["Oacc"])
            S.memset("pool", Dacc[:], 0.0, W=["Dacc"])
            for g in range(3):
                dil, keep = DIL[g], KEEP[g]
                kvp, kvs, kvin = O["kvp%d" % g], O["kvs%d" % g], I["kv%d" % g]
                wi = (h * 3 + g) % 2
                wq, wk, wv = wset[wi]
                wak = "wa%d" % wi
                for t_, i3 in ((wq, 0), (wk, 1), (wv, 2)):
                    S.dma("pool", t_[:], wview(w_in, ((i3 * 3 + g) * 4 + h) * 128, 128), W=[wak])
                for (t0, n) in X["blocks"]:
                    S.dma("sp", ct[:, 0:n], I["c_cosA"][:, t0:t0 + n], W=["ct"])
                    S.dma("sp", sn[:, 0:n], I["c_sinA"][:, t0:t0 + n], W=["sn"])
                    recs = []
                    for si, (which, w_t, gcol) in enumerate((("q", wq, aqg), ("k", wk, akg))):
                        Sx = Rec()
                        Pa, Pb_, Pc = (P[0], P[1], P[2]) if si == 0 else (P[4], P[5], P[6])
                        ka, kb_, kc_ = ("P0", "P1", "P2") if si == 0 else ("P4", "P5", "P6")
                        sq_, rn_, qb_, t1_, t2_ = sqb2[si], rn2[si], qb2[si], t12[si], t22[si]
                        sfx = str(si)
                        for kc in range(KD):
                            Sx.mm(Pa[:, 0:n], w_t[:, kc, :], hT[:, kc, t0:t0 + n], start=(kc == 0), stop=(kc == KD - 1),
                                  R=[wak, "hT"], W=[ka] if kc == 0 else [], A=[ka] if kc > 0 else [])
                        Sx.act(sq_[:, 0:n], Pa[:, 0:n], AF.Square, R=[ka], W=["sqb" + sfx])
                        Sx.mm(Pb_[:, 0:n], onesb[:], sq_[:, 0:n], R=["onesb", "sqb" + sfx], W=[kb_])
                        Sx.act(rn_[:, 0:n], Pb_[:, 0:n], AF.Ln, bias=EPS, scale=1.0 / 128, R=[kb_], W=["rn" + sfx])
                        Sx.act(rn_[:, 0:n], rn_[:, 0:n], AF.Exp, scale=-0.5, R=["rn" + sfx], W=["rn" + sfx])
                        Sx.stt("dve", qb_[:, 0:n], Pa[:, 0:n], gcol[:, 0:1], rn_[:, 0:n], ALU.mult, ALU.mult, R=[ka, "aqg", "rn" + sfx], W=["qb" + sfx])
                        Sx.mm(Pc[:, 0:n], rotA[:], qb_[:, 0:n], R=["rotA", "qb" + sfx], W=[kc_])
                        Sx.tt("dve", t1_[:, 0:n], qb_[:, 0:n], ct[:, 0:n], ALU.mult, R=["qb" + sfx, "ct"], W=["t1" + sfx])
                        Sx.tt("dve", t2_[:, 0:n], Pc[:, 0:n], sn[:, 0:n], ALU.mult, R=[kc_, "sn"], W=["t2" + sfx])
                        if which == "q":
                            Sx.tt("pool", QT[:, t0:t0 + n], t1_[:, 0:n], t2_[:, 0:n], ALU.add, R=["t1" + sfx, "t2" + sfx], W=["QT"])
                        else:
                            Sx.tt("pool", K32[:, t0:t0 + n], t1_[:, 0:n], t2_[:, 0:n], ALU.add, R=["t1" + sfx, "t2" + sfx], W=["K32"])
                            Sx.cp("act", KT[:, t0:t0 + n], K32[:, t0:t0 + n], R=["K32"], W=["KT"])
                        recs.append(Sx)
                    replay_interleaved(S, recs)
                S.cp("pool", QTs[:, h, g, :, :], QT[:, T:T + 256].rearrange("p (s k) -> p s k", s=4)[:, :, 0:8], R=["QT"], W=["QTs"])
                S.cp("pool", KTs[:, h, g, :], KT[:, T:T + 256], R=["KT"], W=["KTs"])
                nbk = T // (128 * dil)
                def tcols(r, nb):
                    st0 = dil * 128 * nb + r
                    return slice(st0, st0 + dil * 127 + 1, dil)
                tiles = [(r, nb, tcols(r, nb), r * nbk + nb) for r in range(dil) for nb in range(nbk)]
                tiles += [(None, s3, slice(T + 128 * s3, T + 128 * s3 + 128), NTT + s3) for s3 in range(2)]
                for vi, (r, nb, cols, ti) in enumerate(tiles):
                    pv, pvk = P[(0, 1, 2, 3)[vi % 4]], "P%d" % (vi % 4)
                    for kc in range(KD):
                        S.mm(pv[:, 0:128], hT[:, kc, cols], wv[:, kc, :], start=(kc == 0), stop=(kc == KD - 1),
                             R=[wak, "hT"], W=[pvk] if kc == 0 else [], A=[pvk] if kc > 0 else [])
                    S.cp("act", V[:, ti, :], pv[:, 0:128], R=[pvk], W=["V"])
                    if r is None:
                        S.cp("dve", Vs[:, nb, g * 4 + h, :], pv[:, 0:128], R=[pvk], W=["Vs"])
                        S.cp("dve", v32[:], pv[:, 0:128], R=[pvk], W=["v32"])
                        for s2 in range(2):
                            S.dma("sp", kvs[l, 2 * nb + s2, :, 1, h, :], v32[64 * s2:64 * s2 + 8, :], R=["v32"])
                    elif dil * 128 * nb >= T - keep:
                        S.cp("dve", v32[:], pv[:, 0:128], R=[pvk], W=["v32"])
                        r0 = dil * 128 * nb + r - (T - keep)
                        S.dma("sp", kvp[l, r0:r0 + dil * 127 + 1:dil, 1, h, :], v32[:], R=["v32"])
                for ti in list(range((T - keep) // 128, NTT)) + [NTT, NTT + 1]:
                    S.tr(P[3][:, 128:256], K32[:, ti * 128:(ti + 1) * 128], ident[:], R=["K32", "ident"], W=["P3"])
                    S.cp("dve", k32[:], P[3][:, 128:256], R=["P3"], W=["k32"])
                    if ti < NTT:
                        r0 = ti * 128 - (T - keep)
                        S.dma("sp", kvp[l, r0:r0 + 128, 0, h, :], k32[:], R=["k32"])
                    else:
                        for s2 in range(2):
                            S.dma("sp", kvs[l, 2 * (ti - NTT) + s2, :, 0, h, :], k32[64 * s2:64 * s2 + 8, :], R=["k32"])
                recs = [Rec(), Rec()]
                bi = 0
                for r in range(dil):
                    for nb in range(nbk):
                        si = bi % 2
                        bi += 1
                        Sx = recs[si]
                        Ps, Po, Pd = (P[0], P[1], P[2]) if si == 0 else (P[4], P[5], P[6])
                        ks, ko, kd = ("P0", "P1", "P2") if si == 0 else ("P4", "P5", "P6")
                        pT_, pTk = pT2[si], "pT%d" % si
                        qc = tcols(r, nb)
                        kbs = [kb for kb in (nb - 1, nb) if kb >= 0]
                        for i, kb in enumerate(kbs):
                            Sx.mm(Ps[:, 0:128], KT[:, tcols(r, kb)], QT[:, qc], R=["KT", "QT"], W=[ks])
                            Sx.act(pT_[:], Ps[:, 0:128], AF.Exp, bias=nshift[:, 0:1], scale=sc_, R=[ks, "nshift"], W=[pTk])
                            Sx.tt("pool", pT_[:], pT_[:], (mcur if kb == nb else mprev)[:], ALU.mult, R=[pTk, "mask"], W=[pTk])
                            Sx.mm(Po[:, 0:128], V[:, r * nbk + kb, :], pT_[:], start=(i == 0), stop=(i == len(kbs) - 1),
                                  R=["V", pTk], W=[ko] if i == 0 else [], A=[ko] if i > 0 else [])
                            Sx.mm(Pd[:, 0:128], onesb[:], pT_[:], start=(i == 0), stop=(i == len(kbs) - 1),
                                  R=["onesb", pTk], W=[kd] if i == 0 else [], A=[kd] if i > 0 else [])
                        Sx.tt("dve", Oacc[:, qc], Oacc[:, qc], Po[:, 0:128], ALU.add, R=["Oacc", ko], W=["Oacc"])
                        Sx.tt("dve", Dacc[:, qc], Dacc[:, qc], Pd[:, 0:128], ALU.add, R=["Dacc", kd], W=["Dacc"])
                replay_interleaved(S, recs)
            for (t0, n) in X["blocks"]:
                if t0 >= T:
                    continue
                S.ts("dve", rn[:, 0:n], Dacc[:, t0:t0 + n], 1e-30, None, ALU.max, R=["Dacc"], W=["rn"])
                S.op("dve", (lambda a, b: (lambda e: e.reciprocal(a, b)))(rn[:, 0:n], rn[:, 0:n]), ["rn"], ["rn"])
                S.tt("dve", ost[:, 0:n], Oacc[:, t0:t0 + n], rn[:, 0:n], ALU.mult, R=["Oacc", "rn"], W=["ost"])
                S.dma("sp", oT_d[h * 128:(h + 1) * 128, t0:t0 + n], ost[:, 0:n], R=["ost"])
        S.barrier()
        hs.close()
        sbp = sbo
        kvc2 = [[sbp("kvc%d%d" % (i, j), [128, 1024]) for j in range(3)] for i in range(2)]
        kcT2 = [sbp("kcT%d" % i, [128, 512], BF16) for i in range(2)]
        vcb2 = [sbp("vcb%d" % i, [128, 512], BF16) for i in range(2)]
        pS2 = [sbp("pS%d" % i, [128, 32], BF16) for i in range(2)]
        for g in range(3):
            dil = DIL[g]
            kvin = I["kv%d" % g]
            idx0 = (0, 1, 5)[g]
            nr = min(dil, 8)
            recs = [Rec(), Rec()]
            for s in range(NS):
                si = s % 2
                Sx = recs[si]
                Pt, Ps, Po = (P[0], P[1], P[2]) if si == 0 else (P[4], P[5], P[6])
                kt, ks, ko = ("P0", "P1", "P2") if si == 0 else ("P4", "P5", "P6")
                pS_, pSk = pS2[si], "pS%d" % si
                kcT_, kcTk = kcT2[si], "kcT%d" % si
                vcb_, vcbk = vcb2[si], "vcb%d" % si
                q0 = 64 * s
                tl = s // 2
                for r in range(nr + 1):
                    if r < nr:
                        kv_, kvk = kvc2[si][r % 3], "kvc%d%d" % (si, r % 3)
                        Sx.dma("sp", kv_[:], kvin[l, s, r:r + dil * 127 + 1:dil, :], W=[kvk])
                        for hh in range(4):
                            Sx.tr(Pt[:, hh * 128:(hh + 1) * 128], kv_[:, hh * 128:(hh + 1) * 128], ident[:], R=[kvk, "ident"],
                                  W=[kt] if hh == 0 else [], A=[kt] if hh else [])
                        Sx.cp("act", kcT_[:], Pt[:], R=[kt], W=[kcTk])
                        Sx.cp("dve", vcb_[:], kv_[:, 512:1024], R=[kvk], W=[vcbk])
                        for hh in range(4):
                            Sx.mm(Ps[:, hh * 8:(hh + 1) * 8], kcT_[:, hh * 128:(hh + 1) * 128], QTs[:, hh, g, s, :], R=[kcTk, "QTs"],
                                  W=[ks] if hh == 0 else [], A=[ks] if hh else [])
                        mk = msc[:, (idx0 + r) * 32:(idx0 + r) * 32 + 32]
                        vvs = [vcb_[:, hh * 128:(hh + 1) * 128] for hh in range(4)]
                    else:
                        for hh in range(4):
                            Sx.mm(Ps[:, hh * 8:(hh + 1) * 8], KTs[:, hh, g, tl * 128:(tl + 1) * 128], QTs[:, hh, g, s, :], R=["KTs", "QTs"],
                                  W=[ks] if hh == 0 else [], A=[ks] if hh else [])
                        mk = msn[:, (g * 2 + s % 2) * 32:(g * 2 + s % 2) * 32 + 32]
                        vvs = [Vs[:, tl, g * 4 + hh, :] for hh in range(4)]
                    Sx.act(pS_[:], Ps[:, 0:32], AF.Exp, bias=nshift[:, 0:1], scale=sc_, R=[ks, "nshift"], W=[pSk])
                    Sx.tt("pool", pS_[:], pS_[:], mk, ALU.mult, R=[pSk, "mask"], W=[pSk])
                    for hh in range(4):
                        Sx.mm(Po[:, hh * 8:(hh + 1) * 8], vvs[hh], pS_[:, hh * 8:(hh + 1) * 8], R=[vcbk, "Vs", pSk],
                              W=[ko] if hh == 0 else [], A=[ko] if hh else [])
                    Sx.mm(Po[:, 32:64], onesb[:], pS_[:], R=["onesb", pSk], A=[ko])
                    Sx.tt("dve", OaccS[:, :, q0:q0 + 8], OaccS[:, :, q0:q0 + 8], Po[:, 0:32].rearrange("p (hh q) -> p hh q", hh=4), ALU.add,
                          R=["OaccS", ko], W=["OaccS"])
                    Sx.tt("dve", DaccS[:, :, q0:q0 + 8], DaccS[:, :, q0:q0 + 8], Po[:, 32:64].rearrange("p (hh q) -> p hh q", hh=4), ALU.add,
                          R=["DaccS", ko], W=["DaccS"])
            replay_interleaved(S, recs)
        for h in range(4):
            S.ts("dve", rn[:, 0:256], DaccS[:, h, :], 1e-30, None, ALU.max, R=["DaccS"], W=["rn"])
            S.op("dve", (lambda a, b: (lambda e: e.reciprocal(a, b)))(rn[:, 0:256], rn[:, 0:256]), ["rn"], ["rn"])
            S.tt("dve", ost[:, 0:256], OaccS[:, h, :], rn[:, 0:256], ALU.mult, R=["OaccS", "rn"], W=["ost"])
            S.dma("sp", oT_d[h * 128:(h + 1) * 128, T:T + 256], ost[:, 0:256], R=["ost"])
```
